# Optimizing a Trainium2 kernel written in Bass

```python
import math
import jax
import jax.numpy as jnp
from jax import lax
import numpy as np

D_MODEL = 1024
BATCH = 1
SEQ = 16384
DEPTH = 4

HEAD_DIM = 64
GLA_HEADS = 4
GLA_DK = 64
GLA_DV = 128
GLA_RANK = 16
GLA_TAU = 16.0
GLA_CHUNK = 64
DIL_PATTERNS = ((128, 1), (512, 4), (2048, 16))
DIL_GROUPS = 3
DIL_HEADS = 4
DIFF_HEADS = 4
DIFF_DV = 2 * HEAD_DIM
DIFF_QBLOCK = 128
HY_WIDTH = 512
HY_ORDER = 2
HY_EMB = 33
HY_FFN = 64
HY_SHORT = 3
HY_DECAY_TARGET = 1e-2
HY_FAST_DECAY = 0.3
HY_SLOW_DECAY = 1.5
T5_BUCKETS = 32
T5_MAX_DIST = 1024
N_BIAS_HEADS = DIL_GROUPS * DIL_HEADS + DIFF_HEADS
D_FF = 4 * D_MODEL
N_BRANCH = 4
RMS_EPS = 1e-6

GLA_QK = GLA_HEADS * GLA_DK
GLA_VW = GLA_HEADS * GLA_DV
DIL_W = DIL_GROUPS * DIL_HEADS * HEAD_DIM
DIL_OUT = DIL_HEADS * HEAD_DIM
DIFF_QK = DIFF_HEADS * 2 * HEAD_DIM
DIFF_VW = DIFF_HEADS * DIFF_DV
HY_PROJ = (HY_ORDER + 1) * HY_WIDTH
SPLIT_SIZES = (GLA_QK, GLA_QK, GLA_VW, GLA_VW, 2 * GLA_RANK, DIL_W, DIL_W, DIL_W, DIFF_QK, DIFF_QK, DIFF_VW, HY_PROJ, N_BRANCH * D_MODEL)
N_IN = sum(SPLIT_SIZES)

kernel_name = 'hybrid_gla_dilated_diff_hyena_encoder'


def _split_points():
    pts, acc = [], 0
    for n in SPLIT_SIZES[:-1]:
        acc += n
        pts.append(acc)
    return pts


def rmsnorm(x, g):
    xf = x.astype(jnp.float32)
    y = xf * lax.rsqrt(jnp.mean(xf * xf, axis=-1, keepdims=True) + RMS_EPS)
    return y.astype(x.dtype) * g


def t5_bucket(rel):
    half = T5_BUCKETS // 2
    max_exact = half // 2
    ret = jnp.where(rel > 0, half, 0)
    n = jnp.abs(rel)
    nf = jnp.maximum(n, 1).astype(jnp.float32)
    large = max_exact + (jnp.log(nf / max_exact) / math.log(T5_MAX_DIST / max_exact) * (half - max_exact)).astype(jnp.int32)
    large = jnp.minimum(large, half - 1)
    return ret + jnp.where(n < max_exact, n, large)


def gla_chunked(q, k, v, g, include_diag):
    B, S, H, dk = q.shape
    dv = v.shape[-1]
    C = GLA_CHUNK
    N = S // C
    f32 = jnp.float32
    q, k, v, g = [t.astype(f32).reshape(B, N, C, H, t.shape[-1]) for t in (q, k, v, g)]
    b = jnp.cumsum(g, axis=2)
    b_last = b[:, :, -1:]
    qg = q * jnp.exp(b)
    kg = k * jnp.exp(-b)
    A = jnp.einsum('bnthk,bnshk->bnhts', qg, kg)
    mask = jnp.tril(jnp.ones((C, C), dtype=bool), 0 if include_diag else -1)
    A = jnp.where(mask, A, 0.0)
    o = jnp.einsum('bnhts,bnshv->bnthv', A, v)
    dS = jnp.einsum('bnshk,bnshv->nbhkv', k * jnp.exp(b_last - b), v)
    decay = jnp.exp(b_last[:, :, 0]).transpose(1, 0, 2, 3)

    def step(state, inp):
        ds, a = inp
        return a[..., None] * state + ds, state

    _, s_prev = lax.scan(step, jnp.zeros((B, H, dk, dv), f32), (dS, decay))
    o = o + jnp.einsum('bnthk,nbhkv->bnthv', qg, s_prev)
    return o.reshape(B, S, H, dv)


def gla_mixer(q, k, v, r, lr, gate_w, gate_b, norm_g):
    B, S, _ = q.shape
    dtype = q.dtype
    q = q.reshape(B, S, GLA_HEADS, GLA_DK) * (GLA_DK ** -0.5)
    k = k.reshape(B, S, GLA_HEADS, GLA_DK)
    v = v.reshape(B, S, GLA_HEADS, GLA_DV)
    lr = lr.reshape(B, S, 2, GLA_RANK)
    logits = jnp.einsum('bsjr,jrk->bsjk', lr, gate_w) + gate_b
    g = (jax.nn.log_sigmoid(logits.astype(jnp.float32)) / GLA_TAU).reshape(B, S, 2, GLA_HEADS, GLA_DK)
    flip = lambda t: jnp.flip(t, axis=1)
    fwd = gla_chunked(q, k, v, g[:, :, 0], True)
    bwd = flip(gla_chunked(flip(q), flip(k), flip(v), flip(g[:, :, 1]), False))
    o = (fwd + bwd).astype(dtype)
    o = rmsnorm(o, norm_g) * jax.nn.silu(r.reshape(B, S, GLA_HEADS, GLA_DV))
    return o.reshape(B, S, GLA_VW)


def dilated_group(q, k, v, dil, half, bias_g):
    B, S, H, dh = q.shape
    W = half
    M = S // dil
    nb = -(-M // W)
    Mp = nb * W
    f32 = jnp.float32

    def to_sub(t):
        t = t.reshape(B, M, dil, H, dh).transpose(0, 2, 1, 3, 4)
        return jnp.pad(t, ((0, 0), (0, 0), (0, Mp - M), (0, 0), (0, 0)))

    def windows(t):
        t = jnp.pad(to_sub(t), ((0, 0), (0, 0), (W, W), (0, 0), (0, 0))).reshape(B, dil, nb + 2, W, H, dh)
        return jnp.concatenate([t[:, :, :-2], t[:, :, 1:-1], t[:, :, 2:]], axis=3)

    qs = to_sub(q).reshape(B, dil, nb, W, H, dh)
    kw, vw = windows(k), windows(v)
    s = jnp.einsum('brnqhe,brnkhe->brnhqk', qs, kw).astype(f32) * (dh ** -0.5)
    a = jnp.arange(W)[:, None]
    c = jnp.arange(3 * W)[None, :]
    delta = c - W - a
    bias = jnp.transpose(bias_g[t5_bucket(delta * dil)], (2, 0, 1)).astype(f32)
    mq = (jnp.arange(nb) * W)[:, None, None] + a[None]
    mk = mq + delta[None]
    valid = (jnp.abs(delta) <= W)[None] & (mk >= 0) & (mk < M)
    s = jnp.where(valid[:, None], s + bias, -1e30)
    lse = jax.nn.logsumexp(s, axis=-1)
    p = jnp.exp(s - lse[..., None])
    o = jnp.einsum('brnhqk,brnkhe->brnqhe', p, vw.astype(f32))
    o = o.reshape(B, dil, Mp, H, dh)[:, :, :M].transpose(0, 2, 1, 3, 4).reshape(B, S, H, dh)
    lse = lse.transpose(0, 1, 2, 4, 3).reshape(B, dil, Mp, H)[:, :, :M].transpose(0, 2, 1, 3).reshape(B, S, H)
    return o, lse


def dilated_mixer(q, k, v, qn_g, kn_g, bias_table):
    B, S, _ = q.shape
    dtype = q.dtype
    shape = (B, S, DIL_GROUPS, DIL_HEADS, HEAD_DIM)
    q = rmsnorm(q.reshape(shape), qn_g[:, None])
    k = rmsnorm(k.reshape(shape), kn_g[:, None])
    v = v.reshape(shape)
    outs, lses = [], []
    for gi, (win, dil) in enumerate(DIL_PATTERNS):
        o, l = dilated_group(q[:, :, gi], k[:, :, gi], v[:, :, gi], dil, win // (2 * dil),
                             bias_table[:, gi * DIL_HEADS:(gi + 1) * DIL_HEADS])
        outs.append(o)
        lses.append(l)
    w = jax.nn.softmax(jnp.stack(lses), axis=0)
    o = jnp.sum(w[..., None] * jnp.stack(outs), axis=0)
    return o.reshape(B, S, DIL_OUT).astype(dtype)


def diff_mixer(q, k, v, qn_g, kn_g, lam_p, subln_g, bias_c, lam_init):
    B, S, _ = q.shape
    dtype = q.dtype
    H, dh = DIFF_HEADS, HEAD_DIM
    f32 = jnp.float32
    q = rmsnorm(q.reshape(B, S, H, 2, dh), qn_g)
    k = rmsnorm(k.reshape(B, S, H, 2, dh), kn_g)
    vf = v.reshape(B, S, H, DIFF_DV).astype(f32)
    lp = lam_p.astype(f32)
    lam = jnp.exp(jnp.sum(lp[0] * lp[1])) - jnp.exp(jnp.sum(lp[2] * lp[3])) + lam_init
    nq = S // DIFF_QBLOCK
    qb = q.reshape(B, nq, DIFF_QBLOCK, H, 2, dh).transpose(1, 0, 2, 3, 4, 5)
    kpos = jnp.arange(S)
    scale = dh ** -0.5

    def block(args):
        qi, i = args
        s = jnp.einsum('bqhce,bkhce->bhcqk', qi, k).astype(f32) * scale
        qpos = i * DIFF_QBLOCK + jnp.arange(DIFF_QBLOCK)
        bias = bias_c[t5_bucket(kpos[None, :] - qpos[:, None])]
        s = s + jnp.transpose(bias, (2, 0, 1))[None, :, None].astype(f32)
        p = jax.nn.softmax(s, axis=-1)
        att = p[:, :, 0] - lam * p[:, :, 1]
        return jnp.einsum('bhqk,bkhe->bqhe', att, vf)

    o = lax.map(block, (qb, jnp.arange(nq)))
    o = o.transpose(1, 0, 2, 3, 4).reshape(B, S, H, DIFF_DV)
    o = rmsnorm(o, subln_g) * (1.0 - lam_init)
    return o.reshape(B, S, DIFF_VW).astype(dtype)


def hyena_filters(L, w1, b1, fr1, w2, b2, fr2, w3):
    f32 = jnp.float32
    t = jnp.linspace(0.0, 1.0, L, dtype=f32)[:, None]
    bands = (HY_EMB - 1) // 2
    freqs = jnp.linspace(1e-4, bands - 1, bands, dtype=f32)[None]
    w = 2.0 * math.pi * jnp.arange(L, dtype=f32)[:, None] / L
    z = jnp.concatenate([t, jnp.cos(freqs * w), -jnp.sin(freqs * w)], axis=-1)
    h = jnp.sin(fr1 * (z @ w1 + b1))
    h = jnp.sin(fr2 * (h @ w2 + b2))
    h = (h @ w3).astype(f32).reshape(L, 2, HY_WIDTH)
    min_decay = math.log(HY_DECAY_TARGET) / HY_SLOW_DECAY
    max_decay = math.log(HY_DECAY_TARGET) / HY_FAST_DECAY
    deltas = jnp.abs(jnp.linspace(min_decay, max_decay, HY_WIDTH, dtype=f32))
    h = h * jnp.exp(-t * deltas)[:, None]
    return h[:, 0], h[:, 1]


def hyena_mixer(u, conv_w, conv_b, w1, b1, fr1, w2, b2, fr2, w3, skip):
    B, L, _ = u.shape
    f32 = jnp.float32
    pad = HY_SHORT // 2
    up = jnp.pad(u, ((0, 0), (pad, pad), (0, 0)))
    u = conv_b + sum(up[:, j:j + L] * conv_w[j] for j in range(HY_SHORT))
    x0, x1, v = jnp.split(u, HY_ORDER + 1, axis=-1)
    z = (x1 * v).astype(f32)
    hf, hb = hyena_filters(L, w1, b1, fr1, w2, b2, fr2, w3)
    kern = jnp.concatenate([hf, jnp.zeros((1, HY_WIDTH), f32), jnp.flip(hb[1:], axis=0)], axis=0)
    kern = kern / jnp.sum(jnp.abs(kern), axis=0, keepdims=True)
    n = 2 * L
    y = jnp.fft.irfft(jnp.fft.rfft(z, n=n, axis=1) * jnp.fft.rfft(kern, n=n, axis=0)[None], n=n, axis=1)[:, :L]
    y = y + z * skip.astype(f32)
    return x0 * y.astype(x0.dtype)


def setup_inputs(seed: int = 0) -> dict:
    key = jax.random.key(seed)
    ks = iter(jax.random.split(key, 48))

    def nrm(shape, scale):
        return jax.random.normal(next(ks), shape, jnp.float32) * scale

    def gain(shape):
        return 1.0 + nrm(shape, 0.02)

    L = DEPTH
    return {
        'x': nrm((BATCH, SEQ, D_MODEL), 1.0),
        't5_bias': nrm((T5_BUCKETS, N_BIAS_HEADS), 0.2),
        'norm1_g': gain((L, D_MODEL)),
        'w_in': nrm((L, D_MODEL, N_IN), D_MODEL ** -0.5),
        'gla_gate_w': nrm((L, 2, GLA_RANK, GLA_QK), GLA_RANK ** -0.5),
        'gla_gate_b': nrm((L, 2, GLA_QK), 0.1),
        'gla_norm_g': gain((L, GLA_DV)),
        'dil_qnorm_g': gain((L, DIL_GROUPS, HEAD_DIM)),
        'dil_knorm_g': gain((L, DIL_GROUPS, HEAD_DIM)),
        'diff_qnorm_g': gain((L, HEAD_DIM)),
        'diff_knorm_g': gain((L, HEAD_DIM)),
        'diff_lambda': nrm((L, 4, HEAD_DIM), 0.1),
        'diff_subln_g': gain((L, DIFF_DV)),
        'hy_conv_w': nrm((L, HY_SHORT, HY_PROJ), HY_SHORT ** -0.5),
        'hy_conv_b': nrm((L, HY_PROJ), 0.02),
        'hy_w1': nrm((L, HY_EMB, HY_FFN), HY_EMB ** -0.5),
        'hy_b1': nrm((L, HY_FFN), 0.1),
        'hy_freq1': gain((L, HY_FFN)),
        'hy_w2': nrm((L, HY_FFN, HY_FFN), HY_FFN ** -0.5),
        'hy_b2': nrm((L, HY_FFN), 0.1),
        'hy_freq2': gain((L, HY_FFN)),
        'hy_w3': nrm((L, HY_FFN, 2 * HY_WIDTH), HY_FFN ** -0.5),
        'hy_skip': nrm((L, HY_WIDTH), 0.5),
        'proj_a': nrm((L, GLA_VW, D_MODEL), GLA_VW ** -0.5),
        'proj_b': nrm((L, DIL_OUT, D_MODEL), DIL_OUT ** -0.5),
        'proj_c': nrm((L, DIFF_VW, D_MODEL), DIFF_VW ** -0.5),
        'proj_d': nrm((L, HY_WIDTH, D_MODEL), HY_WIDTH ** -0.5),
        'w_out': nrm((L, D_MODEL, D_MODEL), D_MODEL ** -0.5),
        'norm2_g': gain((L, D_MODEL)),
        'mlp_w1': nrm((L, D_MODEL, D_FF), D_MODEL ** -0.5),
        'mlp_w2': nrm((L, D_FF, D_MODEL), D_FF ** -0.5),
    }


def reference(x, t5_bias, norm1_g, w_in, gla_gate_w, gla_gate_b, gla_norm_g, dil_qnorm_g, dil_knorm_g,
              diff_qnorm_g, diff_knorm_g, diff_lambda, diff_subln_g, hy_conv_w, hy_conv_b, hy_w1, hy_b1,
              hy_freq1, hy_w2, hy_b2, hy_freq2, hy_w3, hy_skip, proj_a, proj_b, proj_c, proj_d, w_out,
              norm2_g, mlp_w1, mlp_w2):
    B, S, _ = x.shape
    pts = _split_points()
    n_dil_bias = DIL_GROUPS * DIL_HEADS
    for i in range(DEPTH):
        h = rmsnorm(x, norm1_g[i])
        z = h @ w_in[i]
        (a_q, a_k, a_v, a_r, a_lr, b_q, b_k, b_v, c_q, c_k, c_v, d_u, gates) = jnp.split(z, pts, axis=-1)
        ya = gla_mixer(a_q, a_k, a_v, a_r, a_lr, gla_gate_w[i], gla_gate_b[i], gla_norm_g[i])
        yb = dilated_mixer(b_q, b_k, b_v, dil_qnorm_g[i], dil_knorm_g[i], t5_bias[:, :n_dil_bias])
        lam_init = 0.8 - 0.6 * math.exp(-0.3 * i)
        yc = diff_mixer(c_q, c_k, c_v, diff_qnorm_g[i], diff_knorm_g[i], diff_lambda[i], diff_subln_g[i],
                        t5_bias[:, n_dil_bias:], lam_init)
        yd = hyena_mixer(d_u, hy_conv_w[i], hy_conv_b[i], hy_w1[i], hy_b1[i], hy_freq1[i], hy_w2[i],
                         hy_b2[i], hy_freq2[i], hy_w3[i], hy_skip[i])
        gt = jax.nn.sigmoid(gates.reshape(B, S, N_BRANCH, D_MODEL))
        m = (gt[:, :, 0] * (ya @ proj_a[i]) + gt[:, :, 1] * (yb @ proj_b[i])
             + gt[:, :, 2] * (yc @ proj_c[i]) + gt[:, :, 3] * (yd @ proj_d[i]))
        x = x + m @ w_out[i]
        h2 = rmsnorm(x, norm2_g[i])
        x = x + jnp.square(jax.nn.relu(h2 @ mlp_w1[i])) @ mlp_w2[i]
    return x
```

```python
import numpy as np
from contextlib import ExitStack
import concourse.bass as bass
import concourse.mybir as mybir
from concourse.bass_utils import run_bass_kernel_spmd

F32 = mybir.dt.float32
BF16 = mybir.dt.bfloat16
AF = mybir.ActivationFunctionType
ALU = mybir.AluOpType
AX = mybir.AxisListType

class Buf:
    __slots__ = ("t", "w", "r", "name")
    def __init__(self, t, name=""):
        self.t = t; self.w = None; self.r = {}; self.name = name
    def __getitem__(self, idx):
        return self.t[idx]

class Prog:
    NDSEM = 6
    def __init__(self, nc, es):
        self.nc = nc; self.es = es
        self.eng = {"pe": nc.tensor, "act": nc.scalar, "dve": nc.vector, "pool": nc.gpsimd, "sp": nc.sync}
        self.sem = {}; self.cnt = {}
        for e in self.eng:
            self.sem[e] = es.enter_context(nc.semaphore("s_" + e)); self.cnt[e] = 0
        self.seen = {e: {} for e in self.eng}
        self.dq = {}
        for q in ("sp", "pool", "act"):
            self.dq[q] = dict(n=0, sems=[es.enter_context(nc.semaphore(f"d_{q}{i}")) for i in range(self.NDSEM)])
            for i in range(self.NDSEM):
                self.sem[(q, i)] = self.dq[q]["sems"][i]
        self.ninst = 0
    def sbuf(self, shape, dt, name):
        return Buf(self.es.enter_context(self.nc.sbuf_tensor(name, list(shape), dt)), name)
    def psum(self, shape, dt, name):
        return Buf(self.es.enter_context(self.nc.psum_tensor(name, list(shape), dt)), name)
    def dram(self, name, shape, dt, kind="Internal"):
        return Buf(self.nc.dram_tensor(name, list(shape), dt, kind=kind), name)
    def _wait(self, e, tok):
        if tok is None: return
        s, v = tok
        if self.seen[e].get(s, 0) >= v: return
        self.eng[e].wait_ge(self.sem[s], v)
        self.seen[e][s] = v
    def _deps(self, e, reads, writes, same_ok=False):
        for b in reads:
            if b.w is not None and not (same_ok and b.w[0] == e):
                self._wait(e, b.w)
        for b in writes:
            if b.w is not None and not (same_ok and b.w[0] == e):
                self._wait(e, b.w)
            for s, v in b.r.items():
                if same_ok and s == e: continue
                self._wait(e, (s, v))
    def _mark(self, tok, reads, writes):
        for b in reads:
            if b.r.get(tok[0], 0) < tok[1]: b.r[tok[0]] = tok[1]
        for b in writes:
            b.w = tok; b.r = {}
    def op(self, e, reads, writes, fn, same_ok=False):
        self._deps(e, reads, writes, same_ok)
        ins = fn(self.eng[e])
        self.cnt[e] += 1
        ins.then_inc(self.sem[e], 1)
        tok = (e, self.cnt[e])
        self.seen[e][e] = max(self.seen[e].get(e, 0), 0)
        self._mark(tok, reads, writes)
        self.ninst += 1
        return tok
    def dma(self, q, out_ap, in_ap, reads, writes, **kw):
        d = self.dq[q]; j = d["n"]; slot = j % self.NDSEM; val = 16 * (j // self.NDSEM + 1)
        if j >= self.NDSEM:
            self._wait(q, ((q, slot), val - 16))
        self._deps(q, reads, writes)
        ins = self.eng[q].dma_start(out=out_ap, in_=in_ap, **kw)
        ins.then_inc(d["sems"][slot], 16)
        d["n"] += 1
        tok = ((q, slot), val)
        self._mark(tok, reads, writes)
        self.ninst += 1
        return tok
    def finish(self, bufs):
        for b in bufs:
            self._wait("sp", b.w)
        for q in self.dq:
            d = self.dq[q]
            for j in range(max(0, d["n"] - self.NDSEM), d["n"]):
                self._wait("sp", ((q, j % self.NDSEM), 16 * (j // self.NDSEM + 1)))

def _barrier(self):
    for e in self.eng:
        for e2 in self.eng:
            if e2 != e and self.cnt[e2] > 0:
                self._wait(e, (e2, self.cnt[e2]))
        for q, d in self.dq.items():
            for j in range(max(0, d["n"] - self.NDSEM), d["n"]):
                self._wait(e, ((q, j % self.NDSEM), 16 * (j // self.NDSEM + 1)))
Prog.barrier = _barrier

class _Scope:
    def __init__(self, p): self.p = p
    def __enter__(self):
        self.old = self.p.es; self.es = ExitStack(); self.es.__enter__(); self.p.es = self.es; return self
    def __exit__(self, *a):
        self.p.barrier(); self.p.es = self.old; return self.es.__exit__(*a)
Prog.scope = lambda self: _Scope(self)

S = 16384; NC = 8; T = S // NC; D = 1024; NIN = 11040
EPS = 1e-6

def consts(p):
    c = {}
    c["ones"] = p.sbuf([128, 128], F32, "c_ones")
    p.op("pool", [], [c["ones"]], lambda e: e.memset(c["ones"][:, :], 1.0))
    c["eps"] = p.sbuf([128, 1], F32, "c_eps")
    p.op("pool", [], [c["eps"]], lambda e: e.memset(c["eps"][:, :], EPS))
    c["ps"] = [p.psum([128, 512], F32, f"ps{i}") for i in range(8)]
    return c

def rms_fm(p, c, xs, gs, hs, nch, Tn, tmp, rstds, psi=0):
    nfeat = nch * 128
    k = 0
    for tb in range(Tn // 512):
        sl = slice(tb * 512, (tb + 1) * 512)
        ps = c["ps"][psi + tb % 2]
        for ch in range(nch):
            s = tmp[k % 2]; k += 1
            if ch % 2 == 0:
                p.op("act", [xs[ch]], [s], lambda e: e.activation(out=s[:, :], in_=xs[ch][:, sl], func=AF.Square))
            else:
                p.op("dve", [xs[ch]], [s], lambda e: e.tensor_tensor(out=s[:, :], in0=xs[ch][:, sl], in1=xs[ch][:, sl], op=ALU.mult))
            p.op("pe", [c["ones"], s], [ps], lambda e: e.matmul(ps[:, :], lhsT=c["ones"][:, :], rhs=s[:, :], start=(ch == 0), stop=(ch == nch - 1)), same_ok=True)
        r = rstds[tb]
        p.op("act", [ps, c["eps"]], [r], lambda e: e.activation(out=r[:, :], in_=ps[:, :], func=AF.Sqrt, bias=c["eps"][:, 0:1], scale=1.0 / nfeat))
        p.op("dve", [r], [r], lambda e: e.reciprocal(out=r[:, :], in_=r[:, :]))
        for ch in range(nch):
            eng = "dve"
            p.op(eng, [xs[ch], r, gs], [hs[ch]], lambda e: e.scalar_tensor_tensor(out=hs[ch][:, sl], in0=xs[ch][:, sl], scalar=gs[:, ch:ch + 1], in1=r[:, :], op0=ALU.mult, op1=ALU.mult))

def linear_stream(p, c, w_ap, K, N, hs, Tn, evac, wst, wbf, psi=2, npsi=4, gw=512):
    kc = K // 128
    wD = Buf(None)
    ng = (N + gw - 1) // gw
    k2 = 0
    for gi in range(ng):
        c0 = gi * gw; wg = min(gw, N - c0)
        st = wst[gi % 2]; wb = wbf[gi % 2]
        p.dma("sp", st[:, :kc, :wg], w_ap[:, c0:c0 + wg].rearrange("(c p) n -> p c n", p=128), [wD], [st])
        h = kc // 2
        p.op("pool", [st], [wb], lambda e: e.tensor_copy(out=wb[:, :h, :wg], in_=st[:, :h, :wg]))
        p.op("dve", [st], [wb], lambda e: e.tensor_copy(out=wb[:, h:kc, :wg], in_=st[:, h:kc, :wg]))
        for mi in range((wg + 127) // 128):
            mw = min(128, wg - mi * 128)
            for tb in range(Tn // 512):
                ps = c["ps"][psi + k2 % npsi]; k2 += 1
                for ch in range(kc):
                    p.op("pe", [wb, hs[ch]], [ps], lambda e: e.matmul(ps[:mw, :], lhsT=wb[:, ch, mi * 128:mi * 128 + mw], rhs=hs[ch][:, tb * 512:(tb + 1) * 512], start=(ch == 0), stop=(ch == kc - 1)), same_ok=True)
                evac(c0 + mi * 128, mw, tb, ps)

def build_k1():
    nc = bass.Bass("TRN2", target_bir_lowering=False)
    xT = nc.dram_tensor("xT", [D, T], F32, kind="ExternalInput")
    g = nc.dram_tensor("g", [128, 8], F32, kind="ExternalInput")
    w = nc.dram_tensor("w", [D, NIN], F32, kind="ExternalInput")
    zT = nc.dram_tensor("zT", [NIN, T], F32, kind="ExternalOutput")
    with ExitStack() as es:
        p = Prog(nc, es)
        c = consts(p)
        xs = [p.sbuf([128, T], F32, f"x{i}") for i in range(8)]
        hs = [p.sbuf([128, T], BF16, f"h{i}") for i in range(8)]
        gs = p.sbuf([128, 8], F32, "gs")
        xD = Buf(None); zD = Buf(None)
        p.dma("pool", gs[:, :], g[:, :], [xD], [gs])
        for i in range(8):
            p.dma("pool", xs[i][:, :], xT[i * 128:(i + 1) * 128, :], [xD], [xs[i]])
        tmp = [p.sbuf([128, 512], F32, f"tmp{i}") for i in range(2)]
        rstds = [p.sbuf([128, 512], F32, f"rstd{i}") for i in range(T // 512)]
        rms_fm(p, c, xs, gs, hs, 8, T, tmp, rstds)
        wst = [p.sbuf([128, 8, 512], F32, f"wst{i}") for i in range(2)]
        wbf = [p.sbuf([128, 8, 512], BF16, f"wbf{i}") for i in range(2)]
        ost = [p.sbuf([128, T], F32, f"ost{i}") for i in range(2)]
        state = {"k": 0}
        def evac(m0, mw, tb, ps):
            o = ost[state["k"] % 2]
            p.op("act", [ps], [o], lambda e: e.copy(out=o[:mw, tb * 512:(tb + 1) * 512], in_=ps[:mw, :]))
            if tb == T // 512 - 1:
                p.dma("act", zT[m0:m0 + mw, :], o[:mw, :], [o], [zD])
                state["k"] += 1
        linear_stream(p, c, w.ap() if hasattr(w, "ap") else w, D, NIN, hs, T, evac, wst, wbf)
        p.finish([zD])
        print("k1 ninst", p.ninst)
    return nc


S = 16384
NEAR_LO, NEAR_HI = -5, 8
RW = 2304

def t5_bucket_np(rel):
    import math
    half = 16; max_exact = 8
    ret = np.where(rel > 0, half, 0)
    n = np.abs(rel)
    nf = np.maximum(n, 1).astype(np.float32)
    large = max_exact + (np.log(nf / max_exact) / np.float32(math.log(1024 / max_exact)) * (half - max_exact)).astype(np.int32)
    large = np.minimum(large, half - 1)
    return ret + np.where(n < max_exact, n, large)

def qknorm_fm(p, c, src, gcol, dst, rows, Tn, tmp, rtmp, psi):
    for tb in range(Tn // 512):
        sl = slice(tb * 512, (tb + 1) * 512)
        s = tmp[tb % 2]; r = rtmp[tb % 2]; ps = c["ps"][psi + tb % 2]
        p.op("act", [src], [s], lambda e: e.activation(out=s[:rows, :], in_=src[:rows, sl], func=AF.Square))
        p.op("pe", [c["ones"], s], [ps], lambda e: e.matmul(ps[:rows, :], lhsT=c["ones"][:rows, :rows], rhs=s[:rows, :], start=True, stop=True))
        p.op("act", [ps, c["eps"]], [r], lambda e: e.activation(out=r[:rows, :], in_=ps[:rows, :], func=AF.Sqrt, bias=c["eps"][:rows, 0:1], scale=1.0 / rows))
        p.op("dve", [r], [r], lambda e: e.reciprocal(out=r[:rows, :], in_=r[:rows, :]))
        p.op("dve", [src, r, gcol], [dst], lambda e: e.scalar_tensor_tensor(out=dst[:rows, sl], in0=src[:rows, sl], scalar=gcol[:rows, 0:1], in1=r[:rows, :], op0=ALU.mult, op1=ALU.mult))

def build_k2c(NQB=S // 512, NKT=S // 128):
    nc = bass.Bass("TRN2", target_bir_lowering=False)
    qT = nc.dram_tensor("qT", [64, S], F32, kind="ExternalInput")
    kT = nc.dram_tensor("kT", [64, S], F32, kind="ExternalInput")
    v = nc.dram_tensor("v", [128, S // 128, 128], F32, kind="ExternalInput")
    gq = nc.dram_tensor("gq", [64, 1], F32, kind="ExternalInput")
    gk = nc.dram_tensor("gk", [64, 1], F32, kind="ExternalInput")
    Rb = nc.dram_tensor("Rb", [128, RW], F32, kind="ExternalInput")
    bfar = nc.dram_tensor("bfar", [128, 2], F32, kind="ExternalInput")
    oT = nc.dram_tensor("oT", [128, S], F32, kind="ExternalOutput")
    with ExitStack() as es:
        p = Prog(nc, es)
        c = consts(p)
        iD = Buf(None); oD = Buf(None)
        qb16 = p.sbuf([64, S], BF16, "qb16"); kb16 = p.sbuf([64, S], BF16, "kb16")
        vb16 = p.sbuf([128, S // 128, 128], BF16, "vb16")
        rb = p.sbuf([128, RW], F32, "rb"); bf_ = p.sbuf([128, 2], F32, "bfar_s")
        gqs = p.sbuf([64, 1], F32, "gqs"); gks = p.sbuf([64, 1], F32, "gks")
        p.dma("pool", rb[:, :], Rb[:, :], [iD], [rb]); p.dma("pool", bf_[:, :], bfar[:, :], [iD], [bf_])
        p.dma("pool", gqs[:, :], gq[:, :], [iD], [gqs]); p.dma("pool", gks[:, :], gk[:, :], [iD], [gks])
        tmp = [p.sbuf([128, 512], F32, f"tmp{i}") for i in range(2)]
        rtmp = [p.sbuf([128, 512], F32, f"rtmp{i}") for i in range(2)]
        stg = [p.sbuf([128, 4096], F32, f"stg{i}") for i in range(2)]
        k = 0
        for (src, gcol, dst) in ((qT, gqs, qb16), (kT, gks, kb16)):
            for blk in range(S // 4096):
                st = stg[k % 2]; k += 1
                p.dma("sp", st[:64, :], src[:, blk * 4096:(blk + 1) * 4096], [iD], [st])
                class V:
                    pass
                for tb in range(8):
                    sl = slice(tb * 512, (tb + 1) * 512); dsl = slice(blk * 4096 + tb * 512, blk * 4096 + (tb + 1) * 512)
                    s = tmp[tb % 2]; r = rtmp[tb % 2]; ps = c["ps"][6 + tb % 2]
                    p.op("act", [st], [s], lambda e: e.activation(out=s[:64, :], in_=st[:64, sl], func=AF.Square))
                    p.op("pe", [c["ones"], s], [ps], lambda e: e.matmul(ps[:64, :], lhsT=c["ones"][:64, :64], rhs=s[:64, :], start=True, stop=True))
                    p.op("act", [ps, c["eps"]], [r], lambda e: e.activation(out=r[:64, :], in_=ps[:64, :], func=AF.Sqrt, bias=c["eps"][:64, 0:1], scale=1.0 / 64))
                    p.op("dve", [r], [r], lambda e: e.reciprocal(out=r[:64, :], in_=r[:64, :]))
                    p.op("dve", [st, r, gcol], [dst], lambda e: e.scalar_tensor_tensor(out=dst[:64, dsl], in0=st[:64, sl], scalar=gcol[:64, 0:1], in1=r[:64, :], op0=ALU.mult, op1=ALU.mult))
        for blk in range(S // 128 // 32):
            st = stg[k % 2]; k += 1
            p.dma("sp", st[:, :].rearrange("p (a b) -> p a b", b=128), v[:, blk * 32:(blk + 1) * 32, :], [iD], [st])
            p.op("pool", [st], [vb16], lambda e: e.tensor_copy(out=vb16[:, blk * 32:(blk + 1) * 32, :], in_=st[:, :].rearrange("p (a b) -> p a b", b=128)))
        pt = [p.sbuf([128, 512], BF16, f"pt{i}") for i in range(4)]
        nt = [p.sbuf([128, 512], F32, f"nt{i}") for i in range(2)]
        lacc = [p.sbuf([128, 512], F32, f"lacc{i}") for i in range(2)]
        rl = p.sbuf([128, 512], F32, "rl")
        ob = [p.sbuf([128, 512], F32, f"ob{i}") for i in range(2)]
        it = 0; nn = 0
        for qb in range(NQB):
            qsl = slice(qb * 512, (qb + 1) * 512)
            po = c["ps"][4 + qb % 2]; la = lacc[qb % 2]
            for j in range(NKT):
                ps = c["ps"][it % 4]; P = pt[it % 4]; it += 1
                p.op("pe", [kb16, qb16], [ps], lambda e: e.matmul(ps[:, :], lhsT=kb16[:, j * 128:(j + 1) * 128], rhs=qb16[:, qsl], start=True, stop=True))
                m = j - 4 * qb
                if NEAR_LO <= m <= NEAR_HI:
                    t = nt[nn % 2]; nn += 1
                    off = 1024 - 128 * m
                    p.op("dve", [ps, rb], [t], lambda e: e.scalar_tensor_tensor(out=t[:, :], in0=ps[:, :], scalar=0.125, in1=rb[:, off:off + 512], op0=ALU.mult, op1=ALU.add))
                    p.op("act", [t], [P], lambda e: e.activation(out=P[:, :], in_=t[:, :], func=AF.Exp))
                else:
                    side = 0 if m < 0 else 1
                    p.op("act", [ps, bf_], [P], lambda e: e.activation(out=P[:, :], in_=ps[:, :], func=AF.Exp, bias=bf_[:, side:side + 1], scale=0.125))
                p.op("pe", [vb16, P], [po], lambda e: e.matmul(po[:, :], lhsT=vb16[:, j, :], rhs=P[:, :], start=(j == 0), stop=(j == NKT - 1)), same_ok=True)
                if j == 0:
                    p.op("dve", [P], [la], lambda e: e.tensor_copy(out=la[:, :], in_=P[:, :]))
                else:
                    p.op("dve", [P, la], [la], lambda e: e.tensor_tensor(out=la[:, :], in0=la[:, :], in1=P[:, :], op=ALU.add))
            pl = c["ps"][6]
            p.op("pe", [c["ones"], la], [pl], lambda e: e.matmul(pl[:, :], lhsT=c["ones"][:, :], rhs=la[:, :], start=True, stop=True))
            p.op("dve", [pl], [rl], lambda e: e.reciprocal(out=rl[:, :], in_=pl[:, :]))
            o = ob[qb % 2]
            p.op("dve", [po, rl], [o], lambda e: e.tensor_tensor(out=o[:, :], in0=po[:, :], in1=rl[:, :], op=ALU.mult))
            p.dma("pool", oT[:, qsl], o[:, :], [o], [oD])
        p.finish([oD])
        print("k2c ninst", p.ninst)
    return nc

def host_inputs_k2c(zT, inp, layer):
    CQ = 256 + 256 + 512 + 512 + 32 + 768 * 3
    CK = CQ + 512; CV = CK + 512
    t5 = inp["t5_bias"]
    kk = np.arange(128)[:, None]; jj = np.arange(RW)[None, :]
    bidx = t5_bucket_np(kk - jj + 1024)
    assert (t5_bucket_np(np.arange(-20000, -5 * 128 + 128 - 1 - 127 + 1)) == 15).all()
    maps = []
    for h in range(4):
        vT = zT[CV + h * 128:CV + (h + 1) * 128, :]
        vv = np.ascontiguousarray(vT.T.reshape(S // 128, 128, 128).transpose(1, 0, 2))
        Rb = np.ascontiguousarray(t5[:, 12 + h][bidx]).astype(np.float32)
        bfar = np.ascontiguousarray(np.broadcast_to(np.array([t5[15, 12 + h], t5[31, 12 + h]], np.float32)[None, :], (128, 2)))
        for cc in range(2):
            r0 = h * 128 + cc * 64
            maps.append({"qT": np.ascontiguousarray(zT[CQ + r0:CQ + r0 + 64]), "kT": np.ascontiguousarray(zT[CK + r0:CK + r0 + 64]), "v": vv,
                         "gq": np.ascontiguousarray(inp["diff_qnorm_g"][layer].reshape(64, 1)), "gk": np.ascontiguousarray(inp["diff_knorm_g"][layer].reshape(64, 1)),
                         "Rb": Rb, "bfar": bfar})
    return maps


S = 16384
NT = S // 128

def build_k2a(NTL=NT):
    nc = bass.Bass("TRN2", target_bir_lowering=False)
    I = lambda n, s: nc.dram_tensor(n, s, F32, kind="ExternalInput")
    qT = I("qT", [64, S]); kT = I("kT", [64, S]); ktm = I("ktm", [128, NT, 64]); vtm = I("vtm", [128, NT, 128])
    lrT = I("lrT", [16, S]); gw = I("gw", [16, 64]); gbb = I("gbb", [128, 64])
    tri = I("tri", [128, 128]); m1 = I("m1", [128, 128]); cind = I("cind", [128, 2]); maskA = I("maskA", [128, 128])
    oT = nc.dram_tensor("oT", [128, S], F32, kind="ExternalOutput")
    with ExitStack() as es:
        p = Prog(nc, es)
        c = consts(p)
        iD = Buf(None); oD = Buf(None)
        def ld(name, shape, src, q="pool"):
            b = p.sbuf(shape, F32, name); p.dma(q, b[tuple(slice(None) for _ in shape)], src, [iD], [b]); return b
        gws = ld("gws", [16, 64], gw[:, :]); gbs = ld("gbs", [128, 64], gbb[:, :])
        tris = ld("tris", [128, 128], tri[:, :]); m1s = ld("m1s", [128, 128], m1[:, :]); cis = ld("cis", [128, 2], cind[:, :])
        mAs = ld("mAs", [128, 128], maskA[:, :])
        one1 = p.sbuf([128, 1], F32, "one1"); p.op("pool", [], [one1], lambda e: e.memset(one1[:, :], 1.0))
        lrs = ld("lrs", [16, S], lrT[:, :], "sp")
        St = p.sbuf([64, 128], F32, "St"); Sb = p.sbuf([64, 128], BF16, "Sb")
        p.op("pool", [], [St], lambda e: e.memset(St[:, :], 0.0)); p.op("pool", [], [Sb], lambda e: e.memset(Sb[:, :], 0.0))
        NB = 16
        def stg(name, shape): return [p.sbuf(shape, F32, f"{name}{i}") for i in range(2)]
        qst = stg("qst", [64, NB * 128]); kst = stg("kst", [64, NB * 128]); ktst = stg("ktst", [128, NB, 64]); vst = stg("vst", [128, NB, 128])
        vbf = [p.sbuf([128, NB, 128], BF16, f"vbf{i}") for i in range(2)]
        ost = [p.sbuf([128, NB * 128], F32, f"ost{i}") for i in range(2)]
        R = lambda n, shp, dt=F32, k=2: [p.sbuf(shp, dt, f"{n}{i}") for i in range(k)]
        xs = R("xs", [128, 64]); es_ = R("es", [128, 64]); ls = R("ls", [128, 64])
        eb = R("eb", [64, 128]); enb = R("enb", [64, 128]); ed = R("ed", [128, 64]); av = R("av", [64, 2])
        qg = R("qg", [64, 128], BF16); kg = R("kg", [64, 128], BF16); kd = R("kd", [128, 64], BF16); Am = R("Am", [128, 128], BF16)
        for i in range(NTL):
            blk, ib = divmod(i, NB); par = i % 2
            if ib == 0:
                bsl = slice(blk * NB * 128, (blk + 1) * NB * 128); tsl = slice(blk * NB, (blk + 1) * NB)
                p.dma("sp", qst[blk % 2][:, :], qT[:, bsl], [iD], [qst[blk % 2]])
                p.dma("sp", kst[blk % 2][:, :], kT[:, bsl], [iD], [kst[blk % 2]])
                p.dma("sp", ktst[blk % 2][:, :, :], ktm[:, tsl, :], [iD], [ktst[blk % 2]])
                p.dma("sp", vst[blk % 2][:, :, :], vtm[:, tsl, :], [iD], [vst[blk % 2]])
                vb = vbf[blk % 2]
                p.op("pool", [vst[blk % 2]], [vb], lambda e: e.tensor_copy(out=vb[:, :, :], in_=vst[blk % 2][:, :, :]))
            q_s = qst[blk % 2]; k_s = kst[blk % 2]; kt_s = ktst[blk % 2]; vb = vbf[blk % 2]; o_s = ost[blk % 2]
            tsl = slice(i * 128, (i + 1) * 128); lsl = slice(ib * 128, (ib + 1) * 128)
            pa = c["ps"][par]; pb = c["ps"][2 + par]; pc = c["ps"][4 + par]; po = c["ps"][6]; pf = c["ps"][7]
            x = xs[par]; e_ = es_[par]; l = ls[par]
            p.op("pe", [lrs, gws], [pa], lambda e: e.matmul(pa[:, 0:64], lhsT=lrs[:, tsl], rhs=gws[:, :], start=True, stop=True))
            p.op("dve", [pa, gbs], [x], lambda e: e.tensor_tensor(out=x[:, :], in0=pa[:, 0:64], in1=gbs[:, :], op=ALU.add))
            p.op("act", [x], [e_], lambda e: e.activation(out=e_[:, :], in_=x[:, :], func=AF.Exp, scale=-1.0))
            p.op("act", [e_, one1], [l], lambda e: e.activation(out=l[:, :], in_=e_[:, :], func=AF.Ln, bias=one1[:, 0:1], scale=1.0))
            p.op("pe", [l, tris], [pb], lambda e: e.matmul(pb[:64, 0:128], lhsT=l[:, :], rhs=tris[:, :], start=True, stop=True))
            p.op("pe", [l, cis], [pb], lambda e: e.matmul(pb[:64, 128:130], lhsT=l[:, :], rhs=cis[:, :], start=True, stop=True), same_ok=True)
            p.op("pe", [m1s, l], [pc], lambda e: e.matmul(pc[:, 0:64], lhsT=m1s[:, :], rhs=l[:, :], start=True, stop=True))
            p.op("act", [pb], [eb[par]], lambda e: e.activation(out=eb[par][:, :], in_=pb[:64, 0:128], func=AF.Exp, scale=-1.0 / 16))
            p.op("act", [pb], [enb[par]], lambda e: e.activation(out=enb[par][:, :], in_=pb[:64, 0:128], func=AF.Exp, scale=1.0 / 16))
            p.op("act", [pb], [av[par]], lambda e: e.activation(out=av[par][:, :], in_=pb[:64, 128:130], func=AF.Exp, scale=-1.0 / 16))
            p.op("act", [pc], [ed[par]], lambda e: e.activation(out=ed[par][:, :], in_=pc[:, 0:64], func=AF.Exp, scale=-1.0 / 16))
            p.op("dve", [q_s, eb[par]], [qg[par]], lambda e: e.scalar_tensor_tensor(out=qg[par][:, :], in0=q_s[:, lsl], scalar=0.125, in1=eb[par][:, :], op0=ALU.mult, op1=ALU.mult))
            p.op("dve", [k_s, enb[par]], [kg[par]], lambda e: e.tensor_tensor(out=kg[par][:, :], in0=k_s[:, lsl], in1=enb[par][:, :], op=ALU.mult))
            p.op("dve", [kt_s, ed[par]], [kd[par]], lambda e: e.tensor_tensor(out=kd[par][:, :], in0=kt_s[:, ib, :], in1=ed[par][:, :], op=ALU.mult))
            p.op("pe", [kg[par], qg[par]], [pc], lambda e: e.matmul(pc[:, 128:256], lhsT=kg[par][:, :], rhs=qg[par][:, :], start=True, stop=True))
            p.op("dve", [pc, mAs], [Am[par]], lambda e: e.tensor_tensor(out=Am[par][:, :], in0=pc[:, 128:256], in1=mAs[:, :], op=ALU.mult))
            p.op("pe", [vb, Am[par]], [po], lambda e: e.matmul(po[:, 0:128], lhsT=vb[:, ib, :], rhs=Am[par][:, :], start=True, stop=False))
            for ci in range(2):
                cs = slice(ci * 64, (ci + 1) * 64)
                p.op("pe", [Sb, qg[par]], [po], lambda e: e.matmul(po[:, cs], lhsT=Sb[:, :], rhs=qg[par][:, cs], start=False, stop=(ci == 1)), same_ok=True)
                p.op("pe", [kd[par], vb], [pf], lambda e: e.matmul(pf[:64, 0:128], lhsT=kd[par][cs, :], rhs=vb[cs, ib, :], start=True, stop=True))
                p.op("dve", [St, av[par], pf], [St], lambda e: e.scalar_tensor_tensor(out=St[:, :], in0=St[:, :], scalar=av[par][:, ci:ci + 1], in1=pf[:64, 0:128], op0=ALU.mult, op1=ALU.add))
                p.op("act", [St], [Sb], lambda e: e.copy(out=Sb[:, :], in_=St[:, :]))
            p.op("act", [po], [o_s], lambda e: e.copy(out=o_s[:, lsl], in_=po[:, 0:128]))
            if ib == NB - 1 or i == NTL - 1:
                p.dma("act", oT[:, blk * NB * 128:(blk * NB + ib + 1) * 128], o_s[:, :(ib + 1) * 128], [o_s], [oD])
        p.finish([oD])
        print("k2a ninst", p.ninst)
    return nc

def host_inputs_k2a(zT, inp, layer):
    AQ, AK, AV, AR, ALR = 0, 256, 512, 1024, 1536
    ss = np.arange(128)[:, None]; tt = np.arange(128)[None, :]
    same = (ss // 64) == (tt // 64)
    tri = (same & (ss <= tt)).astype(np.float32); m1 = (same & (ss > tt)).astype(np.float32)
    cind = (np.arange(128)[:, None] // 64 == np.arange(2)[None, :]).astype(np.float32)
    maps = []
    for h in range(4):
        for d in range(2):
            f = (lambda a: a[:, ::-1]) if d == 1 else (lambda a: a)
            q = np.ascontiguousarray(f(zT[AQ + h * 64:AQ + (h + 1) * 64])); k = np.ascontiguousarray(f(zT[AK + h * 64:AK + (h + 1) * 64]))
            v = f(zT[AV + h * 128:AV + (h + 1) * 128]); lr = np.ascontiguousarray(f(zT[ALR + d * 16:ALR + (d + 1) * 16]))
            ktm = np.ascontiguousarray(k.T.reshape(NT, 128, 64).transpose(1, 0, 2)); vtm = np.ascontiguousarray(v.T.reshape(NT, 128, 128).transpose(1, 0, 2))
            mA = (same & ((ss <= tt) if d == 0 else (ss < tt))).astype(np.float32)
            maps.append({"qT": q, "kT": k, "ktm": ktm, "vtm": vtm, "lrT": lr,
                         "gw": np.ascontiguousarray(inp["gla_gate_w"][layer, d][:, h * 64:(h + 1) * 64]),
                         "gbb": np.ascontiguousarray(np.broadcast_to(inp["gla_gate_b"][layer, d][None, h * 64:(h + 1) * 64], (128, 64))),
                         "tri": tri, "m1": m1, "cind": cind, "maskA": mA})
    return maps


S = 16384
DILS = (1, 4, 16)
NEG = -100.0

import os
STAGE = int(os.environ.get('K2B_STAGE', '3'))
def build_k2b():
    nc = bass.Bass("TRN2", target_bir_lowering=False)
    I = lambda n, s: nc.dram_tensor(n, s, F32, kind="ExternalInput")
    qw = I("qw", [3, 256, 2048]); kw = I("kw", [3, 256, 4096]); vw = I("vw", [3, 128, 16 * 2 * 4 * 64])
    gq = I("gq", [128, 3]); gk = I("gk", [128, 3]); kval = I("kval", [128, 3 * 32]); bm = I("bm", [3, 2, 128, 512]); bones = I("bones", [128, 128])
    O = nc.dram_tensor("O", [3, 2, 64, 4, 2048], F32, kind="ExternalOutput")
    with ExitStack() as es:
        p = Prog(nc, es)
        c = consts(p)
        iD = Buf(None); oD = Buf(None)
        def ld(name, shape, src, q="pool"):
            b = p.sbuf(shape, F32, name); p.dma(q, b[tuple(slice(None) for _ in shape)], src, [iD], [b]); return b
        gqs = ld("gqs", [128, 3], gq[:, :]); gks = ld("gks", [128, 3], gk[:, :]); kvs = ld("kvs", [128, 96], kval[:, :]); bos = ld("bos", [128, 128], bones[:, :])
        bms = [[ld(f"bm{g}{t}", [128, 512], bm[g, t, :, :]) for t in range(2)] for g in range(3)]
        qst = [p.sbuf([128, 2048], F32, f"qst{i}") for i in range(2)]; kst = [p.sbuf([128, 4096], F32, f"kst{i}") for i in range(2)]
        vst = p.sbuf([128, 16 * 2 * 4 * 64], F32, "vst"); vb = p.sbuf([128, 16, 2, 4, 64], BF16, "vb")
        qn = [p.sbuf([64, 2048], BF16, f"qn{i}") for i in range(4)]; kn = [p.sbuf([64, 4096], BF16, f"kn{i}") for i in range(4)]
        tmp = [p.sbuf([128, 512], F32, f"tmp{i}") for i in range(2)]; rtmp = [p.sbuf([128, 512], F32, f"rtmp{i}") for i in range(2)]
        tt = [p.sbuf([128, 512], F32, f"tt{i}") for i in range(2)]; P = [p.sbuf([128, 512], BF16, f"P{i}") for i in range(2)]
        ost = [p.sbuf([64, 2, 512], F32, f"ost{i}") for i in range(2)]
        onb = p.sbuf([128, 64], BF16, "onb"); p.op("pool", [], [onb], lambda e: e.memset(onb[:, :], 1.0))
        it = 0
        def norm(st, gcol, dst, n):
            for tb in range(n // 512):
                sl = slice(tb * 512, (tb + 1) * 512)
                s = tmp[tb % 2]; r = rtmp[tb % 2]; ps = c["ps"][6 + tb % 2]
                p.op("act", [st], [s], lambda e: e.activation(out=s[:64, :], in_=st[:64, sl], func=AF.Square))
                p.op("pe", [c["ones"], s], [ps], lambda e: e.matmul(ps[:64, :], lhsT=c["ones"][:64, :64], rhs=s[:64, :], start=True, stop=True))
                p.op("act", [ps, c["eps"]], [r], lambda e: e.activation(out=r[:64, :], in_=ps[:64, :], func=AF.Sqrt, bias=c["eps"][:64, 0:1], scale=1.0 / 64))
                p.op("dve", [r], [r], lambda e: e.reciprocal(out=r[:64, :], in_=r[:64, :]))
                p.op("dve", [st, r, gcol[0]], [dst], lambda e: e.scalar_tensor_tensor(out=dst[:, sl], in0=st[:64, sl], scalar=gcol[0][:64, gcol[1]:gcol[1] + 1], in1=r[:64, :], op0=ALU.mult, op1=ALU.mult))
        for g in range(3):
            p.dma("sp", vst[:, :], vw[g, :, :], [iD], [vst])
            p.op("pool", [vst], [vb], lambda e: e.tensor_copy(out=vb[:, :, :, :, :].rearrange("p a b c d -> p (a b c d)"), in_=vst[:, :]))
            for h in range(4):
                p.dma("sp", qst[h % 2][:64, :], qw[g, h * 64:(h + 1) * 64, :], [iD], [qst[h % 2]])
                p.dma("sp", kst[h % 2][:64, :], kw[g, h * 64:(h + 1) * 64, :], [iD], [kst[h % 2]])
                norm(qst[h % 2], (gqs, g), qn[h], 2048); norm(kst[h % 2], (gks, g), kn[h], 4096)
            for b in range(16):
                po = c["ps"][4 + b % 2]; pl = c["ps"][6 + b % 2]
                for tau in range(2 if STAGE >= 2 else 0):
                    ps = c["ps"][it % 4]; t = tt[it % 2]; Pt = P[it % 2]; it += 1
                    for h in range(4):
                        p.op("pe", [kn[h], qn[h]], [ps], lambda e: e.matmul(ps[:, h * 128:(h + 1) * 128], lhsT=kn[h][:, b * 256 + tau * 128:b * 256 + tau * 128 + 128], rhs=qn[h][:, b * 128:(b + 1) * 128], start=True, stop=True), same_ok=True)
                    p.op("dve", [ps, bms[g][tau]], [t], lambda e: e.scalar_tensor_tensor(out=t[:, :], in0=ps[:, :], scalar=0.125, in1=bms[g][tau][:, :], op0=ALU.mult, op1=ALU.add))
                    col = g * 32 + b * 2 + tau
                    p.op("act", [t, kvs], [Pt], lambda e: e.activation(out=Pt[:, :], in_=t[:, :], func=AF.Exp, bias=kvs[:, col:col + 1], scale=1.0))
                    if STAGE < 3: continue
                    for h in range(4):
                        p.op("pe", [vb, Pt], [po], lambda e: e.matmul(po[:64, h * 128:(h + 1) * 128], lhsT=vb[:, b, tau, h, :], rhs=Pt[:, h * 128:(h + 1) * 128], start=(tau == 0 and h == 0), stop=(tau == 1 and h == 3)), same_ok=True)
                    p.op("pe", [onb, Pt], [pl], lambda e: e.matmul(pl[:64, :], lhsT=onb[:, :], rhs=Pt[:, :], start=(tau == 0), stop=(tau == 1)), same_ok=True)
                o = ost[b % 2]
                if STAGE < 3:
                    p.op("dve", [], [o], lambda e: e.memset(o[:, :, :], 1.0))
                    for w_ in range(2):
                        p.dma("act", O[g, w_, :, :, b * 128:(b + 1) * 128], o[:, w_, :].rearrange("r (h q) -> r h q", q=128), [o], [oD])
                    continue
                p.op("act", [po], [o], lambda e: e.copy(out=o[:, 0, :], in_=po[:64, :]))
                p.op("dve", [pl], [o], lambda e: e.tensor_copy(out=o[:, 1, :], in_=pl[:64, :]))
                for w_ in range(2):
                    p.dma("act", O[g, w_, :, :, b * 128:(b + 1) * 128], o[:, w_, :].rearrange("r (h q) -> r h q", q=128), [o], [oD])
        p.finish([oD])
        print("k2b ninst", p.ninst)
    return nc

def perm_index(d):
    M = S // d
    pos = np.arange(S)
    return (pos % M) * d + pos // M

def host_inputs_k2b(zT, inp, layer):
    BQ = 1568; BK = BQ + 768; BV = BK + 768
    t5 = inp["t5_bias"]
    kk = np.arange(128)[:, None]; qq = np.arange(128)[None, :]
    bm = np.zeros((3, 2, 128, 512), np.float32)
    for g, d in enumerate(DILS):
        for tau in range(2):
            delta = kk + (-64 if tau == 0 else 64) - qq
            ok = np.abs(delta) <= 64
            bidx = t5_bucket_np(delta * d)
            for h in range(4):
                bm[g, tau, :, h * 128:(h + 1) * 128] = np.where(ok, t5[:, g * 4 + h][bidx], np.float32(NEG))
    bones = (np.arange(128)[:, None] // 64 == np.arange(128)[None, :] // 64).astype(np.float32)
    gq = np.ascontiguousarray(np.tile(inp["dil_qnorm_g"][layer].T, (2, 1))); gk = np.ascontiguousarray(np.tile(inp["dil_knorm_g"][layer].T, (2, 1)))
    qP, kP, vP, valid = [], [], [], []
    for g, d in enumerate(DILS):
        idx = perm_index(d); M = S // d
        qP.append(zT[BQ + g * 256:BQ + (g + 1) * 256][:, idx])
        kP.append(zT[BK + g * 256:BK + (g + 1) * 256][:, idx]); vP.append(zT[BV + g * 256:BV + (g + 1) * 256][:, idx])
    maps = []
    for r in range(8):
        qw = np.zeros((3, 256, 2048), np.float32); kw = np.zeros((3, 256, 4096), np.float32)
        vw = np.zeros((3, 128, 16, 2, 4, 64), np.float32); kval = np.zeros((128, 96), np.float32)
        for g, d in enumerate(DILS):
            M = S // d
            qw[g] = qP[g][:, r * 2048:(r + 1) * 2048]
            for b in range(16):
                P0 = r * 2048 + b * 128; sub0 = (P0 // M) * M
                kpos = np.arange(P0 - 64, P0 + 192)
                ok = (kpos >= sub0) & (kpos < sub0 + M)
                kc = np.clip(kpos, 0, S - 1)
                kwin = np.where(ok[None, :], kP[g][:, kc], np.float32(0))
                kw[g][:, b * 256:(b + 1) * 256] = kwin
                vwin = np.where(ok[None, :], vP[g][:, kc], np.float32(0))
                vv = vwin.reshape(4, 64, 2, 128).transpose(3, 2, 0, 1)
                vw[g][:, b] = vv
                kval[:, g * 32 + b * 2:g * 32 + b * 2 + 2] = np.where(ok.reshape(2, 128).T, np.float32(0), np.float32(NEG))
        maps.append({"qw": qw, "kw": kw, "vw": np.ascontiguousarray(vw.reshape(3, 128, -1)), "gq": gq, "gk": gk, "kval": kval, "bm": bm, "bones": bones})
    return maps

def host_post_k2b(results):
    N = np.zeros((3, 256, S), np.float32); lb = np.zeros((3, 256, S), np.float32)
    for g, d in enumerate(DILS):
        Og = np.concatenate([r["O"][g] for r in results], axis=3)
        idx = perm_index(d)
        N[g][:, idx] = Og[0].transpose(1, 0, 2).reshape(256, S)
        lb[g][:, idx] = Og[1].transpose(1, 0, 2).reshape(256, S)
    return N, lb


import math
S = 16384
I32 = mybir.dt.int32
TWO_PI = float(2 * np.pi)

def build_k2d(NROW=128):
    nc = bass.Bass("TRN2", target_bir_lowering=False)
    I = lambda n, s: nc.dram_tensor(n, s, F32, kind="ExternalInput")
    ux1 = I("ux1", [128, S]); uv = I("uv", [128, S]); ux0 = I("ux0", [64, S])
    cw1 = I("cw1", [128, 4]); cwv = I("cwv", [128, 4]); cw0 = I("cw0", [64, 4])
    embT = I("embT", [33, S]); w1 = I("w1", [33, 64]); w2 = I("w2", [64, 64]); w3 = I("w3", [64, 128])
    bf1 = I("bf1", [64, 2]); bf2 = I("bf2", [64, 2])
    dlrow = I("dlrow", [1, 128]); tvec = I("tvec", [1, S]); pair = I("pair", [128, 128])
    Y = nc.dram_tensor("Y", [128, 128, 128], F32, kind="ExternalOutput")
    zz = nc.dram_tensor("zz", [64, S], F32, kind="ExternalOutput")
    x0c = nc.dram_tensor("x0c", [64, S], F32, kind="ExternalOutput")
    kx = nc.dram_tensor("kx", [128, S + 128], BF16, kind="Internal")
    with ExitStack() as es:
        p = Prog(nc, es)
        c = consts(p)
        iD = Buf(None); oD = Buf(None); kxD = Buf(None)
        def ld(name, shape, src, q="pool"):
            b = p.sbuf(shape, F32, name); p.dma(q, b[tuple(slice(None) for _ in shape)], src, [iD], [b]); return b
        cw1s = ld("cw1s", [128, 4], cw1[:, :]); cwvs = ld("cwvs", [128, 4], cwv[:, :]); cw0s = ld("cw0s", [64, 4], cw0[:, :])
        w1s = ld("w1s", [33, 64], w1[:, :]); w2s = ld("w2s", [64, 64], w2[:, :]); w3s = ld("w3s", [64, 128], w3[:, :])
        bf1s = ld("bf1s", [64, 2], bf1[:, :]); bf2s = ld("bf2s", [64, 2], bf2[:, :])
        dls = ld("dls", [1, 128], dlrow[:, :]); prs = ld("prs", [128, 128], pair[:, :])
        for b in (bf1s, bf2s):
            p.op("dve", [b], [b], lambda e: e.tensor_scalar(out=b[:, 1:2], in0=b[:, 1:2], scalar1=1.0 / TWO_PI, scalar2=None, op0=ALU.mult))
        idf = p.sbuf([128, 128], F32, "idf"); idb = p.sbuf([128, 128], BF16, "idb")
        p.op("pool", [], [idf], lambda e: e.memset(idf[:, :], 1.0))
        p.op("pool", [idf], [idf], lambda e: e.affine_select(out=idf[:, :], in_=idf[:, :], pattern=[[-1, 128]], compare_op=ALU.is_equal, fill=0.0, base=0, channel_multiplier=1))
        p.op("dve", [idf], [idb], lambda e: e.tensor_copy(out=idb[:, :], in_=idf[:, :]))
        zpad = p.sbuf([128, 128], BF16, "zpad"); p.op("pool", [], [zpad], lambda e: e.memset(zpad[:, :], 0.0))
        p.dma("pool", kx[:, S:S + 128], zpad[:, :], [zpad], [kxD])
        Zall = p.sbuf([128, 128, 128], BF16, "Zall")
        rn = p.sbuf([128, 128], F32, "rn")
        scA = p.scope(); scA.__enter__()
        zzb = p.sbuf([128, S], BF16, "zzb")
        CB = 2048
        ub = [p.sbuf([128, CB + 2], F32, f"ub{i}") for i in range(2)]
        acc = [p.sbuf([128, CB], F32, f"acc{i}") for i in range(3)]
        k = 0
        def conv(src, rows, cws, blk, dst):
            nonlocal k
            u = ub[k % 2]; k += 1
            lo = blk * CB - 1; hi = (blk + 1) * CB + 1
            if blk == 0:
                p.op("pool", [], [u], lambda e: e.memset(u[:rows, 0:1], 0.0));
                p.dma("sp", u[:rows, 1:CB + 2], src[:, 0:hi], [iD], [u])
            elif blk == S // CB - 1:
                p.op("pool", [], [u], lambda e: e.memset(u[:rows, CB + 1:CB + 2], 0.0))
                p.dma("sp", u[:rows, 0:CB + 1], src[:, lo:S], [iD], [u])
            else:
                p.dma("sp", u[:rows, :], src[:, lo:hi], [iD], [u])
            p.op("dve", [u, cws], [dst], lambda e: e.tensor_scalar(out=dst[:rows, :], in0=u[:rows, 1:CB + 1], scalar1=cws[:rows, 1:2], scalar2=cws[:rows, 3:4], op0=ALU.mult, op1=ALU.add))
            p.op("dve", [u, cws, dst], [dst], lambda e: e.scalar_tensor_tensor(out=dst[:rows, :], in0=u[:rows, 0:CB], scalar=cws[:rows, 0:1], in1=dst[:rows, :], op0=ALU.mult, op1=ALU.add))
            p.op("dve", [u, cws, dst], [dst], lambda e: e.scalar_tensor_tensor(out=dst[:rows, :], in0=u[:rows, 2:CB + 2], scalar=cws[:rows, 2:3], in1=dst[:rows, :], op0=ALU.mult, op1=ALU.add))
        for blk in range(S // CB):
            bsl = slice(blk * CB, (blk + 1) * CB)
            conv(ux1, 128, cw1s, blk, acc[0]); conv(uv, 128, cwvs, blk, acc[1])
            p.op("pool", [acc[0], acc[1]], [acc[0]], lambda e: e.tensor_tensor(out=acc[0][:, :], in0=acc[0][:, :], in1=acc[1][:, :], op=ALU.mult))
            p.op("act", [acc[0]], [zzb], lambda e: e.copy(out=zzb[:, bsl], in_=acc[0][:, :]))
            p.dma("act", zz[:, bsl], acc[0][:64, :], [acc[0]], [oD])
            conv(ux0, 64, cw0s, blk, acc[2])
            p.dma("act", x0c[:, bsl], acc[2][:64, :], [acc[2]], [oD])
        for g in range(32):
            ps = c["ps"][g % 2]
            for q in range(4):
                sb = g * 4 + q
                p.op("pe", [zzb, idb], [ps], lambda e: e.matmul(ps[:, q * 128:(q + 1) * 128], lhsT=zzb[:, sb * 128:(sb + 1) * 128], rhs=idb[:, :], start=True, stop=True), same_ok=True)
            eng = "act" if g % 2 == 0 else "dve"
            if eng == "act":
                p.op("act", [ps], [Zall], lambda e: e.copy(out=Zall[:, :, g * 4:(g + 1) * 4].rearrange("p r b -> p b r"), in_=ps[:, :].rearrange("p (b r) -> p b r", r=128)))
            else:
                p.op("dve", [ps], [Zall], lambda e: e.tensor_copy(out=Zall[:, :, g * 4:(g + 1) * 4].rearrange("p r b -> p b r"), in_=ps[:, :].rearrange("p (b r) -> p b r", r=128)))
        scA.__exit__(None, None, None)
        scC = p.scope(); scC.__enter__()
        tvb = [p.sbuf([1, 512], F32, f"tvb{i}") for i in range(2)]
        emb = [p.sbuf([33, 512], F32, f"emb{i}") for i in range(2)]
        R = lambda n, shp, dt=F32, kk=2: [p.sbuf(shp, dt, f"{n}{i}") for i in range(kk)]
        ua = R("ua", [64, 512]); ti = R("ti", [64, 512], I32); tf = R("tf", [64, 512]); h1 = R("h1", [64, 512]); h2 = R("h2", [64, 512])
        dec = R("dec", [128, 512]); kf = R("kf", [128, 512]); kb = R("kb", [128, 512], BF16)
        npart = p.sbuf([128, 32], F32, "npart")
        def sin_layer(ps, bfs, out, par):
            u = ua[par]
            p.op("dve", [ps, bfs], [u], lambda e: e.tensor_scalar(out=u[:, :], in0=ps[:64, :], scalar1=bfs[:, 0:1], scalar2=bfs[:, 1:2], op0=ALU.add, op1=ALU.mult))
            p.op("dve", [u], [ti[par]], lambda e: e.tensor_copy(out=ti[par][:, :], in_=u[:, :]))
            p.op("dve", [ti[par]], [tf[par]], lambda e: e.tensor_copy(out=tf[par][:, :], in_=ti[par][:, :]))
            p.op("dve", [u, tf[par]], [u], lambda e: e.tensor_tensor(out=u[:, :], in0=u[:, :], in1=tf[par][:, :], op=ALU.subtract))
            p.op("act", [u], [out], lambda e: e.activation(out=out[:, :], in_=u[:, :], func=AF.Sin, scale=TWO_PI))
        for blk in range(32):
            par = blk % 2; bsl = slice(blk * 512, (blk + 1) * 512)
            em = emb[par]
            p.dma("sp", em[:, :], embT[:, bsl], [iD], [em])
            p.dma("sp", tvb[par][:, :], tvec[:, bsl], [iD], [tvb[par]])
            p1 = c["ps"][2 + par]; p2 = c["ps"][4 + par]; p3 = c["ps"][6]; pd = c["ps"][7]
            p.op("pe", [w1s, em], [p1], lambda e: e.matmul(p1[:64, :], lhsT=w1s[:, :], rhs=em[:, :], start=True, stop=True))
            sin_layer(p1, bf1s, h1[par], par)
            p.op("pe", [w2s, h1[par]], [p2], lambda e: e.matmul(p2[:64, :], lhsT=w2s[:, :], rhs=h1[par][:, :], start=True, stop=True))
            sin_layer(p2, bf2s, h2[par], par)
            p.op("pe", [w3s, h2[par]], [p3], lambda e: e.matmul(p3[:, :], lhsT=w3s[:, :], rhs=h2[par][:, :], start=True, stop=True))
            p.op("pe", [dls, tvb[par]], [pd], lambda e: e.matmul(pd[:, :], lhsT=dls[:, :], rhs=tvb[par][:, :], start=True, stop=True))
            p.op("act", [pd], [dec[par]], lambda e: e.activation(out=dec[par][:, :], in_=pd[:, :], func=AF.Exp))
            p.op("dve", [p3, dec[par]], [kf[par]], lambda e: e.tensor_tensor(out=kf[par][:, :], in0=p3[:, :], in1=dec[par][:, :], op=ALU.mult))
            if blk == 31:
                p.op("dve", [kf[par]], [kf[par]], lambda e: e.memset(kf[par][64:128, 511:512], 0.0))
            p.op("dve", [kf[par]], [npart], lambda e: e.tensor_reduce(out=npart[:, blk:blk + 1], in_=kf[par][:, :], axis=AX.X, op=ALU.add, apply_absolute_value=True))
            p.op("pool", [kf[par]], [kb[par]], lambda e: e.tensor_copy(out=kb[par][:, :], in_=kf[par][:, :]))
            p.dma("pool", kx[:, bsl], kb[par][:, :], [kb[par]], [kxD])
        nsum = p.sbuf([128, 1], F32, "nsum"); prn = p.sbuf([128, 128], F32, "prn")
        p.op("dve", [npart], [nsum], lambda e: e.tensor_reduce(out=nsum[:, 0:1], in_=npart[:, :], axis=AX.X, op=ALU.add))
        p.op("dve", [prs, nsum], [prn], lambda e: e.tensor_scalar(out=prn[:, :], in0=prs[:, :], scalar1=nsum[:, 0:1], scalar2=None, op0=ALU.mult))
        pn = c["ps"][0]
        p.op("pe", [c["ones"], prn], [pn], lambda e: e.matmul(pn[:, 0:128], lhsT=c["ones"][:, :], rhs=prn[:, :], start=True, stop=True))
        p.op("dve", [pn], [rn], lambda e: e.reciprocal(out=rn[:, :], in_=pn[:, 0:128]))
        scC.__exit__(None, None, None)
        strip = [p.sbuf([128, S], BF16, f"strip{i}") for i in range(2)]
        yst = [p.sbuf([128, 8, 128], F32, f"yst{i}") for i in range(2)]
        for row in range(NROW):
            st = strip[row % 2]
            src = bass.AP(tensor=kx, offset=row * (S + 128), ap=[[1, 128], [1, S]])
            p.dma("sp" if row % 2 == 0 else "pool", st[:, :], src, [kxD], [st])
            py = c["ps"][1 + row % 2]
            for d in range(128):
                base = 16256 - 128 * d
                p.op("pe", [st, Zall], [py], lambda e: e.matmul(py[:, d:128], lhsT=st[:, base:base + 128], rhs=Zall[:, row, 0:128 - d], start=(d == 0), stop=(d == 127)), same_ok=True)
            ys = yst[(row // 8) % 2]
            p.op("act", [py, rn], [ys], lambda e: e.activation(out=ys[:, row % 8, :], in_=py[:, 0:128], func=AF.Copy, scale=rn[:, row:row + 1]))
            if row % 8 == 7:
                p.dma("act", Y[:, row - 7:row + 1, :], ys[:, :, :], [ys], [oD])
        p.finish([oD])
        print("k2d ninst", p.ninst)
    return nc

def hy_consts():
    L = S
    t = np.linspace(0.0, 1.0, L, dtype=np.float32)[:, None]
    bands = 16
    freqs = np.linspace(1e-4, bands - 1, bands, dtype=np.float32)[None]
    w = (np.float32(2.0 * math.pi) * np.arange(L, dtype=np.float32)[:, None] / np.float32(L)).astype(np.float32)
    z = np.concatenate([t, np.cos(freqs * w), -np.sin(freqs * w)], axis=-1).astype(np.float32)
    min_decay = math.log(1e-2) / 1.5; max_decay = math.log(1e-2) / 0.3
    deltas = np.abs(np.linspace(min_decay, max_decay, 512, dtype=np.float32))
    return z, t[:, 0], deltas

def host_inputs_k2d(zT, inp, layer):
    DU = 11040 - 4096 - 1536
    z, t, deltas = hy_consts()
    embT = np.ascontiguousarray(z[::-1].T); tvec = np.ascontiguousarray(t[::-1][None, :])
    pair = (np.arange(128)[:, None] % 64 == np.arange(128)[None, :] % 64).astype(np.float32)
    cw = inp["hy_conv_w"][layer]; cb = inp["hy_conv_b"][layer]
    def cwpack(cols, rev):
        a = cw[:, cols].T
        if rev: a = a[:, ::-1]
        return np.ascontiguousarray(np.concatenate([a, cb[cols][:, None]], 1).astype(np.float32))
    maps = []
    for r in range(8):
        ch = np.arange(64 * r, 64 * r + 64)
        x0 = zT[DU + ch]; x1 = zT[DU + 512 + ch]; v = zT[DU + 1024 + ch]
        m = {"ux1": np.ascontiguousarray(np.concatenate([x1, x1[:, ::-1]], 0)), "uv": np.ascontiguousarray(np.concatenate([v, v[:, ::-1]], 0)), "ux0": np.ascontiguousarray(x0),
             "cw1": np.concatenate([cwpack(512 + ch, False), cwpack(512 + ch, True)], 0), "cwv": np.concatenate([cwpack(1024 + ch, False), cwpack(1024 + ch, True)], 0), "cw0": cwpack(ch, False),
             "embT": embT, "w1": np.ascontiguousarray(inp["hy_w1"][layer]), "w2": np.ascontiguousarray(inp["hy_w2"][layer]),
             "w3": np.ascontiguousarray(np.concatenate([inp["hy_w3"][layer][:, ch], inp["hy_w3"][layer][:, 512 + ch]], 1)),
             "bf1": np.ascontiguousarray(np.stack([inp["hy_b1"][layer], inp["hy_freq1"][layer]], 1)), "bf2": np.ascontiguousarray(np.stack([inp["hy_b2"][layer], inp["hy_freq2"][layer]], 1)),
             "dlrow": np.ascontiguousarray(-np.concatenate([deltas[ch], deltas[ch]])[None, :].astype(np.float32)), "tvec": tvec, "pair": pair}
        maps.append(m)
    return maps

def host_post_k2d(results):
    ys, zs, xs = [], [], []
    for r in results:
        Yr = r["Y"]
        y = Yr[::-1].transpose(1, 2, 0).reshape(128, S)
        ys.append(y[:64] + y[64:, ::-1]); zs.append(r["zz"]); xs.append(r["x0c"])
    return np.concatenate(ys, 0), np.concatenate(zs, 0), np.concatenate(xs, 0)


S = 16384; T = 2048; D = 1024; NIN = 11040; DFF = 4096

def linear2(p, c, w_ap, K, N, rhs_fn, ntb, evac, wst, wbf, gw, psi=2, npsi=4):
    kc = K // 128
    wD = Buf(None)
    ng = (N + gw - 1) // gw
    k2 = 0
    for gi in range(ng):
        c0 = gi * gw; wg = min(gw, N - c0)
        st = wst[gi % len(wst)]; wb = wbf[gi % len(wbf)]
        p.dma("sp", st[:, :kc, :wg], w_ap[:, c0:c0 + wg].rearrange("(c p) n -> p c n", p=128), [wD], [st])
        h = kc // 2
        p.op("pool", [st], [wb], lambda e: e.tensor_copy(out=wb[:, :h, :wg], in_=st[:, :h, :wg]))
        p.op("dve", [st], [wb], lambda e: e.tensor_copy(out=wb[:, h:kc, :wg], in_=st[:, h:kc, :wg]))
        for mi in range((wg + 127) // 128):
            mw = min(128, wg - mi * 128)
            for tb in range(ntb):
                ps = c["ps"][psi + k2 % npsi]; k2 += 1
                for ch in range(kc):
                    rb, rap = rhs_fn(ch, tb)
                    p.op("pe", [wb, rb], [ps], lambda e: e.matmul(ps[:mw, :], lhsT=wb[:, ch, mi * 128:mi * 128 + mw], rhs=rap, start=(ch == 0), stop=(ch == kc - 1)), same_ok=True)
                evac(c0 + mi * 128, mw, tb, ps)

def build_k3(lam_init, last):
    nc = bass.Bass("TRN2", target_bir_lowering=False)
    I = lambda n, s: nc.dram_tensor(n, s, F32, kind="ExternalInput")
    xT = I("xT", [D, T])
    of = I("of", [512, T]); ob = I("ob", [512, T]); rT = I("rT", [512, T]); gla_g = I("gla_g", [128, 1])
    dN = I("dN", [3, 256, T]); dL = I("dL", [3, 256, T])
    oc0 = I("oc0", [512, T]); oc1 = I("oc1", [512, T]); lamb = I("lamb", [128, 256]); sub_g = I("sub_g", [128, 1])
    yconv = I("yconv", [512, T]); zzT = I("zzT", [512, T]); x0c = I("x0c", [512, T]); skip = I("skip", [128, 4])
    gates = I("gates", [4096, T])
    projs = I("projs", [1792, D]); w_out = I("w_out", [D, D]); g2 = I("g2", [128, 8]); w1 = I("w1", [D, DFF]); w2 = I("w2", [DFF, D])
    xo = nc.dram_tensor("xo", [D, T], F32, kind="ExternalOutput")
    if not last:
        g1n = I("g1n", [128, 8]); w_in = I("w_in", [D, NIN])
        zT = nc.dram_tensor("zT", [NIN, T], F32, kind="ExternalOutput")
    with ExitStack() as es:
        p = Prog(nc, es)
        c = consts(p)
        iD = Buf(None); oD = Buf(None)
        def ld(name, shape, src, q="pool"):
            b = p.sbuf(shape, F32, name); p.dma(q, b[tuple(slice(None) for _ in shape)], src, [iD], [b]); return b
        glag = ld("glag", [128, 1], gla_g[:, :]); subg = ld("subg", [128, 1], sub_g[:, :]); lams = ld("lams", [128, 256], lamb[:, :]); skp = ld("skp", [128, 4], skip[:, :])
        g2s = ld("g2s", [128, 8], g2[:, :])
        if not last: g1s = ld("g1s", [128, 8], g1n[:, :])
        tmp = [p.sbuf([128, 512], F32, f"tmp{i}") for i in range(2)]
        rstds = [p.sbuf([128, 512], F32, f"rstd{i}") for i in range(4)]
        mb = [p.sbuf([128, T], BF16, f"mb{i}") for i in range(8)]
        lcol = p.sbuf([128, 4], F32, "lcol"); lpr = p.sbuf([128, 128], F32, "lpr")
        p.op("dve", [lams], [lpr], lambda e: e.tensor_tensor(out=lpr[:, 0:64], in0=lams[:, 0:64], in1=lams[:, 64:128], op=ALU.mult))
        p.op("dve", [lams], [lpr], lambda e: e.tensor_tensor(out=lpr[:, 64:128], in0=lams[:, 128:192], in1=lams[:, 192:256], op=ALU.mult))
        p.op("dve", [lpr], [lcol], lambda e: e.tensor_reduce(out=lcol[:, 0:1], in_=lpr[:, 0:64], axis=AX.X, op=ALU.add))
        p.op("dve", [lpr], [lcol], lambda e: e.tensor_reduce(out=lcol[:, 1:2], in_=lpr[:, 64:128], axis=AX.X, op=ALU.add))
        p.op("act", [lcol], [lcol], lambda e: e.activation(out=lcol[:, 0:2], in_=lcol[:, 0:2], func=AF.Exp))
        p.op("dve", [lcol], [lcol], lambda e: e.tensor_tensor(out=lcol[:, 2:3], in0=lcol[:, 0:1], in1=lcol[:, 1:2], op=ALU.subtract))
        p.op("dve", [lcol], [lcol], lambda e: e.tensor_scalar(out=lcol[:, 3:4], in0=lcol[:, 2:3], scalar1=float(lam_init), scalar2=-1.0, op0=ALU.add, op1=ALU.mult))
        sc1 = p.scope(); sc1.__enter__()
        ybf = [p.sbuf([128, T], BF16, f"ybf{i}") for i in range(14)]
        A = [p.sbuf([128, T], F32, f"A{i}") for i in range(2)]; B = [p.sbuf([128, T], F32, f"B{i}") for i in range(2)]; Cc = [p.sbuf([128, T], F32, f"C{i}") for i in range(2)]
        def rms128(src, gcol, dst, extra_scale=None):
            for tb in range(T // 512):
                sl = slice(tb * 512, (tb + 1) * 512); s = tmp[tb % 2]; r = rstds[tb % 4]; ps = c["ps"][tb % 2]
                p.op("act", [src], [s], lambda e: e.activation(out=s[:, :], in_=src[:, sl], func=AF.Square))
                p.op("pe", [c["ones"], s], [ps], lambda e: e.matmul(ps[:, :], lhsT=c["ones"][:, :], rhs=s[:, :], start=True, stop=True))
                p.op("act", [ps, c["eps"]], [r], lambda e: e.activation(out=r[:, :], in_=ps[:, :], func=AF.Sqrt, bias=c["eps"][:, 0:1], scale=1.0 / 128))
                p.op("dve", [r], [r], lambda e: e.reciprocal(out=r[:, :], in_=r[:, :]))
                p.op("dve", [src, r, gcol], [dst], lambda e: e.scalar_tensor_tensor(out=dst[:, sl], in0=src[:, sl], scalar=gcol[:, 0:1], in1=r[:, :], op0=ALU.mult, op1=ALU.mult))
        for h in range(4):
            hs = slice(h * 128, (h + 1) * 128); a = A[h % 2]; b = B[h % 2]; cc = Cc[h % 2]
            p.dma("sp", a[:, :], of[hs, :], [iD], [a]); p.dma("sp", b[:, :], ob[hs, :], [iD], [b]); p.dma("sp", cc[:, :], rT[hs, :], [iD], [cc])
            p.op("pool", [a, b], [a], lambda e: e.tensor_tensor(out=a[:, :], in0=a[:, :], in1=b[:, :], op=ALU.add))
            rms128(a, glag, b)
            p.op("act", [cc], [cc], lambda e: e.activation(out=cc[:, :], in_=cc[:, :], func=AF.Silu))
            p.op("dve", [b, cc], [ybf[h]], lambda e: e.tensor_tensor(out=ybf[h][:, :], in0=b[:, :], in1=cc[:, :], op=ALU.mult))
        for h in range(4):
            hs = slice(h * 128, (h + 1) * 128); a = A[h % 2]; b = B[h % 2]; cc = Cc[h % 2]
            p.dma("sp", a[:, :], oc0[hs, :], [iD], [a]); p.dma("sp", b[:, :], oc1[hs, :], [iD], [b])
            p.op("dve", [a, b, lcol], [a], lambda e: e.scalar_tensor_tensor(out=a[:, :], in0=b[:, :], scalar=lcol[:, 3:4], in1=a[:, :], op0=ALU.mult, op1=ALU.add))
            rms128(a, subg, cc)
            p.op("pool", [cc], [ybf[6 + h]], lambda e: e.tensor_scalar(out=ybf[6 + h][:, :], in0=cc[:, :], scalar1=float(1.0 - lam_init), scalar2=None, op0=ALU.mult))
        for h in range(4):
            hs = slice(h * 128, (h + 1) * 128); a = A[h % 2]; b = B[h % 2]; cc = Cc[h % 2]
            p.dma("sp", a[:, :], yconv[hs, :], [iD], [a]); p.dma("sp", b[:, :], zzT[hs, :], [iD], [b]); p.dma("sp", cc[:, :], x0c[hs, :], [iD], [cc])
            p.op("dve", [a, b, skp], [a], lambda e: e.scalar_tensor_tensor(out=a[:, :], in0=b[:, :], scalar=skp[:, h:h + 1], in1=a[:, :], op0=ALU.mult, op1=ALU.add))
            p.op("pool", [a, cc], [ybf[10 + h]], lambda e: e.tensor_tensor(out=ybf[10 + h][:, :], in0=a[:, :], in1=cc[:, :], op=ALU.mult))
        for hh in range(2):
            hs = slice(hh * 128, (hh + 1) * 128); a = A[hh % 2]; b = B[hh % 2]; cc = Cc[hh % 2]
            p.dma("sp", a[:, :], dN[0, hs, :], [iD], [a]); p.dma("sp", b[:, :], dL[0, hs, :], [iD], [b])
            for g in (1, 2):
                p.dma("sp", cc[:, :], dN[g, hs, :], [iD], [cc])
                p.op("dve", [a, cc], [a], lambda e: e.tensor_tensor(out=a[:, :], in0=a[:, :], in1=cc[:, :], op=ALU.add))
                p.dma("sp", cc[:, :], dL[g, hs, :], [iD], [cc])
                p.op("dve", [b, cc], [b], lambda e: e.tensor_tensor(out=b[:, :], in0=b[:, :], in1=cc[:, :], op=ALU.add))
            p.op("dve", [b], [b], lambda e: e.reciprocal(out=b[:, :], in_=b[:, :]))
            p.op("dve", [a, b], [ybf[4 + hh]], lambda e: e.tensor_tensor(out=ybf[4 + hh][:, :], in0=a[:, :], in1=b[:, :], op=ALU.mult))
        pst = [p.sbuf([128, D], F32, f"pst{i}") for i in range(2)]; pbf = p.sbuf([128, 14, D], BF16, "pbf")
        for kc_ in range(14):
            p.dma("sp", pst[kc_ % 2][:, :], projs[kc_ * 128:(kc_ + 1) * 128, :], [iD], [pst[kc_ % 2]])
            p.op("pool", [pst[kc_ % 2]], [pbf], lambda e: e.tensor_copy(out=pbf[:, kc_, :], in_=pst[kc_ % 2][:, :]))
        branches = [(0, 4), (4, 6), (6, 10), (10, 14)]
        gst = A + B
        macc = Cc
        k = 0; kk = 0
        for mi in range(8):
            ms = slice(mi * 128, (mi + 1) * 128); ma = macc[mi % 2]
            for bi, (k0, k1) in enumerate(branches):
                gt = gst[k % 4]; k += 1
                p.dma("sp", gt[:, :], gates[bi * 1024 + mi * 128:bi * 1024 + (mi + 1) * 128, :], [iD], [gt])
                p.op("act", [gt], [gt], lambda e: e.activation(out=gt[:, :], in_=gt[:, :], func=AF.Sigmoid))
                for tb in range(4):
                    sl = slice(tb * 512, (tb + 1) * 512); ps = c["ps"][2 + kk % 4]; kk += 1
                    for kc_ in range(k0, k1):
                        p.op("pe", [pbf, ybf[kc_]], [ps], lambda e: e.matmul(ps[:, :], lhsT=pbf[:, kc_, ms], rhs=ybf[kc_][:, sl], start=(kc_ == k0), stop=(kc_ == k1 - 1)), same_ok=True)
                    if bi == 0:
                        p.op("dve", [ps, gt], [ma], lambda e: e.tensor_tensor(out=ma[:, sl], in0=ps[:, :], in1=gt[:, sl], op=ALU.mult))
                    else:
                        t = tmp[kk % 2]
                        p.op("dve", [ps, gt], [t], lambda e: e.tensor_tensor(out=t[:, :], in0=ps[:, :], in1=gt[:, sl], op=ALU.mult))
                        p.op("pool", [t, ma], [ma], lambda e: e.tensor_tensor(out=ma[:, sl], in0=ma[:, sl], in1=t[:, :], op=ALU.add))
            p.op("act", [ma], [mb[mi]], lambda e: e.copy(out=mb[mi][:, :], in_=ma[:, :]))
        sc1.__exit__(None, None, None)
        xs = [p.sbuf([128, T], F32, f"x{i}") for i in range(8)]
        for i in range(8):
            p.dma("pool", xs[i][:, :], xT[i * 128:(i + 1) * 128, :], [iD], [xs[i]])
        GW = 256
        wst = [p.sbuf([128, 8, GW], F32, f"wst{i}") for i in range(2)]; wbf = [p.sbuf([128, 8, GW], BF16, f"wbf{i}") for i in range(2)]
        def evac_addx(m0, mw, tb, ps):
            ci = m0 // 128; sl = slice(tb * 512, (tb + 1) * 512)
            p.op("dve", [ps, xs[ci]], [xs[ci]], lambda e: e.tensor_tensor(out=xs[ci][:, sl], in0=xs[ci][:, sl], in1=ps[:, :], op=ALU.add))
        linear2(p, c, w_out, D, D, lambda ch, tb: (mb[ch], mb[ch][:, tb * 512:(tb + 1) * 512]), 4, evac_addx, wst, wbf, GW)
        hs_ = mb
        rms_fm(p, c, xs, g2s, hs_, 8, T, tmp, rstds)
        sc2 = p.scope(); sc2.__enter__()
        ub = [p.sbuf([128, 512], BF16, f"ub{i}") for i in range(32)]
        w2st = [p.sbuf([128, 32, 128], F32, "w2st0")]; w2bf = [p.sbuf([128, 32, 128], BF16, f"w2bf{i}") for i in range(2)]
        for tb in range(4):
            sl = slice(tb * 512, (tb + 1) * 512)
            def evac_u(m0, mw, tb_, ps):
                mi = m0 // 128; t = tmp[mi % 2]
                p.op("act", [ps], [t], lambda e: e.activation(out=t[:, :], in_=ps[:, :], func=AF.Relu))
                p.op("pool", [t], [ub[mi]], lambda e: e.tensor_tensor(out=ub[mi][:, :], in0=t[:, :], in1=t[:, :], op=ALU.mult))
            linear2(p, c, w1, D, DFF, lambda ch, tb_: (hs_[ch], hs_[ch][:, sl]), 1, evac_u, wst, wbf, GW)
            def evac_x2(m0, mw, tb_, ps):
                ci = m0 // 128
                p.op("dve", [ps, xs[ci]], [xs[ci]], lambda e: e.tensor_tensor(out=xs[ci][:mw, sl], in0=xs[ci][:mw, sl], in1=ps[:mw, :], op=ALU.add))
            linear2(p, c, w2, DFF, D, lambda ch, tb_: (ub[ch], ub[ch][:, :]), 1, evac_x2, w2st, w2bf, 128)
        sc2.__exit__(None, None, None)
        for i in range(8):
            p.dma("act", xo[i * 128:(i + 1) * 128, :], xs[i][:, :], [xs[i]], [oD])
        if not last:
            rms_fm(p, c, xs, g1s, hs_, 8, T, tmp, rstds)
            ost = [p.sbuf([128, T], F32, f"ost{i}") for i in range(2)]
            state = {"k": 0}
            def evac_z(m0, mw, tb, ps):
                o = ost[state["k"] % 2]
                p.op("act", [ps], [o], lambda e: e.copy(out=o[:mw, tb * 512:(tb + 1) * 512], in_=ps[:mw, :]))
                if tb == 3:
                    p.dma("act", zT[m0:m0 + mw, :], o[:mw, :], [o], [oD]); state["k"] += 1
            linear2(p, c, w_in, D, NIN, lambda ch, tb: (hs_[ch], hs_[ch][:, tb * 512:(tb + 1) * 512]), 4, evac_z, wst, wbf, GW)
        p.finish([oD])
        print("k3 ninst", p.ninst)
    return nc


import math as _math
_NC_CACHE = {}
def _get_nc(key, builder):
    if key not in _NC_CACHE:
        _NC_CACHE[key] = builder()
    return _NC_CACHE[key]

def _run(nc, maps):
    return run_bass_kernel_spmd(nc, maps, core_ids=list(range(8))).results

def kernel(**inputs):
    inp = {k: np.asarray(v) for k, v in inputs.items()}
    x = inp["x"][0].astype(np.float32)
    NCORE = 8; Tn = S // NCORE
    xT = np.ascontiguousarray(x.T)
    g = np.ascontiguousarray(inp["norm1_g"][0].reshape(8, 128).T)
    maps = [{"xT": np.ascontiguousarray(xT[:, r * Tn:(r + 1) * Tn]), "g": g, "w": np.ascontiguousarray(inp["w_in"][0])} for r in range(NCORE)]
    res = _run(_get_nc("k1", build_k1), maps)
    zT = np.concatenate([r["zT"] for r in res], axis=1)
    for layer in range(4):
        last = layer == 3
        lam_init = 0.8 - 0.6 * _math.exp(-0.3 * layer)
        ra = _run(_get_nc("k2a", build_k2a), host_inputs_k2a(zT, inp, layer))
        rb = _run(_get_nc("k2b", build_k2b), host_inputs_k2b(zT, inp, layer))
        rc = _run(_get_nc("k2c", build_k2c), host_inputs_k2c(zT, inp, layer))
        rd = _run(_get_nc("k2d", build_k2d), host_inputs_k2d(zT, inp, layer))
        of = np.concatenate([ra[2 * h]["oT"] for h in range(4)], 0); ob = np.concatenate([ra[2 * h + 1]["oT"][:, ::-1] for h in range(4)], 0)
        dN, dL = host_post_k2b(rb)
        oc0 = np.concatenate([rc[2 * h]["oT"] for h in range(4)], 0); oc1 = np.concatenate([rc[2 * h + 1]["oT"] for h in range(4)], 0)
        yconv, zzf, x0c = host_post_k2d(rd)
        projs = np.ascontiguousarray(np.concatenate([inp["proj_a"][layer], inp["proj_b"][layer], inp["proj_c"][layer], inp["proj_d"][layer]], 0))
        common = {"gla_g": np.ascontiguousarray(inp["gla_norm_g"][layer].reshape(128, 1)),
                  "lamb": np.ascontiguousarray(np.broadcast_to(inp["diff_lambda"][layer].reshape(1, 256), (128, 256))),
                  "sub_g": np.ascontiguousarray(inp["diff_subln_g"][layer].reshape(128, 1)),
                  "skip": np.ascontiguousarray(inp["hy_skip"][layer].reshape(4, 128).T),
                  "projs": projs, "w_out": np.ascontiguousarray(inp["w_out"][layer]),
                  "g2": np.ascontiguousarray(inp["norm2_g"][layer].reshape(8, 128).T),
                  "w1": np.ascontiguousarray(inp["mlp_w1"][layer]), "w2": np.ascontiguousarray(inp["mlp_w2"][layer])}
        if not last:
            common["g1n"] = np.ascontiguousarray(inp["norm1_g"][layer + 1].reshape(8, 128).T)
            common["w_in"] = np.ascontiguousarray(inp["w_in"][layer + 1])
        maps = []
        for r in range(NCORE):
            cs = slice(r * Tn, (r + 1) * Tn)
            cc = lambda a: np.ascontiguousarray(a[..., cs])
            m = dict(common)
            m.update({"xT": cc(xT), "of": cc(of), "ob": cc(ob), "rT": cc(zT[1024:1536]), "dN": cc(dN), "dL": cc(dL),
                      "oc0": cc(oc0), "oc1": cc(oc1), "yconv": cc(yconv), "zzT": cc(zzf), "x0c": cc(x0c), "gates": cc(zT[6944:11040])})
            maps.append(m)
        res = _run(_get_nc(("k3", layer), lambda: build_k3(lam_init, last)), maps)
        xT = np.concatenate([r["xo"] for r in res], axis=1)
        if not last:
            zT = np.concatenate([r["zT"] for r in res], axis=1)
    return np.ascontiguousarray(xT.T)[None].astype(np.float32)
```

```python
import numpy as np
from contextlib import ExitStack
import concourse.bass as bass
import concourse.mybir as mybir
from concourse.bass_utils import run_bass_kernel_spmd

F32 = mybir.dt.float32
BF16 = mybir.dt.bfloat16
AF = mybir.ActivationFunctionType
ALU = mybir.AluOpType
AX = mybir.AxisListType

class Buf:
    __slots__ = ("t", "w", "r", "name")
    def __init__(self, t, name=""):
        self.t = t; self.w = None; self.r = {}; self.name = name
    def __getitem__(self, idx):
        return self.t[idx]

class Prog:
    NDSEM = 6
    def __init__(self, nc, es):
        self.nc = nc; self.es = es
        self.eng = {"pe": nc.tensor, "act": nc.scalar, "dve": nc.vector, "pool": nc.gpsimd, "sp": nc.sync}
        self.sem = {}; self.cnt = {}
        for e in self.eng:
            self.sem[e] = es.enter_context(nc.semaphore("s_" + e)); self.cnt[e] = 0
        self.seen = {e: {} for e in self.eng}
        self.dq = {}
        for q in ("sp", "pool", "act"):
            self.dq[q] = dict(n=0, sems=[es.enter_context(nc.semaphore(f"d_{q}{i}")) for i in range(self.NDSEM)])
            for i in range(self.NDSEM):
                self.sem[(q, i)] = self.dq[q]["sems"][i]
        self.ninst = 0
    def sbuf(self, shape, dt, name):
        return Buf(self.es.enter_context(self.nc.sbuf_tensor(name, list(shape), dt)), name)
    def psum(self, shape, dt, name):
        return Buf(self.es.enter_context(self.nc.psum_tensor(name, list(shape), dt)), name)
    def dram(self, name, shape, dt, kind="Internal"):
        return Buf(self.nc.dram_tensor(name, list(shape), dt, kind=kind), name)
    def _wait(self, e, tok):
        if tok is None: return
        s, v = tok
        if self.seen[e].get(s, 0) >= v: return
        self.eng[e].wait_ge(self.sem[s], v)
        self.seen[e][s] = v
    def _deps(self, e, reads, writes, same_ok=False):
        for b in reads:
            if b.w is not None and not (same_ok and b.w[0] == e):
                self._wait(e, b.w)
        for b in writes:
            if b.w is not None and not (same_ok and b.w[0] == e):
                self._wait(e, b.w)
            for s, v in b.r.items():
                if same_ok and s == e: continue
                self._wait(e, (s, v))
    def _mark(self, tok, reads, writes):
        for b in reads:
            if b.r.get(tok[0], 0) < tok[1]: b.r[tok[0]] = tok[1]
        for b in writes:
            b.w = tok; b.r = {}
    def op(self, e, reads, writes, fn, same_ok=False):
        self._deps(e, reads, writes, same_ok)
        ins = fn(self.eng[e])
        self.cnt[e] += 1
        ins.then_inc(self.sem[e], 1)
        tok = (e, self.cnt[e])
        self.seen[e][e] = max(self.seen[e].get(e, 0), 0)
        self._mark(tok, reads, writes)
        self.ninst += 1
        return tok
    def dma(self, q, out_ap, in_ap, reads, writes, **kw):
        d = self.dq[q]; j = d["n"]; slot = j % self.NDSEM; val = 16 * (j // self.NDSEM + 1)
        if j >= self.NDSEM:
            self._wait(q, ((q, slot), val - 16))
        self._deps(q, reads, writes)
        ins = self.eng[q].dma_start(out=out_ap, in_=in_ap, **kw)
        ins.then_inc(d["sems"][slot], 16)
        d["n"] += 1
        tok = ((q, slot), val)
        self._mark(tok, reads, writes)
        self.ninst += 1
        return tok
    def finish(self, bufs):
        for b in bufs:
            self._wait("sp", b.w)
        for q in self.dq:
            d = self.dq[q]
            for j in range(max(0, d["n"] - self.NDSEM), d["n"]):
                self._wait("sp", ((q, j % self.NDSEM), 16 * (j // self.NDSEM + 1)))

def _barrier(self):
    for e in self.eng:
        for e2 in self.eng:
            if e2 != e and self.cnt[e2] > 0:
                self._wait(e, (e2, self.cnt[e2]))
        for q, d in self.dq.items():
            for j in range(max(0, d["n"] - self.NDSEM), d["n"]):
                self._wait(e, ((q, j % self.NDSEM), 16 * (j // self.NDSEM + 1)))
Prog.barrier = _barrier

class _Scope:
    def __init__(self, p): self.p = p
    def __enter__(self):
        self.old = self.p.es; self.es = ExitStack(); self.es.__enter__(); self.p.es = self.es; return self
    def __exit__(self, *a):
        self.p.barrier(); self.p.es = self.old; return self.es.__exit__(*a)
Prog.scope = lambda self: _Scope(self)

S = 16384; NC = 8; T = S // NC; D = 1024; NIN = 11040
EPS = 1e-6

def consts(p):
    c = {}
    c["ones"] = p.sbuf([128, 128], F32, "c_ones")
    p.op("pool", [], [c["ones"]], lambda e: e.memset(c["ones"][:, :], 1.0))
    c["eps"] = p.sbuf([128, 1], F32, "c_eps")
    p.op("pool", [], [c["eps"]], lambda e: e.memset(c["eps"][:, :], EPS))
    c["ps"] = [p.psum([128, 512], F32, f"ps{i}") for i in range(8)]
    return c

def rms_fm(p, c, xs, gs, hs, nch, Tn, tmp, rstds, psi=0):
    nfeat = nch * 128
    k = 0
    for tb in range(Tn // 512):
        sl = slice(tb * 512, (tb + 1) * 512)
        ps = c["ps"][psi + tb % 2]
        for ch in range(nch):
            s = tmp[k % 2]; k += 1
            if ch % 2 == 0:
                p.op("act", [xs[ch]], [s], lambda e: e.activation(out=s[:, :], in_=xs[ch][:, sl], func=AF.Square))
            else:
                p.op("dve", [xs[ch]], [s], lambda e: e.tensor_tensor(out=s[:, :], in0=xs[ch][:, sl], in1=xs[ch][:, sl], op=ALU.mult))
            p.op("pe", [c["ones"], s], [ps], lambda e: e.matmul(ps[:, :], lhsT=c["ones"][:, :], rhs=s[:, :], start=(ch == 0), stop=(ch == nch - 1)), same_ok=True)
        r = rstds[tb]
        p.op("act", [ps, c["eps"]], [r], lambda e: e.activation(out=r[:, :], in_=ps[:, :], func=AF.Sqrt, bias=c["eps"][:, 0:1], scale=1.0 / nfeat))
        p.op("dve", [r], [r], lambda e: e.reciprocal(out=r[:, :], in_=r[:, :]))
        for ch in range(nch):
            eng = "dve"
            p.op(eng, [xs[ch], r, gs], [hs[ch]], lambda e: e.scalar_tensor_tensor(out=hs[ch][:, sl], in0=xs[ch][:, sl], scalar=gs[:, ch:ch + 1], in1=r[:, :], op0=ALU.mult, op1=ALU.mult))

def linear_stream(p, c, w_ap, K, N, hs, Tn, evac, wst, wbf, psi=2, npsi=4, gw=512):
    kc = K // 128
    wD = Buf(None)
    ng = (N + gw - 1) // gw
    k2 = 0
    for gi in range(ng):
        c0 = gi * gw; wg = min(gw, N - c0)
        st = wst[gi % 2]; wb = wbf[gi % 2]
        p.dma("sp", st[:, :kc, :wg], w_ap[:, c0:c0 + wg].rearrange("(c p) n -> p c n", p=128), [wD], [st])
        h = kc // 2
        p.op("pool", [st], [wb], lambda e: e.tensor_copy(out=wb[:, :h, :wg], in_=st[:, :h, :wg]))
        p.op("dve", [st], [wb], lambda e: e.tensor_copy(out=wb[:, h:kc, :wg], in_=st[:, h:kc, :wg]))
        for mi in range((wg + 127) // 128):
            mw = min(128, wg - mi * 128)
            for tb in range(Tn // 512):
                ps = c["ps"][psi + k2 % npsi]; k2 += 1
                for ch in range(kc):
                    p.op("pe", [wb, hs[ch]], [ps], lambda e: e.matmul(ps[:mw, :], lhsT=wb[:, ch, mi * 128:mi * 128 + mw], rhs=hs[ch][:, tb * 512:(tb + 1) * 512], start=(ch == 0), stop=(ch == kc - 1)), same_ok=True)
                evac(c0 + mi * 128, mw, tb, ps)

def build_k1():
    nc = bass.Bass("TRN2", target_bir_lowering=False)
    xT = nc.dram_tensor("xT", [D, T], F32, kind="ExternalInput")
    g = nc.dram_tensor("g", [128, 8], F32, kind="ExternalInput")
    w = nc.dram_tensor("w", [D, NIN], F32, kind="ExternalInput")
    zT = nc.dram_tensor("zT", [NIN, T], F32, kind="ExternalOutput")
    with ExitStack() as es:
        p = Prog(nc, es)
        c = consts(p)
        xs = [p.sbuf([128, T], F32, f"x{i}") for i in range(8)]
        hs = [p.sbuf([128, T], BF16, f"h{i}") for i in range(8)]
        gs = p.sbuf([128, 8], F32, "gs")
        xD = Buf(None); zD = Buf(None)
        p.dma("pool", gs[:, :], g[:, :], [xD], [gs])
        for i in range(8):
            p.dma("pool", xs[i][:, :], xT[i * 128:(i + 1) * 128, :], [xD], [xs[i]])
        tmp = [p.sbuf([128, 512], F32, f"tmp{i}") for i in range(2)]
        rstds = [p.sbuf([128, 512], F32, f"rstd{i}") for i in range(T // 512)]
        rms_fm(p, c, xs, gs, hs, 8, T, tmp, rstds)
        wst = [p.sbuf([128, 8, 512], F32, f"wst{i}") for i in range(2)]
        wbf = [p.sbuf([128, 8, 512], BF16, f"wbf{i}") for i in range(2)]
        ost = [p.sbuf([128, T], F32, f"ost{i}") for i in range(2)]
        state = {"k": 0}
        def evac(m0, mw, tb, ps):
            o = ost[state["k"] % 2]
            p.op("act", [ps], [o], lambda e: e.copy(out=o[:mw, tb * 512:(tb + 1) * 512], in_=ps[:mw, :]))
            if tb == T // 512 - 1:
                p.dma("act", zT[m0:m0 + mw, :], o[:mw, :], [o], [zD])
                state["k"] += 1
        linear_stream(p, c, w.ap() if hasattr(w, "ap") else w, D, NIN, hs, T, evac, wst, wbf)
        p.finish([zD])
        print("k1 ninst", p.ninst)
    return nc


S = 16384
NEAR_LO, NEAR_HI = -5, 8
RW = 2304

def t5_bucket_np(rel):
    import math
    half = 16; max_exact = 8
    ret = np.where(rel > 0, half, 0)
    n = np.abs(rel)
    nf = np.maximum(n, 1).astype(np.float32)
    large = max_exact + (np.log(nf / max_exact) / np.float32(math.log(1024 / max_exact)) * (half - max_exact)).astype(np.int32)
    large = np.minimum(large, half - 1)
    return ret + np.where(n < max_exact, n, large)

def qknorm_fm(p, c, src, gcol, dst, rows, Tn, tmp, rtmp, psi):
    for tb in range(Tn // 512):
        sl = slice(tb * 512, (tb + 1) * 512)
        s = tmp[tb % 2]; r = rtmp[tb % 2]; ps = c["ps"][psi + tb % 2]
        p.op("act", [src], [s], lambda e: e.activation(out=s[:rows, :], in_=src[:rows, sl], func=AF.Square))
        p.op("pe", [c["ones"], s], [ps], lambda e: e.matmul(ps[:rows, :], lhsT=c["ones"][:rows, :rows], rhs=s[:rows, :], start=True, stop=True))
        p.op("act", [ps, c["eps"]], [r], lambda e: e.activation(out=r[:rows, :], in_=ps[:rows, :], func=AF.Sqrt, bias=c["eps"][:rows, 0:1], scale=1.0 / rows))
        p.op("dve", [r], [r], lambda e: e.reciprocal(out=r[:rows, :], in_=r[:rows, :]))
        p.op("dve", [src, r, gcol], [dst], lambda e: e.scalar_tensor_tensor(out=dst[:rows, sl], in0=src[:rows, sl], scalar=gcol[:rows, 0:1], in1=r[:rows, :], op0=ALU.mult, op1=ALU.mult))

def build_k2c(NQB=S // 512, NKT=S // 128):
    nc = bass.Bass("TRN2", target_bir_lowering=False)
    qT = nc.dram_tensor("qT", [64, S], F32, kind="ExternalInput")
    kT = nc.dram_tensor("kT", [64, S], F32, kind="ExternalInput")
    v = nc.dram_tensor("v", [128, S // 128, 128], F32, kind="ExternalInput")
    gq = nc.dram_tensor("gq", [64, 1], F32, kind="ExternalInput")
    gk = nc.dram_tensor("gk", [64, 1], F32, kind="ExternalInput")
    Rb = nc.dram_tensor("Rb", [128, RW], F32, kind="ExternalInput")
    bfar = nc.dram_tensor("bfar", [128, 2], F32, kind="ExternalInput")
    oT = nc.dram_tensor("oT", [128, S], F32, kind="ExternalOutput")
    with ExitStack() as es:
        p = Prog(nc, es)
        c = consts(p)
        iD = Buf(None); oD = Buf(None)
        qb16 = p.sbuf([64, S], BF16, "qb16"); kb16 = p.sbuf([64, S], BF16, "kb16")
        vb16 = p.sbuf([128, S // 128, 128], BF16, "vb16")
        rb = p.sbuf([128, RW], F32, "rb"); bf_ = p.sbuf([128, 2], F32, "bfar_s")
        gqs = p.sbuf([64, 1], F32, "gqs"); gks = p.sbuf([64, 1], F32, "gks")
        p.dma("pool", rb[:, :], Rb[:, :], [iD], [rb]); p.dma("pool", bf_[:, :], bfar[:, :], [iD], [bf_])
        p.dma("pool", gqs[:, :], gq[:, :], [iD], [gqs]); p.dma("pool", gks[:, :], gk[:, :], [iD], [gks])
        tmp = [p.sbuf([128, 512], F32, f"tmp{i}") for i in range(2)]
        rtmp = [p.sbuf([128, 512], F32, f"rtmp{i}") for i in range(2)]
        stg = [p.sbuf([128, 4096], F32, f"stg{i}") for i in range(2)]
        k = 0
        for (src, gcol, dst) in ((qT, gqs, qb16), (kT, gks, kb16)):
            for blk in range(S // 4096):
                st = stg[k % 2]; k += 1
                p.dma("sp", st[:64, :], src[:, blk * 4096:(blk + 1) * 4096], [iD], [st])
                class V:
                    pass
                for tb in range(8):
                    sl = slice(tb * 512, (tb + 1) * 512); dsl = slice(blk * 4096 + tb * 512, blk * 4096 + (tb + 1) * 512)
                    s = tmp[tb % 2]; r = rtmp[tb % 2]; ps = c["ps"][6 + tb % 2]
                    p.op("act", [st], [s], lambda e: e.activation(out=s[:64, :], in_=st[:64, sl], func=AF.Square))
                    p.op("pe", [c["ones"], s], [ps], lambda e: e.matmul(ps[:64, :], lhsT=c["ones"][:64, :64], rhs=s[:64, :], start=True, stop=True))
                    p.op("act", [ps, c["eps"]], [r], lambda e: e.activation(out=r[:64, :], in_=ps[:64, :], func=AF.Sqrt, bias=c["eps"][:64, 0:1], scale=1.0 / 64))
                    p.op("dve", [r], [r], lambda e: e.reciprocal(out=r[:64, :], in_=r[:64, :]))
                    p.op("dve", [st, r, gcol], [dst], lambda e: e.scalar_tensor_tensor(out=dst[:64, dsl], in0=st[:64, sl], scalar=gcol[:64, 0:1], in1=r[:64, :], op0=ALU.mult, op1=ALU.mult))
        for blk in range(S // 128 // 32):
            st = stg[k % 2]; k += 1
            p.dma("sp", st[:, :].rearrange("p (a b) -> p a b", b=128), v[:, blk * 32:(blk + 1) * 32, :], [iD], [st])
            p.op("pool", [st], [vb16], lambda e: e.tensor_copy(out=vb16[:, blk * 32:(blk + 1) * 32, :], in_=st[:, :].rearrange("p (a b) -> p a b", b=128)))
        pt = [p.sbuf([128, 512], BF16, f"pt{i}") for i in range(4)]
        nt = [p.sbuf([128, 512], F32, f"nt{i}") for i in range(2)]
        lacc = [p.sbuf([128, 512], F32, f"lacc{i}") for i in range(2)]; lacc2 = [p.sbuf([128, 512], F32, f"laccp{i}") for i in range(2)]
        rl = [p.sbuf([128, 512], F32, f"rl{i}") for i in range(2)]
        ob = [p.sbuf([128, 512], F32, f"ob{i}") for i in range(2)]
        LOOK = 2
        onesb = p.sbuf([128, 128], BF16, "onesb"); p.op("pool", [], [onesb], lambda e: e.memset(onesb[:, :], 1.0))
        tiles = [(qb, j) for qb in range(NQB) for j in range(NKT)]
        NTL = len(tiles)
        nn = 0
        def front(t):
            nonlocal nn
            qb, j = tiles[t]
            qsl = slice(qb * 512, (qb + 1) * 512)
            ps = c["ps"][t % 4]; P = pt[t % 4]
            p.op("pe", [kb16, qb16], [ps], lambda e: e.matmul(ps[:, :], lhsT=kb16[:, j * 128:(j + 1) * 128], rhs=qb16[:, qsl], start=True, stop=True))
            m = j - 4 * qb
            if NEAR_LO <= m <= NEAR_HI:
                tt_ = nt[nn % 2]; nn += 1
                off = 1024 - 128 * m
                p.op("dve", [ps, rb], [tt_], lambda e: e.scalar_tensor_tensor(out=tt_[:, :], in0=ps[:, :], scalar=0.125, in1=rb[:, off:off + 512], op0=ALU.mult, op1=ALU.add))
                p.op("act", [tt_], [P], lambda e: e.activation(out=P[:, :], in_=tt_[:, :], func=AF.Exp))
            else:
                side = 0 if m < 0 else 1
                p.op("act", [ps, bf_], [P], lambda e: e.activation(out=P[:, :], in_=ps[:, :], func=AF.Exp, bias=bf_[:, side:side + 1], scale=0.125))
        def back(t):
            qb, j = tiles[t]
            qsl = slice(qb * 512, (qb + 1) * 512)
            P = pt[t % 4]; po = c["ps"][4 + qb % 2]; la = lacc[qb % 2]
            p.op("pe", [vb16, P], [po], lambda e: e.matmul(po[:, :], lhsT=vb16[:, j, :], rhs=P[:, :], start=(j == 0), stop=(j == NKT - 1)), same_ok=True)
            la2 = lacc2[qb % 2]
            use_pool = (j % 3 == 2)
            lx = la2 if use_pool else la; le = "pool" if use_pool else "dve"
            if j == 0 or j == 2:
                p.op(le, [P], [lx], lambda e: e.tensor_copy(out=lx[:, :], in_=P[:, :]))
            else:
                p.op(le, [P, lx], [lx], lambda e: e.tensor_tensor(out=lx[:, :], in0=lx[:, :], in1=P[:, :], op=ALU.add))
            if j == NKT - 1:
                pl = c["ps"][6 + qb % 2]; r_ = rl[qb % 2]
                p.op("pe", [c["ones"], la], [pl], lambda e: e.matmul(pl[:, :], lhsT=c["ones"][:, :], rhs=la[:, :], start=True, stop=False))
                p.op("pe", [c["ones"], la2], [pl], lambda e: e.matmul(pl[:, :], lhsT=c["ones"][:, :], rhs=la2[:, :], start=False, stop=True), same_ok=True)
                p.op("dve", [pl], [r_], lambda e: e.reciprocal(out=r_[:, :], in_=pl[:, :]))
                o = ob[qb % 2]
                p.op("dve", [po, r_], [o], lambda e: e.tensor_tensor(out=o[:, :], in0=po[:, :], in1=r_[:, :], op=ALU.mult))
                p.dma("pool", oT[:, qsl], o[:, :], [o], [oD])
        for t in range(NTL + LOOK):
            if t < NTL: front(t)
            if t >= LOOK: back(t - LOOK)
        p.finish([oD])
        print("k2c ninst", p.ninst)
    return nc

def host_inputs_k2c(zT, inp, layer):
    CQ = 256 + 256 + 512 + 512 + 32 + 768 * 3
    CK = CQ + 512; CV = CK + 512
    t5 = inp["t5_bias"]
    kk = np.arange(128)[:, None]; jj = np.arange(RW)[None, :]
    bidx = t5_bucket_np(kk - jj + 1024)
    assert (t5_bucket_np(np.arange(-20000, -5 * 128 + 128 - 1 - 127 + 1)) == 15).all()
    maps = []
    for h in range(4):
        vT = zT[CV + h * 128:CV + (h + 1) * 128, :]
        vv = np.ascontiguousarray(vT.T.reshape(S // 128, 128, 128).transpose(1, 0, 2))
        Rb = np.ascontiguousarray(t5[:, 12 + h][bidx]).astype(np.float32)
        bfar = np.ascontiguousarray(np.broadcast_to(np.array([t5[15, 12 + h], t5[31, 12 + h]], np.float32)[None, :], (128, 2)))
        for cc in range(2):
            r0 = h * 128 + cc * 64
            maps.append({"qT": np.ascontiguousarray(zT[CQ + r0:CQ + r0 + 64]), "kT": np.ascontiguousarray(zT[CK + r0:CK + r0 + 64]), "v": vv,
                         "gq": np.ascontiguousarray(inp["diff_qnorm_g"][layer].reshape(64, 1)), "gk": np.ascontiguousarray(inp["diff_knorm_g"][layer].reshape(64, 1)),
                         "Rb": Rb, "bfar": bfar})
    return maps


S = 16384
NT = S // 128

def build_k2a(NTL=NT):
    nc = bass.Bass("TRN2", target_bir_lowering=False)
    I = lambda n, s: nc.dram_tensor(n, s, F32, kind="ExternalInput")
    qT = I("qT", [64, S]); kT = I("kT", [64, S]); ktm = I("ktm", [128, NT, 64]); vtm = I("vtm", [128, NT, 128])
    lrT = I("lrT", [16, S]); gw = I("gw", [16, 64]); gbb = I("gbb", [128, 64])
    tri = I("tri", [128, 128]); m1 = I("m1", [128, 128]); cind = I("cind", [128, 2]); maskA = I("maskA", [128, 128])
    oT = nc.dram_tensor("oT", [128, S], F32, kind="ExternalOutput")
    with ExitStack() as es:
        p = Prog(nc, es)
        c = consts(p)
        iD = Buf(None); oD = Buf(None)
        def ld(name, shape, src, q="pool"):
            b = p.sbuf(shape, F32, name); p.dma(q, b[tuple(slice(None) for _ in shape)], src, [iD], [b]); return b
        gws = ld("gws", [16, 64], gw[:, :]); gbs = ld("gbs", [128, 64], gbb[:, :])
        tris = ld("tris", [128, 128], tri[:, :]); m1s = ld("m1s", [128, 128], m1[:, :]); cis = ld("cis", [128, 2], cind[:, :])
        mAs = ld("mAs", [128, 128], maskA[:, :])
        one1 = p.sbuf([128, 1], F32, "one1"); p.op("pool", [], [one1], lambda e: e.memset(one1[:, :], 1.0))
        lrs = ld("lrs", [16, S], lrT[:, :], "sp")
        St = p.sbuf([64, 128], F32, "St"); Sb = p.sbuf([64, 128], BF16, "Sb")
        p.op("pool", [], [St], lambda e: e.memset(St[:, :], 0.0)); p.op("pool", [], [Sb], lambda e: e.memset(Sb[:, :], 0.0))
        NB = 16
        def stg(name, shape): return [p.sbuf(shape, F32, f"{name}{i}") for i in range(2)]
        qst = stg("qst", [64, NB * 128]); kst = stg("kst", [64, NB * 128]); ktst = stg("ktst", [128, NB, 64]); vst = stg("vst", [128, NB, 128])
        vbf = [p.sbuf([128, NB, 128], BF16, f"vbf{i}") for i in range(2)]
        ost = [p.sbuf([128, NB * 128], F32, f"ost{i}") for i in range(2)]
        R = lambda n, shp, dt=F32, k=2: [p.sbuf(shp, dt, f"{n}{i}") for i in range(k)]
        xs = R("xs", [128, 64]); es_ = R("es", [128, 64]); ls = R("ls", [128, 64])
        eb = R("eb", [64, 128]); enb = R("enb", [64, 128]); ed = R("ed", [128, 64]); av = R("av", [64, 2])
        qg = R("qg", [64, 128], BF16); kg = R("kg", [64, 128], BF16); kd = R("kd", [128, 64], BF16); Am = R("Am", [128, 128], BF16)
        for i in range(NTL):
            blk, ib = divmod(i, NB); par = i % 2
            if ib == 0:
                bsl = slice(blk * NB * 128, (blk + 1) * NB * 128); tsl = slice(blk * NB, (blk + 1) * NB)
                p.dma("sp", qst[blk % 2][:, :], qT[:, bsl], [iD], [qst[blk % 2]])
                p.dma("sp", kst[blk % 2][:, :], kT[:, bsl], [iD], [kst[blk % 2]])
                p.dma("sp", ktst[blk % 2][:, :, :], ktm[:, tsl, :], [iD], [ktst[blk % 2]])
                p.dma("sp", vst[blk % 2][:, :, :], vtm[:, tsl, :], [iD], [vst[blk % 2]])
                vb = vbf[blk % 2]
                p.op("pool", [vst[blk % 2]], [vb], lambda e: e.tensor_copy(out=vb[:, :, :], in_=vst[blk % 2][:, :, :]))
            q_s = qst[blk % 2]; k_s = kst[blk % 2]; kt_s = ktst[blk % 2]; vb = vbf[blk % 2]; o_s = ost[blk % 2]
            tsl = slice(i * 128, (i + 1) * 128); lsl = slice(ib * 128, (ib + 1) * 128)
            pa = c["ps"][par]; pb = c["ps"][2 + par]; pc = c["ps"][4 + par]; po = c["ps"][6]; pf = c["ps"][7]
            x = xs[par]; e_ = es_[par]; l = ls[par]
            p.op("pe", [lrs, gws], [pa], lambda e: e.matmul(pa[:, 0:64], lhsT=lrs[:, tsl], rhs=gws[:, :], start=True, stop=True))
            p.op("dve", [pa, gbs], [x], lambda e: e.tensor_tensor(out=x[:, :], in0=pa[:, 0:64], in1=gbs[:, :], op=ALU.add))
            p.op("act", [x], [e_], lambda e: e.activation(out=e_[:, :], in_=x[:, :], func=AF.Exp, scale=-1.0))
            p.op("act", [e_, one1], [l], lambda e: e.activation(out=l[:, :], in_=e_[:, :], func=AF.Ln, bias=one1[:, 0:1], scale=1.0))
            p.op("pe", [l, tris], [pb], lambda e: e.matmul(pb[:64, 0:128], lhsT=l[:, :], rhs=tris[:, :], start=True, stop=True))
            p.op("pe", [l, cis], [pb], lambda e: e.matmul(pb[:64, 128:130], lhsT=l[:, :], rhs=cis[:, :], start=True, stop=True), same_ok=True)
            p.op("pe", [m1s, l], [pc], lambda e: e.matmul(pc[:, 0:64], lhsT=m1s[:, :], rhs=l[:, :], start=True, stop=True))
            p.op("act", [pb], [eb[par]], lambda e: e.activation(out=eb[par][:, :], in_=pb[:64, 0:128], func=AF.Exp, scale=-1.0 / 16))
            p.op("act", [pb], [enb[par]], lambda e: e.activation(out=enb[par][:, :], in_=pb[:64, 0:128], func=AF.Exp, scale=1.0 / 16))
            p.op("act", [pb], [av[par]], lambda e: e.activation(out=av[par][:, :], in_=pb[:64, 128:130], func=AF.Exp, scale=-1.0 / 16))
            p.op("act", [pc], [ed[par]], lambda e: e.activation(out=ed[par][:, :], in_=pc[:, 0:64], func=AF.Exp, scale=-1.0 / 16))
            p.op("dve", [q_s, eb[par]], [qg[par]], lambda e: e.scalar_tensor_tensor(out=qg[par][:, :], in0=q_s[:, lsl], scalar=0.125, in1=eb[par][:, :], op0=ALU.mult, op1=ALU.mult))
            p.op("dve", [k_s, enb[par]], [kg[par]], lambda e: e.tensor_tensor(out=kg[par][:, :], in0=k_s[:, lsl], in1=enb[par][:, :], op=ALU.mult))
            p.op("dve", [kt_s, ed[par]], [kd[par]], lambda e: e.tensor_tensor(out=kd[par][:, :], in0=kt_s[:, ib, :], in1=ed[par][:, :], op=ALU.mult))
            p.op("pe", [kg[par], qg[par]], [pc], lambda e: e.matmul(pc[:, 128:256], lhsT=kg[par][:, :], rhs=qg[par][:, :], start=True, stop=True))
            p.op("dve", [pc, mAs], [Am[par]], lambda e: e.tensor_tensor(out=Am[par][:, :], in0=pc[:, 128:256], in1=mAs[:, :], op=ALU.mult))
            p.op("pe", [vb, Am[par]], [po], lambda e: e.matmul(po[:, 0:128], lhsT=vb[:, ib, :], rhs=Am[par][:, :], start=True, stop=False))
            for ci in range(2):
                cs = slice(ci * 64, (ci + 1) * 64)
                p.op("pe", [Sb, qg[par]], [po], lambda e: e.matmul(po[:, cs], lhsT=Sb[:, :], rhs=qg[par][:, cs], start=False, stop=(ci == 1)), same_ok=True)
                p.op("pe", [kd[par], vb], [pf], lambda e: e.matmul(pf[:64, 0:128], lhsT=kd[par][cs, :], rhs=vb[cs, ib, :], start=True, stop=True))
                p.op("dve", [St, av[par], pf], [St], lambda e: e.scalar_tensor_tensor(out=St[:, :], in0=St[:, :], scalar=av[par][:, ci:ci + 1], in1=pf[:64, 0:128], op0=ALU.mult, op1=ALU.add))
                p.op("act", [St], [Sb], lambda e: e.copy(out=Sb[:, :], in_=St[:, :]))
            p.op("act", [po], [o_s], lambda e: e.copy(out=o_s[:, lsl], in_=po[:, 0:128]))
            if ib == NB - 1 or i == NTL - 1:
                p.dma("act", oT[:, blk * NB * 128:(blk * NB + ib + 1) * 128], o_s[:, :(ib + 1) * 128], [o_s], [oD])
        p.finish([oD])
        print("k2a ninst", p.ninst)
    return nc

def host_inputs_k2a(zT, inp, layer):
    AQ, AK, AV, AR, ALR = 0, 256, 512, 1024, 1536
    ss = np.arange(128)[:, None]; tt = np.arange(128)[None, :]
    same = (ss // 64) == (tt // 64)
    tri = (same & (ss <= tt)).astype(np.float32); m1 = (same & (ss > tt)).astype(np.float32)
    cind = (np.arange(128)[:, None] // 64 == np.arange(2)[None, :]).astype(np.float32)
    maps = []
    for h in range(4):
        for d in range(2):
            f = (lambda a: a[:, ::-1]) if d == 1 else (lambda a: a)
            q = np.ascontiguousarray(f(zT[AQ + h * 64:AQ + (h + 1) * 64])); k = np.ascontiguousarray(f(zT[AK + h * 64:AK + (h + 1) * 64]))
            v = f(zT[AV + h * 128:AV + (h + 1) * 128]); lr = np.ascontiguousarray(f(zT[ALR + d * 16:ALR + (d + 1) * 16]))
            ktm = np.ascontiguousarray(k.T.reshape(NT, 128, 64).transpose(1, 0, 2)); vtm = np.ascontiguousarray(v.T.reshape(NT, 128, 128).transpose(1, 0, 2))
            mA = (same & ((ss <= tt) if d == 0 else (ss < tt))).astype(np.float32)
            maps.append({"qT": q, "kT": k, "ktm": ktm, "vtm": vtm, "lrT": lr,
                         "gw": np.ascontiguousarray(inp["gla_gate_w"][layer, d][:, h * 64:(h + 1) * 64]),
                         "gbb": np.ascontiguousarray(np.broadcast_to(inp["gla_gate_b"][layer, d][None, h * 64:(h + 1) * 64], (128, 64))),
                         "tri": tri, "m1": m1, "cind": cind, "maskA": mA})
    return maps


S = 16384
DILS = (1, 4, 16)
NEG = -100.0

STAGE = 3
def build_k2b():
    nc = bass.Bass("TRN2", target_bir_lowering=False)
    I = lambda n, s: nc.dram_tensor(n, s, F32, kind="ExternalInput")
    qw = I("qw", [3, 256, 2048]); kw = I("kw", [3, 256, 4096]); vw = I("vw", [3, 128, 16 * 2 * 4 * 64])
    gq = I("gq", [128, 3]); gk = I("gk", [128, 3]); kval = I("kval", [128, 3 * 32]); bm = I("bm", [3, 2, 128, 512]); bones = I("bones", [128, 128])
    O = nc.dram_tensor("O", [3, 2, 64, 4, 2048], F32, kind="ExternalOutput")
    with ExitStack() as es:
        p = Prog(nc, es)
        c = consts(p)
        iD = Buf(None); oD = Buf(None)
        def ld(name, shape, src, q="pool"):
            b = p.sbuf(shape, F32, name); p.dma(q, b[tuple(slice(None) for _ in shape)], src, [iD], [b]); return b
        gqs = ld("gqs", [128, 3], gq[:, :]); gks = ld("gks", [128, 3], gk[:, :]); kvs = ld("kvs", [128, 96], kval[:, :]); bos = ld("bos", [128, 128], bones[:, :])
        bms = [[ld(f"bm{g}{t}", [128, 512], bm[g, t, :, :]) for t in range(2)] for g in range(3)]
        qst = [p.sbuf([128, 2048], F32, f"qst{i}") for i in range(2)]; kst = [p.sbuf([128, 4096], F32, f"kst{i}") for i in range(2)]
        vst = p.sbuf([128, 16 * 2 * 4 * 64], F32, "vst"); vb = p.sbuf([128, 16, 2, 4, 64], BF16, "vb")
        qn = [p.sbuf([64, 2048], BF16, f"qn{i}") for i in range(4)]; kn = [p.sbuf([64, 4096], BF16, f"kn{i}") for i in range(4)]
        tmp = [p.sbuf([128, 512], F32, f"tmp{i}") for i in range(2)]; rtmp = [p.sbuf([128, 512], F32, f"rtmp{i}") for i in range(2)]
        tt = [p.sbuf([128, 512], F32, f"tt{i}") for i in range(2)]; P = [p.sbuf([128, 512], BF16, f"P{i}") for i in range(2)]
        ost = [p.sbuf([64, 2, 512], F32, f"ost{i}") for i in range(2)]
        onb = p.sbuf([128, 64], BF16, "onb"); p.op("pool", [], [onb], lambda e: e.memset(onb[:, :], 1.0))
        it = 0
        def norm(st, gcol, dst, n):
            for tb in range(n // 512):
                sl = slice(tb * 512, (tb + 1) * 512)
                s = tmp[tb % 2]; r = rtmp[tb % 2]; ps = c["ps"][6 + tb % 2]
                p.op("act", [st], [s], lambda e: e.activation(out=s[:64, :], in_=st[:64, sl], func=AF.Square))
                p.op("pe", [c["ones"], s], [ps], lambda e: e.matmul(ps[:64, :], lhsT=c["ones"][:64, :64], rhs=s[:64, :], start=True, stop=True))
                p.op("act", [ps, c["eps"]], [r], lambda e: e.activation(out=r[:64, :], in_=ps[:64, :], func=AF.Sqrt, bias=c["eps"][:64, 0:1], scale=1.0 / 64))
                p.op("dve", [r], [r], lambda e: e.reciprocal(out=r[:64, :], in_=r[:64, :]))
                p.op("dve", [st, r, gcol[0]], [dst], lambda e: e.scalar_tensor_tensor(out=dst[:, sl], in0=st[:64, sl], scalar=gcol[0][:64, gcol[1]:gcol[1] + 1], in1=r[:64, :], op0=ALU.mult, op1=ALU.mult))
        for g in range(3):
            p.dma("sp", vst[:, :], vw[g, :, :], [iD], [vst])
            p.op("pool", [vst], [vb], lambda e: e.tensor_copy(out=vb[:, :, :, :, :].rearrange("p a b c d -> p (a b c d)"), in_=vst[:, :]))
            for h in range(4):
                p.dma("sp", qst[h % 2][:64, :], qw[g, h * 64:(h + 1) * 64, :], [iD], [qst[h % 2]])
                p.dma("sp", kst[h % 2][:64, :], kw[g, h * 64:(h + 1) * 64, :], [iD], [kst[h % 2]])
                norm(qst[h % 2], (gqs, g), qn[h], 2048); norm(kst[h % 2], (gks, g), kn[h], 4096)
            for b in range(16):
                po = c["ps"][4 + b % 2]; pl = c["ps"][6 + b % 2]
                for tau in range(2 if STAGE >= 2 else 0):
                    ps = c["ps"][it % 4]; t = tt[it % 2]; Pt = P[it % 2]; it += 1
                    for h in range(4):
                        p.op("pe", [kn[h], qn[h]], [ps], lambda e: e.matmul(ps[:, h * 128:(h + 1) * 128], lhsT=kn[h][:, b * 256 + tau * 128:b * 256 + tau * 128 + 128], rhs=qn[h][:, b * 128:(b + 1) * 128], start=True, stop=True), same_ok=True)
                    p.op("dve", [ps, bms[g][tau]], [t], lambda e: e.scalar_tensor_tensor(out=t[:, :], in0=ps[:, :], scalar=0.125, in1=bms[g][tau][:, :], op0=ALU.mult, op1=ALU.add))
                    col = g * 32 + b * 2 + tau
                    p.op("act", [t, kvs], [Pt], lambda e: e.activation(out=Pt[:, :], in_=t[:, :], func=AF.Exp, bias=kvs[:, col:col + 1], scale=1.0))
                    if STAGE < 3: continue
                    for h in range(4):
                        p.op("pe", [vb, Pt], [po], lambda e: e.matmul(po[:64, h * 128:(h + 1) * 128], lhsT=vb[:, b, tau, h, :], rhs=Pt[:, h * 128:(h + 1) * 128], start=(tau == 0 and h == 0), stop=(tau == 1 and h == 3)), same_ok=True)
                    p.op("pe", [onb, Pt], [pl], lambda e: e.matmul(pl[:64, :], lhsT=onb[:, :], rhs=Pt[:, :], start=(tau == 0), stop=(tau == 1)), same_ok=True)
                o = ost[b % 2]
                if STAGE < 3:
                    p.op("dve", [], [o], lambda e: e.memset(o[:, :, :], 1.0))
                    for w_ in range(2):
                        p.dma("act", O[g, w_, :, :, b * 128:(b + 1) * 128], o[:, w_, :].rearrange("r (h q) -> r h q", q=128), [o], [oD])
                    continue
                p.op("act", [po], [o], lambda e: e.copy(out=o[:, 0, :], in_=po[:64, :]))
                p.op("dve", [pl], [o], lambda e: e.tensor_copy(out=o[:, 1, :], in_=pl[:64, :]))
                for w_ in range(2):
                    p.dma("act", O[g, w_, :, :, b * 128:(b + 1) * 128], o[:, w_, :].rearrange("r (h q) -> r h q", q=128), [o], [oD])
        p.finish([oD])
        print("k2b ninst", p.ninst)
    return nc

def perm_index(d):
    M = S // d
    pos = np.arange(S)
    return (pos % M) * d + pos // M

def host_inputs_k2b(zT, inp, layer):
    BQ = 1568; BK = BQ + 768; BV = BK + 768
    t5 = inp["t5_bias"]
    kk = np.arange(128)[:, None]; qq = np.arange(128)[None, :]
    bm = np.zeros((3, 2, 128, 512), np.float32)
    for g, d in enumerate(DILS):
        for tau in range(2):
            delta = kk + (-64 if tau == 0 else 64) - qq
            ok = np.abs(delta) <= 64
            bidx = t5_bucket_np(delta * d)
            for h in range(4):
                bm[g, tau, :, h * 128:(h + 1) * 128] = np.where(ok, t5[:, g * 4 + h][bidx], np.float32(NEG))
    bones = (np.arange(128)[:, None] // 64 == np.arange(128)[None, :] // 64).astype(np.float32)
    gq = np.ascontiguousarray(np.tile(inp["dil_qnorm_g"][layer].T, (2, 1))); gk = np.ascontiguousarray(np.tile(inp["dil_knorm_g"][layer].T, (2, 1)))
    qP, kP, vP, valid = [], [], [], []
    for g, d in enumerate(DILS):
        idx = perm_index(d); M = S // d
        qP.append(zT[BQ + g * 256:BQ + (g + 1) * 256][:, idx])
        kP.append(zT[BK + g * 256:BK + (g + 1) * 256][:, idx]); vP.append(zT[BV + g * 256:BV + (g + 1) * 256][:, idx])
    maps = []
    for r in range(8):
        qw = np.zeros((3, 256, 2048), np.float32); kw = np.zeros((3, 256, 4096), np.float32)
        vw = np.zeros((3, 128, 16, 2, 4, 64), np.float32); kval = np.zeros((128, 96), np.float32)
        for g, d in enumerate(DILS):
            M = S // d
            qw[g] = qP[g][:, r * 2048:(r + 1) * 2048]
            for b in range(16):
                P0 = r * 2048 + b * 128; sub0 = (P0 // M) * M
                kpos = np.arange(P0 - 64, P0 + 192)
                ok = (kpos >= sub0) & (kpos < sub0 + M)
                kc = np.clip(kpos, 0, S - 1)
                kwin = np.where(ok[None, :], kP[g][:, kc], np.float32(0))
                kw[g][:, b * 256:(b + 1) * 256] = kwin
                vwin = np.where(ok[None, :], vP[g][:, kc], np.float32(0))
                vv = vwin.reshape(4, 64, 2, 128).transpose(3, 2, 0, 1)
                vw[g][:, b] = vv
                kval[:, g * 32 + b * 2:g * 32 + b * 2 + 2] = np.where(ok.reshape(2, 128).T, np.float32(0), np.float32(NEG))
        maps.append({"qw": qw, "kw": kw, "vw": np.ascontiguousarray(vw.reshape(3, 128, -1)), "gq": gq, "gk": gk, "kval": kval, "bm": bm, "bones": bones})
    return maps

def host_post_k2b(results):
    N = np.zeros((3, 256, S), np.float32); lb = np.zeros((3, 256, S), np.float32)
    for g, d in enumerate(DILS):
        Og = np.concatenate([r["O"][g] for r in results], axis=3)
        idx = perm_index(d)
        N[g][:, idx] = Og[0].transpose(1, 0, 2).reshape(256, S)
        lb[g][:, idx] = Og[1].transpose(1, 0, 2).reshape(256, S)
    return N, lb


import math
S = 16384
I32 = mybir.dt.int32
TWO_PI = float(2 * np.pi)

def build_k2d(NROW=128):
    nc = bass.Bass("TRN2", target_bir_lowering=False)
    I = lambda n, s: nc.dram_tensor(n, s, F32, kind="ExternalInput")
    ux1 = I("ux1", [128, S]); uv = I("uv", [128, S]); ux0 = I("ux0", [64, S])
    cw1 = I("cw1", [128, 4]); cwv = I("cwv", [128, 4]); cw0 = I("cw0", [64, 4])
    embT = I("embT", [33, S]); w1 = I("w1", [33, 64]); w2 = I("w2", [64, 64]); w3 = I("w3", [64, 128])
    bf1 = I("bf1", [64, 2]); bf2 = I("bf2", [64, 2])
    dlrow = I("dlrow", [1, 128]); tvec = I("tvec", [1, S]); pair = I("pair", [128, 128])
    Y = nc.dram_tensor("Y", [128, 128, 128], F32, kind="ExternalOutput")
    zz = nc.dram_tensor("zz", [64, S], F32, kind="ExternalOutput")
    x0c = nc.dram_tensor("x0c", [64, S], F32, kind="ExternalOutput")
    kx = nc.dram_tensor("kx", [128, S + 128], BF16, kind="Internal")
    with ExitStack() as es:
        p = Prog(nc, es)
        c = consts(p)
        iD = Buf(None); oD = Buf(None); kxD = Buf(None)
        def ld(name, shape, src, q="pool"):
            b = p.sbuf(shape, F32, name); p.dma(q, b[tuple(slice(None) for _ in shape)], src, [iD], [b]); return b
        cw1s = ld("cw1s", [128, 4], cw1[:, :]); cwvs = ld("cwvs", [128, 4], cwv[:, :]); cw0s = ld("cw0s", [64, 4], cw0[:, :])
        w1s = ld("w1s", [33, 64], w1[:, :]); w2s = ld("w2s", [64, 64], w2[:, :]); w3s = ld("w3s", [64, 128], w3[:, :])
        bf1s = ld("bf1s", [64, 2], bf1[:, :]); bf2s = ld("bf2s", [64, 2], bf2[:, :])
        dls = ld("dls", [1, 128], dlrow[:, :]); prs = ld("prs", [128, 128], pair[:, :])
        for b in (bf1s, bf2s):
            p.op("dve", [b], [b], lambda e: e.tensor_scalar(out=b[:, 1:2], in0=b[:, 1:2], scalar1=1.0 / TWO_PI, scalar2=None, op0=ALU.mult))
        idf = p.sbuf([128, 128], F32, "idf"); idb = p.sbuf([128, 128], BF16, "idb")
        p.op("pool", [], [idf], lambda e: e.memset(idf[:, :], 1.0))
        p.op("pool", [idf], [idf], lambda e: e.affine_select(out=idf[:, :], in_=idf[:, :], pattern=[[-1, 128]], compare_op=ALU.is_equal, fill=0.0, base=0, channel_multiplier=1))
        p.op("dve", [idf], [idb], lambda e: e.tensor_copy(out=idb[:, :], in_=idf[:, :]))
        zpad = p.sbuf([128, 128], BF16, "zpad"); p.op("pool", [], [zpad], lambda e: e.memset(zpad[:, :], 0.0))
        p.dma("pool", kx[:, S:S + 128], zpad[:, :], [zpad], [kxD])
        Zall = p.sbuf([128, 128, 128], BF16, "Zall")
        rn = p.sbuf([128, 128], F32, "rn")
        scA = p.scope(); scA.__enter__()
        zzb = p.sbuf([128, S], BF16, "zzb")
        CB = 2048
        ub = [p.sbuf([128, CB + 2], F32, f"ub{i}") for i in range(2)]
        acc = [p.sbuf([128, CB], F32, f"acc{i}") for i in range(3)]
        k = 0
        def conv(src, rows, cws, blk, dst):
            nonlocal k
            u = ub[k % 2]; k += 1
            lo = blk * CB - 1; hi = (blk + 1) * CB + 1
            if blk == 0:
                p.op("pool", [], [u], lambda e: e.memset(u[:rows, 0:1], 0.0));
                p.dma("sp", u[:rows, 1:CB + 2], src[:, 0:hi], [iD], [u])
            elif blk == S // CB - 1:
                p.op("pool", [], [u], lambda e: e.memset(u[:rows, CB + 1:CB + 2], 0.0))
                p.dma("sp", u[:rows, 0:CB + 1], src[:, lo:S], [iD], [u])
            else:
                p.dma("sp", u[:rows, :], src[:, lo:hi], [iD], [u])
            p.op("dve", [u, cws], [dst], lambda e: e.tensor_scalar(out=dst[:rows, :], in0=u[:rows, 1:CB + 1], scalar1=cws[:rows, 1:2], scalar2=cws[:rows, 3:4], op0=ALU.mult, op1=ALU.add))
            p.op("dve", [u, cws, dst], [dst], lambda e: e.scalar_tensor_tensor(out=dst[:rows, :], in0=u[:rows, 0:CB], scalar=cws[:rows, 0:1], in1=dst[:rows, :], op0=ALU.mult, op1=ALU.add))
            p.op("dve", [u, cws, dst], [dst], lambda e: e.scalar_tensor_tensor(out=dst[:rows, :], in0=u[:rows, 2:CB + 2], scalar=cws[:rows, 2:3], in1=dst[:rows, :], op0=ALU.mult, op1=ALU.add))
        for blk in range(S // CB):
            bsl = slice(blk * CB, (blk + 1) * CB)
            conv(ux1, 128, cw1s, blk, acc[0]); conv(uv, 128, cwvs, blk, acc[1])
            p.op("pool", [acc[0], acc[1]], [acc[0]], lambda e: e.tensor_tensor(out=acc[0][:, :], in0=acc[0][:, :], in1=acc[1][:, :], op=ALU.mult))
            p.op("act", [acc[0]], [zzb], lambda e: e.copy(out=zzb[:, bsl], in_=acc[0][:, :]))
            p.dma("act", zz[:, bsl], acc[0][:64, :], [acc[0]], [oD])
            conv(ux0, 64, cw0s, blk, acc[2])
            p.dma("act", x0c[:, bsl], acc[2][:64, :], [acc[2]], [oD])
        for g in range(32):
            ps = c["ps"][g % 2]
            for q in range(4):
                sb = g * 4 + q
                p.op("pe", [zzb, idb], [ps], lambda e: e.matmul(ps[:, q * 128:(q + 1) * 128], lhsT=zzb[:, sb * 128:(sb + 1) * 128], rhs=idb[:, :], start=True, stop=True), same_ok=True)
            eng = "act" if g % 2 == 0 else "dve"
            if eng == "act":
                p.op("act", [ps], [Zall], lambda e: e.copy(out=Zall[:, :, g * 4:(g + 1) * 4].rearrange("p r b -> p b r"), in_=ps[:, :].rearrange("p (b r) -> p b r", r=128)))
            else:
                p.op("dve", [ps], [Zall], lambda e: e.tensor_copy(out=Zall[:, :, g * 4:(g + 1) * 4].rearrange("p r b -> p b r"), in_=ps[:, :].rearrange("p (b r) -> p b r", r=128)))
        scA.__exit__(None, None, None)
        scC = p.scope(); scC.__enter__()
        tvb = [p.sbuf([1, 512], F32, f"tvb{i}") for i in range(2)]
        emb = [p.sbuf([33, 512], F32, f"emb{i}") for i in range(2)]
        R = lambda n, shp, dt=F32, kk=2: [p.sbuf(shp, dt, f"{n}{i}") for i in range(kk)]
        ua = R("ua", [64, 512]); ti = R("ti", [64, 512], I32); tf = R("tf", [64, 512]); h1 = R("h1", [64, 512]); h2 = R("h2", [64, 512])
        dec = R("dec", [128, 512]); kf = R("kf", [128, 512]); kb = R("kb", [128, 512], BF16)
        npart = p.sbuf([128, 32], F32, "npart")
        def sin_layer(ps, bfs, out, par):
            u = ua[par]
            p.op("dve", [ps, bfs], [u], lambda e: e.tensor_scalar(out=u[:, :], in0=ps[:64, :], scalar1=bfs[:, 0:1], scalar2=bfs[:, 1:2], op0=ALU.add, op1=ALU.mult))
            p.op("dve", [u], [ti[par]], lambda e: e.tensor_copy(out=ti[par][:, :], in_=u[:, :]))
            p.op("dve", [ti[par]], [tf[par]], lambda e: e.tensor_copy(out=tf[par][:, :], in_=ti[par][:, :]))
            p.op("dve", [u, tf[par]], [u], lambda e: e.tensor_tensor(out=u[:, :], in0=u[:, :], in1=tf[par][:, :], op=ALU.subtract))
            p.op("act", [u], [out], lambda e: e.activation(out=out[:, :], in_=u[:, :], func=AF.Sin, scale=TWO_PI))
        for blk in range(32):
            par = blk % 2; bsl = slice(blk * 512, (blk + 1) * 512)
            em = emb[par]
            p.dma("sp", em[:, :], embT[:, bsl], [iD], [em])
            p.dma("sp", tvb[par][:, :], tvec[:, bsl], [iD], [tvb[par]])
            p1 = c["ps"][2 + par]; p2 = c["ps"][4 + par]; p3 = c["ps"][6]; pd = c["ps"][7]
            p.op("pe", [w1s, em], [p1], lambda e: e.matmul(p1[:64, :], lhsT=w1s[:, :], rhs=em[:, :], start=True, stop=True))
            sin_layer(p1, bf1s, h1[par], par)
            p.op("pe", [w2s, h1[par]], [p2], lambda e: e.matmul(p2[:64, :], lhsT=w2s[:, :], rhs=h1[par][:, :], start=True, stop=True))
            sin_layer(p2, bf2s, h2[par], par)
            p.op("pe", [w3s, h2[par]], [p3], lambda e: e.matmul(p3[:, :], lhsT=w3s[:, :], rhs=h2[par][:, :], start=True, stop=True))
            p.op("pe", [dls, tvb[par]], [pd], lambda e: e.matmul(pd[:, :], lhsT=dls[:, :], rhs=tvb[par][:, :], start=True, stop=True))
            p.op("act", [pd], [dec[par]], lambda e: e.activation(out=dec[par][:, :], in_=pd[:, :], func=AF.Exp))
            p.op("dve", [p3, dec[par]], [kf[par]], lambda e: e.tensor_tensor(out=kf[par][:, :], in0=p3[:, :], in1=dec[par][:, :], op=ALU.mult))
            if blk == 31:
                p.op("dve", [kf[par]], [kf[par]], lambda e: e.memset(kf[par][64:128, 511:512], 0.0))
            p.op("dve", [kf[par]], [npart], lambda e: e.tensor_reduce(out=npart[:, blk:blk + 1], in_=kf[par][:, :], axis=AX.X, op=ALU.add, apply_absolute_value=True))
            p.op("pool", [kf[par]], [kb[par]], lambda e: e.tensor_copy(out=kb[par][:, :], in_=kf[par][:, :]))
            p.dma("pool", kx[:, bsl], kb[par][:, :], [kb[par]], [kxD])
        nsum = p.sbuf([128, 1], F32, "nsum"); prn = p.sbuf([128, 128], F32, "prn")
        p.op("dve", [npart], [nsum], lambda e: e.tensor_reduce(out=nsum[:, 0:1], in_=npart[:, :], axis=AX.X, op=ALU.add))
        p.op("dve", [prs, nsum], [prn], lambda e: e.tensor_scalar(out=prn[:, :], in0=prs[:, :], scalar1=nsum[:, 0:1], scalar2=None, op0=ALU.mult))
        pn = c["ps"][0]
        p.op("pe", [c["ones"], prn], [pn], lambda e: e.matmul(pn[:, 0:128], lhsT=c["ones"][:, :], rhs=prn[:, :], start=True, stop=True))
        p.op("dve", [pn], [rn], lambda e: e.reciprocal(out=rn[:, :], in_=pn[:, 0:128]))
        scC.__exit__(None, None, None)
        strip = [p.sbuf([128, S], BF16, f"strip{i}") for i in range(2)]
        yst = [p.sbuf([128, 8, 128], F32, f"yst{i}") for i in range(2)]
        for row in range(NROW):
            st = strip[row % 2]
            src = bass.AP(tensor=kx, offset=row * (S + 128), ap=[[1, 128], [1, S]])
            p.dma("sp" if row % 2 == 0 else "pool", st[:, :], src, [kxD], [st])
            py = c["ps"][1 + row % 2]
            for d in range(128):
                base = 16256 - 128 * d
                p.op("pe", [st, Zall], [py], lambda e: e.matmul(py[:, d:128], lhsT=st[:, base:base + 128], rhs=Zall[:, row, 0:128 - d], start=(d == 0), stop=(d == 127)), same_ok=True)
            ys = yst[(row // 8) % 2]
            p.op("act", [py, rn], [ys], lambda e: e.activation(out=ys[:, row % 8, :], in_=py[:, 0:128], func=AF.Copy, scale=rn[:, row:row + 1]))
            if row % 8 == 7:
                p.dma("act", Y[:, row - 7:row + 1, :], ys[:, :, :], [ys], [oD])
        p.finish([oD])
        print("k2d ninst", p.ninst)
    return nc

def hy_consts():
    L = S
    t = np.linspace(0.0, 1.0, L, dtype=np.float32)[:, None]
    bands = 16
    freqs = np.linspace(1e-4, bands - 1, bands, dtype=np.float32)[None]
    w = (np.float32(2.0 * math.pi) * np.arange(L, dtype=np.float32)[:, None] / np.float32(L)).astype(np.float32)
    z = np.concatenate([t, np.cos(freqs * w), -np.sin(freqs * w)], axis=-1).astype(np.float32)
    min_decay = math.log(1e-2) / 1.5; max_decay = math.log(1e-2) / 0.3
    deltas = np.abs(np.linspace(min_decay, max_decay, 512, dtype=np.float32))
    return z, t[:, 0], deltas

def host_inputs_k2d(zT, inp, layer):
    DU = 11040 - 4096 - 1536
    z, t, deltas = hy_consts()
    embT = np.ascontiguousarray(z[::-1].T); tvec = np.ascontiguousarray(t[::-1][None, :])
    pair = (np.arange(128)[:, None] % 64 == np.arange(128)[None, :] % 64).astype(np.float32)
    cw = inp["hy_conv_w"][layer]; cb = inp["hy_conv_b"][layer]
    def cwpack(cols, rev):
        a = cw[:, cols].T
        if rev: a = a[:, ::-1]
        return np.ascontiguousarray(np.concatenate([a, cb[cols][:, None]], 1).astype(np.float32))
    maps = []
    for r in range(8):
        ch = np.arange(64 * r, 64 * r + 64)
        x0 = zT[DU + ch]; x1 = zT[DU + 512 + ch]; v = zT[DU + 1024 + ch]
        m = {"ux1": np.ascontiguousarray(np.concatenate([x1, x1[:, ::-1]], 0)), "uv": np.ascontiguousarray(np.concatenate([v, v[:, ::-1]], 0)), "ux0": np.ascontiguousarray(x0),
             "cw1": np.concatenate([cwpack(512 + ch, False), cwpack(512 + ch, True)], 0), "cwv": np.concatenate([cwpack(1024 + ch, False), cwpack(1024 + ch, True)], 0), "cw0": cwpack(ch, False),
             "embT": embT, "w1": np.ascontiguousarray(inp["hy_w1"][layer]), "w2": np.ascontiguousarray(inp["hy_w2"][layer]),
             "w3": np.ascontiguousarray(np.concatenate([inp["hy_w3"][layer][:, ch], inp["hy_w3"][layer][:, 512 + ch]], 1)),
             "bf1": np.ascontiguousarray(np.stack([inp["hy_b1"][layer], inp["hy_freq1"][layer]], 1)), "bf2": np.ascontiguousarray(np.stack([inp["hy_b2"][layer], inp["hy_freq2"][layer]], 1)),
             "dlrow": np.ascontiguousarray(-np.concatenate([deltas[ch], deltas[ch]])[None, :].astype(np.float32)), "tvec": tvec, "pair": pair}
        maps.append(m)
    return maps

def host_post_k2d(results):
    ys, zs, xs = [], [], []
    for r in results:
        Yr = r["Y"]
        y = Yr[::-1].transpose(1, 2, 0).reshape(128, S)
        ys.append(y[:64] + y[64:, ::-1]); zs.append(r["zz"]); xs.append(r["x0c"])
    return np.concatenate(ys, 0), np.concatenate(zs, 0), np.concatenate(xs, 0)


S = 16384; T = 2048; D = 1024; NIN = 11040; DFF = 4096

def linear2(p, c, w_ap, K, N, rhs_fn, ntb, evac, wst, wbf, gw, psi=2, npsi=4):
    kc = K // 128
    wD = Buf(None)
    ng = (N + gw - 1) // gw
    k2 = 0
    for gi in range(ng):
        c0 = gi * gw; wg = min(gw, N - c0)
        st = wst[gi % len(wst)]; wb = wbf[gi % len(wbf)]
        p.dma("sp", st[:, :kc, :wg], w_ap[:, c0:c0 + wg].rearrange("(c p) n -> p c n", p=128), [wD], [st])
        h = kc // 2
        p.op("pool", [st], [wb], lambda e: e.tensor_copy(out=wb[:, :h, :wg], in_=st[:, :h, :wg]))
        p.op("dve", [st], [wb], lambda e: e.tensor_copy(out=wb[:, h:kc, :wg], in_=st[:, h:kc, :wg]))
        for mi in range((wg + 127) // 128):
            mw = min(128, wg - mi * 128)
            for tb in range(ntb):
                ps = c["ps"][psi + k2 % npsi]; k2 += 1
                for ch in range(kc):
                    rb, rap = rhs_fn(ch, tb)
                    p.op("pe", [wb, rb], [ps], lambda e: e.matmul(ps[:mw, :], lhsT=wb[:, ch, mi * 128:mi * 128 + mw], rhs=rap, start=(ch == 0), stop=(ch == kc - 1)), same_ok=True)
                evac(c0 + mi * 128, mw, tb, ps)

def build_k3(lam_init, last):
    nc = bass.Bass("TRN2", target_bir_lowering=False)
    I = lambda n, s: nc.dram_tensor(n, s, F32, kind="ExternalInput")
    xT = I("xT", [D, T])
    of = I("of", [512, T]); ob = I("ob", [512, T]); rT = I("rT", [512, T]); gla_g = I("gla_g", [128, 1])
    dN = I("dN", [3, 256, T]); dL = I("dL", [3, 256, T])
    oc0 = I("oc0", [512, T]); oc1 = I("oc1", [512, T]); lamb = I("lamb", [128, 256]); sub_g = I("sub_g", [128, 1])
    yconv = I("yconv", [512, T]); zzT = I("zzT", [512, T]); x0c = I("x0c", [512, T]); skip = I("skip", [128, 4])
    gates = I("gates", [4096, T])
    projs = I("projs", [1792, D]); w_out = I("w_out", [D, D]); g2 = I("g2", [128, 8]); w1 = I("w1", [D, DFF]); w2 = I("w2", [DFF, D])
    xo = nc.dram_tensor("xo", [D, T], F32, kind="ExternalOutput")
    if not last:
        g1n = I("g1n", [128, 8]); w_in = I("w_in", [D, NIN])
        zT = nc.dram_tensor("zT", [NIN, T], F32, kind="ExternalOutput")
    with ExitStack() as es:
        p = Prog(nc, es)
        c = consts(p)
        iD = Buf(None); oD = Buf(None)
        def ld(name, shape, src, q="pool"):
            b = p.sbuf(shape, F32, name); p.dma(q, b[tuple(slice(None) for _ in shape)], src, [iD], [b]); return b
        glag = ld("glag", [128, 1], gla_g[:, :]); subg = ld("subg", [128, 1], sub_g[:, :]); lams = ld("lams", [128, 256], lamb[:, :]); skp = ld("skp", [128, 4], skip[:, :])
        g2s = ld("g2s", [128, 8], g2[:, :])
        if not last: g1s = ld("g1s", [128, 8], g1n[:, :])
        tmp = [p.sbuf([128, 512], F32, f"tmp{i}") for i in range(2)]
        rstds = [p.sbuf([128, 512], F32, f"rstd{i}") for i in range(4)]
        mb = [p.sbuf([128, T], BF16, f"mb{i}") for i in range(8)]
        lcol = p.sbuf([128, 4], F32, "lcol"); lpr = p.sbuf([128, 128], F32, "lpr")
        p.op("dve", [lams], [lpr], lambda e: e.tensor_tensor(out=lpr[:, 0:64], in0=lams[:, 0:64], in1=lams[:, 64:128], op=ALU.mult))
        p.op("dve", [lams], [lpr], lambda e: e.tensor_tensor(out=lpr[:, 64:128], in0=lams[:, 128:192], in1=lams[:, 192:256], op=ALU.mult))
        p.op("dve", [lpr], [lcol], lambda e: e.tensor_reduce(out=lcol[:, 0:1], in_=lpr[:, 0:64], axis=AX.X, op=ALU.add))
        p.op("dve", [lpr], [lcol], lambda e: e.tensor_reduce(out=lcol[:, 1:2], in_=lpr[:, 64:128], axis=AX.X, op=ALU.add))
        p.op("act", [lcol], [lcol], lambda e: e.activation(out=lcol[:, 0:2], in_=lcol[:, 0:2], func=AF.Exp))
        p.op("dve", [lcol], [lcol], lambda e: e.tensor_tensor(out=lcol[:, 2:3], in0=lcol[:, 0:1], in1=lcol[:, 1:2], op=ALU.subtract))
        p.op("dve", [lcol], [lcol], lambda e: e.tensor_scalar(out=lcol[:, 3:4], in0=lcol[:, 2:3], scalar1=float(lam_init), scalar2=-1.0, op0=ALU.add, op1=ALU.mult))
        sc1 = p.scope(); sc1.__enter__()
        ybf = [p.sbuf([128, T], BF16, f"ybf{i}") for i in range(14)]
        A = [p.sbuf([128, T], F32, f"A{i}") for i in range(2)]; B = [p.sbuf([128, T], F32, f"B{i}") for i in range(2)]; Cc = [p.sbuf([128, T], F32, f"C{i}") for i in range(2)]
        def rms128(src, gcol, dst, extra_scale=None):
            for tb in range(T // 512):
                sl = slice(tb * 512, (tb + 1) * 512); s = tmp[tb % 2]; r = rstds[tb % 4]; ps = c["ps"][tb % 2]
                p.op("act", [src], [s], lambda e: e.activation(out=s[:, :], in_=src[:, sl], func=AF.Square))
                p.op("pe", [c["ones"], s], [ps], lambda e: e.matmul(ps[:, :], lhsT=c["ones"][:, :], rhs=s[:, :], start=True, stop=True))
                p.op("act", [ps, c["eps"]], [r], lambda e: e.activation(out=r[:, :], in_=ps[:, :], func=AF.Sqrt, bias=c["eps"][:, 0:1], scale=1.0 / 128))
                p.op("dve", [r], [r], lambda e: e.reciprocal(out=r[:, :], in_=r[:, :]))
                p.op("dve", [src, r, gcol], [dst], lambda e: e.scalar_tensor_tensor(out=dst[:, sl], in0=src[:, sl], scalar=gcol[:, 0:1], in1=r[:, :], op0=ALU.mult, op1=ALU.mult))
        for h in range(4):
            hs = slice(h * 128, (h + 1) * 128); a = A[h % 2]; b = B[h % 2]; cc = Cc[h % 2]
            p.dma("sp", a[:, :], of[hs, :], [iD], [a]); p.dma("sp", b[:, :], ob[hs, :], [iD], [b]); p.dma("sp", cc[:, :], rT[hs, :], [iD], [cc])
            p.op("pool", [a, b], [a], lambda e: e.tensor_tensor(out=a[:, :], in0=a[:, :], in1=b[:, :], op=ALU.add))
            rms128(a, glag, b)
            p.op("act", [cc], [cc], lambda e: e.activation(out=cc[:, :], in_=cc[:, :], func=AF.Silu))
            p.op("dve", [b, cc], [ybf[h]], lambda e: e.tensor_tensor(out=ybf[h][:, :], in0=b[:, :], in1=cc[:, :], op=ALU.mult))
        for h in range(4):
            hs = slice(h * 128, (h + 1) * 128); a = A[h % 2]; b = B[h % 2]; cc = Cc[h % 2]
            p.dma("sp", a[:, :], oc0[hs, :], [iD], [a]); p.dma("sp", b[:, :], oc1[hs, :], [iD], [b])
            p.op("dve", [a, b, lcol], [a], lambda e: e.scalar_tensor_tensor(out=a[:, :], in0=b[:, :], scalar=lcol[:, 3:4], in1=a[:, :], op0=ALU.mult, op1=ALU.add))
            rms128(a, subg, cc)
            p.op("pool", [cc], [ybf[6 + h]], lambda e: e.tensor_scalar(out=ybf[6 + h][:, :], in0=cc[:, :], scalar1=float(1.0 - lam_init), scalar2=None, op0=ALU.mult))
        for h in range(4):
            hs = slice(h * 128, (h + 1) * 128); a = A[h % 2]; b = B[h % 2]; cc = Cc[h % 2]
            p.dma("sp", a[:, :], yconv[hs, :], [iD], [a]); p.dma("sp", b[:, :], zzT[hs, :], [iD], [b]); p.dma("sp", cc[:, :], x0c[hs, :], [iD], [cc])
            p.op("dve", [a, b, skp], [a], lambda e: e.scalar_tensor_tensor(out=a[:, :], in0=b[:, :], scalar=skp[:, h:h + 1], in1=a[:, :], op0=ALU.mult, op1=ALU.add))
            p.op("pool", [a, cc], [ybf[10 + h]], lambda e: e.tensor_tensor(out=ybf[10 + h][:, :], in0=a[:, :], in1=cc[:, :], op=ALU.mult))
        for hh in range(2):
            hs = slice(hh * 128, (hh + 1) * 128); a = A[hh % 2]; b = B[hh % 2]; cc = Cc[hh % 2]
            p.dma("sp", a[:, :], dN[0, hs, :], [iD], [a]); p.dma("sp", b[:, :], dL[0, hs, :], [iD], [b])
            for g in (1, 2):
                p.dma("sp", cc[:, :], dN[g, hs, :], [iD], [cc])
                p.op("dve", [a, cc], [a], lambda e: e.tensor_tensor(out=a[:, :], in0=a[:, :], in1=cc[:, :], op=ALU.add))
                p.dma("sp", cc[:, :], dL[g, hs, :], [iD], [cc])
                p.op("dve", [b, cc], [b], lambda e: e.tensor_tensor(out=b[:, :], in0=b[:, :], in1=cc[:, :], op=ALU.add))
            p.op("dve", [b], [b], lambda e: e.reciprocal(out=b[:, :], in_=b[:, :]))
            p.op("dve", [a, b], [ybf[4 + hh]], lambda e: e.tensor_tensor(out=ybf[4 + hh][:, :], in0=a[:, :], in1=b[:, :], op=ALU.mult))
        pst = [p.sbuf([128, D], F32, f"pst{i}") for i in range(2)]; pbf = p.sbuf([128, 14, D], BF16, "pbf")
        for kc_ in range(14):
            p.dma("sp", pst[kc_ % 2][:, :], projs[kc_ * 128:(kc_ + 1) * 128, :], [iD], [pst[kc_ % 2]])
            p.op("pool", [pst[kc_ % 2]], [pbf], lambda e: e.tensor_copy(out=pbf[:, kc_, :], in_=pst[kc_ % 2][:, :]))
        branches = [(0, 4), (4, 6), (6, 10), (10, 14)]
        gst = A + B
        macc = Cc
        k = 0; kk = 0
        for mi in range(8):
            ms = slice(mi * 128, (mi + 1) * 128); ma = macc[mi % 2]
            for bi, (k0, k1) in enumerate(branches):
                gt = gst[k % 4]; k += 1
                p.dma("sp", gt[:, :], gates[bi * 1024 + mi * 128:bi * 1024 + (mi + 1) * 128, :], [iD], [gt])
                p.op("act", [gt], [gt], lambda e: e.activation(out=gt[:, :], in_=gt[:, :], func=AF.Sigmoid))
                for tb in range(4):
                    sl = slice(tb * 512, (tb + 1) * 512); ps = c["ps"][2 + kk % 4]; kk += 1
                    for kc_ in range(k0, k1):
                        p.op("pe", [pbf, ybf[kc_]], [ps], lambda e: e.matmul(ps[:, :], lhsT=pbf[:, kc_, ms], rhs=ybf[kc_][:, sl], start=(kc_ == k0), stop=(kc_ == k1 - 1)), same_ok=True)
                    if bi == 0:
                        p.op("dve", [ps, gt], [ma], lambda e: e.tensor_tensor(out=ma[:, sl], in0=ps[:, :], in1=gt[:, sl], op=ALU.mult))
                    else:
                        t = tmp[kk % 2]
                        p.op("dve", [ps, gt], [t], lambda e: e.tensor_tensor(out=t[:, :], in0=ps[:, :], in1=gt[:, sl], op=ALU.mult))
                        p.op("pool", [t, ma], [ma], lambda e: e.tensor_tensor(out=ma[:, sl], in0=ma[:, sl], in1=t[:, :], op=ALU.add))
            p.op("act", [ma], [mb[mi]], lambda e: e.copy(out=mb[mi][:, :], in_=ma[:, :]))
        sc1.__exit__(None, None, None)
        xs = [p.sbuf([128, T], F32, f"x{i}") for i in range(8)]
        for i in range(8):
            p.dma("pool", xs[i][:, :], xT[i * 128:(i + 1) * 128, :], [iD], [xs[i]])
        GW = 256
        wst = [p.sbuf([128, 8, GW], F32, f"wst{i}") for i in range(2)]; wbf = [p.sbuf([128, 8, GW], BF16, f"wbf{i}") for i in range(2)]
        def evac_addx(m0, mw, tb, ps):
            ci = m0 // 128; sl = slice(tb * 512, (tb + 1) * 512)
            p.op("dve", [ps, xs[ci]], [xs[ci]], lambda e: e.tensor_tensor(out=xs[ci][:, sl], in0=xs[ci][:, sl], in1=ps[:, :], op=ALU.add))
        linear2(p, c, w_out, D, D, lambda ch, tb: (mb[ch], mb[ch][:, tb * 512:(tb + 1) * 512]), 4, evac_addx, wst, wbf, GW)
        hs_ = mb
        rms_fm(p, c, xs, g2s, hs_, 8, T, tmp, rstds)
        sc2 = p.scope(); sc2.__enter__()
        ub = [p.sbuf([128, 512], BF16, f"ub{i}") for i in range(32)]
        w2st = [p.sbuf([128, 32, 128], F32, "w2st0")]; w2bf = [p.sbuf([128, 32, 128], BF16, f"w2bf{i}") for i in range(2)]
        for tb in range(4):
            sl = slice(tb * 512, (tb + 1) * 512)
            def evac_u(m0, mw, tb_, ps):
                mi = m0 // 128; t = tmp[mi % 2]
                p.op("act", [ps], [t], lambda e: e.activation(out=t[:, :], in_=ps[:, :], func=AF.Relu))
                p.op("pool", [t], [ub[mi]], lambda e: e.tensor_tensor(out=ub[mi][:, :], in0=t[:, :], in1=t[:, :], op=ALU.mult))
            linear2(p, c, w1, D, DFF, lambda ch, tb_: (hs_[ch], hs_[ch][:, sl]), 1, evac_u, wst, wbf, GW)
            def evac_x2(m0, mw, tb_, ps):
                ci = m0 // 128
                p.op("dve", [ps, xs[ci]], [xs[ci]], lambda e: e.tensor_tensor(out=xs[ci][:mw, sl], in0=xs[ci][:mw, sl], in1=ps[:mw, :], op=ALU.add))
            linear2(p, c, w2, DFF, D, lambda ch, tb_: (ub[ch], ub[ch][:, :]), 1, evac_x2, w2st, w2bf, 128)
        sc2.__exit__(None, None, None)
        for i in range(8):
            p.dma("act", xo[i * 128:(i + 1) * 128, :], xs[i][:, :], [xs[i]], [oD])
        if not last:
            rms_fm(p, c, xs, g1s, hs_, 8, T, tmp, rstds)
            ost = [p.sbuf([128, T], F32, f"ost{i}") for i in range(2)]
            state = {"k": 0}
            def evac_z(m0, mw, tb, ps):
                o = ost[state["k"] % 2]
                p.op("act", [ps], [o], lambda e: e.copy(out=o[:mw, tb * 512:(tb + 1) * 512], in_=ps[:mw, :]))
                if tb == 3:
                    p.dma("act", zT[m0:m0 + mw, :], o[:mw, :], [o], [oD]); state["k"] += 1
            linear2(p, c, w_in, D, NIN, lambda ch, tb: (hs_[ch], hs_[ch][:, tb * 512:(tb + 1) * 512]), 4, evac_z, wst, wbf, GW)
        p.finish([oD])
        print("k3 ninst", p.ninst)
    return nc


import math as _math
_NC_CACHE = {}
def _get_nc(key, builder):
    if key not in _NC_CACHE:
        _NC_CACHE[key] = builder()
    return _NC_CACHE[key]

def _run(nc, maps):
    return run_bass_kernel_spmd(nc, maps, core_ids=list(range(8))).results

def kernel(**inputs):
    inp = {k: np.asarray(v) for k, v in inputs.items()}
    x = inp["x"][0].astype(np.float32)
    NCORE = 8; Tn = S // NCORE
    xT = np.ascontiguousarray(x.T)
    g = np.ascontiguousarray(inp["norm1_g"][0].reshape(8, 128).T)
    maps = [{"xT": np.ascontiguousarray(xT[:, r * Tn:(r + 1) * Tn]), "g": g, "w": np.ascontiguousarray(inp["w_in"][0])} for r in range(NCORE)]
    res = _run(_get_nc("k1", build_k1), maps)
    zT = np.concatenate([r["zT"] for r in res], axis=1)
    for layer in range(4):
        last = layer == 3
        lam_init = 0.8 - 0.6 * _math.exp(-0.3 * layer)
        ra = _run(_get_nc("k2a", build_k2a), host_inputs_k2a(zT, inp, layer))
        rb = _run(_get_nc("k2b", build_k2b), host_inputs_k2b(zT, inp, layer))
        rc = _run(_get_nc("k2c", build_k2c), host_inputs_k2c(zT, inp, layer))
        rd = _run(_get_nc("k2d", build_k2d), host_inputs_k2d(zT, inp, layer))
        of = np.concatenate([ra[2 * h]["oT"] for h in range(4)], 0); ob = np.concatenate([ra[2 * h + 1]["oT"][:, ::-1] for h in range(4)], 0)
        dN, dL = host_post_k2b(rb)
        oc0 = np.concatenate([rc[2 * h]["oT"] for h in range(4)], 0); oc1 = np.concatenate([rc[2 * h + 1]["oT"] for h in range(4)], 0)
        yconv, zzf, x0c = host_post_k2d(rd)
        projs = np.ascontiguousarray(np.concatenate([inp["proj_a"][layer], inp["proj_b"][layer], inp["proj_c"][layer], inp["proj_d"][layer]], 0))
        common = {"gla_g": np.ascontiguousarray(inp["gla_norm_g"][layer].reshape(128, 1)),
                  "lamb": np.ascontiguousarray(np.broadcast_to(inp["diff_lambda"][layer].reshape(1, 256), (128, 256))),
                  "sub_g": np.ascontiguousarray(inp["diff_subln_g"][layer].reshape(128, 1)),
                  "skip": np.ascontiguousarray(inp["hy_skip"][layer].reshape(4, 128).T),
                  "projs": projs, "w_out": np.ascontiguousarray(inp["w_out"][layer]),
                  "g2": np.ascontiguousarray(inp["norm2_g"][layer].reshape(8, 128).T),
                  "w1": np.ascontiguousarray(inp["mlp_w1"][layer]), "w2": np.ascontiguousarray(inp["mlp_w2"][layer])}
        if not last:
            common["g1n"] = np.ascontiguousarray(inp["norm1_g"][layer + 1].reshape(8, 128).T)
            common["w_in"] = np.ascontiguousarray(inp["w_in"][layer + 1])
        maps = []
        for r in range(NCORE):
            cs = slice(r * Tn, (r + 1) * Tn)
            cc = lambda a: np.ascontiguousarray(a[..., cs])
            m = dict(common)
            m.update({"xT": cc(xT), "of": cc(of), "ob": cc(ob), "rT": cc(zT[1024:1536]), "dN": cc(dN), "dL": cc(dL),
                      "oc0": cc(oc0), "oc1": cc(oc1), "yconv": cc(yconv), "zzT": cc(zzf), "x0c": cc(x0c), "gates": cc(zT[6944:11040])})
            maps.append(m)
        res = _run(_get_nc(("k3", layer), lambda: build_k3(lam_init, last)), maps)
        xT = np.concatenate([r["xo"] for r in res], axis=1)
        if not last:
            zT = np.concatenate([r["zT"] for r in res], axis=1)
    return np.ascontiguousarray(xT.T)[None].astype(np.float32)
```

```python
import numpy as np
from contextlib import ExitStack
import concourse.bass as bass
import concourse.mybir as mybir
from concourse.bass_utils import run_bass_kernel_spmd

F32 = mybir.dt.float32
BF16 = mybir.dt.bfloat16
AF = mybir.ActivationFunctionType
ALU = mybir.AluOpType
AX = mybir.AxisListType

class Buf:
    __slots__ = ("t", "w", "r", "name")
    def __init__(self, t, name=""):
        self.t = t; self.w = None; self.r = {}; self.name = name
    def __getitem__(self, idx):
        return self.t[idx]

class Prog:
    NDSEM = 6
    def __init__(self, nc, es):
        self.nc = nc; self.es = es
        self.eng = {"pe": nc.tensor, "act": nc.scalar, "dve": nc.vector, "pool": nc.gpsimd, "sp": nc.sync}
        self.sem = {}; self.cnt = {}
        for e in self.eng:
            self.sem[e] = es.enter_context(nc.semaphore("s_" + e)); self.cnt[e] = 0
        self.seen = {e: {} for e in self.eng}
        self.dq = {}
        for q in ("sp", "pool", "act"):
            self.dq[q] = dict(n=0, sems=[es.enter_context(nc.semaphore(f"d_{q}{i}")) for i in range(self.NDSEM)])
            for i in range(self.NDSEM):
                self.sem[(q, i)] = self.dq[q]["sems"][i]
        self.ninst = 0
        self.prefix = ""
    def sbuf(self, shape, dt, name):
        return Buf(self.es.enter_context(self.nc.sbuf_tensor(self.prefix + name, list(shape), dt)), name)
    def psum(self, shape, dt, name):
        return Buf(self.es.enter_context(self.nc.psum_tensor(name, list(shape), dt)), name)
    def dram(self, name, shape, dt, kind="Internal"):
        return Buf(self.nc.dram_tensor(name, list(shape), dt, kind=kind), name)
    def _wait(self, e, tok):
        if tok is None: return
        s, v = tok
        if self.seen[e].get(s, 0) >= v: return
        self.eng[e].wait_ge(self.sem[s], v)
        self.seen[e][s] = v
    def _deps(self, e, reads, writes, same_ok=False):
        for b in reads:
            if b.w is not None and not (same_ok and b.w[0] == e):
                self._wait(e, b.w)
        for b in writes:
            if b.w is not None and not (same_ok and b.w[0] == e):
                self._wait(e, b.w)
            for s, v in b.r.items():
                if same_ok and s == e: continue
                self._wait(e, (s, v))
    def _mark(self, tok, reads, writes):
        for b in reads:
            if b.r.get(tok[0], 0) < tok[1]: b.r[tok[0]] = tok[1]
        for b in writes:
            b.w = tok; b.r = {}
    def op(self, e, reads, writes, fn, same_ok=False):
        self._deps(e, reads, writes, same_ok)
        ins = fn(self.eng[e])
        self.cnt[e] += 1
        ins.then_inc(self.sem[e], 1)
        tok = (e, self.cnt[e])
        self.seen[e][e] = max(self.seen[e].get(e, 0), 0)
        self._mark(tok, reads, writes)
        self.ninst += 1
        return tok
    def dma(self, q, out_ap, in_ap, reads, writes, **kw):
        d = self.dq[q]; j = d["n"]; slot = j % self.NDSEM; val = 16 * (j // self.NDSEM + 1)
        if j >= self.NDSEM:
            self._wait(q, ((q, slot), val - 16))
        self._deps(q, reads, writes)
        ins = self.eng[q].dma_start(out=out_ap, in_=in_ap, **kw)
        ins.then_inc(d["sems"][slot], 16)
        d["n"] += 1
        tok = ((q, slot), val)
        self._mark(tok, reads, writes)
        self.ninst += 1
        return tok
    def finish(self, bufs):
        for b in bufs:
            self._wait("sp", b.w)
        for q in self.dq:
            d = self.dq[q]
            for j in range(max(0, d["n"] - self.NDSEM), d["n"]):
                self._wait("sp", ((q, j % self.NDSEM), 16 * (j // self.NDSEM + 1)))

def _barrier(self):
    for e in self.eng:
        for e2 in self.eng:
            if e2 != e and self.cnt[e2] > 0:
                self._wait(e, (e2, self.cnt[e2]))
        for q, d in self.dq.items():
            for j in range(max(0, d["n"] - self.NDSEM), d["n"]):
                self._wait(e, ((q, j % self.NDSEM), 16 * (j // self.NDSEM + 1)))
Prog.barrier = _barrier

class _Scope:
    def __init__(self, p): self.p = p
    def __enter__(self):
        self.old = self.p.es; self.es = ExitStack(); self.es.__enter__(); self.p.es = self.es; return self
    def __exit__(self, *a):
        self.p.barrier(); self.p.es = self.old; return self.es.__exit__(*a)
Prog.scope = lambda self: _Scope(self)

S = 16384; NC = 8; T = S // NC; D = 1024; NIN = 11040
EPS = 1e-6

def consts(p):
    c = {}
    c["ones"] = p.sbuf([128, 128], F32, "c_ones")
    p.op("pool", [], [c["ones"]], lambda e: e.memset(c["ones"][:, :], 1.0))
    c["eps"] = p.sbuf([128, 1], F32, "c_eps")
    p.op("pool", [], [c["eps"]], lambda e: e.memset(c["eps"][:, :], EPS))
    c["ps"] = [p.psum([128, 512], F32, f"ps{i}") for i in range(8)]
    return c

def rms_fm(p, c, xs, gs, hs, nch, Tn, tmp, rstds, psi=0):
    nfeat = nch * 128
    k = 0
    for tb in range(Tn // 512):
        sl = slice(tb * 512, (tb + 1) * 512)
        ps = c["ps"][psi + tb % 2]
        for ch in range(nch):
            s = tmp[k % 2]; k += 1
            if ch % 2 == 0:
                p.op("act", [xs[ch]], [s], lambda e: e.activation(out=s[:, :], in_=xs[ch][:, sl], func=AF.Square))
            else:
                p.op("dve", [xs[ch]], [s], lambda e: e.tensor_tensor(out=s[:, :], in0=xs[ch][:, sl], in1=xs[ch][:, sl], op=ALU.mult))
            p.op("pe", [c["ones"], s], [ps], lambda e: e.matmul(ps[:, :], lhsT=c["ones"][:, :], rhs=s[:, :], start=(ch == 0), stop=(ch == nch - 1)), same_ok=True)
        r = rstds[tb]
        p.op("act", [ps, c["eps"]], [r], lambda e: e.activation(out=r[:, :], in_=ps[:, :], func=AF.Sqrt, bias=c["eps"][:, 0:1], scale=1.0 / nfeat))
        p.op("dve", [r], [r], lambda e: e.reciprocal(out=r[:, :], in_=r[:, :]))
        for ch in range(nch):
            eng = "dve"
            p.op(eng, [xs[ch], r, gs], [hs[ch]], lambda e: e.scalar_tensor_tensor(out=hs[ch][:, sl], in0=xs[ch][:, sl], scalar=gs[:, ch:ch + 1], in1=r[:, :], op0=ALU.mult, op1=ALU.mult))

def linear_stream(p, c, w_ap, K, N, hs, Tn, evac, wst, wbf, psi=2, npsi=4, gw=512):
    kc = K // 128
    wD = Buf(None)
    ng = (N + gw - 1) // gw
    k2 = 0
    for gi in range(ng):
        c0 = gi * gw; wg = min(gw, N - c0)
        st = wst[gi % 2]; wb = wbf[gi % 2]
        p.dma("sp", st[:, :kc, :wg], w_ap[:, c0:c0 + wg].rearrange("(c p) n -> p c n", p=128), [wD], [st])
        h = kc // 2
        p.op("pool", [st], [wb], lambda e: e.tensor_copy(out=wb[:, :h, :wg], in_=st[:, :h, :wg]))
        p.op("dve", [st], [wb], lambda e: e.tensor_copy(out=wb[:, h:kc, :wg], in_=st[:, h:kc, :wg]))
        for mi in range((wg + 127) // 128):
            mw = min(128, wg - mi * 128)
            for tb in range(Tn // 512):
                ps = c["ps"][psi + k2 % npsi]; k2 += 1
                for ch in range(kc):
                    p.op("pe", [wb, hs[ch]], [ps], lambda e: e.matmul(ps[:mw, :], lhsT=wb[:, ch, mi * 128:mi * 128 + mw], rhs=hs[ch][:, tb * 512:(tb + 1) * 512], start=(ch == 0), stop=(ch == kc - 1)), same_ok=True)
                evac(c0 + mi * 128, mw, tb, ps)

def build_k1():
    nc = bass.Bass("TRN2", target_bir_lowering=False)
    xT = nc.dram_tensor("xT", [D, T], F32, kind="ExternalInput")
    g = nc.dram_tensor("g", [128, 8], F32, kind="ExternalInput")
    w = nc.dram_tensor("w", [D, NIN], F32, kind="ExternalInput")
    zT = nc.dram_tensor("zT", [NIN, T], F32, kind="ExternalOutput")
    with ExitStack() as es:
        p = Prog(nc, es)
        c = consts(p)
        xs = [p.sbuf([128, T], F32, f"x{i}") for i in range(8)]
        hs = [p.sbuf([128, T], BF16, f"h{i}") for i in range(8)]
        gs = p.sbuf([128, 8], F32, "gs")
        xD = Buf(None); zD = Buf(None)
        p.dma("pool", gs[:, :], g[:, :], [xD], [gs])
        for i in range(8):
            p.dma("pool", xs[i][:, :], xT[i * 128:(i + 1) * 128, :], [xD], [xs[i]])
        tmp = [p.sbuf([128, 512], F32, f"tmp{i}") for i in range(2)]
        rstds = [p.sbuf([128, 512], F32, f"rstd{i}") for i in range(T // 512)]
        rms_fm(p, c, xs, gs, hs, 8, T, tmp, rstds)
        wst = [p.sbuf([128, 8, 512], F32, f"wst{i}") for i in range(2)]
        wbf = [p.sbuf([128, 8, 512], BF16, f"wbf{i}") for i in range(2)]
        ost = [p.sbuf([128, T], F32, f"ost{i}") for i in range(2)]
        state = {"k": 0}
        def evac(m0, mw, tb, ps):
            o = ost[state["k"] % 2]
            p.op("act", [ps], [o], lambda e: e.copy(out=o[:mw, tb * 512:(tb + 1) * 512], in_=ps[:mw, :]))
            if tb == T // 512 - 1:
                p.dma("act", zT[m0:m0 + mw, :], o[:mw, :], [o], [zD])
                state["k"] += 1
        linear_stream(p, c, w.ap() if hasattr(w, "ap") else w, D, NIN, hs, T, evac, wst, wbf)
        p.finish([zD])
        print("k1 ninst", p.ninst)
    return nc


S = 16384
NEAR_LO, NEAR_HI = -5, 8
RW = 2304

def t5_bucket_np(rel):
    import math
    half = 16; max_exact = 8
    ret = np.where(rel > 0, half, 0)
    n = np.abs(rel)
    nf = np.maximum(n, 1).astype(np.float32)
    large = max_exact + (np.log(nf / max_exact) / np.float32(math.log(1024 / max_exact)) * (half - max_exact)).astype(np.int32)
    large = np.minimum(large, half - 1)
    return ret + np.where(n < max_exact, n, large)

def qknorm_fm(p, c, src, gcol, dst, rows, Tn, tmp, rtmp, psi):
    for tb in range(Tn // 512):
        sl = slice(tb * 512, (tb + 1) * 512)
        s = tmp[tb % 2]; r = rtmp[tb % 2]; ps = c["ps"][psi + tb % 2]
        p.op("act", [src], [s], lambda e: e.activation(out=s[:rows, :], in_=src[:rows, sl], func=AF.Square))
        p.op("pe", [c["ones"], s], [ps], lambda e: e.matmul(ps[:rows, :], lhsT=c["ones"][:rows, :rows], rhs=s[:rows, :], start=True, stop=True))
        p.op("act", [ps, c["eps"]], [r], lambda e: e.activation(out=r[:rows, :], in_=ps[:rows, :], func=AF.Sqrt, bias=c["eps"][:rows, 0:1], scale=1.0 / rows))
        p.op("dve", [r], [r], lambda e: e.reciprocal(out=r[:rows, :], in_=r[:rows, :]))
        p.op("dve", [src, r, gcol], [dst], lambda e: e.scalar_tensor_tensor(out=dst[:rows, sl], in0=src[:rows, sl], scalar=gcol[:rows, 0:1], in1=r[:rows, :], op0=ALU.mult, op1=ALU.mult))

def emit_k2c(nc, p, c, pre, NQB=S // 512, NKT=S // 128):
    qT = nc.dram_tensor(pre + "qT", [64, S], F32, kind="ExternalInput")
    kT = nc.dram_tensor(pre + "kT", [64, S], F32, kind="ExternalInput")
    v = nc.dram_tensor(pre + "v", [128, S // 128, 128], F32, kind="ExternalInput")
    gq = nc.dram_tensor(pre + "gq", [64, 1], F32, kind="ExternalInput")
    gk = nc.dram_tensor(pre + "gk", [64, 1], F32, kind="ExternalInput")
    Rb = nc.dram_tensor(pre + "Rb", [128, RW], F32, kind="ExternalInput")
    bfar = nc.dram_tensor(pre + "bfar", [128, 2], F32, kind="ExternalInput")
    oT = nc.dram_tensor(pre + "oT", [128, S], F32, kind="ExternalOutput")
    if True:
        iD = Buf(None); oD = Buf(None)
        qb16 = p.sbuf([64, S], BF16, "qb16"); kb16 = p.sbuf([64, S], BF16, "kb16")
        vb16 = p.sbuf([128, S // 128, 128], BF16, "vb16")
        rb = p.sbuf([128, RW], F32, "rb"); bf_ = p.sbuf([128, 2], F32, "bfar_s")
        gqs = p.sbuf([64, 1], F32, "gqs"); gks = p.sbuf([64, 1], F32, "gks")
        p.dma("pool", rb[:, :], Rb[:, :], [iD], [rb]); p.dma("pool", bf_[:, :], bfar[:, :], [iD], [bf_])
        p.dma("pool", gqs[:, :], gq[:, :], [iD], [gqs]); p.dma("pool", gks[:, :], gk[:, :], [iD], [gks])
        tmp = [p.sbuf([128, 512], F32, f"tmp{i}") for i in range(2)]
        rtmp = [p.sbuf([128, 512], F32, f"rtmp{i}") for i in range(2)]
        stg = [p.sbuf([128, 4096], F32, f"stg{i}") for i in range(2)]
        k = 0
        for (src, gcol, dst) in ((qT, gqs, qb16), (kT, gks, kb16)):
            for blk in range(S // 4096):
                st = stg[k % 2]; k += 1
                p.dma("sp", st[:64, :], src[:, blk * 4096:(blk + 1) * 4096], [iD], [st])
                class V:
                    pass
                for tb in range(8):
                    sl = slice(tb * 512, (tb + 1) * 512); dsl = slice(blk * 4096 + tb * 512, blk * 4096 + (tb + 1) * 512)
                    s = tmp[tb % 2]; r = rtmp[tb % 2]; ps = c["ps"][6 + tb % 2]
                    p.op("act", [st], [s], lambda e: e.activation(out=s[:64, :], in_=st[:64, sl], func=AF.Square))
                    p.op("pe", [c["ones"], s], [ps], lambda e: e.matmul(ps[:64, :], lhsT=c["ones"][:64, :64], rhs=s[:64, :], start=True, stop=True))
                    p.op("act", [ps, c["eps"]], [r], lambda e: e.activation(out=r[:64, :], in_=ps[:64, :], func=AF.Sqrt, bias=c["eps"][:64, 0:1], scale=1.0 / 64))
                    p.op("dve", [r], [r], lambda e: e.reciprocal(out=r[:64, :], in_=r[:64, :]))
                    p.op("dve", [st, r, gcol], [dst], lambda e: e.scalar_tensor_tensor(out=dst[:64, dsl], in0=st[:64, sl], scalar=gcol[:64, 0:1], in1=r[:64, :], op0=ALU.mult, op1=ALU.mult))
        for blk in range(S // 128 // 32):
            st = stg[k % 2]; k += 1
            p.dma("sp", st[:, :].rearrange("p (a b) -> p a b", b=128), v[:, blk * 32:(blk + 1) * 32, :], [iD], [st])
            p.op("pool", [st], [vb16], lambda e: e.tensor_copy(out=vb16[:, blk * 32:(blk + 1) * 32, :], in_=st[:, :].rearrange("p (a b) -> p a b", b=128)))
        pt = [p.sbuf([128, 512], BF16, f"pt{i}") for i in range(4)]
        nt = [p.sbuf([128, 512], F32, f"nt{i}") for i in range(2)]
        lacc = [p.sbuf([128, 512], F32, f"lacc{i}") for i in range(2)]; lacc2 = [p.sbuf([128, 512], F32, f"laccp{i}") for i in range(2)]
        rl = [p.sbuf([128, 512], F32, f"rl{i}") for i in range(2)]
        ob = [p.sbuf([128, 512], F32, f"ob{i}") for i in range(2)]
        LOOK = 2
        onesb = p.sbuf([128, 128], BF16, "onesb"); p.op("pool", [], [onesb], lambda e: e.memset(onesb[:, :], 1.0))
        tiles = [(qb, j) for qb in range(NQB) for j in range(NKT)]
        NTL = len(tiles)
        nn = 0
        def front(t):
            nonlocal nn
            qb, j = tiles[t]
            qsl = slice(qb * 512, (qb + 1) * 512)
            ps = c["ps"][t % 4]; P = pt[t % 4]
            p.op("pe", [kb16, qb16], [ps], lambda e: e.matmul(ps[:, :], lhsT=kb16[:, j * 128:(j + 1) * 128], rhs=qb16[:, qsl], start=True, stop=True))
            m = j - 4 * qb
            if NEAR_LO <= m <= NEAR_HI:
                tt_ = nt[nn % 2]; nn += 1
                off = 1024 - 128 * m
                p.op("dve", [ps, rb], [tt_], lambda e: e.scalar_tensor_tensor(out=tt_[:, :], in0=ps[:, :], scalar=0.125, in1=rb[:, off:off + 512], op0=ALU.mult, op1=ALU.add))
                p.op("act", [tt_], [P], lambda e: e.activation(out=P[:, :], in_=tt_[:, :], func=AF.Exp))
            else:
                side = 0 if m < 0 else 1
                p.op("act", [ps, bf_], [P], lambda e: e.activation(out=P[:, :], in_=ps[:, :], func=AF.Exp, bias=bf_[:, side:side + 1], scale=0.125))
        def back(t):
            qb, j = tiles[t]
            qsl = slice(qb * 512, (qb + 1) * 512)
            P = pt[t % 4]; po = c["ps"][4 + qb % 2]; la = lacc[qb % 2]
            p.op("pe", [vb16, P], [po], lambda e: e.matmul(po[:, :], lhsT=vb16[:, j, :], rhs=P[:, :], start=(j == 0), stop=(j == NKT - 1)), same_ok=True)
            la2 = lacc2[qb % 2]
            use_pool = (j % 3 == 2)
            lx = la2 if use_pool else la; le = "pool" if use_pool else "dve"
            if j == 0 or j == 2:
                p.op(le, [P], [lx], lambda e: e.tensor_copy(out=lx[:, :], in_=P[:, :]))
            else:
                p.op(le, [P, lx], [lx], lambda e: e.tensor_tensor(out=lx[:, :], in0=lx[:, :], in1=P[:, :], op=ALU.add))
            if j == NKT - 1:
                pl = c["ps"][6 + qb % 2]; r_ = rl[qb % 2]
                p.op("pe", [c["ones"], la], [pl], lambda e: e.matmul(pl[:, :], lhsT=c["ones"][:, :], rhs=la[:, :], start=True, stop=False))
                p.op("pe", [c["ones"], la2], [pl], lambda e: e.matmul(pl[:, :], lhsT=c["ones"][:, :], rhs=la2[:, :], start=False, stop=True), same_ok=True)
                p.op("dve", [pl], [r_], lambda e: e.reciprocal(out=r_[:, :], in_=pl[:, :]))
                o = ob[qb % 2]
                p.op("dve", [po, r_], [o], lambda e: e.tensor_tensor(out=o[:, :], in0=po[:, :], in1=r_[:, :], op=ALU.mult))
                p.dma("pool", oT[:, qsl], o[:, :], [o], [oD])
        for t in range(NTL + LOOK):
            if t < NTL: front(t)
            if t >= LOOK: back(t - LOOK)
        print("k2c ninst", p.ninst)
    return None

def host_inputs_k2c(zT, inp, layer):
    CQ = 256 + 256 + 512 + 512 + 32 + 768 * 3
    CK = CQ + 512; CV = CK + 512
    t5 = inp["t5_bias"]
    kk = np.arange(128)[:, None]; jj = np.arange(RW)[None, :]
    bidx = t5_bucket_np(kk - jj + 1024)
    assert (t5_bucket_np(np.arange(-20000, -5 * 128 + 128 - 1 - 127 + 1)) == 15).all()
    maps = []
    for h in range(4):
        vT = zT[CV + h * 128:CV + (h + 1) * 128, :]
        vv = np.ascontiguousarray(vT.T.reshape(S // 128, 128, 128).transpose(1, 0, 2))
        Rb = np.ascontiguousarray(t5[:, 12 + h][bidx]).astype(np.float32)
        bfar = np.ascontiguousarray(np.broadcast_to(np.array([t5[15, 12 + h], t5[31, 12 + h]], np.float32)[None, :], (128, 2)))
        for cc in range(2):
            r0 = h * 128 + cc * 64
            maps.append({"qT": np.ascontiguousarray(zT[CQ + r0:CQ + r0 + 64]), "kT": np.ascontiguousarray(zT[CK + r0:CK + r0 + 64]), "v": vv,
                         "gq": np.ascontiguousarray(inp["diff_qnorm_g"][layer].reshape(64, 1)), "gk": np.ascontiguousarray(inp["diff_knorm_g"][layer].reshape(64, 1)),
                         "Rb": Rb, "bfar": bfar})
    return maps


S = 16384
NT = S // 128

def emit_k2a(nc, p, c, pre, NTL=NT):
    I = lambda n, s: nc.dram_tensor(pre + n, s, F32, kind="ExternalInput")
    qT = I("qT", [64, S]); kT = I("kT", [64, S]); ktm = I("ktm", [128, NT, 64]); vtm = I("vtm", [128, NT, 128])
    lrT = I("lrT", [16, S]); gw = I("gw", [16, 64]); gbb = I("gbb", [128, 64])
    tri = I("tri", [128, 128]); m1 = I("m1", [128, 128]); cind = I("cind", [128, 2]); maskA = I("maskA", [128, 128])
    oT = nc.dram_tensor(pre + "oT", [128, S], F32, kind="ExternalOutput")
    if True:
        iD = Buf(None); oD = Buf(None)
        def ld(name, shape, src, q="pool"):
            b = p.sbuf(shape, F32, name); p.dma(q, b[tuple(slice(None) for _ in shape)], src, [iD], [b]); return b
        gws = ld("gws", [16, 64], gw[:, :]); gbs = ld("gbs", [128, 64], gbb[:, :])
        tris = ld("tris", [128, 128], tri[:, :]); m1s = ld("m1s", [128, 128], m1[:, :]); cis = ld("cis", [128, 2], cind[:, :])
        mAs = ld("mAs", [128, 128], maskA[:, :])
        one1 = p.sbuf([128, 1], F32, "one1"); p.op("pool", [], [one1], lambda e: e.memset(one1[:, :], 1.0))
        lrs = ld("lrs", [16, S], lrT[:, :], "sp")
        St = p.sbuf([64, 128], F32, "St"); Sb = p.sbuf([64, 128], BF16, "Sb")
        p.op("pool", [], [St], lambda e: e.memset(St[:, :], 0.0)); p.op("pool", [], [Sb], lambda e: e.memset(Sb[:, :], 0.0))
        NB = 16
        def stg(name, shape): return [p.sbuf(shape, F32, f"{name}{i}") for i in range(2)]
        qst = stg("qst", [64, NB * 128]); kst = stg("kst", [64, NB * 128]); ktst = stg("ktst", [128, NB, 64]); vst = stg("vst", [128, NB, 128])
        vbf = [p.sbuf([128, NB, 128], BF16, f"vbf{i}") for i in range(2)]
        ost = [p.sbuf([128, NB * 128], F32, f"ost{i}") for i in range(2)]
        R = lambda n, shp, dt=F32, k=2: [p.sbuf(shp, dt, f"{n}{i}") for i in range(k)]
        xs = R("xs", [128, 64]); es_ = R("es", [128, 64]); ls = R("ls", [128, 64])
        eb = R("eb", [64, 128]); enb = R("enb", [64, 128]); ed = R("ed", [128, 64]); av = R("av", [64, 2])
        qg = R("qg", [64, 128], BF16); kg = R("kg", [64, 128], BF16); kd = R("kd", [128, 64], BF16); Am = R("Am", [128, 128], BF16)
        for i in range(NTL):
            blk, ib = divmod(i, NB); par = i % 2
            if ib == 0:
                bsl = slice(blk * NB * 128, (blk + 1) * NB * 128); tsl = slice(blk * NB, (blk + 1) * NB)
                p.dma("sp", qst[blk % 2][:, :], qT[:, bsl], [iD], [qst[blk % 2]])
                p.dma("sp", kst[blk % 2][:, :], kT[:, bsl], [iD], [kst[blk % 2]])
                p.dma("sp", ktst[blk % 2][:, :, :], ktm[:, tsl, :], [iD], [ktst[blk % 2]])
                p.dma("sp", vst[blk % 2][:, :, :], vtm[:, tsl, :], [iD], [vst[blk % 2]])
                vb = vbf[blk % 2]
                p.op("pool", [vst[blk % 2]], [vb], lambda e: e.tensor_copy(out=vb[:, :, :], in_=vst[blk % 2][:, :, :]))
            q_s = qst[blk % 2]; k_s = kst[blk % 2]; kt_s = ktst[blk % 2]; vb = vbf[blk % 2]; o_s = ost[blk % 2]
            tsl = slice(i * 128, (i + 1) * 128); lsl = slice(ib * 128, (ib + 1) * 128)
            pa = c["ps"][par]; pb = c["ps"][2 + par]; pc = c["ps"][4 + par]; po = c["ps"][6]; pf = c["ps"][7]
            x = xs[par]; e_ = es_[par]; l = ls[par]
            p.op("pe", [lrs, gws], [pa], lambda e: e.matmul(pa[:, 0:64], lhsT=lrs[:, tsl], rhs=gws[:, :], start=True, stop=True))
            p.op("dve", [pa, gbs], [x], lambda e: e.tensor_tensor(out=x[:, :], in0=pa[:, 0:64], in1=gbs[:, :], op=ALU.add))
            p.op("act", [x], [e_], lambda e: e.activation(out=e_[:, :], in_=x[:, :], func=AF.Exp, scale=-1.0))
            p.op("act", [e_, one1], [l], lambda e: e.activation(out=l[:, :], in_=e_[:, :], func=AF.Ln, bias=one1[:, 0:1], scale=1.0))
            p.op("pe", [l, tris], [pb], lambda e: e.matmul(pb[:64, 0:128], lhsT=l[:, :], rhs=tris[:, :], start=True, stop=True))
            p.op("pe", [l, cis], [pb], lambda e: e.matmul(pb[:64, 128:130], lhsT=l[:, :], rhs=cis[:, :], start=True, stop=True), same_ok=True)
            p.op("pe", [m1s, l], [pc], lambda e: e.matmul(pc[:, 0:64], lhsT=m1s[:, :], rhs=l[:, :], start=True, stop=True))
            p.op("act", [pb], [eb[par]], lambda e: e.activation(out=eb[par][:, :], in_=pb[:64, 0:128], func=AF.Exp, scale=-1.0 / 16))
            p.op("act", [pb], [enb[par]], lambda e: e.activation(out=enb[par][:, :], in_=pb[:64, 0:128], func=AF.Exp, scale=1.0 / 16))
            p.op("act", [pb], [av[par]], lambda e: e.activation(out=av[par][:, :], in_=pb[:64, 128:130], func=AF.Exp, scale=-1.0 / 16))
            p.op("act", [pc], [ed[par]], lambda e: e.activation(out=ed[par][:, :], in_=pc[:, 0:64], func=AF.Exp, scale=-1.0 / 16))
            p.op("dve", [q_s, eb[par]], [qg[par]], lambda e: e.scalar_tensor_tensor(out=qg[par][:, :], in0=q_s[:, lsl], scalar=0.125, in1=eb[par][:, :], op0=ALU.mult, op1=ALU.mult))
            p.op("dve", [k_s, enb[par]], [kg[par]], lambda e: e.tensor_tensor(out=kg[par][:, :], in0=k_s[:, lsl], in1=enb[par][:, :], op=ALU.mult))
            p.op("dve", [kt_s, ed[par]], [kd[par]], lambda e: e.tensor_tensor(out=kd[par][:, :], in0=kt_s[:, ib, :], in1=ed[par][:, :], op=ALU.mult))
            p.op("pe", [kg[par], qg[par]], [pc], lambda e: e.matmul(pc[:, 128:256], lhsT=kg[par][:, :], rhs=qg[par][:, :], start=True, stop=True))
            p.op("dve", [pc, mAs], [Am[par]], lambda e: e.tensor_tensor(out=Am[par][:, :], in0=pc[:, 128:256], in1=mAs[:, :], op=ALU.mult))
            p.op("pe", [vb, Am[par]], [po], lambda e: e.matmul(po[:, 0:128], lhsT=vb[:, ib, :], rhs=Am[par][:, :], start=True, stop=False))
            for ci in range(2):
                cs = slice(ci * 64, (ci + 1) * 64)
                p.op("pe", [Sb, qg[par]], [po], lambda e: e.matmul(po[:, cs], lhsT=Sb[:, :], rhs=qg[par][:, cs], start=False, stop=(ci == 1)), same_ok=True)
                p.op("pe", [kd[par], vb], [pf], lambda e: e.matmul(pf[:64, 0:128], lhsT=kd[par][cs, :], rhs=vb[cs, ib, :], start=True, stop=True))
                p.op("dve", [St, av[par], pf], [St], lambda e: e.scalar_tensor_tensor(out=St[:, :], in0=St[:, :], scalar=av[par][:, ci:ci + 1], in1=pf[:64, 0:128], op0=ALU.mult, op1=ALU.add))
                p.op("act", [St], [Sb], lambda e: e.copy(out=Sb[:, :], in_=St[:, :]))
            p.op("act", [po], [o_s], lambda e: e.copy(out=o_s[:, lsl], in_=po[:, 0:128]))
            if ib == NB - 1 or i == NTL - 1:
                p.dma("act", oT[:, blk * NB * 128:(blk * NB + ib + 1) * 128], o_s[:, :(ib + 1) * 128], [o_s], [oD])
        print("k2a ninst", p.ninst)
    return None

def host_inputs_k2a(zT, inp, layer):
    AQ, AK, AV, AR, ALR = 0, 256, 512, 1024, 1536
    ss = np.arange(128)[:, None]; tt = np.arange(128)[None, :]
    same = (ss // 64) == (tt // 64)
    tri = (same & (ss <= tt)).astype(np.float32); m1 = (same & (ss > tt)).astype(np.float32)
    cind = (np.arange(128)[:, None] // 64 == np.arange(2)[None, :]).astype(np.float32)
    maps = []
    for h in range(4):
        for d in range(2):
            f = (lambda a: a[:, ::-1]) if d == 1 else (lambda a: a)
            q = np.ascontiguousarray(f(zT[AQ + h * 64:AQ + (h + 1) * 64])); k = np.ascontiguousarray(f(zT[AK + h * 64:AK + (h + 1) * 64]))
            v = f(zT[AV + h * 128:AV + (h + 1) * 128]); lr = np.ascontiguousarray(f(zT[ALR + d * 16:ALR + (d + 1) * 16]))
            ktm = np.ascontiguousarray(k.T.reshape(NT, 128, 64).transpose(1, 0, 2)); vtm = np.ascontiguousarray(v.T.reshape(NT, 128, 128).transpose(1, 0, 2))
            mA = (same & ((ss <= tt) if d == 0 else (ss < tt))).astype(np.float32)
            maps.append({"qT": q, "kT": k, "ktm": ktm, "vtm": vtm, "lrT": lr,
                         "gw": np.ascontiguousarray(inp["gla_gate_w"][layer, d][:, h * 64:(h + 1) * 64]),
                         "gbb": np.ascontiguousarray(np.broadcast_to(inp["gla_gate_b"][layer, d][None, h * 64:(h + 1) * 64], (128, 64))),
                         "tri": tri, "m1": m1, "cind": cind, "maskA": mA})
    return maps


S = 16384
DILS = (1, 4, 16)
NEG = -100.0

STAGE = 3
def emit_k2b(nc, p, c, pre):
    I = lambda n, s: nc.dram_tensor(pre + n, s, F32, kind="ExternalInput")
    qw = I("qw", [3, 256, 2048]); kw = I("kw", [3, 256, 4096]); vw = I("vw", [3, 128, 16 * 2 * 4 * 64])
    gq = I("gq", [128, 3]); gk = I("gk", [128, 3]); kval = I("kval", [128, 3 * 32]); bm = I("bm", [3, 2, 128, 512]); bones = I("bones", [128, 128])
    O = nc.dram_tensor(pre + "O", [3, 2, 64, 4, 2048], F32, kind="ExternalOutput")
    if True:
        iD = Buf(None); oD = Buf(None)
        def ld(name, shape, src, q="pool"):
            b = p.sbuf(shape, F32, name); p.dma(q, b[tuple(slice(None) for _ in shape)], src, [iD], [b]); return b
        gqs = ld("gqs", [128, 3], gq[:, :]); gks = ld("gks", [128, 3], gk[:, :]); kvs = ld("kvs", [128, 96], kval[:, :]); bos = ld("bos", [128, 128], bones[:, :])
        bms = [[ld(f"bm{g}{t}", [128, 512], bm[g, t, :, :]) for t in range(2)] for g in range(3)]
        qst = [p.sbuf([128, 2048], F32, f"qst{i}") for i in range(2)]; kst = [p.sbuf([128, 4096], F32, f"kst{i}") for i in range(2)]
        vst = p.sbuf([128, 16 * 2 * 4 * 64], F32, "vst"); vb = p.sbuf([128, 16, 2, 4, 64], BF16, "vb")
        qn = [p.sbuf([64, 2048], BF16, f"qn{i}") for i in range(4)]; kn = [p.sbuf([64, 4096], BF16, f"kn{i}") for i in range(4)]
        tmp = [p.sbuf([128, 512], F32, f"tmp{i}") for i in range(2)]; rtmp = [p.sbuf([128, 512], F32, f"rtmp{i}") for i in range(2)]
        tt = [p.sbuf([128, 512], F32, f"tt{i}") for i in range(2)]; P = [p.sbuf([128, 512], BF16, f"P{i}") for i in range(2)]
        ost = [p.sbuf([64, 2, 512], F32, f"ost{i}") for i in range(2)]
        onb = p.sbuf([128, 64], BF16, "onb"); p.op("pool", [], [onb], lambda e: e.memset(onb[:, :], 1.0))
        it = 0
        def norm(st, gcol, dst, n):
            for tb in range(n // 512):
                sl = slice(tb * 512, (tb + 1) * 512)
                s = tmp[tb % 2]; r = rtmp[tb % 2]; ps = c["ps"][6 + tb % 2]
                p.op("act", [st], [s], lambda e: e.activation(out=s[:64, :], in_=st[:64, sl], func=AF.Square))
                p.op("pe", [c["ones"], s], [ps], lambda e: e.matmul(ps[:64, :], lhsT=c["ones"][:64, :64], rhs=s[:64, :], start=True, stop=True))
                p.op("act", [ps, c["eps"]], [r], lambda e: e.activation(out=r[:64, :], in_=ps[:64, :], func=AF.Sqrt, bias=c["eps"][:64, 0:1], scale=1.0 / 64))
                p.op("dve", [r], [r], lambda e: e.reciprocal(out=r[:64, :], in_=r[:64, :]))
                p.op("dve", [st, r, gcol[0]], [dst], lambda e: e.scalar_tensor_tensor(out=dst[:, sl], in0=st[:64, sl], scalar=gcol[0][:64, gcol[1]:gcol[1] + 1], in1=r[:64, :], op0=ALU.mult, op1=ALU.mult))
        for g in range(3):
            p.dma("sp", vst[:, :], vw[g, :, :], [iD], [vst])
            p.op("pool", [vst], [vb], lambda e: e.tensor_copy(out=vb[:, :, :, :, :].rearrange("p a b c d -> p (a b c d)"), in_=vst[:, :]))
            for h in range(4):
                p.dma("sp", qst[h % 2][:64, :], qw[g, h * 64:(h + 1) * 64, :], [iD], [qst[h % 2]])
                p.dma("sp", kst[h % 2][:64, :], kw[g, h * 64:(h + 1) * 64, :], [iD], [kst[h % 2]])
                norm(qst[h % 2], (gqs, g), qn[h], 2048); norm(kst[h % 2], (gks, g), kn[h], 4096)
            for b in range(16):
                po = c["ps"][4 + b % 2]; pl = c["ps"][6 + b % 2]
                for tau in range(2 if STAGE >= 2 else 0):
                    ps = c["ps"][it % 4]; t = tt[it % 2]; Pt = P[it % 2]; it += 1
                    for h in range(4):
                        p.op("pe", [kn[h], qn[h]], [ps], lambda e: e.matmul(ps[:, h * 128:(h + 1) * 128], lhsT=kn[h][:, b * 256 + tau * 128:b * 256 + tau * 128 + 128], rhs=qn[h][:, b * 128:(b + 1) * 128], start=True, stop=True), same_ok=True)
                    p.op("dve", [ps, bms[g][tau]], [t], lambda e: e.scalar_tensor_tensor(out=t[:, :], in0=ps[:, :], scalar=0.125, in1=bms[g][tau][:, :], op0=ALU.mult, op1=ALU.add))
                    col = g * 32 + b * 2 + tau
                    p.op("act", [t, kvs], [Pt], lambda e: e.activation(out=Pt[:, :], in_=t[:, :], func=AF.Exp, bias=kvs[:, col:col + 1], scale=1.0))
                    if STAGE < 3: continue
                    for h in range(4):
                        p.op("pe", [vb, Pt], [po], lambda e: e.matmul(po[:64, h * 128:(h + 1) * 128], lhsT=vb[:, b, tau, h, :], rhs=Pt[:, h * 128:(h + 1) * 128], start=(tau == 0 and h == 0), stop=(tau == 1 and h == 3)), same_ok=True)
                    p.op("pe", [onb, Pt], [pl], lambda e: e.matmul(pl[:64, :], lhsT=onb[:, :], rhs=Pt[:, :], start=(tau == 0), stop=(tau == 1)), same_ok=True)
                o = ost[b % 2]
                if STAGE < 3:
                    p.op("dve", [], [o], lambda e: e.memset(o[:, :, :], 1.0))
                    for w_ in range(2):
                        p.dma("act", O[g, w_, :, :, b * 128:(b + 1) * 128], o[:, w_, :].rearrange("r (h q) -> r h q", q=128), [o], [oD])
                    continue
                p.op("act", [po], [o], lambda e: e.copy(out=o[:, 0, :], in_=po[:64, :]))
                p.op("dve", [pl], [o], lambda e: e.tensor_copy(out=o[:, 1, :], in_=pl[:64, :]))
                for w_ in range(2):
                    p.dma("act", O[g, w_, :, :, b * 128:(b + 1) * 128], o[:, w_, :].rearrange("r (h q) -> r h q", q=128), [o], [oD])
        print("k2b ninst", p.ninst)
    return None

def perm_index(d):
    M = S // d
    pos = np.arange(S)
    return (pos % M) * d + pos // M

def host_inputs_k2b(zT, inp, layer):
    BQ = 1568; BK = BQ + 768; BV = BK + 768
    t5 = inp["t5_bias"]
    kk = np.arange(128)[:, None]; qq = np.arange(128)[None, :]
    bm = np.zeros((3, 2, 128, 512), np.float32)
    for g, d in enumerate(DILS):
        for tau in range(2):
            delta = kk + (-64 if tau == 0 else 64) - qq
            ok = np.abs(delta) <= 64
            bidx = t5_bucket_np(delta * d)
            for h in range(4):
                bm[g, tau, :, h * 128:(h + 1) * 128] = np.where(ok, t5[:, g * 4 + h][bidx], np.float32(NEG))
    bones = (np.arange(128)[:, None] // 64 == np.arange(128)[None, :] // 64).astype(np.float32)
    gq = np.ascontiguousarray(np.tile(inp["dil_qnorm_g"][layer].T, (2, 1))); gk = np.ascontiguousarray(np.tile(inp["dil_knorm_g"][layer].T, (2, 1)))
    qP, kP, vP, valid = [], [], [], []
    for g, d in enumerate(DILS):
        idx = perm_index(d); M = S // d
        qP.append(zT[BQ + g * 256:BQ + (g + 1) * 256][:, idx])
        kP.append(zT[BK + g * 256:BK + (g + 1) * 256][:, idx]); vP.append(zT[BV + g * 256:BV + (g + 1) * 256][:, idx])
    maps = []
    for r in range(8):
        qw = np.zeros((3, 256, 2048), np.float32); kw = np.zeros((3, 256, 4096), np.float32)
        vw = np.zeros((3, 128, 16, 2, 4, 64), np.float32); kval = np.zeros((128, 96), np.float32)
        for g, d in enumerate(DILS):
            M = S // d
            qw[g] = qP[g][:, r * 2048:(r + 1) * 2048]
            for b in range(16):
                P0 = r * 2048 + b * 128; sub0 = (P0 // M) * M
                kpos = np.arange(P0 - 64, P0 + 192)
                ok = (kpos >= sub0) & (kpos < sub0 + M)
                kc = np.clip(kpos, 0, S - 1)
                kwin = np.where(ok[None, :], kP[g][:, kc], np.float32(0))
                kw[g][:, b * 256:(b + 1) * 256] = kwin
                vwin = np.where(ok[None, :], vP[g][:, kc], np.float32(0))
                vv = vwin.reshape(4, 64, 2, 128).transpose(3, 2, 0, 1)
                vw[g][:, b] = vv
                kval[:, g * 32 + b * 2:g * 32 + b * 2 + 2] = np.where(ok.reshape(2, 128).T, np.float32(0), np.float32(NEG))
        maps.append({"qw": qw, "kw": kw, "vw": np.ascontiguousarray(vw.reshape(3, 128, -1)), "gq": gq, "gk": gk, "kval": kval, "bm": bm, "bones": bones})
    return maps

def host_post_k2b(results):
    N = np.zeros((3, 256, S), np.float32); lb = np.zeros((3, 256, S), np.float32)
    for g, d in enumerate(DILS):
        Og = np.concatenate([r["O"][g] for r in results], axis=3)
        idx = perm_index(d)
        N[g][:, idx] = Og[0].transpose(1, 0, 2).reshape(256, S)
        lb[g][:, idx] = Og[1].transpose(1, 0, 2).reshape(256, S)
    return N, lb


import math
S = 16384
I32 = mybir.dt.int32
TWO_PI = float(2 * np.pi)

def emit_k2d(nc, p, c, pre, NROW=128):
    I = lambda n, s: nc.dram_tensor(pre + n, s, F32, kind="ExternalInput")
    ux1 = I("ux1", [128, S]); uv = I("uv", [128, S]); ux0 = I("ux0", [64, S])
    cw1 = I("cw1", [128, 4]); cwv = I("cwv", [128, 4]); cw0 = I("cw0", [64, 4])
    embT = I("embT", [33, S]); w1 = I("w1", [33, 64]); w2 = I("w2", [64, 64]); w3 = I("w3", [64, 128])
    bf1 = I("bf1", [64, 2]); bf2 = I("bf2", [64, 2])
    dlrow = I("dlrow", [1, 128]); tvec = I("tvec", [1, S]); pair = I("pair", [128, 128])
    Y = nc.dram_tensor(pre + "Y", [128, 128, 128], F32, kind="ExternalOutput")
    zz = nc.dram_tensor(pre + "zz", [64, S], F32, kind="ExternalOutput")
    x0c = nc.dram_tensor(pre + "x0c", [64, S], F32, kind="ExternalOutput")
    kx = nc.dram_tensor(pre + "kx", [128, S + 128], BF16, kind="Internal")
    if True:
        iD = Buf(None); oD = Buf(None); kxD = Buf(None)
        def ld(name, shape, src, q="pool"):
            b = p.sbuf(shape, F32, name); p.dma(q, b[tuple(slice(None) for _ in shape)], src, [iD], [b]); return b
        cw1s = ld("cw1s", [128, 4], cw1[:, :]); cwvs = ld("cwvs", [128, 4], cwv[:, :]); cw0s = ld("cw0s", [64, 4], cw0[:, :])
        w1s = ld("w1s", [33, 64], w1[:, :]); w2s = ld("w2s", [64, 64], w2[:, :]); w3s = ld("w3s", [64, 128], w3[:, :])
        bf1s = ld("bf1s", [64, 2], bf1[:, :]); bf2s = ld("bf2s", [64, 2], bf2[:, :])
        dls = ld("dls", [1, 128], dlrow[:, :]); prs = ld("prs", [128, 128], pair[:, :])
        for b in (bf1s, bf2s):
            p.op("dve", [b], [b], lambda e: e.tensor_scalar(out=b[:, 1:2], in0=b[:, 1:2], scalar1=1.0 / TWO_PI, scalar2=None, op0=ALU.mult))
        idf = p.sbuf([128, 128], F32, "idf"); idb = p.sbuf([128, 128], BF16, "idb")
        p.op("pool", [], [idf], lambda e: e.memset(idf[:, :], 1.0))
        p.op("pool", [idf], [idf], lambda e: e.affine_select(out=idf[:, :], in_=idf[:, :], pattern=[[-1, 128]], compare_op=ALU.is_equal, fill=0.0, base=0, channel_multiplier=1))
        p.op("dve", [idf], [idb], lambda e: e.tensor_copy(out=idb[:, :], in_=idf[:, :]))
        zpad = p.sbuf([128, 128], BF16, "zpad"); p.op("pool", [], [zpad], lambda e: e.memset(zpad[:, :], 0.0))
        p.dma("pool", kx[:, S:S + 128], zpad[:, :], [zpad], [kxD])
        Zall = p.sbuf([128, 128, 128], BF16, "Zall")
        rn = p.sbuf([128, 128], F32, "rn")
        scA = p.scope(); scA.__enter__()
        zzb = p.sbuf([128, S], BF16, "zzb")
        CB = 2048
        ub = [p.sbuf([128, CB + 2], F32, f"ub{i}") for i in range(2)]
        acc = [p.sbuf([128, CB], F32, f"acc{i}") for i in range(3)]
        k = 0
        def conv(src, rows, cws, blk, dst):
            nonlocal k
            u = ub[k % 2]; k += 1
            lo = blk * CB - 1; hi = (blk + 1) * CB + 1
            if blk == 0:
                p.op("pool", [], [u], lambda e: e.memset(u[:rows, 0:1], 0.0));
                p.dma("sp", u[:rows, 1:CB + 2], src[:, 0:hi], [iD], [u])
            elif blk == S // CB - 1:
                p.op("pool", [], [u], lambda e: e.memset(u[:rows, CB + 1:CB + 2], 0.0))
                p.dma("sp", u[:rows, 0:CB + 1], src[:, lo:S], [iD], [u])
            else:
                p.dma("sp", u[:rows, :], src[:, lo:hi], [iD], [u])
            p.op("dve", [u, cws], [dst], lambda e: e.tensor_scalar(out=dst[:rows, :], in0=u[:rows, 1:CB + 1], scalar1=cws[:rows, 1:2], scalar2=cws[:rows, 3:4], op0=ALU.mult, op1=ALU.add))
            p.op("dve", [u, cws, dst], [dst], lambda e: e.scalar_tensor_tensor(out=dst[:rows, :], in0=u[:rows, 0:CB], scalar=cws[:rows, 0:1], in1=dst[:rows, :], op0=ALU.mult, op1=ALU.add))
            p.op("dve", [u, cws, dst], [dst], lambda e: e.scalar_tensor_tensor(out=dst[:rows, :], in0=u[:rows, 2:CB + 2], scalar=cws[:rows, 2:3], in1=dst[:rows, :], op0=ALU.mult, op1=ALU.add))
        for blk in range(S // CB):
            bsl = slice(blk * CB, (blk + 1) * CB)
            conv(ux1, 128, cw1s, blk, acc[0]); conv(uv, 128, cwvs, blk, acc[1])
            p.op("pool", [acc[0], acc[1]], [acc[0]], lambda e: e.tensor_tensor(out=acc[0][:, :], in0=acc[0][:, :], in1=acc[1][:, :], op=ALU.mult))
            p.op("act", [acc[0]], [zzb], lambda e: e.copy(out=zzb[:, bsl], in_=acc[0][:, :]))
            p.dma("act", zz[:, bsl], acc[0][:64, :], [acc[0]], [oD])
            conv(ux0, 64, cw0s, blk, acc[2])
            p.dma("act", x0c[:, bsl], acc[2][:64, :], [acc[2]], [oD])
        for g in range(32):
            ps = c["ps"][g % 2]
            for q in range(4):
                sb = g * 4 + q
                p.op("pe", [zzb, idb], [ps], lambda e: e.matmul(ps[:, q * 128:(q + 1) * 128], lhsT=zzb[:, sb * 128:(sb + 1) * 128], rhs=idb[:, :], start=True, stop=True), same_ok=True)
            eng = "act" if g % 2 == 0 else "dve"
            if eng == "act":
                p.op("act", [ps], [Zall], lambda e: e.copy(out=Zall[:, :, g * 4:(g + 1) * 4].rearrange("p r b -> p b r"), in_=ps[:, :].rearrange("p (b r) -> p b r", r=128)))
            else:
                p.op("dve", [ps], [Zall], lambda e: e.tensor_copy(out=Zall[:, :, g * 4:(g + 1) * 4].rearrange("p r b -> p b r"), in_=ps[:, :].rearrange("p (b r) -> p b r", r=128)))
        scA.__exit__(None, None, None)
        scC = p.scope(); scC.__enter__()
        tvb = [p.sbuf([1, 512], F32, f"tvb{i}") for i in range(2)]
        emb = [p.sbuf([33, 512], F32, f"emb{i}") for i in range(2)]
        R = lambda n, shp, dt=F32, kk=2: [p.sbuf(shp, dt, f"{n}{i}") for i in range(kk)]
        ua = R("ua", [64, 512]); ti = R("ti", [64, 512], I32); tf = R("tf", [64, 512]); h1 = R("h1", [64, 512]); h2 = R("h2", [64, 512])
        dec = R("dec", [128, 512]); kf = R("kf", [128, 512]); kb = R("kb", [128, 512], BF16)
        npart = p.sbuf([128, 32], F32, "npart")
        def sin_layer(ps, bfs, out, par):
            u = ua[par]
            p.op("dve", [ps, bfs], [u], lambda e: e.tensor_scalar(out=u[:, :], in0=ps[:64, :], scalar1=bfs[:, 0:1], scalar2=bfs[:, 1:2], op0=ALU.add, op1=ALU.mult))
            p.op("dve", [u], [ti[par]], lambda e: e.tensor_copy(out=ti[par][:, :], in_=u[:, :]))
            p.op("dve", [ti[par]], [tf[par]], lambda e: e.tensor_copy(out=tf[par][:, :], in_=ti[par][:, :]))
            p.op("dve", [u, tf[par]], [u], lambda e: e.tensor_tensor(out=u[:, :], in0=u[:, :], in1=tf[par][:, :], op=ALU.subtract))
            p.op("act", [u], [out], lambda e: e.activation(out=out[:, :], in_=u[:, :], func=AF.Sin, scale=TWO_PI))
        for blk in range(32):
            par = blk % 2; bsl = slice(blk * 512, (blk + 1) * 512)
            em = emb[par]
            p.dma("sp", em[:, :], embT[:, bsl], [iD], [em])
            p.dma("sp", tvb[par][:, :], tvec[:, bsl], [iD], [tvb[par]])
            p1 = c["ps"][2 + par]; p2 = c["ps"][4 + par]; p3 = c["ps"][6]; pd = c["ps"][7]
            p.op("pe", [w1s, em], [p1], lambda e: e.matmul(p1[:64, :], lhsT=w1s[:, :], rhs=em[:, :], start=True, stop=True))
            sin_layer(p1, bf1s, h1[par], par)
            p.op("pe", [w2s, h1[par]], [p2], lambda e: e.matmul(p2[:64, :], lhsT=w2s[:, :], rhs=h1[par][:, :], start=True, stop=True))
            sin_layer(p2, bf2s, h2[par], par)
            p.op("pe", [w3s, h2[par]], [p3], lambda e: e.matmul(p3[:, :], lhsT=w3s[:, :], rhs=h2[par][:, :], start=True, stop=True))
            p.op("pe", [dls, tvb[par]], [pd], lambda e: e.matmul(pd[:, :], lhsT=dls[:, :], rhs=tvb[par][:, :], start=True, stop=True))
            p.op("act", [pd], [dec[par]], lambda e: e.activation(out=dec[par][:, :], in_=pd[:, :], func=AF.Exp))
            p.op("dve", [p3, dec[par]], [kf[par]], lambda e: e.tensor_tensor(out=kf[par][:, :], in0=p3[:, :], in1=dec[par][:, :], op=ALU.mult))
            if blk == 31:
                p.op("dve", [kf[par]], [kf[par]], lambda e: e.memset(kf[par][64:128, 511:512], 0.0))
            p.op("dve", [kf[par]], [npart], lambda e: e.tensor_reduce(out=npart[:, blk:blk + 1], in_=kf[par][:, :], axis=AX.X, op=ALU.add, apply_absolute_value=True))
            p.op("pool", [kf[par]], [kb[par]], lambda e: e.tensor_copy(out=kb[par][:, :], in_=kf[par][:, :]))
            p.dma("pool", kx[:, bsl], kb[par][:, :], [kb[par]], [kxD])
        nsum = p.sbuf([128, 1], F32, "nsum"); prn = p.sbuf([128, 128], F32, "prn")
        p.op("dve", [npart], [nsum], lambda e: e.tensor_reduce(out=nsum[:, 0:1], in_=npart[:, :], axis=AX.X, op=ALU.add))
        p.op("dve", [prs, nsum], [prn], lambda e: e.tensor_scalar(out=prn[:, :], in0=prs[:, :], scalar1=nsum[:, 0:1], scalar2=None, op0=ALU.mult))
        pn = c["ps"][0]
        p.op("pe", [c["ones"], prn], [pn], lambda e: e.matmul(pn[:, 0:128], lhsT=c["ones"][:, :], rhs=prn[:, :], start=True, stop=True))
        p.op("dve", [pn], [rn], lambda e: e.reciprocal(out=rn[:, :], in_=pn[:, 0:128]))
        scC.__exit__(None, None, None)
        strip = [p.sbuf([128, S], BF16, f"strip{i}") for i in range(2)]
        yst = [p.sbuf([128, 8, 128], F32, f"yst{i}") for i in range(2)]
        for row in range(NROW):
            st = strip[row % 2]
            src = bass.AP(tensor=kx, offset=row * (S + 128), ap=[[1, 128], [1, S]])
            p.dma("sp" if row % 2 == 0 else "pool", st[:, :], src, [kxD], [st])
            py = c["ps"][1 + row % 2]
            for d in range(128):
                base = 16256 - 128 * d
                p.op("pe", [st, Zall], [py], lambda e: e.matmul(py[:, d:128], lhsT=st[:, base:base + 128], rhs=Zall[:, row, 0:128 - d], start=(d == 0), stop=(d == 127)), same_ok=True)
            ys = yst[(row // 8) % 2]
            p.op("act", [py, rn], [ys], lambda e: e.activation(out=ys[:, row % 8, :], in_=py[:, 0:128], func=AF.Copy, scale=rn[:, row:row + 1]))
            if row % 8 == 7:
                p.dma("act", Y[:, row - 7:row + 1, :], ys[:, :, :], [ys], [oD])
        print("k2d ninst", p.ninst)
    return None

def hy_consts():
    L = S
    t = np.linspace(0.0, 1.0, L, dtype=np.float32)[:, None]
    bands = 16
    freqs = np.linspace(1e-4, bands - 1, bands, dtype=np.float32)[None]
    w = (np.float32(2.0 * math.pi) * np.arange(L, dtype=np.float32)[:, None] / np.float32(L)).astype(np.float32)
    z = np.concatenate([t, np.cos(freqs * w), -np.sin(freqs * w)], axis=-1).astype(np.float32)
    min_decay = math.log(1e-2) / 1.5; max_decay = math.log(1e-2) / 0.3
    deltas = np.abs(np.linspace(min_decay, max_decay, 512, dtype=np.float32))
    return z, t[:, 0], deltas

def host_inputs_k2d(zT, inp, layer):
    DU = 11040 - 4096 - 1536
    z, t, deltas = hy_consts()
    embT = np.ascontiguousarray(z[::-1].T); tvec = np.ascontiguousarray(t[::-1][None, :])
    pair = (np.arange(128)[:, None] % 64 == np.arange(128)[None, :] % 64).astype(np.float32)
    cw = inp["hy_conv_w"][layer]; cb = inp["hy_conv_b"][layer]
    def cwpack(cols, rev):
        a = cw[:, cols].T
        if rev: a = a[:, ::-1]
        return np.ascontiguousarray(np.concatenate([a, cb[cols][:, None]], 1).astype(np.float32))
    maps = []
    for r in range(8):
        ch = np.arange(64 * r, 64 * r + 64)
        x0 = zT[DU + ch]; x1 = zT[DU + 512 + ch]; v = zT[DU + 1024 + ch]
        m = {"ux1": np.ascontiguousarray(np.concatenate([x1, x1[:, ::-1]], 0)), "uv": np.ascontiguousarray(np.concatenate([v, v[:, ::-1]], 0)), "ux0": np.ascontiguousarray(x0),
             "cw1": np.concatenate([cwpack(512 + ch, False), cwpack(512 + ch, True)], 0), "cwv": np.concatenate([cwpack(1024 + ch, False), cwpack(1024 + ch, True)], 0), "cw0": cwpack(ch, False),
             "embT": embT, "w1": np.ascontiguousarray(inp["hy_w1"][layer]), "w2": np.ascontiguousarray(inp["hy_w2"][layer]),
             "w3": np.ascontiguousarray(np.concatenate([inp["hy_w3"][layer][:, ch], inp["hy_w3"][layer][:, 512 + ch]], 1)),
             "bf1": np.ascontiguousarray(np.stack([inp["hy_b1"][layer], inp["hy_freq1"][layer]], 1)), "bf2": np.ascontiguousarray(np.stack([inp["hy_b2"][layer], inp["hy_freq2"][layer]], 1)),
             "dlrow": np.ascontiguousarray(-np.concatenate([deltas[ch], deltas[ch]])[None, :].astype(np.float32)), "tvec": tvec, "pair": pair}
        maps.append(m)
    return maps

def host_post_k2d(results):
    ys, zs, xs = [], [], []
    for r in results:
        Yr = r["Y"]
        y = Yr[::-1].transpose(1, 2, 0).reshape(128, S)
        ys.append(y[:64] + y[64:, ::-1]); zs.append(r["zz"]); xs.append(r["x0c"])
    return np.concatenate(ys, 0), np.concatenate(zs, 0), np.concatenate(xs, 0)


S = 16384; T = 2048; D = 1024; NIN = 11040; DFF = 4096

def linear2(p, c, w_ap, K, N, rhs_fn, ntb, evac, wst, wbf, gw, psi=2, npsi=4):
    kc = K // 128
    wD = Buf(None)
    ng = (N + gw - 1) // gw
    k2 = 0
    for gi in range(ng):
        c0 = gi * gw; wg = min(gw, N - c0)
        st = wst[gi % len(wst)]; wb = wbf[gi % len(wbf)]
        p.dma("sp", st[:, :kc, :wg], w_ap[:, c0:c0 + wg].rearrange("(c p) n -> p c n", p=128), [wD], [st])
        h = kc // 2
        p.op("pool", [st], [wb], lambda e: e.tensor_copy(out=wb[:, :h, :wg], in_=st[:, :h, :wg]))
        p.op("dve", [st], [wb], lambda e: e.tensor_copy(out=wb[:, h:kc, :wg], in_=st[:, h:kc, :wg]))
        for mi in range((wg + 127) // 128):
            mw = min(128, wg - mi * 128)
            for tb in range(ntb):
                ps = c["ps"][psi + k2 % npsi]; k2 += 1
                for ch in range(kc):
                    rb, rap = rhs_fn(ch, tb)
                    p.op("pe", [wb, rb], [ps], lambda e: e.matmul(ps[:mw, :], lhsT=wb[:, ch, mi * 128:mi * 128 + mw], rhs=rap, start=(ch == 0), stop=(ch == kc - 1)), same_ok=True)
                evac(c0 + mi * 128, mw, tb, ps)

def build_k3(lam_init, last):
    nc = bass.Bass("TRN2", target_bir_lowering=False)
    I = lambda n, s: nc.dram_tensor(n, s, F32, kind="ExternalInput")
    xT = I("xT", [D, T])
    of = I("of", [512, T]); ob = I("ob", [512, T]); rT = I("rT", [512, T]); gla_g = I("gla_g", [128, 1])
    dN = I("dN", [3, 256, T]); dL = I("dL", [3, 256, T])
    oc0 = I("oc0", [512, T]); oc1 = I("oc1", [512, T]); lamb = I("lamb", [128, 256]); sub_g = I("sub_g", [128, 1])
    yconv = I("yconv", [512, T]); zzT = I("zzT", [512, T]); x0c = I("x0c", [512, T]); skip = I("skip", [128, 4])
    gates = I("gates", [4096, T])
    projs = I("projs", [1792, D]); w_out = I("w_out", [D, D]); g2 = I("g2", [128, 8]); w1 = I("w1", [D, DFF]); w2 = I("w2", [DFF, D])
    xo = nc.dram_tensor("xo", [D, T], F32, kind="ExternalOutput")
    if not last:
        g1n = I("g1n", [128, 8]); w_in = I("w_in", [D, NIN])
        zT = nc.dram_tensor("zT", [NIN, T], F32, kind="ExternalOutput")
    with ExitStack() as es:
        p = Prog(nc, es)
        c = consts(p)
        iD = Buf(None); oD = Buf(None)
        def ld(name, shape, src, q="pool"):
            b = p.sbuf(shape, F32, name); p.dma(q, b[tuple(slice(None) for _ in shape)], src, [iD], [b]); return b
        glag = ld("glag", [128, 1], gla_g[:, :]); subg = ld("subg", [128, 1], sub_g[:, :]); lams = ld("lams", [128, 256], lamb[:, :]); skp = ld("skp", [128, 4], skip[:, :])
        g2s = ld("g2s", [128, 8], g2[:, :])
        if not last: g1s = ld("g1s", [128, 8], g1n[:, :])
        tmp = [p.sbuf([128, 512], F32, f"tmp{i}") for i in range(2)]
        rstds = [p.sbuf([128, 512], F32, f"rstd{i}") for i in range(4)]
        mb = [p.sbuf([128, T], BF16, f"mb{i}") for i in range(8)]
        lcol = p.sbuf([128, 4], F32, "lcol"); lpr = p.sbuf([128, 128], F32, "lpr")
        p.op("dve", [lams], [lpr], lambda e: e.tensor_tensor(out=lpr[:, 0:64], in0=lams[:, 0:64], in1=lams[:, 64:128], op=ALU.mult))
        p.op("dve", [lams], [lpr], lambda e: e.tensor_tensor(out=lpr[:, 64:128], in0=lams[:, 128:192], in1=lams[:, 192:256], op=ALU.mult))
        p.op("dve", [lpr], [lcol], lambda e: e.tensor_reduce(out=lcol[:, 0:1], in_=lpr[:, 0:64], axis=AX.X, op=ALU.add))
        p.op("dve", [lpr], [lcol], lambda e: e.tensor_reduce(out=lcol[:, 1:2], in_=lpr[:, 64:128], axis=AX.X, op=ALU.add))
        p.op("act", [lcol], [lcol], lambda e: e.activation(out=lcol[:, 0:2], in_=lcol[:, 0:2], func=AF.Exp))
        p.op("dve", [lcol], [lcol], lambda e: e.tensor_tensor(out=lcol[:, 2:3], in0=lcol[:, 0:1], in1=lcol[:, 1:2], op=ALU.subtract))
        p.op("dve", [lcol], [lcol], lambda e: e.tensor_scalar(out=lcol[:, 3:4], in0=lcol[:, 2:3], scalar1=float(lam_init), scalar2=-1.0, op0=ALU.add, op1=ALU.mult))
        sc1 = p.scope(); sc1.__enter__()
        ybf = [p.sbuf([128, T], BF16, f"ybf{i}") for i in range(14)]
        A = [p.sbuf([128, T], F32, f"A{i}") for i in range(2)]; B = [p.sbuf([128, T], F32, f"B{i}") for i in range(2)]; Cc = [p.sbuf([128, T], F32, f"C{i}") for i in range(2)]
        def rms128(src, gcol, dst, extra_scale=None):
            for tb in range(T // 512):
                sl = slice(tb * 512, (tb + 1) * 512); s = tmp[tb % 2]; r = rstds[tb % 4]; ps = c["ps"][tb % 2]
                p.op("act", [src], [s], lambda e: e.activation(out=s[:, :], in_=src[:, sl], func=AF.Square))
                p.op("pe", [c["ones"], s], [ps], lambda e: e.matmul(ps[:, :], lhsT=c["ones"][:, :], rhs=s[:, :], start=True, stop=True))
                p.op("act", [ps, c["eps"]], [r], lambda e: e.activation(out=r[:, :], in_=ps[:, :], func=AF.Sqrt, bias=c["eps"][:, 0:1], scale=1.0 / 128))
                p.op("dve", [r], [r], lambda e: e.reciprocal(out=r[:, :], in_=r[:, :]))
                p.op("dve", [src, r, gcol], [dst], lambda e: e.scalar_tensor_tensor(out=dst[:, sl], in0=src[:, sl], scalar=gcol[:, 0:1], in1=r[:, :], op0=ALU.mult, op1=ALU.mult))
        for h in range(4):
            hs = slice(h * 128, (h + 1) * 128); a = A[h % 2]; b = B[h % 2]; cc = Cc[h % 2]
            p.dma("sp", a[:, :], of[hs, :], [iD], [a]); p.dma("sp", b[:, :], ob[hs, :], [iD], [b]); p.dma("sp", cc[:, :], rT[hs, :], [iD], [cc])
            p.op("pool", [a, b], [a], lambda e: e.tensor_tensor(out=a[:, :], in0=a[:, :], in1=b[:, :], op=ALU.add))
            rms128(a, glag, b)
            p.op("act", [cc], [cc], lambda e: e.activation(out=cc[:, :], in_=cc[:, :], func=AF.Silu))
            p.op("dve", [b, cc], [ybf[h]], lambda e: e.tensor_tensor(out=ybf[h][:, :], in0=b[:, :], in1=cc[:, :], op=ALU.mult))
        for h in range(4):
            hs = slice(h * 128, (h + 1) * 128); a = A[h % 2]; b = B[h % 2]; cc = Cc[h % 2]
            p.dma("sp", a[:, :], oc0[hs, :], [iD], [a]); p.dma("sp", b[:, :], oc1[hs, :], [iD], [b])
            p.op("dve", [a, b, lcol], [a], lambda e: e.scalar_tensor_tensor(out=a[:, :], in0=b[:, :], scalar=lcol[:, 3:4], in1=a[:, :], op0=ALU.mult, op1=ALU.add))
            rms128(a, subg, cc)
            p.op("pool", [cc], [ybf[6 + h]], lambda e: e.tensor_scalar(out=ybf[6 + h][:, :], in0=cc[:, :], scalar1=float(1.0 - lam_init), scalar2=None, op0=ALU.mult))
        for h in range(4):
            hs = slice(h * 128, (h + 1) * 128); a = A[h % 2]; b = B[h % 2]; cc = Cc[h % 2]
            p.dma("sp", a[:, :], yconv[hs, :], [iD], [a]); p.dma("sp", b[:, :], zzT[hs, :], [iD], [b]); p.dma("sp", cc[:, :], x0c[hs, :], [iD], [cc])
            p.op("dve", [a, b, skp], [a], lambda e: e.scalar_tensor_tensor(out=a[:, :], in0=b[:, :], scalar=skp[:, h:h + 1], in1=a[:, :], op0=ALU.mult, op1=ALU.add))
            p.op("pool", [a, cc], [ybf[10 + h]], lambda e: e.tensor_tensor(out=ybf[10 + h][:, :], in0=a[:, :], in1=cc[:, :], op=ALU.mult))
        for hh in range(2):
            hs = slice(hh * 128, (hh + 1) * 128); a = A[hh % 2]; b = B[hh % 2]; cc = Cc[hh % 2]
            p.dma("sp", a[:, :], dN[0, hs, :], [iD], [a]); p.dma("sp", b[:, :], dL[0, hs, :], [iD], [b])
            for g in (1, 2):
                p.dma("sp", cc[:, :], dN[g, hs, :], [iD], [cc])
                p.op("dve", [a, cc], [a], lambda e: e.tensor_tensor(out=a[:, :], in0=a[:, :], in1=cc[:, :], op=ALU.add))
                p.dma("sp", cc[:, :], dL[g, hs, :], [iD], [cc])
                p.op("dve", [b, cc], [b], lambda e: e.tensor_tensor(out=b[:, :], in0=b[:, :], in1=cc[:, :], op=ALU.add))
            p.op("dve", [b], [b], lambda e: e.reciprocal(out=b[:, :], in_=b[:, :]))
            p.op("dve", [a, b], [ybf[4 + hh]], lambda e: e.tensor_tensor(out=ybf[4 + hh][:, :], in0=a[:, :], in1=b[:, :], op=ALU.mult))
        pst = [p.sbuf([128, D], F32, f"pst{i}") for i in range(2)]; pbf = p.sbuf([128, 14, D], BF16, "pbf")
        for kc_ in range(14):
            p.dma("sp", pst[kc_ % 2][:, :], projs[kc_ * 128:(kc_ + 1) * 128, :], [iD], [pst[kc_ % 2]])
            p.op("pool", [pst[kc_ % 2]], [pbf], lambda e: e.tensor_copy(out=pbf[:, kc_, :], in_=pst[kc_ % 2][:, :]))
        branches = [(0, 4), (4, 6), (6, 10), (10, 14)]
        gst = A + B
        macc = Cc
        k = 0; kk = 0
        for mi in range(8):
            ms = slice(mi * 128, (mi + 1) * 128); ma = macc[mi % 2]
            for bi, (k0, k1) in enumerate(branches):
                gt = gst[k % 4]; k += 1
                p.dma("sp", gt[:, :], gates[bi * 1024 + mi * 128:bi * 1024 + (mi + 1) * 128, :], [iD], [gt])
                p.op("act", [gt], [gt], lambda e: e.activation(out=gt[:, :], in_=gt[:, :], func=AF.Sigmoid))
                for tb in range(4):
                    sl = slice(tb * 512, (tb + 1) * 512); ps = c["ps"][2 + kk % 4]; kk += 1
                    for kc_ in range(k0, k1):
                        p.op("pe", [pbf, ybf[kc_]], [ps], lambda e: e.matmul(ps[:, :], lhsT=pbf[:, kc_, ms], rhs=ybf[kc_][:, sl], start=(kc_ == k0), stop=(kc_ == k1 - 1)), same_ok=True)
                    if bi == 0:
                        p.op("dve", [ps, gt], [ma], lambda e: e.tensor_tensor(out=ma[:, sl], in0=ps[:, :], in1=gt[:, sl], op=ALU.mult))
                    else:
                        t = tmp[kk % 2]
                        p.op("dve", [ps, gt], [t], lambda e: e.tensor_tensor(out=t[:, :], in0=ps[:, :], in1=gt[:, sl], op=ALU.mult))
                        p.op("pool", [t, ma], [ma], lambda e: e.tensor_tensor(out=ma[:, sl], in0=ma[:, sl], in1=t[:, :], op=ALU.add))
            p.op("act", [ma], [mb[mi]], lambda e: e.copy(out=mb[mi][:, :], in_=ma[:, :]))
        sc1.__exit__(None, None, None)
        xs = [p.sbuf([128, T], F32, f"x{i}") for i in range(8)]
        for i in range(8):
            p.dma("pool", xs[i][:, :], xT[i * 128:(i + 1) * 128, :], [iD], [xs[i]])
        GW = 256
        wst = [p.sbuf([128, 8, GW], F32, f"wst{i}") for i in range(2)]; wbf = [p.sbuf([128, 8, GW], BF16, f"wbf{i}") for i in range(2)]
        def evac_addx(m0, mw, tb, ps):
            ci = m0 // 128; sl = slice(tb * 512, (tb + 1) * 512)
            p.op("dve", [ps, xs[ci]], [xs[ci]], lambda e: e.tensor_tensor(out=xs[ci][:, sl], in0=xs[ci][:, sl], in1=ps[:, :], op=ALU.add))
        linear2(p, c, w_out, D, D, lambda ch, tb: (mb[ch], mb[ch][:, tb * 512:(tb + 1) * 512]), 4, evac_addx, wst, wbf, GW)
        hs_ = mb
        rms_fm(p, c, xs, g2s, hs_, 8, T, tmp, rstds)
        sc2 = p.scope(); sc2.__enter__()
        ub = [p.sbuf([128, 512], BF16, f"ub{i}") for i in range(32)]
        w2st = [p.sbuf([128, 32, 128], F32, "w2st0")]; w2bf = [p.sbuf([128, 32, 128], BF16, f"w2bf{i}") for i in range(2)]
        for tb in range(4):
            sl = slice(tb * 512, (tb + 1) * 512)
            def evac_u(m0, mw, tb_, ps):
                mi = m0 // 128; t = tmp[mi % 2]
                p.op("act", [ps], [t], lambda e: e.activation(out=t[:, :], in_=ps[:, :], func=AF.Relu))
                p.op("pool", [t], [ub[mi]], lambda e: e.tensor_tensor(out=ub[mi][:, :], in0=t[:, :], in1=t[:, :], op=ALU.mult))
            linear2(p, c, w1, D, DFF, lambda ch, tb_: (hs_[ch], hs_[ch][:, sl]), 1, evac_u, wst, wbf, GW)
            def evac_x2(m0, mw, tb_, ps):
                ci = m0 // 128
                p.op("dve", [ps, xs[ci]], [xs[ci]], lambda e: e.tensor_tensor(out=xs[ci][:mw, sl], in0=xs[ci][:mw, sl], in1=ps[:mw, :], op=ALU.add))
            linear2(p, c, w2, DFF, D, lambda ch, tb_: (ub[ch], ub[ch][:, :]), 1, evac_x2, w2st, w2bf, 128)
        sc2.__exit__(None, None, None)
        for i in range(8):
            p.dma("act", xo[i * 128:(i + 1) * 128, :], xs[i][:, :], [xs[i]], [oD])
        if not last:
            rms_fm(p, c, xs, g1s, hs_, 8, T, tmp, rstds)
            ost = [p.sbuf([128, T], F32, f"ost{i}") for i in range(2)]
            state = {"k": 0}
            def evac_z(m0, mw, tb, ps):
                o = ost[state["k"] % 2]
                p.op("act", [ps], [o], lambda e: e.copy(out=o[:mw, tb * 512:(tb + 1) * 512], in_=ps[:mw, :]))
                if tb == 3:
                    p.dma("act", zT[m0:m0 + mw, :], o[:mw, :], [o], [oD]); state["k"] += 1
            linear2(p, c, w_in, D, NIN, lambda ch, tb: (hs_[ch], hs_[ch][:, tb * 512:(tb + 1) * 512]), 4, evac_z, wst, wbf, GW)
        p.finish([oD])
        print("k3 ninst", p.ninst)
    return nc


def build_B():
    nc = bass.Bass("TRN2", target_bir_lowering=False)
    with ExitStack() as es:
        p = Prog(nc, es)
        c = consts(p)
        for pre, emit in (("a_", emit_k2a), ("d_", emit_k2d), ("b_", emit_k2b), ("c_", emit_k2c)):
            p.prefix = pre
            with p.scope():
                emit(nc, p, c, pre)
        p.prefix = ""
        p.finish([])
        print("B ninst", p.ninst)
    return nc


import math as _math
_NC_CACHE = {}
def _get_nc(key, builder):
    if key not in _NC_CACHE:
        _NC_CACHE[key] = builder()
    return _NC_CACHE[key]

def _run(nc, maps):
    return run_bass_kernel_spmd(nc, maps, core_ids=list(range(8))).results

def kernel(**inputs):
    inp = {k: np.asarray(v) for k, v in inputs.items()}
    x = inp["x"][0].astype(np.float32)
    NCORE = 8; Tn = S // NCORE
    xT = np.ascontiguousarray(x.T)
    g = np.ascontiguousarray(inp["norm1_g"][0].reshape(8, 128).T)
    maps = [{"xT": np.ascontiguousarray(xT[:, r * Tn:(r + 1) * Tn]), "g": g, "w": np.ascontiguousarray(inp["w_in"][0])} for r in range(NCORE)]
    res = _run(_get_nc("k1", build_k1), maps)
    zT = np.concatenate([r["zT"] for r in res], axis=1)
    for layer in range(4):
        last = layer == 3
        lam_init = 0.8 - 0.6 * _math.exp(-0.3 * layer)
        ma = host_inputs_k2a(zT, inp, layer); mb_ = host_inputs_k2b(zT, inp, layer); mc = host_inputs_k2c(zT, inp, layer); md = host_inputs_k2d(zT, inp, layer)
        mapsB = []
        for r in range(NCORE):
            m = {}
            for pre, mm in (("a_", ma), ("b_", mb_), ("c_", mc), ("d_", md)):
                m.update({pre + k: v for k, v in mm[r].items()})
            mapsB.append(m)
        rB = _run(_get_nc("B", build_B), mapsB)
        ra = [{"oT": r["a_oT"]} for r in rB]; rb = [{"O": r["b_O"]} for r in rB]; rc = [{"oT": r["c_oT"]} for r in rB]
        rd = [{"Y": r["d_Y"], "zz": r["d_zz"], "x0c": r["d_x0c"]} for r in rB]
        of = np.concatenate([ra[2 * h]["oT"] for h in range(4)], 0); ob = np.concatenate([ra[2 * h + 1]["oT"][:, ::-1] for h in range(4)], 0)
        dN, dL = host_post_k2b(rb)
        oc0 = np.concatenate([rc[2 * h]["oT"] for h in range(4)], 0); oc1 = np.concatenate([rc[2 * h + 1]["oT"] for h in range(4)], 0)
        yconv, zzf, x0c = host_post_k2d(rd)
        projs = np.ascontiguousarray(np.concatenate([inp["proj_a"][layer], inp["proj_b"][layer], inp["proj_c"][layer], inp["proj_d"][layer]], 0))
        common = {"gla_g": np.ascontiguousarray(inp["gla_norm_g"][layer].reshape(128, 1)),
                  "lamb": np.ascontiguousarray(np.broadcast_to(inp["diff_lambda"][layer].reshape(1, 256), (128, 256))),
                  "sub_g": np.ascontiguousarray(inp["diff_subln_g"][layer].reshape(128, 1)),
                  "skip": np.ascontiguousarray(inp["hy_skip"][layer].reshape(4, 128).T),
                  "projs": projs, "w_out": np.ascontiguousarray(inp["w_out"][layer]),
                  "g2": np.ascontiguousarray(inp["norm2_g"][layer].reshape(8, 128).T),
                  "w1": np.ascontiguousarray(inp["mlp_w1"][layer]), "w2": np.ascontiguousarray(inp["mlp_w2"][layer])}
        if not last:
            common["g1n"] = np.ascontiguousarray(inp["norm1_g"][layer + 1].reshape(8, 128).T)
            common["w_in"] = np.ascontiguousarray(inp["w_in"][layer + 1])
        maps = []
        for r in range(NCORE):
            cs = slice(r * Tn, (r + 1) * Tn)
            cc = lambda a: np.ascontiguousarray(a[..., cs])
            m = dict(common)
            m.update({"xT": cc(xT), "of": cc(of), "ob": cc(ob), "rT": cc(zT[1024:1536]), "dN": cc(dN), "dL": cc(dL),
                      "oc0": cc(oc0), "oc1": cc(oc1), "yconv": cc(yconv), "zzT": cc(zzf), "x0c": cc(x0c), "gates": cc(zT[6944:11040])})
            maps.append(m)
        res = _run(_get_nc(("k3", layer), lambda: build_k3(lam_init, last)), maps)
        xT = np.concatenate([r["xo"] for r in res], axis=1)
        if not last:
            zT = np.concatenate([r["zT"] for r in res], axis=1)
    return np.ascontiguousarray(xT.T)[None].astype(np.float32)
```

```python
import numpy as np
from contextlib import ExitStack
import concourse.bass as bass
import concourse.mybir as mybir
from concourse.bass_utils import run_bass_kernel_spmd

F32 = mybir.dt.float32
BF16 = mybir.dt.bfloat16
AF = mybir.ActivationFunctionType
ALU = mybir.AluOpType
AX = mybir.AxisListType

class Buf:
    __slots__ = ("t", "w", "r", "name")
    def __init__(self, t, name=""):
        self.t = t; self.w = None; self.r = {}; self.name = name
    def __getitem__(self, idx):
        return self.t[idx]

class Prog:
    NDSEM = 6
    def __init__(self, nc, es):
        self.nc = nc; self.es = es
        self.eng = {"pe": nc.tensor, "act": nc.scalar, "dve": nc.vector, "pool": nc.gpsimd, "sp": nc.sync}
        self.sem = {}; self.cnt = {}
        for e in self.eng:
            self.sem[e] = es.enter_context(nc.semaphore("s_" + e)); self.cnt[e] = 0
        self.seen = {e: {} for e in self.eng}
        self.dq = {}
        for q in ("sp", "pool", "act"):
            self.dq[q] = dict(n=0, sems=[es.enter_context(nc.semaphore(f"d_{q}{i}")) for i in range(self.NDSEM)])
            for i in range(self.NDSEM):
                self.sem[(q, i)] = self.dq[q]["sems"][i]
        self.ninst = 0
        self.prefix = ""
    def sbuf(self, shape, dt, name):
        return Buf(self.es.enter_context(self.nc.sbuf_tensor(self.prefix + name, list(shape), dt)), name)
    def psum(self, shape, dt, name):
        return Buf(self.es.enter_context(self.nc.psum_tensor(name, list(shape), dt)), name)
    def dram(self, name, shape, dt, kind="Internal"):
        return Buf(self.nc.dram_tensor(name, list(shape), dt, kind=kind), name)
    def _wait(self, e, tok):
        if tok is None: return
        s, v = tok
        if self.seen[e].get(s, 0) >= v: return
        self.eng[e].wait_ge(self.sem[s], v)
        self.seen[e][s] = v
    def _deps(self, e, reads, writes, same_ok=False):
        for b in reads:
            if b.w is not None and not (same_ok and b.w[0] == e):
                self._wait(e, b.w)
        for b in writes:
            if b.w is not None and not (same_ok and b.w[0] == e):
                self._wait(e, b.w)
            for s, v in b.r.items():
                if same_ok and s == e: continue
                self._wait(e, (s, v))
    def _mark(self, tok, reads, writes):
        for b in reads:
            if b.r.get(tok[0], 0) < tok[1]: b.r[tok[0]] = tok[1]
        for b in writes:
            b.w = tok; b.r = {}
    def op(self, e, reads, writes, fn, same_ok=False):
        self._deps(e, reads, writes, same_ok)
        ins = fn(self.eng[e])
        self.cnt[e] += 1
        ins.then_inc(self.sem[e], 1)
        tok = (e, self.cnt[e])
        self.seen[e][e] = max(self.seen[e].get(e, 0), 0)
        self._mark(tok, reads, writes)
        self.ninst += 1
        return tok
    def dma(self, q, out_ap, in_ap, reads, writes, **kw):
        d = self.dq[q]; j = d["n"]; slot = j % self.NDSEM; val = 16 * (j // self.NDSEM + 1)
        if j >= self.NDSEM:
            self._wait(q, ((q, slot), val - 16))
        self._deps(q, reads, writes)
        ins = self.eng[q].dma_start(out=out_ap, in_=in_ap, **kw)
        ins.then_inc(d["sems"][slot], 16)
        d["n"] += 1
        tok = ((q, slot), val)
        self._mark(tok, reads, writes)
        self.ninst += 1
        return tok
    def finish(self, bufs):
        for b in bufs:
            self._wait("sp", b.w)
        for q in self.dq:
            d = self.dq[q]
            for j in range(max(0, d["n"] - self.NDSEM), d["n"]):
                self._wait("sp", ((q, j % self.NDSEM), 16 * (j // self.NDSEM + 1)))

def _barrier(self):
    for e in self.eng:
        for e2 in self.eng:
            if e2 != e and self.cnt[e2] > 0:
                self._wait(e, (e2, self.cnt[e2]))
        for q, d in self.dq.items():
            for j in range(max(0, d["n"] - self.NDSEM), d["n"]):
                self._wait(e, ((q, j % self.NDSEM), 16 * (j // self.NDSEM + 1)))
Prog.barrier = _barrier

class _Scope:
    def __init__(self, p): self.p = p
    def __enter__(self):
        self.old = self.p.es; self.es = ExitStack(); self.es.__enter__(); self.p.es = self.es; return self
    def __exit__(self, *a):
        self.p.barrier(); self.p.es = self.old; return self.es.__exit__(*a)
Prog.scope = lambda self: _Scope(self)

S = 16384; NC = 8; T = S // NC; D = 1024; NIN = 11040
EPS = 1e-6

def consts(p):
    c = {}
    c["ones"] = p.sbuf([128, 128], F32, "c_ones")
    p.op("pool", [], [c["ones"]], lambda e: e.memset(c["ones"][:, :], 1.0))
    c["eps"] = p.sbuf([128, 1], F32, "c_eps")
    p.op("pool", [], [c["eps"]], lambda e: e.memset(c["eps"][:, :], EPS))
    c["ps"] = [p.psum([128, 512], F32, f"ps{i}") for i in range(8)]
    return c

def rms_fm(p, c, xs, gs, hs, nch, Tn, tmp, rstds, psi=0):
    nfeat = nch * 128
    k = 0
    for tb in range(Tn // 512):
        sl = slice(tb * 512, (tb + 1) * 512)
        ps = c["ps"][psi + tb % 2]
        for ch in range(nch):
            s = tmp[k % 2]; k += 1
            if ch % 2 == 0:
                p.op("act", [xs[ch]], [s], lambda e: e.activation(out=s[:, :], in_=xs[ch][:, sl], func=AF.Square))
            else:
                p.op("dve", [xs[ch]], [s], lambda e: e.tensor_tensor(out=s[:, :], in0=xs[ch][:, sl], in1=xs[ch][:, sl], op=ALU.mult))
            p.op("pe", [c["ones"], s], [ps], lambda e: e.matmul(ps[:, :], lhsT=c["ones"][:, :], rhs=s[:, :], start=(ch == 0), stop=(ch == nch - 1)), same_ok=True)
        r = rstds[tb]
        p.op("act", [ps, c["eps"]], [r], lambda e: e.activation(out=r[:, :], in_=ps[:, :], func=AF.Sqrt, bias=c["eps"][:, 0:1], scale=1.0 / nfeat))
        p.op("dve", [r], [r], lambda e: e.reciprocal(out=r[:, :], in_=r[:, :]))
        for ch in range(nch):
            eng = "dve"
            p.op(eng, [xs[ch], r, gs], [hs[ch]], lambda e: e.scalar_tensor_tensor(out=hs[ch][:, sl], in0=xs[ch][:, sl], scalar=gs[:, ch:ch + 1], in1=r[:, :], op0=ALU.mult, op1=ALU.mult))

def linear_stream(p, c, w_ap, K, N, hs, Tn, evac, wst, wbf, psi=2, npsi=4, gw=512):
    kc = K // 128
    wD = Buf(None)
    ng = (N + gw - 1) // gw
    k2 = 0
    for gi in range(ng):
        c0 = gi * gw; wg = min(gw, N - c0)
        st = wst[gi % 2]; wb = wbf[gi % 2]
        p.dma("sp", st[:, :kc, :wg], w_ap[:, c0:c0 + wg].rearrange("(c p) n -> p c n", p=128), [wD], [st])
        h = kc // 2
        p.op("pool", [st], [wb], lambda e: e.tensor_copy(out=wb[:, :h, :wg], in_=st[:, :h, :wg]))
        p.op("dve", [st], [wb], lambda e: e.tensor_copy(out=wb[:, h:kc, :wg], in_=st[:, h:kc, :wg]))
        for mi in range((wg + 127) // 128):
            mw = min(128, wg - mi * 128)
            for tb in range(Tn // 512):
                ps = c["ps"][psi + k2 % npsi]; k2 += 1
                for ch in range(kc):
                    p.op("pe", [wb, hs[ch]], [ps], lambda e: e.matmul(ps[:mw, :], lhsT=wb[:, ch, mi * 128:mi * 128 + mw], rhs=hs[ch][:, tb * 512:(tb + 1) * 512], start=(ch == 0), stop=(ch == kc - 1)), same_ok=True)
                evac(c0 + mi * 128, mw, tb, ps)

def build_k1():
    nc = bass.Bass("TRN2", target_bir_lowering=False)
    xT = nc.dram_tensor("xT", [D, T], F32, kind="ExternalInput")
    g = nc.dram_tensor("g", [128, 8], F32, kind="ExternalInput")
    w = nc.dram_tensor("w", [D, NIN], F32, kind="ExternalInput")
    zT = nc.dram_tensor("zT", [NIN, T], F32, kind="ExternalOutput")
    with ExitStack() as es:
        p = Prog(nc, es)
        c = consts(p)
        xs = [p.sbuf([128, T], F32, f"x{i}") for i in range(8)]
        hs = [p.sbuf([128, T], BF16, f"h{i}") for i in range(8)]
        gs = p.sbuf([128, 8], F32, "gs")
        xD = Buf(None); zD = Buf(None)
        p.dma("pool", gs[:, :], g[:, :], [xD], [gs])
        for i in range(8):
            p.dma("pool", xs[i][:, :], xT[i * 128:(i + 1) * 128, :], [xD], [xs[i]])
        tmp = [p.sbuf([128, 512], F32, f"tmp{i}") for i in range(2)]
        rstds = [p.sbuf([128, 512], F32, f"rstd{i}") for i in range(T // 512)]
        rms_fm(p, c, xs, gs, hs, 8, T, tmp, rstds)
        wst = [p.sbuf([128, 8, 512], F32, f"wst{i}") for i in range(2)]
        wbf = [p.sbuf([128, 8, 512], BF16, f"wbf{i}") for i in range(2)]
        ost = [p.sbuf([128, T], F32, f"ost{i}") for i in range(2)]
        state = {"k": 0}
        def evac(m0, mw, tb, ps):
            o = ost[state["k"] % 2]
            p.op("act", [ps], [o], lambda e: e.copy(out=o[:mw, tb * 512:(tb + 1) * 512], in_=ps[:mw, :]))
            if tb == T // 512 - 1:
                p.dma("act", zT[m0:m0 + mw, :], o[:mw, :], [o], [zD])
                state["k"] += 1
        linear_stream(p, c, w.ap() if hasattr(w, "ap") else w, D, NIN, hs, T, evac, wst, wbf)
        p.finish([zD])
        print("k1 ninst", p.ninst)
    return nc


S = 16384
NEAR_LO, NEAR_HI = -5, 8
RW = 2304

def t5_bucket_np(rel):
    import math
    half = 16; max_exact = 8
    ret = np.where(rel > 0, half, 0)
    n = np.abs(rel)
    nf = np.maximum(n, 1).astype(np.float32)
    large = max_exact + (np.log(nf / max_exact) / np.float32(math.log(1024 / max_exact)) * (half - max_exact)).astype(np.int32)
    large = np.minimum(large, half - 1)
    return ret + np.where(n < max_exact, n, large)

def qknorm_fm(p, c, src, gcol, dst, rows, Tn, tmp, rtmp, psi):
    for tb in range(Tn // 512):
        sl = slice(tb * 512, (tb + 1) * 512)
        s = tmp[tb % 2]; r = rtmp[tb % 2]; ps = c["ps"][psi + tb % 2]
        p.op("act", [src], [s], lambda e: e.activation(out=s[:rows, :], in_=src[:rows, sl], func=AF.Square))
        p.op("pe", [c["ones"], s], [ps], lambda e: e.matmul(ps[:rows, :], lhsT=c["ones"][:rows, :rows], rhs=s[:rows, :], start=True, stop=True))
        p.op("act", [ps, c["eps"]], [r], lambda e: e.activation(out=r[:rows, :], in_=ps[:rows, :], func=AF.Sqrt, bias=c["eps"][:rows, 0:1], scale=1.0 / rows))
        p.op("dve", [r], [r], lambda e: e.reciprocal(out=r[:rows, :], in_=r[:rows, :]))
        p.op("dve", [src, r, gcol], [dst], lambda e: e.scalar_tensor_tensor(out=dst[:rows, sl], in0=src[:rows, sl], scalar=gcol[:rows, 0:1], in1=r[:rows, :], op0=ALU.mult, op1=ALU.mult))

def emit_k2c(nc, p, c, pre, NQB=S // 512, NKT=S // 128):
    qT = nc.dram_tensor(pre + "qT", [64, S], F32, kind="ExternalInput")
    kT = nc.dram_tensor(pre + "kT", [64, S], F32, kind="ExternalInput")
    v = nc.dram_tensor(pre + "v", [128, S // 128, 128], F32, kind="ExternalInput")
    gq = nc.dram_tensor(pre + "gq", [64, 1], F32, kind="ExternalInput")
    gk = nc.dram_tensor(pre + "gk", [64, 1], F32, kind="ExternalInput")
    Rb = nc.dram_tensor(pre + "Rb", [128, RW], F32, kind="ExternalInput")
    bfar = nc.dram_tensor(pre + "bfar", [128, 2], F32, kind="ExternalInput")
    oT = nc.dram_tensor(pre + "oT", [128, S], F32, kind="ExternalOutput")
    if True:
        iD = Buf(None); oD = Buf(None)
        qb16 = p.sbuf([128, S], BF16, "qb16"); kb16 = p.sbuf([128, S], BF16, "kb16")
        for b_ in (qb16, kb16):
            for cb_ in range(S // 2048):
                p.op("pool", [], [b_], lambda e: e.memset(b_[64:128, cb_ * 2048:(cb_ + 1) * 2048], 0.0))
        vb16 = p.sbuf([128, S // 128, 128], BF16, "vb16")
        rb = p.sbuf([128, RW], F32, "rb"); bf_ = p.sbuf([128, 2], F32, "bfar_s")
        gqs = p.sbuf([64, 1], F32, "gqs"); gks = p.sbuf([64, 1], F32, "gks")
        p.dma("pool", rb[:, :], Rb[:, :], [iD], [rb]); p.dma("pool", bf_[:, :], bfar[:, :], [iD], [bf_])
        p.dma("pool", gqs[:, :], gq[:, :], [iD], [gqs]); p.dma("pool", gks[:, :], gk[:, :], [iD], [gks])
        tmp = [p.sbuf([128, 512], F32, f"tmp{i}") for i in range(2)]
        rtmp = [p.sbuf([128, 512], F32, f"rtmp{i}") for i in range(2)]
        stg = [p.sbuf([128, 4096], F32, f"stg{i}") for i in range(2)]
        k = 0
        for (src, gcol, dst) in ((qT, gqs, qb16), (kT, gks, kb16)):
            for blk in range(S // 4096):
                st = stg[k % 2]; k += 1
                p.dma("sp", st[:64, :], src[:, blk * 4096:(blk + 1) * 4096], [iD], [st])
                class V:
                    pass
                for tb in range(8):
                    sl = slice(tb * 512, (tb + 1) * 512); dsl = slice(blk * 4096 + tb * 512, blk * 4096 + (tb + 1) * 512)
                    s = tmp[tb % 2]; r = rtmp[tb % 2]; ps = c["ps"][6 + tb % 2]
                    p.op("act", [st], [s], lambda e: e.activation(out=s[:64, :], in_=st[:64, sl], func=AF.Square))
                    p.op("pe", [c["ones"], s], [ps], lambda e: e.matmul(ps[:64, :], lhsT=c["ones"][:64, :64], rhs=s[:64, :], start=True, stop=True))
                    p.op("act", [ps, c["eps"]], [r], lambda e: e.activation(out=r[:64, :], in_=ps[:64, :], func=AF.Sqrt, bias=c["eps"][:64, 0:1], scale=1.0 / 64))
                    p.op("dve", [r], [r], lambda e: e.reciprocal(out=r[:64, :], in_=r[:64, :]))
                    p.op("dve", [st, r, gcol], [dst], lambda e: e.scalar_tensor_tensor(out=dst[:64, dsl], in0=st[:64, sl], scalar=gcol[:64, 0:1], in1=r[:64, :], op0=ALU.mult, op1=ALU.mult))
        for blk in range(S // 128 // 32):
            st = stg[k % 2]; k += 1
            p.dma("sp", st[:, :].rearrange("p (a b) -> p a b", b=128), v[:, blk * 32:(blk + 1) * 32, :], [iD], [st])
            p.op("pool", [st], [vb16], lambda e: e.tensor_copy(out=vb16[:, blk * 32:(blk + 1) * 32, :], in_=st[:, :].rearrange("p (a b) -> p a b", b=128)))
        pt = [p.sbuf([128, 512], BF16, f"pt{i}") for i in range(4)]
        nt = [p.sbuf([128, 512], F32, f"nt{i}") for i in range(2)]
        lacc = [p.sbuf([128, 512], F32, f"lacc{i}") for i in range(2)]; lacc2 = [p.sbuf([128, 512], F32, f"laccp{i}") for i in range(2)]
        rl = [p.sbuf([128, 512], F32, f"rl{i}") for i in range(2)]
        ob = [p.sbuf([128, 512], F32, f"ob{i}") for i in range(2)]
        LOOK = 2
        onesb = p.sbuf([128, 128], BF16, "onesb"); p.op("pool", [], [onesb], lambda e: e.memset(onesb[:, :], 1.0))
        tiles = [(qb, j) for qb in range(NQB) for j in range(NKT)]
        NTL = len(tiles)
        nn = 0
        def front(t):
            nonlocal nn
            qb, j = tiles[t]
            qsl = slice(qb * 512, (qb + 1) * 512)
            ps = c["ps"][t % 4]; P = pt[t % 4]
            p.op("pe", [kb16, qb16], [ps], lambda e: e.matmul(ps[:, :], lhsT=kb16[:, j * 128:(j + 1) * 128], rhs=qb16[:, qsl], start=True, stop=True))
            m = j - 4 * qb
            if NEAR_LO <= m <= NEAR_HI:
                tt_ = nt[nn % 2]; nn += 1
                off = 1024 - 128 * m
                p.op("dve", [ps, rb], [tt_], lambda e: e.scalar_tensor_tensor(out=tt_[:, :], in0=ps[:, :], scalar=0.125, in1=rb[:, off:off + 512], op0=ALU.mult, op1=ALU.add))
                p.op("act", [tt_], [P], lambda e: e.activation(out=P[:, :], in_=tt_[:, :], func=AF.Exp))
            else:
                side = 0 if m < 0 else 1
                p.op("act", [ps, bf_], [P], lambda e: e.activation(out=P[:, :], in_=ps[:, :], func=AF.Exp, bias=bf_[:, side:side + 1], scale=0.125))
        def back(t):
            qb, j = tiles[t]
            qsl = slice(qb * 512, (qb + 1) * 512)
            P = pt[t % 4]; po = c["ps"][4 + qb % 2]; la = lacc[qb % 2]
            p.op("pe", [vb16, P], [po], lambda e: e.matmul(po[:, :], lhsT=vb16[:, j, :], rhs=P[:, :], start=(j == 0), stop=(j == NKT - 1)), same_ok=True)
            pl = c["ps"][6 + qb % 2]; r_ = rl[qb % 2]
            if j % 2 == 0:
                p.op("pe", [onesb, P], [pl], lambda e: e.matmul(pl[:, :], lhsT=onesb[:, :], rhs=P[:, :], start=(j == 0), stop=False), same_ok=True)
            elif j == 1:
                p.op("dve", [P], [la], lambda e: e.tensor_copy(out=la[:, :], in_=P[:, :]))
            else:
                p.op("dve", [P, la], [la], lambda e: e.tensor_tensor(out=la[:, :], in0=la[:, :], in1=P[:, :], op=ALU.add))
            if j == NKT - 1:
                p.op("pe", [c["ones"], la], [pl], lambda e: e.matmul(pl[:, :], lhsT=c["ones"][:, :], rhs=la[:, :], start=False, stop=True), same_ok=True)
                p.op("dve", [pl], [r_], lambda e: e.reciprocal(out=r_[:, :], in_=pl[:, :]))
                o = ob[qb % 2]
                p.op("dve", [po, r_], [o], lambda e: e.tensor_tensor(out=o[:, :], in0=po[:, :], in1=r_[:, :], op=ALU.mult))
                p.dma("pool", oT[:, qsl], o[:, :], [o], [oD])
        for t in range(NTL + LOOK):
            if t < NTL: front(t)
            if t >= LOOK: back(t - LOOK)
        print("k2c ninst", p.ninst)
    return None

def host_inputs_k2c(zT, inp, layer):
    CQ = 256 + 256 + 512 + 512 + 32 + 768 * 3
    CK = CQ + 512; CV = CK + 512
    t5 = inp["t5_bias"]
    kk = np.arange(128)[:, None]; jj = np.arange(RW)[None, :]
    bidx = t5_bucket_np(kk - jj + 1024)
    assert (t5_bucket_np(np.arange(-20000, -5 * 128 + 128 - 1 - 127 + 1)) == 15).all()
    maps = []
    for h in range(4):
        vT = zT[CV + h * 128:CV + (h + 1) * 128, :]
        vv = np.ascontiguousarray(vT.T.reshape(S // 128, 128, 128).transpose(1, 0, 2))
        Rb = np.ascontiguousarray(t5[:, 12 + h][bidx]).astype(np.float32)
        bfar = np.ascontiguousarray(np.broadcast_to(np.array([t5[15, 12 + h], t5[31, 12 + h]], np.float32)[None, :], (128, 2)))
        for cc in range(2):
            r0 = h * 128 + cc * 64
            maps.append({"qT": np.ascontiguousarray(zT[CQ + r0:CQ + r0 + 64]), "kT": np.ascontiguousarray(zT[CK + r0:CK + r0 + 64]), "v": vv,
                         "gq": np.ascontiguousarray(inp["diff_qnorm_g"][layer].reshape(64, 1)), "gk": np.ascontiguousarray(inp["diff_knorm_g"][layer].reshape(64, 1)),
                         "Rb": Rb, "bfar": bfar})
    return maps


S = 16384
NT = S // 128

def emit_k2a(nc, p, c, pre, NTL=NT):
    I = lambda n, s: nc.dram_tensor(pre + n, s, F32, kind="ExternalInput")
    qT = I("qT", [64, S]); kT = I("kT", [64, S]); ktm = I("ktm", [128, NT, 64]); vtm = I("vtm", [128, NT, 128])
    lrT = I("lrT", [16, S]); gw = I("gw", [16, 64]); gbb = I("gbb", [128, 64])
    tri = I("tri", [128, 128]); m1 = I("m1", [128, 128]); cind = I("cind", [128, 2]); maskA = I("maskA", [128, 128])
    oT = nc.dram_tensor(pre + "oT", [128, S], F32, kind="ExternalOutput")
    if True:
        iD = Buf(None); oD = Buf(None)
        def ld(name, shape, src, q="pool"):
            b = p.sbuf(shape, F32, name); p.dma(q, b[tuple(slice(None) for _ in shape)], src, [iD], [b]); return b
        gws = ld("gws", [16, 64], gw[:, :]); gbs = ld("gbs", [128, 64], gbb[:, :])
        tris = ld("tris", [128, 128], tri[:, :]); m1s = ld("m1s", [128, 128], m1[:, :]); cis = ld("cis", [128, 2], cind[:, :])
        mAs = ld("mAs", [128, 128], maskA[:, :])
        one1 = p.sbuf([128, 1], F32, "one1"); p.op("pool", [], [one1], lambda e: e.memset(one1[:, :], 1.0))
        lrs = ld("lrs", [16, S], lrT[:, :], "sp")
        St = p.sbuf([64, 128], F32, "St"); Sb = p.sbuf([64, 128], BF16, "Sb")
        p.op("pool", [], [St], lambda e: e.memset(St[:, :], 0.0)); p.op("pool", [], [Sb], lambda e: e.memset(Sb[:, :], 0.0))
        NB = 16
        def stg(name, shape): return [p.sbuf(shape, F32, f"{name}{i}") for i in range(2)]
        qst = stg("qst", [64, NB * 128]); kst = stg("kst", [64, NB * 128]); ktst = stg("ktst", [128, NB, 64]); vst = stg("vst", [128, NB, 128])
        vbf = [p.sbuf([128, NB, 128], BF16, f"vbf{i}") for i in range(2)]
        ost = [p.sbuf([128, NB * 128], F32, f"ost{i}") for i in range(2)]
        R = lambda n, shp, dt=F32, k=2: [p.sbuf(shp, dt, f"{n}{i}") for i in range(k)]
        xs = R("xs", [128, 64]); es_ = R("es", [128, 64]); ls = R("ls", [128, 64])
        eb = R("eb", [64, 128]); enb = R("enb", [64, 128]); ed = R("ed", [128, 64]); av = R("av", [64, 2])
        qg = R("qg", [64, 128], BF16); kg = R("kg", [64, 128], BF16); kd = R("kd", [128, 64], BF16); Am = R("Am", [128, 128], BF16)
        for i in range(NTL):
            blk, ib = divmod(i, NB); par = i % 2
            if ib == 0:
                bsl = slice(blk * NB * 128, (blk + 1) * NB * 128); tsl = slice(blk * NB, (blk + 1) * NB)
                p.dma("sp", qst[blk % 2][:, :], qT[:, bsl], [iD], [qst[blk % 2]])
                p.dma("sp", kst[blk % 2][:, :], kT[:, bsl], [iD], [kst[blk % 2]])
                p.dma("sp", ktst[blk % 2][:, :, :], ktm[:, tsl, :], [iD], [ktst[blk % 2]])
                p.dma("sp", vst[blk % 2][:, :, :], vtm[:, tsl, :], [iD], [vst[blk % 2]])
                vb = vbf[blk % 2]
                p.op("pool", [vst[blk % 2]], [vb], lambda e: e.tensor_copy(out=vb[:, :, :], in_=vst[blk % 2][:, :, :]))
            q_s = qst[blk % 2]; k_s = kst[blk % 2]; kt_s = ktst[blk % 2]; vb = vbf[blk % 2]; o_s = ost[blk % 2]
            tsl = slice(i * 128, (i + 1) * 128); lsl = slice(ib * 128, (ib + 1) * 128)
            pa = c["ps"][par]; pb = c["ps"][2 + par]; pc = c["ps"][4 + par]; po = c["ps"][6]; pf = c["ps"][7]
            x = xs[par]; e_ = es_[par]; l = ls[par]
            p.op("pe", [lrs, gws], [pa], lambda e: e.matmul(pa[:, 0:64], lhsT=lrs[:, tsl], rhs=gws[:, :], start=True, stop=True))
            p.op("dve", [pa, gbs], [x], lambda e: e.tensor_tensor(out=x[:, :], in0=pa[:, 0:64], in1=gbs[:, :], op=ALU.add))
            p.op("act", [x], [e_], lambda e: e.activation(out=e_[:, :], in_=x[:, :], func=AF.Exp, scale=-1.0))
            p.op("act", [e_, one1], [l], lambda e: e.activation(out=l[:, :], in_=e_[:, :], func=AF.Ln, bias=one1[:, 0:1], scale=1.0))
            p.op("pe", [l, tris], [pb], lambda e: e.matmul(pb[:64, 0:128], lhsT=l[:, :], rhs=tris[:, :], start=True, stop=True))
            p.op("pe", [l, cis], [pb], lambda e: e.matmul(pb[:64, 128:130], lhsT=l[:, :], rhs=cis[:, :], start=True, stop=True), same_ok=True)
            p.op("pe", [m1s, l], [pc], lambda e: e.matmul(pc[:, 0:64], lhsT=m1s[:, :], rhs=l[:, :], start=True, stop=True))
            p.op("act", [pb], [eb[par]], lambda e: e.activation(out=eb[par][:, :], in_=pb[:64, 0:128], func=AF.Exp, scale=-1.0 / 16))
            p.op("act", [pb], [enb[par]], lambda e: e.activation(out=enb[par][:, :], in_=pb[:64, 0:128], func=AF.Exp, scale=1.0 / 16))
            p.op("act", [pb], [av[par]], lambda e: e.activation(out=av[par][:, :], in_=pb[:64, 128:130], func=AF.Exp, scale=-1.0 / 16))
            p.op("act", [pc], [ed[par]], lambda e: e.activation(out=ed[par][:, :], in_=pc[:, 0:64], func=AF.Exp, scale=-1.0 / 16))
            p.op("dve", [q_s, eb[par]], [qg[par]], lambda e: e.scalar_tensor_tensor(out=qg[par][:, :], in0=q_s[:, lsl], scalar=0.125, in1=eb[par][:, :], op0=ALU.mult, op1=ALU.mult))
            p.op("dve", [k_s, enb[par]], [kg[par]], lambda e: e.tensor_tensor(out=kg[par][:, :], in0=k_s[:, lsl], in1=enb[par][:, :], op=ALU.mult))
            p.op("dve", [kt_s, ed[par]], [kd[par]], lambda e: e.tensor_tensor(out=kd[par][:, :], in0=kt_s[:, ib, :], in1=ed[par][:, :], op=ALU.mult))
            p.op("pe", [kg[par], qg[par]], [pc], lambda e: e.matmul(pc[:, 128:256], lhsT=kg[par][:, :], rhs=qg[par][:, :], start=True, stop=True))
            p.op("dve", [pc, mAs], [Am[par]], lambda e: e.tensor_tensor(out=Am[par][:, :], in0=pc[:, 128:256], in1=mAs[:, :], op=ALU.mult))
            p.op("pe", [vb, Am[par]], [po], lambda e: e.matmul(po[:, 0:128], lhsT=vb[:, ib, :], rhs=Am[par][:, :], start=True, stop=False))
            for ci in range(2):
                cs = slice(ci * 64, (ci + 1) * 64)
                p.op("pe", [Sb, qg[par]], [po], lambda e: e.matmul(po[:, cs], lhsT=Sb[:, :], rhs=qg[par][:, cs], start=False, stop=(ci == 1)), same_ok=True)
                p.op("pe", [kd[par], vb], [pf], lambda e: e.matmul(pf[:64, 0:128], lhsT=kd[par][cs, :], rhs=vb[cs, ib, :], start=True, stop=True))
                p.op("dve", [St, av[par], pf], [St], lambda e: e.scalar_tensor_tensor(out=St[:, :], in0=St[:, :], scalar=av[par][:, ci:ci + 1], in1=pf[:64, 0:128], op0=ALU.mult, op1=ALU.add))
                p.op("act", [St], [Sb], lambda e: e.copy(out=Sb[:, :], in_=St[:, :]))
            p.op("act", [po], [o_s], lambda e: e.copy(out=o_s[:, lsl], in_=po[:, 0:128]))
            if ib == NB - 1 or i == NTL - 1:
                p.dma("act", oT[:, blk * NB * 128:(blk * NB + ib + 1) * 128], o_s[:, :(ib + 1) * 128], [o_s], [oD])
        print("k2a ninst", p.ninst)
    return None

def host_inputs_k2a(zT, inp, layer):
    AQ, AK, AV, AR, ALR = 0, 256, 512, 1024, 1536
    ss = np.arange(128)[:, None]; tt = np.arange(128)[None, :]
    same = (ss // 64) == (tt // 64)
    tri = (same & (ss <= tt)).astype(np.float32); m1 = (same & (ss > tt)).astype(np.float32)
    cind = (np.arange(128)[:, None] // 64 == np.arange(2)[None, :]).astype(np.float32)
    maps = []
    for h in range(4):
        for d in range(2):
            f = (lambda a: a[:, ::-1]) if d == 1 else (lambda a: a)
            q = np.ascontiguousarray(f(zT[AQ + h * 64:AQ + (h + 1) * 64])); k = np.ascontiguousarray(f(zT[AK + h * 64:AK + (h + 1) * 64]))
            v = f(zT[AV + h * 128:AV + (h + 1) * 128]); lr = np.ascontiguousarray(f(zT[ALR + d * 16:ALR + (d + 1) * 16]))
            ktm = np.ascontiguousarray(k.T.reshape(NT, 128, 64).transpose(1, 0, 2)); vtm = np.ascontiguousarray(v.T.reshape(NT, 128, 128).transpose(1, 0, 2))
            mA = (same & ((ss <= tt) if d == 0 else (ss < tt))).astype(np.float32)
            maps.append({"qT": q, "kT": k, "ktm": ktm, "vtm": vtm, "lrT": lr,
                         "gw": np.ascontiguousarray(inp["gla_gate_w"][layer, d][:, h * 64:(h + 1) * 64]),
                         "gbb": np.ascontiguousarray(np.broadcast_to(inp["gla_gate_b"][layer, d][None, h * 64:(h + 1) * 64], (128, 64))),
                         "tri": tri, "m1": m1, "cind": cind, "maskA": mA})
    return maps


S = 16384
DILS = (1, 4, 16)
NEG = -100.0

STAGE = 3
def emit_k2b(nc, p, c, pre):
    I = lambda n, s: nc.dram_tensor(pre + n, s, F32, kind="ExternalInput")
    qw = I("qw", [3, 256, 2048]); kw = I("kw", [3, 256, 4096]); vw = I("vw", [3, 128, 16 * 2 * 4 * 64])
    gq = I("gq", [128, 3]); gk = I("gk", [128, 3]); kval = I("kval", [128, 3 * 32]); bm = I("bm", [3, 2, 128, 512]); bones = I("bones", [128, 128])
    O = nc.dram_tensor(pre + "O", [3, 2, 64, 4, 2048], F32, kind="ExternalOutput")
    if True:
        iD = Buf(None); oD = Buf(None)
        def ld(name, shape, src, q="pool"):
            b = p.sbuf(shape, F32, name); p.dma(q, b[tuple(slice(None) for _ in shape)], src, [iD], [b]); return b
        gqs = ld("gqs", [128, 3], gq[:, :]); gks = ld("gks", [128, 3], gk[:, :]); kvs = ld("kvs", [128, 96], kval[:, :]); bos = ld("bos", [128, 128], bones[:, :])
        bms = [[ld(f"bm{g}{t}", [128, 512], bm[g, t, :, :]) for t in range(2)] for g in range(3)]
        qst = [p.sbuf([128, 2048], F32, f"qst{i}") for i in range(2)]; kst = [p.sbuf([128, 4096], F32, f"kst{i}") for i in range(2)]
        vst = p.sbuf([128, 16 * 2 * 4 * 64], F32, "vst"); vb = p.sbuf([128, 16, 2, 4, 64], BF16, "vb")
        qn = [p.sbuf([64, 2048], BF16, f"qn{i}") for i in range(4)]; kn = [p.sbuf([64, 4096], BF16, f"kn{i}") for i in range(4)]
        tmp = [p.sbuf([128, 512], F32, f"tmp{i}") for i in range(2)]; rtmp = [p.sbuf([128, 512], F32, f"rtmp{i}") for i in range(2)]
        tt = [p.sbuf([128, 512], F32, f"tt{i}") for i in range(2)]; P = [p.sbuf([128, 512], BF16, f"P{i}") for i in range(2)]
        ost = [p.sbuf([64, 2, 512], F32, f"ost{i}") for i in range(2)]
        onb = p.sbuf([128, 64], BF16, "onb"); p.op("pool", [], [onb], lambda e: e.memset(onb[:, :], 1.0))
        it = 0
        def norm(st, gcol, dst, n):
            for tb in range(n // 512):
                sl = slice(tb * 512, (tb + 1) * 512)
                s = tmp[tb % 2]; r = rtmp[tb % 2]; ps = c["ps"][6 + tb % 2]
                p.op("act", [st], [s], lambda e: e.activation(out=s[:64, :], in_=st[:64, sl], func=AF.Square))
                p.op("pe", [c["ones"], s], [ps], lambda e: e.matmul(ps[:64, :], lhsT=c["ones"][:64, :64], rhs=s[:64, :], start=True, stop=True))
                p.op("act", [ps, c["eps"]], [r], lambda e: e.activation(out=r[:64, :], in_=ps[:64, :], func=AF.Sqrt, bias=c["eps"][:64, 0:1], scale=1.0 / 64))
                p.op("dve", [r], [r], lambda e: e.reciprocal(out=r[:64, :], in_=r[:64, :]))
                p.op("dve", [st, r, gcol[0]], [dst], lambda e: e.scalar_tensor_tensor(out=dst[:, sl], in0=st[:64, sl], scalar=gcol[0][:64, gcol[1]:gcol[1] + 1], in1=r[:64, :], op0=ALU.mult, op1=ALU.mult))
        for g in range(3):
            p.dma("sp", vst[:, :], vw[g, :, :], [iD], [vst])
            p.op("pool", [vst], [vb], lambda e: e.tensor_copy(out=vb[:, :, :, :, :].rearrange("p a b c d -> p (a b c d)"), in_=vst[:, :]))
            for h in range(4):
                p.dma("sp", qst[h % 2][:64, :], qw[g, h * 64:(h + 1) * 64, :], [iD], [qst[h % 2]])
                p.dma("sp", kst[h % 2][:64, :], kw[g, h * 64:(h + 1) * 64, :], [iD], [kst[h % 2]])
                norm(qst[h % 2], (gqs, g), qn[h], 2048); norm(kst[h % 2], (gks, g), kn[h], 4096)
            for b in range(16):
                po = c["ps"][4 + b % 2]; pl = c["ps"][6 + b % 2]
                for tau in range(2 if STAGE >= 2 else 0):
                    ps = c["ps"][it % 4]; t = tt[it % 2]; Pt = P[it % 2]; it += 1
                    for h in range(4):
                        p.op("pe", [kn[h], qn[h]], [ps], lambda e: e.matmul(ps[:, h * 128:(h + 1) * 128], lhsT=kn[h][:, b * 256 + tau * 128:b * 256 + tau * 128 + 128], rhs=qn[h][:, b * 128:(b + 1) * 128], start=True, stop=True), same_ok=True)
                    p.op("dve", [ps, bms[g][tau]], [t], lambda e: e.scalar_tensor_tensor(out=t[:, :], in0=ps[:, :], scalar=0.125, in1=bms[g][tau][:, :], op0=ALU.mult, op1=ALU.add))
                    col = g * 32 + b * 2 + tau
                    p.op("act", [t, kvs], [Pt], lambda e: e.activation(out=Pt[:, :], in_=t[:, :], func=AF.Exp, bias=kvs[:, col:col + 1], scale=1.0))
                    if STAGE < 3: continue
                    for h in range(4):
                        p.op("pe", [vb, Pt], [po], lambda e: e.matmul(po[:64, h * 128:(h + 1) * 128], lhsT=vb[:, b, tau, h, :], rhs=Pt[:, h * 128:(h + 1) * 128], start=(tau == 0 and h == 0), stop=(tau == 1 and h == 3)), same_ok=True)
                    p.op("pe", [onb, Pt], [pl], lambda e: e.matmul(pl[:64, :], lhsT=onb[:, :], rhs=Pt[:, :], start=(tau == 0), stop=(tau == 1)), same_ok=True)
                o = ost[b % 2]
                if STAGE < 3:
                    p.op("dve", [], [o], lambda e: e.memset(o[:, :, :], 1.0))
                    for w_ in range(2):
                        p.dma("act", O[g, w_, :, :, b * 128:(b + 1) * 128], o[:, w_, :].rearrange("r (h q) -> r h q", q=128), [o], [oD])
                    continue
                p.op("act", [po], [o], lambda e: e.copy(out=o[:, 0, :], in_=po[:64, :]))
                p.op("dve", [pl], [o], lambda e: e.tensor_copy(out=o[:, 1, :], in_=pl[:64, :]))
                for w_ in range(2):
                    p.dma("act", O[g, w_, :, :, b * 128:(b + 1) * 128], o[:, w_, :].rearrange("r (h q) -> r h q", q=128), [o], [oD])
        print("k2b ninst", p.ninst)
    return None

def perm_index(d):
    M = S // d
    pos = np.arange(S)
    return (pos % M) * d + pos // M

def host_inputs_k2b(zT, inp, layer):
    BQ = 1568; BK = BQ + 768; BV = BK + 768
    t5 = inp["t5_bias"]
    kk = np.arange(128)[:, None]; qq = np.arange(128)[None, :]
    bm = np.zeros((3, 2, 128, 512), np.float32)
    for g, d in enumerate(DILS):
        for tau in range(2):
            delta = kk + (-64 if tau == 0 else 64) - qq
            ok = np.abs(delta) <= 64
            bidx = t5_bucket_np(delta * d)
            for h in range(4):
                bm[g, tau, :, h * 128:(h + 1) * 128] = np.where(ok, t5[:, g * 4 + h][bidx], np.float32(NEG))
    bones = (np.arange(128)[:, None] // 64 == np.arange(128)[None, :] // 64).astype(np.float32)
    gq = np.ascontiguousarray(np.tile(inp["dil_qnorm_g"][layer].T, (2, 1))); gk = np.ascontiguousarray(np.tile(inp["dil_knorm_g"][layer].T, (2, 1)))
    qP, kP, vP, valid = [], [], [], []
    for g, d in enumerate(DILS):
        idx = perm_index(d); M = S // d
        qP.append(zT[BQ + g * 256:BQ + (g + 1) * 256][:, idx])
        kP.append(zT[BK + g * 256:BK + (g + 1) * 256][:, idx]); vP.append(zT[BV + g * 256:BV + (g + 1) * 256][:, idx])
    maps = []
    for r in range(8):
        qw = np.zeros((3, 256, 2048), np.float32); kw = np.zeros((3, 256, 4096), np.float32)
        vw = np.zeros((3, 128, 16, 2, 4, 64), np.float32); kval = np.zeros((128, 96), np.float32)
        for g, d in enumerate(DILS):
            M = S // d
            qw[g] = qP[g][:, r * 2048:(r + 1) * 2048]
            for b in range(16):
                P0 = r * 2048 + b * 128; sub0 = (P0 // M) * M
                kpos = np.arange(P0 - 64, P0 + 192)
                ok = (kpos >= sub0) & (kpos < sub0 + M)
                kc = np.clip(kpos, 0, S - 1)
                kwin = np.where(ok[None, :], kP[g][:, kc], np.float32(0))
                kw[g][:, b * 256:(b + 1) * 256] = kwin
                vwin = np.where(ok[None, :], vP[g][:, kc], np.float32(0))
                vv = vwin.reshape(4, 64, 2, 128).transpose(3, 2, 0, 1)
                vw[g][:, b] = vv
                kval[:, g * 32 + b * 2:g * 32 + b * 2 + 2] = np.where(ok.reshape(2, 128).T, np.float32(0), np.float32(NEG))
        maps.append({"qw": qw, "kw": kw, "vw": np.ascontiguousarray(vw.reshape(3, 128, -1)), "gq": gq, "gk": gk, "kval": kval, "bm": bm, "bones": bones})
    return maps

def host_post_k2b(results):
    N = np.zeros((3, 256, S), np.float32); lb = np.zeros((3, 256, S), np.float32)
    for g, d in enumerate(DILS):
        Og = np.concatenate([r["O"][g] for r in results], axis=3)
        idx = perm_index(d)
        N[g][:, idx] = Og[0].transpose(1, 0, 2).reshape(256, S)
        lb[g][:, idx] = Og[1].transpose(1, 0, 2).reshape(256, S)
    return N, lb


import math
S = 16384
I32 = mybir.dt.int32
TWO_PI = float(2 * np.pi)

def emit_k2d(nc, p, c, pre, NROW=128):
    I = lambda n, s: nc.dram_tensor(pre + n, s, F32, kind="ExternalInput")
    ux1 = I("ux1", [128, S]); uv = I("uv", [128, S]); ux0 = I("ux0", [64, S])
    cw1 = I("cw1", [128, 4]); cwv = I("cwv", [128, 4]); cw0 = I("cw0", [64, 4])
    embT = I("embT", [33, S]); w1 = I("w1", [33, 64]); w2 = I("w2", [64, 64]); w3 = I("w3", [64, 128])
    bf1 = I("bf1", [64, 2]); bf2 = I("bf2", [64, 2])
    dlrow = I("dlrow", [1, 128]); tvec = I("tvec", [1, S]); pair = I("pair", [128, 128])
    Y = nc.dram_tensor(pre + "Y", [128, 128, 128], F32, kind="ExternalOutput")
    zz = nc.dram_tensor(pre + "zz", [64, S], F32, kind="ExternalOutput")
    x0c = nc.dram_tensor(pre + "x0c", [64, S], F32, kind="ExternalOutput")
    kx = nc.dram_tensor(pre + "kx", [128, S + 128], BF16, kind="Internal")
    if True:
        iD = Buf(None); oD = Buf(None); kxD = Buf(None)
        def ld(name, shape, src, q="pool"):
            b = p.sbuf(shape, F32, name); p.dma(q, b[tuple(slice(None) for _ in shape)], src, [iD], [b]); return b
        cw1s = ld("cw1s", [128, 4], cw1[:, :]); cwvs = ld("cwvs", [128, 4], cwv[:, :]); cw0s = ld("cw0s", [64, 4], cw0[:, :])
        w1s = ld("w1s", [33, 64], w1[:, :]); w2s = ld("w2s", [64, 64], w2[:, :]); w3s = ld("w3s", [64, 128], w3[:, :])
        bf1s = ld("bf1s", [64, 2], bf1[:, :]); bf2s = ld("bf2s", [64, 2], bf2[:, :])
        dls = ld("dls", [1, 128], dlrow[:, :]); prs = ld("prs", [128, 128], pair[:, :])
        for b in (bf1s, bf2s):
            p.op("dve", [b], [b], lambda e: e.tensor_scalar(out=b[:, 1:2], in0=b[:, 1:2], scalar1=1.0 / TWO_PI, scalar2=None, op0=ALU.mult))
        idf = p.sbuf([128, 128], F32, "idf"); idb = p.sbuf([128, 128], BF16, "idb")
        p.op("pool", [], [idf], lambda e: e.memset(idf[:, :], 1.0))
        p.op("pool", [idf], [idf], lambda e: e.affine_select(out=idf[:, :], in_=idf[:, :], pattern=[[-1, 128]], compare_op=ALU.is_equal, fill=0.0, base=0, channel_multiplier=1))
        p.op("dve", [idf], [idb], lambda e: e.tensor_copy(out=idb[:, :], in_=idf[:, :]))
        zpad = p.sbuf([128, 128], BF16, "zpad"); p.op("pool", [], [zpad], lambda e: e.memset(zpad[:, :], 0.0))
        p.dma("pool", kx[:, S:S + 128], zpad[:, :], [zpad], [kxD])
        Zall = p.sbuf([128, 128, 128], BF16, "Zall")
        rn = p.sbuf([128, 128], F32, "rn")
        scA = p.scope(); scA.__enter__()
        zzb = p.sbuf([128, S], BF16, "zzb")
        CB = 2048
        ub = [p.sbuf([128, CB + 2], F32, f"ub{i}") for i in range(2)]
        acc = [p.sbuf([128, CB], F32, f"acc{i}") for i in range(3)]
        k = 0
        def conv(src, rows, cws, blk, dst):
            nonlocal k
            u = ub[k % 2]; k += 1
            lo = blk * CB - 1; hi = (blk + 1) * CB + 1
            if blk == 0:
                p.op("pool", [], [u], lambda e: e.memset(u[:rows, 0:1], 0.0));
                p.dma("sp", u[:rows, 1:CB + 2], src[:, 0:hi], [iD], [u])
            elif blk == S // CB - 1:
                p.op("pool", [], [u], lambda e: e.memset(u[:rows, CB + 1:CB + 2], 0.0))
                p.dma("sp", u[:rows, 0:CB + 1], src[:, lo:S], [iD], [u])
            else:
                p.dma("sp", u[:rows, :], src[:, lo:hi], [iD], [u])
            p.op("dve", [u, cws], [dst], lambda e: e.tensor_scalar(out=dst[:rows, :], in0=u[:rows, 1:CB + 1], scalar1=cws[:rows, 1:2], scalar2=cws[:rows, 3:4], op0=ALU.mult, op1=ALU.add))
            p.op("dve", [u, cws, dst], [dst], lambda e: e.scalar_tensor_tensor(out=dst[:rows, :], in0=u[:rows, 0:CB], scalar=cws[:rows, 0:1], in1=dst[:rows, :], op0=ALU.mult, op1=ALU.add))
            p.op("dve", [u, cws, dst], [dst], lambda e: e.scalar_tensor_tensor(out=dst[:rows, :], in0=u[:rows, 2:CB + 2], scalar=cws[:rows, 2:3], in1=dst[:rows, :], op0=ALU.mult, op1=ALU.add))
        for blk in range(S // CB):
            bsl = slice(blk * CB, (blk + 1) * CB)
            conv(ux1, 128, cw1s, blk, acc[0]); conv(uv, 128, cwvs, blk, acc[1])
            p.op("pool", [acc[0], acc[1]], [acc[0]], lambda e: e.tensor_tensor(out=acc[0][:, :], in0=acc[0][:, :], in1=acc[1][:, :], op=ALU.mult))
            p.op("act", [acc[0]], [zzb], lambda e: e.copy(out=zzb[:, bsl], in_=acc[0][:, :]))
            p.dma("act", zz[:, bsl], acc[0][:64, :], [acc[0]], [oD])
            conv(ux0, 64, cw0s, blk, acc[2])
            p.dma("act", x0c[:, bsl], acc[2][:64, :], [acc[2]], [oD])
        for g in range(32):
            ps = c["ps"][g % 2]
            for q in range(4):
                sb = g * 4 + q
                p.op("pe", [zzb, idb], [ps], lambda e: e.matmul(ps[:, q * 128:(q + 1) * 128], lhsT=zzb[:, sb * 128:(sb + 1) * 128], rhs=idb[:, :], start=True, stop=True), same_ok=True)
            eng = "act" if g % 2 == 0 else "dve"
            if eng == "act":
                p.op("act", [ps], [Zall], lambda e: e.copy(out=Zall[:, :, g * 4:(g + 1) * 4].rearrange("p r b -> p b r"), in_=ps[:, :].rearrange("p (b r) -> p b r", r=128)))
            else:
                p.op("dve", [ps], [Zall], lambda e: e.tensor_copy(out=Zall[:, :, g * 4:(g + 1) * 4].rearrange("p r b -> p b r"), in_=ps[:, :].rearrange("p (b r) -> p b r", r=128)))
        scA.__exit__(None, None, None)
        scC = p.scope(); scC.__enter__()
        tvb = [p.sbuf([1, 512], F32, f"tvb{i}") for i in range(2)]
        emb = [p.sbuf([33, 512], F32, f"emb{i}") for i in range(2)]
        R = lambda n, shp, dt=F32, kk=2: [p.sbuf(shp, dt, f"{n}{i}") for i in range(kk)]
        ua = R("ua", [64, 512]); ti = R("ti", [64, 512], I32); tf = R("tf", [64, 512]); h1 = R("h1", [64, 512]); h2 = R("h2", [64, 512])
        dec = R("dec", [128, 512]); kf = R("kf", [128, 512]); kb = R("kb", [128, 512], BF16)
        npart = p.sbuf([128, 32], F32, "npart")
        def sin_layer(ps, bfs, out, par):
            u = ua[par]
            p.op("dve", [ps, bfs], [u], lambda e: e.tensor_scalar(out=u[:, :], in0=ps[:64, :], scalar1=bfs[:, 0:1], scalar2=bfs[:, 1:2], op0=ALU.add, op1=ALU.mult))
            p.op("dve", [u], [ti[par]], lambda e: e.tensor_copy(out=ti[par][:, :], in_=u[:, :]))
            p.op("dve", [ti[par]], [tf[par]], lambda e: e.tensor_copy(out=tf[par][:, :], in_=ti[par][:, :]))
            p.op("dve", [u, tf[par]], [u], lambda e: e.tensor_tensor(out=u[:, :], in0=u[:, :], in1=tf[par][:, :], op=ALU.subtract))
            p.op("act", [u], [out], lambda e: e.activation(out=out[:, :], in_=u[:, :], func=AF.Sin, scale=TWO_PI))
        for blk in range(32):
            par = blk % 2; bsl = slice(blk * 512, (blk + 1) * 512)
            em = emb[par]
            p.dma("sp", em[:, :], embT[:, bsl], [iD], [em])
            p.dma("sp", tvb[par][:, :], tvec[:, bsl], [iD], [tvb[par]])
            p1 = c["ps"][2 + par]; p2 = c["ps"][4 + par]; p3 = c["ps"][6]; pd = c["ps"][7]
            p.op("pe", [w1s, em], [p1], lambda e: e.matmul(p1[:64, :], lhsT=w1s[:, :], rhs=em[:, :], start=True, stop=True))
            sin_layer(p1, bf1s, h1[par], par)
            p.op("pe", [w2s, h1[par]], [p2], lambda e: e.matmul(p2[:64, :], lhsT=w2s[:, :], rhs=h1[par][:, :], start=True, stop=True))
            sin_layer(p2, bf2s, h2[par], par)
            p.op("pe", [w3s, h2[par]], [p3], lambda e: e.matmul(p3[:, :], lhsT=w3s[:, :], rhs=h2[par][:, :], start=True, stop=True))
            p.op("pe", [dls, tvb[par]], [pd], lambda e: e.matmul(pd[:, :], lhsT=dls[:, :], rhs=tvb[par][:, :], start=True, stop=True))
            p.op("act", [pd], [dec[par]], lambda e: e.activation(out=dec[par][:, :], in_=pd[:, :], func=AF.Exp))
            p.op("dve", [p3, dec[par]], [kf[par]], lambda e: e.tensor_tensor(out=kf[par][:, :], in0=p3[:, :], in1=dec[par][:, :], op=ALU.mult))
            if blk == 31:
                p.op("dve", [kf[par]], [kf[par]], lambda e: e.memset(kf[par][64:128, 511:512], 0.0))
            p.op("dve", [kf[par]], [npart], lambda e: e.tensor_reduce(out=npart[:, blk:blk + 1], in_=kf[par][:, :], axis=AX.X, op=ALU.add, apply_absolute_value=True))
            p.op("pool", [kf[par]], [kb[par]], lambda e: e.tensor_copy(out=kb[par][:, :], in_=kf[par][:, :]))
            p.dma("pool", kx[:, bsl], kb[par][:, :], [kb[par]], [kxD])
        nsum = p.sbuf([128, 1], F32, "nsum"); prn = p.sbuf([128, 128], F32, "prn")
        p.op("dve", [npart], [nsum], lambda e: e.tensor_reduce(out=nsum[:, 0:1], in_=npart[:, :], axis=AX.X, op=ALU.add))
        p.op("dve", [prs, nsum], [prn], lambda e: e.tensor_scalar(out=prn[:, :], in0=prs[:, :], scalar1=nsum[:, 0:1], scalar2=None, op0=ALU.mult))
        pn = c["ps"][0]
        p.op("pe", [c["ones"], prn], [pn], lambda e: e.matmul(pn[:, 0:128], lhsT=c["ones"][:, :], rhs=prn[:, :], start=True, stop=True))
        p.op("dve", [pn], [rn], lambda e: e.reciprocal(out=rn[:, :], in_=pn[:, 0:128]))
        scC.__exit__(None, None, None)
        strip = [p.sbuf([128, S], BF16, f"strip{i}") for i in range(2)]
        yst = [p.sbuf([128, 8, 128], F32, f"yst{i}") for i in range(2)]
        for row in range(NROW):
            st = strip[row % 2]
            src = bass.AP(tensor=kx, offset=row * (S + 128), ap=[[1, 128], [1, S]])
            p.dma("sp" if row % 2 == 0 else "pool", st[:, :], src, [kxD], [st])
            py = c["ps"][1 + row % 2]
            for d in range(128):
                base = 16256 - 128 * d
                p.op("pe", [st, Zall], [py], lambda e: e.matmul(py[:, d:128], lhsT=st[:, base:base + 128], rhs=Zall[:, row, 0:128 - d], start=(d == 0), stop=(d == 127)), same_ok=True)
            ys = yst[(row // 8) % 2]
            p.op("act", [py, rn], [ys], lambda e: e.activation(out=ys[:, row % 8, :], in_=py[:, 0:128], func=AF.Copy, scale=rn[:, row:row + 1]))
            if row % 8 == 7:
                p.dma("act", Y[:, row - 7:row + 1, :], ys[:, :, :], [ys], [oD])
        print("k2d ninst", p.ninst)
    return None

def hy_consts():
    L = S
    t = np.linspace(0.0, 1.0, L, dtype=np.float32)[:, None]
    bands = 16
    freqs = np.linspace(1e-4, bands - 1, bands, dtype=np.float32)[None]
    w = (np.float32(2.0 * math.pi) * np.arange(L, dtype=np.float32)[:, None] / np.float32(L)).astype(np.float32)
    z = np.concatenate([t, np.cos(freqs * w), -np.sin(freqs * w)], axis=-1).astype(np.float32)
    min_decay = math.log(1e-2) / 1.5; max_decay = math.log(1e-2) / 0.3
    deltas = np.abs(np.linspace(min_decay, max_decay, 512, dtype=np.float32))
    return z, t[:, 0], deltas

def host_inputs_k2d(zT, inp, layer):
    DU = 11040 - 4096 - 1536
    z, t, deltas = hy_consts()
    embT = np.ascontiguousarray(z[::-1].T); tvec = np.ascontiguousarray(t[::-1][None, :])
    pair = (np.arange(128)[:, None] % 64 == np.arange(128)[None, :] % 64).astype(np.float32)
    cw = inp["hy_conv_w"][layer]; cb = inp["hy_conv_b"][layer]
    def cwpack(cols, rev):
        a = cw[:, cols].T
        if rev: a = a[:, ::-1]
        return np.ascontiguousarray(np.concatenate([a, cb[cols][:, None]], 1).astype(np.float32))
    maps = []
    for r in range(8):
        ch = np.arange(64 * r, 64 * r + 64)
        x0 = zT[DU + ch]; x1 = zT[DU + 512 + ch]; v = zT[DU + 1024 + ch]
        m = {"ux1": np.ascontiguousarray(np.concatenate([x1, x1[:, ::-1]], 0)), "uv": np.ascontiguousarray(np.concatenate([v, v[:, ::-1]], 0)), "ux0": np.ascontiguousarray(x0),
             "cw1": np.concatenate([cwpack(512 + ch, False), cwpack(512 + ch, True)], 0), "cwv": np.concatenate([cwpack(1024 + ch, False), cwpack(1024 + ch, True)], 0), "cw0": cwpack(ch, False),
             "embT": embT, "w1": np.ascontiguousarray(inp["hy_w1"][layer]), "w2": np.ascontiguousarray(inp["hy_w2"][layer]),
             "w3": np.ascontiguousarray(np.concatenate([inp["hy_w3"][layer][:, ch], inp["hy_w3"][layer][:, 512 + ch]], 1)),
             "bf1": np.ascontiguousarray(np.stack([inp["hy_b1"][layer], inp["hy_freq1"][layer]], 1)), "bf2": np.ascontiguousarray(np.stack([inp["hy_b2"][layer], inp["hy_freq2"][layer]], 1)),
             "dlrow": np.ascontiguousarray(-np.concatenate([deltas[ch], deltas[ch]])[None, :].astype(np.float32)), "tvec": tvec, "pair": pair}
        maps.append(m)
    return maps

def host_post_k2d(results):
    ys, zs, xs = [], [], []
    for r in results:
        Yr = r["Y"]
        y = Yr[::-1].transpose(1, 2, 0).reshape(128, S)
        ys.append(y[:64] + y[64:, ::-1]); zs.append(r["zz"]); xs.append(r["x0c"])
    return np.concatenate(ys, 0), np.concatenate(zs, 0), np.concatenate(xs, 0)


S = 16384; T = 2048; D = 1024; NIN = 11040; DFF = 4096

def linear2(p, c, w_ap, K, N, rhs_fn, ntb, evac, wst, wbf, gw, psi=2, npsi=4):
    kc = K // 128
    wD = Buf(None)
    ng = (N + gw - 1) // gw
    k2 = 0
    for gi in range(ng):
        c0 = gi * gw; wg = min(gw, N - c0)
        st = wst[gi % len(wst)]; wb = wbf[gi % len(wbf)]
        p.dma("sp", st[:, :kc, :wg], w_ap[:, c0:c0 + wg].rearrange("(c p) n -> p c n", p=128), [wD], [st])
        h = kc // 2
        p.op("act", [st], [wb], lambda e: e.copy(out=wb[:, :h, :wg], in_=st[:, :h, :wg]))
        p.op("dve", [st], [wb], lambda e: e.tensor_copy(out=wb[:, h:kc, :wg], in_=st[:, h:kc, :wg]))
        for mi in range((wg + 127) // 128):
            mw = min(128, wg - mi * 128)
            for tb in range(ntb):
                ps = c["ps"][psi + k2 % npsi]; k2 += 1
                for ch in range(kc):
                    rb, rap = rhs_fn(ch, tb)
                    p.op("pe", [wb, rb], [ps], lambda e: e.matmul(ps[:mw, :], lhsT=wb[:, ch, mi * 128:mi * 128 + mw], rhs=rap, start=(ch == 0), stop=(ch == kc - 1)), same_ok=True)
                evac(c0 + mi * 128, mw, tb, ps)

def build_k3(lam_init, last):
    nc = bass.Bass("TRN2", target_bir_lowering=False)
    I = lambda n, s: nc.dram_tensor(n, s, F32, kind="ExternalInput")
    xT = I("xT", [D, T])
    of = I("of", [512, T]); ob = I("ob", [512, T]); rT = I("rT", [512, T]); gla_g = I("gla_g", [128, 1])
    dN = I("dN", [3, 256, T]); dL = I("dL", [3, 256, T])
    oc0 = I("oc0", [512, T]); oc1 = I("oc1", [512, T]); lamb = I("lamb", [128, 256]); sub_g = I("sub_g", [128, 1])
    yconv = I("yconv", [512, T]); zzT = I("zzT", [512, T]); x0c = I("x0c", [512, T]); skip = I("skip", [128, 4])
    gates = I("gates", [4096, T])
    projs = I("projs", [1792, D]); w_out = I("w_out", [D, D]); g2 = I("g2", [128, 8]); w1 = I("w1", [D, DFF]); w2 = I("w2", [DFF, D])
    xo = nc.dram_tensor("xo", [D, T], F32, kind="ExternalOutput")
    if not last:
        g1n = I("g1n", [128, 8]); w_in = I("w_in", [D, NIN])
        zT = nc.dram_tensor("zT", [NIN, T], F32, kind="ExternalOutput")
    with ExitStack() as es:
        p = Prog(nc, es)
        c = consts(p)
        iD = Buf(None); oD = Buf(None)
        def ld(name, shape, src, q="pool"):
            b = p.sbuf(shape, F32, name); p.dma(q, b[tuple(slice(None) for _ in shape)], src, [iD], [b]); return b
        glag = ld("glag", [128, 1], gla_g[:, :]); subg = ld("subg", [128, 1], sub_g[:, :]); lams = ld("lams", [128, 256], lamb[:, :]); skp = ld("skp", [128, 4], skip[:, :])
        g2s = ld("g2s", [128, 8], g2[:, :])
        if not last: g1s = ld("g1s", [128, 8], g1n[:, :])
        tmp = [p.sbuf([128, 512], F32, f"tmp{i}") for i in range(2)]
        rstds = [p.sbuf([128, 512], F32, f"rstd{i}") for i in range(4)]
        mb = [p.sbuf([128, T], BF16, f"mb{i}") for i in range(8)]
        lcol = p.sbuf([128, 4], F32, "lcol"); lpr = p.sbuf([128, 128], F32, "lpr")
        p.op("dve", [lams], [lpr], lambda e: e.tensor_tensor(out=lpr[:, 0:64], in0=lams[:, 0:64], in1=lams[:, 64:128], op=ALU.mult))
        p.op("dve", [lams], [lpr], lambda e: e.tensor_tensor(out=lpr[:, 64:128], in0=lams[:, 128:192], in1=lams[:, 192:256], op=ALU.mult))
        p.op("dve", [lpr], [lcol], lambda e: e.tensor_reduce(out=lcol[:, 0:1], in_=lpr[:, 0:64], axis=AX.X, op=ALU.add))
        p.op("dve", [lpr], [lcol], lambda e: e.tensor_reduce(out=lcol[:, 1:2], in_=lpr[:, 64:128], axis=AX.X, op=ALU.add))
        p.op("act", [lcol], [lcol], lambda e: e.activation(out=lcol[:, 0:2], in_=lcol[:, 0:2], func=AF.Exp))
        p.op("dve", [lcol], [lcol], lambda e: e.tensor_tensor(out=lcol[:, 2:3], in0=lcol[:, 0:1], in1=lcol[:, 1:2], op=ALU.subtract))
        p.op("dve", [lcol], [lcol], lambda e: e.tensor_scalar(out=lcol[:, 3:4], in0=lcol[:, 2:3], scalar1=float(lam_init), scalar2=-1.0, op0=ALU.add, op1=ALU.mult))
        sc1 = p.scope(); sc1.__enter__()
        ybf = [p.sbuf([128, T], BF16, f"ybf{i}") for i in range(14)]
        A = [p.sbuf([128, T], F32, f"A{i}") for i in range(2)]; B = [p.sbuf([128, T], F32, f"B{i}") for i in range(2)]; Cc = [p.sbuf([128, T], F32, f"C{i}") for i in range(2)]
        def rms128(src, gcol, dst, extra_scale=None):
            for tb in range(T // 512):
                sl = slice(tb * 512, (tb + 1) * 512); s = tmp[tb % 2]; r = rstds[tb % 4]; ps = c["ps"][tb % 2]
                p.op("act", [src], [s], lambda e: e.activation(out=s[:, :], in_=src[:, sl], func=AF.Square))
                p.op("pe", [c["ones"], s], [ps], lambda e: e.matmul(ps[:, :], lhsT=c["ones"][:, :], rhs=s[:, :], start=True, stop=True))
                p.op("act", [ps, c["eps"]], [r], lambda e: e.activation(out=r[:, :], in_=ps[:, :], func=AF.Sqrt, bias=c["eps"][:, 0:1], scale=1.0 / 128))
                p.op("dve", [r], [r], lambda e: e.reciprocal(out=r[:, :], in_=r[:, :]))
                p.op("dve", [src, r, gcol], [dst], lambda e: e.scalar_tensor_tensor(out=dst[:, sl], in0=src[:, sl], scalar=gcol[:, 0:1], in1=r[:, :], op0=ALU.mult, op1=ALU.mult))
        for h in range(4):
            hs = slice(h * 128, (h + 1) * 128); a = A[h % 2]; b = B[h % 2]; cc = Cc[h % 2]
            p.dma("sp", a[:, :], of[hs, :], [iD], [a]); p.dma("sp", b[:, :], ob[hs, :], [iD], [b]); p.dma("sp", cc[:, :], rT[hs, :], [iD], [cc])
            p.op("pool", [a, b], [a], lambda e: e.tensor_tensor(out=a[:, :], in0=a[:, :], in1=b[:, :], op=ALU.add))
            rms128(a, glag, b)
            p.op("act", [cc], [cc], lambda e: e.activation(out=cc[:, :], in_=cc[:, :], func=AF.Silu))
            p.op("dve", [b, cc], [ybf[h]], lambda e: e.tensor_tensor(out=ybf[h][:, :], in0=b[:, :], in1=cc[:, :], op=ALU.mult))
        for h in range(4):
            hs = slice(h * 128, (h + 1) * 128); a = A[h % 2]; b = B[h % 2]; cc = Cc[h % 2]
            p.dma("sp", a[:, :], oc0[hs, :], [iD], [a]); p.dma("sp", b[:, :], oc1[hs, :], [iD], [b])
            p.op("dve", [a, b, lcol], [a], lambda e: e.scalar_tensor_tensor(out=a[:, :], in0=b[:, :], scalar=lcol[:, 3:4], in1=a[:, :], op0=ALU.mult, op1=ALU.add))
            rms128(a, subg, cc)
            p.op("pool", [cc], [ybf[6 + h]], lambda e: e.tensor_scalar(out=ybf[6 + h][:, :], in0=cc[:, :], scalar1=float(1.0 - lam_init), scalar2=None, op0=ALU.mult))
        for h in range(4):
            hs = slice(h * 128, (h + 1) * 128); a = A[h % 2]; b = B[h % 2]; cc = Cc[h % 2]
            p.dma("sp", a[:, :], yconv[hs, :], [iD], [a]); p.dma("sp", b[:, :], zzT[hs, :], [iD], [b]); p.dma("sp", cc[:, :], x0c[hs, :], [iD], [cc])
            p.op("dve", [a, b, skp], [a], lambda e: e.scalar_tensor_tensor(out=a[:, :], in0=b[:, :], scalar=skp[:, h:h + 1], in1=a[:, :], op0=ALU.mult, op1=ALU.add))
            p.op("pool", [a, cc], [ybf[10 + h]], lambda e: e.tensor_tensor(out=ybf[10 + h][:, :], in0=a[:, :], in1=cc[:, :], op=ALU.mult))
        for hh in range(2):
            hs = slice(hh * 128, (hh + 1) * 128); a = A[hh % 2]; b = B[hh % 2]; cc = Cc[hh % 2]
            p.dma("sp", a[:, :], dN[0, hs, :], [iD], [a]); p.dma("sp", b[:, :], dL[0, hs, :], [iD], [b])
            for g in (1, 2):
                p.dma("sp", cc[:, :], dN[g, hs, :], [iD], [cc])
                p.op("dve", [a, cc], [a], lambda e: e.tensor_tensor(out=a[:, :], in0=a[:, :], in1=cc[:, :], op=ALU.add))
                p.dma("sp", cc[:, :], dL[g, hs, :], [iD], [cc])
                p.op("dve", [b, cc], [b], lambda e: e.tensor_tensor(out=b[:, :], in0=b[:, :], in1=cc[:, :], op=ALU.add))
            p.op("dve", [b], [b], lambda e: e.reciprocal(out=b[:, :], in_=b[:, :]))
            p.op("dve", [a, b], [ybf[4 + hh]], lambda e: e.tensor_tensor(out=ybf[4 + hh][:, :], in0=a[:, :], in1=b[:, :], op=ALU.mult))
        pst = [p.sbuf([128, D], F32, f"pst{i}") for i in range(2)]; pbf = p.sbuf([128, 14, D], BF16, "pbf")
        for kc_ in range(14):
            p.dma("sp", pst[kc_ % 2][:, :], projs[kc_ * 128:(kc_ + 1) * 128, :], [iD], [pst[kc_ % 2]])
            p.op("pool", [pst[kc_ % 2]], [pbf], lambda e: e.tensor_copy(out=pbf[:, kc_, :], in_=pst[kc_ % 2][:, :]))
        branches = [(0, 4), (4, 6), (6, 10), (10, 14)]
        gst = A + B
        macc = Cc
        k = 0; kk = 0
        for mi in range(8):
            ms = slice(mi * 128, (mi + 1) * 128); ma = macc[mi % 2]
            for bi, (k0, k1) in enumerate(branches):
                gt = gst[k % 4]; k += 1
                p.dma("sp", gt[:, :], gates[bi * 1024 + mi * 128:bi * 1024 + (mi + 1) * 128, :], [iD], [gt])
                p.op("act", [gt], [gt], lambda e: e.activation(out=gt[:, :], in_=gt[:, :], func=AF.Sigmoid))
                for tb in range(4):
                    sl = slice(tb * 512, (tb + 1) * 512); ps = c["ps"][2 + kk % 4]; kk += 1
                    for kc_ in range(k0, k1):
                        p.op("pe", [pbf, ybf[kc_]], [ps], lambda e: e.matmul(ps[:, :], lhsT=pbf[:, kc_, ms], rhs=ybf[kc_][:, sl], start=(kc_ == k0), stop=(kc_ == k1 - 1)), same_ok=True)
                    if bi == 0:
                        p.op("dve", [ps, gt], [ma], lambda e: e.tensor_tensor(out=ma[:, sl], in0=ps[:, :], in1=gt[:, sl], op=ALU.mult))
                    else:
                        t = tmp[kk % 2]
                        p.op("dve", [ps, gt], [t], lambda e: e.tensor_tensor(out=t[:, :], in0=ps[:, :], in1=gt[:, sl], op=ALU.mult))
                        p.op("pool", [t, ma], [ma], lambda e: e.tensor_tensor(out=ma[:, sl], in0=ma[:, sl], in1=t[:, :], op=ALU.add))
            p.op("act", [ma], [mb[mi]], lambda e: e.copy(out=mb[mi][:, :], in_=ma[:, :]))
        sc1.__exit__(None, None, None)
        xs = [p.sbuf([128, T], F32, f"x{i}") for i in range(8)]
        for i in range(8):
            p.dma("pool", xs[i][:, :], xT[i * 128:(i + 1) * 128, :], [iD], [xs[i]])
        GW = 256
        wst = [p.sbuf([128, 8, GW], F32, f"wst{i}") for i in range(2)]; wbf = [p.sbuf([128, 8, GW], BF16, f"wbf{i}") for i in range(2)]
        def evac_addx(m0, mw, tb, ps):
            ci = m0 // 128; sl = slice(tb * 512, (tb + 1) * 512)
            p.op("dve", [ps, xs[ci]], [xs[ci]], lambda e: e.tensor_tensor(out=xs[ci][:, sl], in0=xs[ci][:, sl], in1=ps[:, :], op=ALU.add))
        linear2(p, c, w_out, D, D, lambda ch, tb: (mb[ch], mb[ch][:, tb * 512:(tb + 1) * 512]), 4, evac_addx, wst, wbf, GW)
        hs_ = mb
        rms_fm(p, c, xs, g2s, hs_, 8, T, tmp, rstds)
        sc2 = p.scope(); sc2.__enter__()
        ub = [p.sbuf([128, 512], BF16, f"ub{i}") for i in range(32)]
        w2st = [p.sbuf([128, 32, 128], F32, "w2st0")]; w2bf = [p.sbuf([128, 32, 128], BF16, f"w2bf{i}") for i in range(2)]
        for tb in range(4):
            sl = slice(tb * 512, (tb + 1) * 512)
            def evac_u(m0, mw, tb_, ps):
                mi = m0 // 128; t = tmp[mi % 2]
                p.op("act", [ps], [t], lambda e: e.activation(out=t[:, :], in_=ps[:, :], func=AF.Relu))
                p.op("dve", [t], [ub[mi]], lambda e: e.tensor_tensor(out=ub[mi][:, :], in0=t[:, :], in1=t[:, :], op=ALU.mult))
            linear2(p, c, w1, D, DFF, lambda ch, tb_: (hs_[ch], hs_[ch][:, sl]), 1, evac_u, wst, wbf, GW)
            def evac_x2(m0, mw, tb_, ps):
                ci = m0 // 128
                p.op("dve", [ps, xs[ci]], [xs[ci]], lambda e: e.tensor_tensor(out=xs[ci][:mw, sl], in0=xs[ci][:mw, sl], in1=ps[:mw, :], op=ALU.add))
            linear2(p, c, w2, DFF, D, lambda ch, tb_: (ub[ch], ub[ch][:, :]), 1, evac_x2, w2st, w2bf, 128)
        sc2.__exit__(None, None, None)
        for i in range(8):
            p.dma("act", xo[i * 128:(i + 1) * 128, :], xs[i][:, :], [xs[i]], [oD])
        if not last:
            rms_fm(p, c, xs, g1s, hs_, 8, T, tmp, rstds)
            ost = [p.sbuf([128, T], F32, f"ost{i}") for i in range(2)]
            state = {"k": 0}
            def evac_z(m0, mw, tb, ps):
                o = ost[state["k"] % 2]
                p.op("act", [ps], [o], lambda e: e.copy(out=o[:mw, tb * 512:(tb + 1) * 512], in_=ps[:mw, :]))
                if tb == 3:
                    p.dma("act", zT[m0:m0 + mw, :], o[:mw, :], [o], [oD]); state["k"] += 1
            linear2(p, c, w_in, D, NIN, lambda ch, tb: (hs_[ch], hs_[ch][:, tb * 512:(tb + 1) * 512]), 4, evac_z, wst, wbf, GW)
        p.finish([oD])
        print("k3 ninst", p.ninst)
    return nc


def build_B():
    nc = bass.Bass("TRN2", target_bir_lowering=False)
    with ExitStack() as es:
        p = Prog(nc, es)
        c = consts(p)
        for pre, emit in (("a_", emit_k2a), ("d_", emit_k2d), ("b_", emit_k2b), ("c_", emit_k2c)):
            p.prefix = pre
            with p.scope():
                emit(nc, p, c, pre)
        p.prefix = ""
        p.finish([])
        print("B ninst", p.ninst)
    return nc


import math as _math
_NC_CACHE = {}
def _get_nc(key, builder):
    if key not in _NC_CACHE:
        _NC_CACHE[key] = builder()
    return _NC_CACHE[key]

def _run(nc, maps):
    return run_bass_kernel_spmd(nc, maps, core_ids=list(range(8))).results

def kernel(**inputs):
    inp = {k: np.asarray(v) for k, v in inputs.items()}
    x = inp["x"][0].astype(np.float32)
    NCORE = 8; Tn = S // NCORE
    xT = np.ascontiguousarray(x.T)
    g = np.ascontiguousarray(inp["norm1_g"][0].reshape(8, 128).T)
    maps = [{"xT": np.ascontiguousarray(xT[:, r * Tn:(r + 1) * Tn]), "g": g, "w": np.ascontiguousarray(inp["w_in"][0])} for r in range(NCORE)]
    res = _run(_get_nc("k1", build_k1), maps)
    zT = np.concatenate([r["zT"] for r in res], axis=1)
    for layer in range(4):
        last = layer == 3
        lam_init = 0.8 - 0.6 * _math.exp(-0.3 * layer)
        ma = host_inputs_k2a(zT, inp, layer); mb_ = host_inputs_k2b(zT, inp, layer); mc = host_inputs_k2c(zT, inp, layer); md = host_inputs_k2d(zT, inp, layer)
        mapsB = []
        for r in range(NCORE):
            m = {}
            for pre, mm in (("a_", ma), ("b_", mb_), ("c_", mc), ("d_", md)):
                m.update({pre + k: v for k, v in mm[r].items()})
            mapsB.append(m)
        rB = _run(_get_nc("B", build_B), mapsB)
        ra = [{"oT": r["a_oT"]} for r in rB]; rb = [{"O": r["b_O"]} for r in rB]; rc = [{"oT": r["c_oT"]} for r in rB]
        rd = [{"Y": r["d_Y"], "zz": r["d_zz"], "x0c": r["d_x0c"]} for r in rB]
        of = np.concatenate([ra[2 * h]["oT"] for h in range(4)], 0); ob = np.concatenate([ra[2 * h + 1]["oT"][:, ::-1] for h in range(4)], 0)
        dN, dL = host_post_k2b(rb)
        oc0 = np.concatenate([rc[2 * h]["oT"] for h in range(4)], 0); oc1 = np.concatenate([rc[2 * h + 1]["oT"] for h in range(4)], 0)
        yconv, zzf, x0c = host_post_k2d(rd)
        projs = np.ascontiguousarray(np.concatenate([inp["proj_a"][layer], inp["proj_b"][layer], inp["proj_c"][layer], inp["proj_d"][layer]], 0))
        common = {"gla_g": np.ascontiguousarray(inp["gla_norm_g"][layer].reshape(128, 1)),
                  "lamb": np.ascontiguousarray(np.broadcast_to(inp["diff_lambda"][layer].reshape(1, 256), (128, 256))),
                  "sub_g": np.ascontiguousarray(inp["diff_subln_g"][layer].reshape(128, 1)),
                  "skip": np.ascontiguousarray(inp["hy_skip"][layer].reshape(4, 128).T),
                  "projs": projs, "w_out": np.ascontiguousarray(inp["w_out"][layer]),
                  "g2": np.ascontiguousarray(inp["norm2_g"][layer].reshape(8, 128).T),
                  "w1": np.ascontiguousarray(inp["mlp_w1"][layer]), "w2": np.ascontiguousarray(inp["mlp_w2"][layer])}
        if not last:
            common["g1n"] = np.ascontiguousarray(inp["norm1_g"][layer + 1].reshape(8, 128).T)
            common["w_in"] = np.ascontiguousarray(inp["w_in"][layer + 1])
        maps = []
        for r in range(NCORE):
            cs = slice(r * Tn, (r + 1) * Tn)
            cc = lambda a: np.ascontiguousarray(a[..., cs])
            m = dict(common)
            m.update({"xT": cc(xT), "of": cc(of), "ob": cc(ob), "rT": cc(zT[1024:1536]), "dN": cc(dN), "dL": cc(dL),
                      "oc0": cc(oc0), "oc1": cc(oc1), "yconv": cc(yconv), "zzT": cc(zzf), "x0c": cc(x0c), "gates": cc(zT[6944:11040])})
            maps.append(m)
        res = _run(_get_nc(("k3", layer), lambda: build_k3(lam_init, last)), maps)
        xT = np.concatenate([r["xo"] for r in res], axis=1)
        if not last:
            zT = np.concatenate([r["zT"] for r in res], axis=1)
    return np.ascontiguousarray(xT.T)[None].astype(np.float32)
```

```python
import numpy as np
from contextlib import ExitStack
import concourse.bass as bass
import concourse.mybir as mybir
from concourse.bass_utils import run_bass_kernel_spmd

F32 = mybir.dt.float32
BF16 = mybir.dt.bfloat16
AF = mybir.ActivationFunctionType
ALU = mybir.AluOpType
AX = mybir.AxisListType

class Buf:
    __slots__ = ("t", "w", "r", "name")
    def __init__(self, t, name=""):
        self.t = t; self.w = None; self.r = {}; self.name = name
    def __getitem__(self, idx):
        return self.t[idx]

class Prog:
    NDSEM = 6
    def __init__(self, nc, es):
        self.nc = nc; self.es = es
        self.eng = {"pe": nc.tensor, "act": nc.scalar, "dve": nc.vector, "pool": nc.gpsimd, "sp": nc.sync}
        self.sem = {}; self.cnt = {}
        for e in self.eng:
            self.sem[e] = es.enter_context(nc.semaphore("s_" + e)); self.cnt[e] = 0
        self.seen = {e: {} for e in self.eng}
        self.dq = {}
        for q in ("sp", "pool", "act"):
            self.dq[q] = dict(n=0, sems=[es.enter_context(nc.semaphore(f"d_{q}{i}")) for i in range(self.NDSEM)])
            for i in range(self.NDSEM):
                self.sem[(q, i)] = self.dq[q]["sems"][i]
        self.ninst = 0
        self.prefix = ""
    def sbuf(self, shape, dt, name):
        return Buf(self.es.enter_context(self.nc.sbuf_tensor(self.prefix + name, list(shape), dt)), name)
    def psum(self, shape, dt, name):
        return Buf(self.es.enter_context(self.nc.psum_tensor(name, list(shape), dt)), name)
    def dram(self, name, shape, dt, kind="Internal"):
        return Buf(self.nc.dram_tensor(name, list(shape), dt, kind=kind), name)
    def _wait(self, e, tok):
        if tok is None: return
        s, v = tok
        if self.seen[e].get(s, 0) >= v: return
        self.eng[e].wait_ge(self.sem[s], v)
        self.seen[e][s] = v
    def _deps(self, e, reads, writes, same_ok=False):
        for b in reads:
            if b.w is not None and not (same_ok and b.w[0] == e):
                self._wait(e, b.w)
        for b in writes:
            if b.w is not None and not (same_ok and b.w[0] == e):
                self._wait(e, b.w)
            for s, v in b.r.items():
                if same_ok and s == e: continue
                self._wait(e, (s, v))
    def _mark(self, tok, reads, writes):
        for b in reads:
            if b.r.get(tok[0], 0) < tok[1]: b.r[tok[0]] = tok[1]
        for b in writes:
            b.w = tok; b.r = {}
    def op(self, e, reads, writes, fn, same_ok=False):
        self._deps(e, reads, writes, same_ok)
        ins = fn(self.eng[e])
        self.cnt[e] += 1
        ins.then_inc(self.sem[e], 1)
        tok = (e, self.cnt[e])
        self.seen[e][e] = max(self.seen[e].get(e, 0), 0)
        self._mark(tok, reads, writes)
        self.ninst += 1
        return tok
    def dma(self, q, out_ap, in_ap, reads, writes, **kw):
        d = self.dq[q]; j = d["n"]; slot = j % self.NDSEM; val = 16 * (j // self.NDSEM + 1)
        if j >= self.NDSEM:
            self._wait(q, ((q, slot), val - 16))
        self._deps(q, reads, writes)
        ins = self.eng[q].dma_start(out=out_ap, in_=in_ap, **kw)
        ins.then_inc(d["sems"][slot], 16)
        d["n"] += 1
        tok = ((q, slot), val)
        self._mark(tok, reads, writes)
        self.ninst += 1
        return tok
    def finish(self, bufs):
        for b in bufs:
            self._wait("sp", b.w)
        for q in self.dq:
            d = self.dq[q]
            for j in range(max(0, d["n"] - self.NDSEM), d["n"]):
                self._wait("sp", ((q, j % self.NDSEM), 16 * (j // self.NDSEM + 1)))

def _barrier(self):
    for e in self.eng:
        for e2 in self.eng:
            if e2 != e and self.cnt[e2] > 0:
                self._wait(e, (e2, self.cnt[e2]))
        for q, d in self.dq.items():
            for j in range(max(0, d["n"] - self.NDSEM), d["n"]):
                self._wait(e, ((q, j % self.NDSEM), 16 * (j // self.NDSEM + 1)))
Prog.barrier = _barrier

class _Scope:
    def __init__(self, p): self.p = p
    def __enter__(self):
        self.old = self.p.es; self.es = ExitStack(); self.es.__enter__(); self.p.es = self.es; return self
    def __exit__(self, *a):
        self.p.barrier(); self.p.es = self.old; return self.es.__exit__(*a)
Prog.scope = lambda self: _Scope(self)

S = 16384; NC = 8; T = S // NC; D = 1024; NIN = 11040
EPS = 1e-6

def consts(p):
    c = {}
    c["ones"] = p.sbuf([128, 128], F32, "c_ones")
    p.op("pool", [], [c["ones"]], lambda e: e.memset(c["ones"][:, :], 1.0))
    c["eps"] = p.sbuf([128, 1], F32, "c_eps")
    p.op("pool", [], [c["eps"]], lambda e: e.memset(c["eps"][:, :], EPS))
    c["ps"] = [p.psum([128, 512], F32, f"ps{i}") for i in range(8)]
    return c

def rms_fm(p, c, xs, gs, hs, nch, Tn, tmp, rstds, psi=0):
    nfeat = nch * 128
    k = 0
    for tb in range(Tn // 512):
        sl = slice(tb * 512, (tb + 1) * 512)
        ps = c["ps"][psi + tb % 2]
        for ch in range(nch):
            s = tmp[k % 2]; k += 1
            if ch % 2 == 0:
                p.op("act", [xs[ch]], [s], lambda e: e.activation(out=s[:, :], in_=xs[ch][:, sl], func=AF.Square))
            else:
                p.op("dve", [xs[ch]], [s], lambda e: e.tensor_tensor(out=s[:, :], in0=xs[ch][:, sl], in1=xs[ch][:, sl], op=ALU.mult))
            p.op("pe", [c["ones"], s], [ps], lambda e: e.matmul(ps[:, :], lhsT=c["ones"][:, :], rhs=s[:, :], start=(ch == 0), stop=(ch == nch - 1)), same_ok=True)
        r = rstds[tb]
        p.op("act", [ps, c["eps"]], [r], lambda e: e.activation(out=r[:, :], in_=ps[:, :], func=AF.Sqrt, bias=c["eps"][:, 0:1], scale=1.0 / nfeat))
        p.op("dve", [r], [r], lambda e: e.reciprocal(out=r[:, :], in_=r[:, :]))
        for ch in range(nch):
            eng = "dve"
            p.op(eng, [xs[ch], r, gs], [hs[ch]], lambda e: e.scalar_tensor_tensor(out=hs[ch][:, sl], in0=xs[ch][:, sl], scalar=gs[:, ch:ch + 1], in1=r[:, :], op0=ALU.mult, op1=ALU.mult))

def linear_stream(p, c, w_ap, K, N, hs, Tn, evac, wst, wbf, psi=2, npsi=4, gw=512):
    kc = K // 128
    wD = Buf(None)
    ng = (N + gw - 1) // gw
    k2 = 0
    for gi in range(ng):
        c0 = gi * gw; wg = min(gw, N - c0)
        st = wst[gi % 2]; wb = wbf[gi % 2]
        p.dma("sp", st[:, :kc, :wg], w_ap[:, c0:c0 + wg].rearrange("(c p) n -> p c n", p=128), [wD], [st])
        h = kc // 2
        p.op("pool", [st], [wb], lambda e: e.tensor_copy(out=wb[:, :h, :wg], in_=st[:, :h, :wg]))
        p.op("dve", [st], [wb], lambda e: e.tensor_copy(out=wb[:, h:kc, :wg], in_=st[:, h:kc, :wg]))
        for mi in range((wg + 127) // 128):
            mw = min(128, wg - mi * 128)
            for tb in range(Tn // 512):
                ps = c["ps"][psi + k2 % npsi]; k2 += 1
                for ch in range(kc):
                    p.op("pe", [wb, hs[ch]], [ps], lambda e: e.matmul(ps[:mw, :], lhsT=wb[:, ch, mi * 128:mi * 128 + mw], rhs=hs[ch][:, tb * 512:(tb + 1) * 512], start=(ch == 0), stop=(ch == kc - 1)), same_ok=True)
                evac(c0 + mi * 128, mw, tb, ps)

def build_k1():
    nc = bass.Bass("TRN2", target_bir_lowering=False)
    xT = nc.dram_tensor("xT", [D, T], F32, kind="ExternalInput")
    g = nc.dram_tensor("g", [128, 8], F32, kind="ExternalInput")
    w = nc.dram_tensor("w", [D, NIN], F32, kind="ExternalInput")
    zT = nc.dram_tensor("zT", [NIN, T], F32, kind="ExternalOutput")
    with ExitStack() as es:
        p = Prog(nc, es)
        c = consts(p)
        xs = [p.sbuf([128, T], F32, f"x{i}") for i in range(8)]
        hs = [p.sbuf([128, T], BF16, f"h{i}") for i in range(8)]
        gs = p.sbuf([128, 8], F32, "gs")
        xD = Buf(None); zD = Buf(None)
        p.dma("pool", gs[:, :], g[:, :], [xD], [gs])
        for i in range(8):
            p.dma("pool", xs[i][:, :], xT[i * 128:(i + 1) * 128, :], [xD], [xs[i]])
        tmp = [p.sbuf([128, 512], F32, f"tmp{i}") for i in range(2)]
        rstds = [p.sbuf([128, 512], F32, f"rstd{i}") for i in range(T // 512)]
        rms_fm(p, c, xs, gs, hs, 8, T, tmp, rstds)
        wst = [p.sbuf([128, 8, 512], F32, f"wst{i}") for i in range(2)]
        wbf = [p.sbuf([128, 8, 512], BF16, f"wbf{i}") for i in range(2)]
        ost = [p.sbuf([128, T], F32, f"ost{i}") for i in range(2)]
        state = {"k": 0}
        def evac(m0, mw, tb, ps):
            o = ost[state["k"] % 2]
            p.op("act", [ps], [o], lambda e: e.copy(out=o[:mw, tb * 512:(tb + 1) * 512], in_=ps[:mw, :]))
            if tb == T // 512 - 1:
                p.dma("act", zT[m0:m0 + mw, :], o[:mw, :], [o], [zD])
                state["k"] += 1
        linear_stream(p, c, w.ap() if hasattr(w, "ap") else w, D, NIN, hs, T, evac, wst, wbf)
        p.finish([zD])
        print("k1 ninst", p.ninst)
    return nc


S = 16384
NEAR_LO, NEAR_HI = -5, 8
RW = 2304

def t5_bucket_np(rel):
    import math
    half = 16; max_exact = 8
    ret = np.where(rel > 0, half, 0)
    n = np.abs(rel)
    nf = np.maximum(n, 1).astype(np.float32)
    large = max_exact + (np.log(nf / max_exact) / np.float32(math.log(1024 / max_exact)) * (half - max_exact)).astype(np.int32)
    large = np.minimum(large, half - 1)
    return ret + np.where(n < max_exact, n, large)

def qknorm_fm(p, c, src, gcol, dst, rows, Tn, tmp, rtmp, psi):
    for tb in range(Tn // 512):
        sl = slice(tb * 512, (tb + 1) * 512)
        s = tmp[tb % 2]; r = rtmp[tb % 2]; ps = c["ps"][psi + tb % 2]
        p.op("act", [src], [s], lambda e: e.activation(out=s[:rows, :], in_=src[:rows, sl], func=AF.Square))
        p.op("pe", [c["ones"], s], [ps], lambda e: e.matmul(ps[:rows, :], lhsT=c["ones"][:rows, :rows], rhs=s[:rows, :], start=True, stop=True))
        p.op("act", [ps, c["eps"]], [r], lambda e: e.activation(out=r[:rows, :], in_=ps[:rows, :], func=AF.Sqrt, bias=c["eps"][:rows, 0:1], scale=1.0 / rows))
        p.op("dve", [r], [r], lambda e: e.reciprocal(out=r[:rows, :], in_=r[:rows, :]))
        p.op("dve", [src, r, gcol], [dst], lambda e: e.scalar_tensor_tensor(out=dst[:rows, sl], in0=src[:rows, sl], scalar=gcol[:rows, 0:1], in1=r[:rows, :], op0=ALU.mult, op1=ALU.mult))

def emit_k2c(nc, p, c, pre, NQB=S // 512, NKT=S // 128):
    qT = nc.dram_tensor(pre + "qT", [64, S], F32, kind="ExternalInput")
    kT = nc.dram_tensor(pre + "kT", [64, S], F32, kind="ExternalInput")
    v = nc.dram_tensor(pre + "v", [128, S // 128, 128], F32, kind="ExternalInput")
    gq = nc.dram_tensor(pre + "gq", [64, 1], F32, kind="ExternalInput")
    gk = nc.dram_tensor(pre + "gk", [64, 1], F32, kind="ExternalInput")
    Rb = nc.dram_tensor(pre + "Rb", [128, RW], F32, kind="ExternalInput")
    bfar = nc.dram_tensor(pre + "bfar", [128, 2], F32, kind="ExternalInput")
    oT = nc.dram_tensor(pre + "oT", [128, S], F32, kind="ExternalOutput")
    if True:
        iD = Buf(None); oD = Buf(None)
        qb16 = p.sbuf([128, S], BF16, "qb16"); kb16 = p.sbuf([128, S], BF16, "kb16")
        for b_ in (qb16, kb16):
            for cb_ in range(S // 2048):
                p.op("pool", [], [b_], lambda e: e.memset(b_[64:128, cb_ * 2048:(cb_ + 1) * 2048], 0.0))
        vb16 = p.sbuf([128, S // 128, 128], BF16, "vb16")
        rb = p.sbuf([128, RW], F32, "rb"); bf_ = p.sbuf([128, 2], F32, "bfar_s")
        gqs = p.sbuf([64, 1], F32, "gqs"); gks = p.sbuf([64, 1], F32, "gks")
        p.dma("pool", rb[:, :], Rb[:, :], [iD], [rb]); p.dma("pool", bf_[:, :], bfar[:, :], [iD], [bf_])
        p.dma("pool", gqs[:, :], gq[:, :], [iD], [gqs]); p.dma("pool", gks[:, :], gk[:, :], [iD], [gks])
        tmp = [p.sbuf([128, 512], F32, f"tmp{i}") for i in range(2)]
        rtmp = [p.sbuf([128, 512], F32, f"rtmp{i}") for i in range(2)]
        stg = [p.sbuf([128, 4096], F32, f"stg{i}") for i in range(2)]
        k = 0
        for (src, gcol, dst) in ((qT, gqs, qb16), (kT, gks, kb16)):
            for blk in range(S // 4096):
                st = stg[k % 2]; k += 1
                p.dma("sp", st[:64, :], src[:, blk * 4096:(blk + 1) * 4096], [iD], [st])
                class V:
                    pass
                for tb in range(8):
                    sl = slice(tb * 512, (tb + 1) * 512); dsl = slice(blk * 4096 + tb * 512, blk * 4096 + (tb + 1) * 512)
                    s = tmp[tb % 2]; r = rtmp[tb % 2]; ps = c["ps"][6 + tb % 2]
                    p.op("act", [st], [s], lambda e: e.activation(out=s[:64, :], in_=st[:64, sl], func=AF.Square))
                    p.op("pe", [c["ones"], s], [ps], lambda e: e.matmul(ps[:64, :], lhsT=c["ones"][:64, :64], rhs=s[:64, :], start=True, stop=True))
                    p.op("act", [ps, c["eps"]], [r], lambda e: e.activation(out=r[:64, :], in_=ps[:64, :], func=AF.Sqrt, bias=c["eps"][:64, 0:1], scale=1.0 / 64))
                    p.op("dve", [r], [r], lambda e: e.reciprocal(out=r[:64, :], in_=r[:64, :]))
                    p.op("dve", [st, r, gcol], [dst], lambda e: e.scalar_tensor_tensor(out=dst[:64, dsl], in0=st[:64, sl], scalar=gcol[:64, 0:1], in1=r[:64, :], op0=ALU.mult, op1=ALU.mult))
        for blk in range(S // 128 // 32):
            st = stg[k % 2]; k += 1
            p.dma("sp", st[:, :].rearrange("p (a b) -> p a b", b=128), v[:, blk * 32:(blk + 1) * 32, :], [iD], [st])
            p.op("pool", [st], [vb16], lambda e: e.tensor_copy(out=vb16[:, blk * 32:(blk + 1) * 32, :], in_=st[:, :].rearrange("p (a b) -> p a b", b=128)))
        pt = [p.sbuf([128, 512], BF16, f"pt{i}") for i in range(4)]
        nt = [p.sbuf([128, 512], F32, f"nt{i}") for i in range(2)]
        lacc = [p.sbuf([128, 512], F32, f"lacc{i}") for i in range(2)]; lacc2 = [p.sbuf([128, 512], F32, f"laccp{i}") for i in range(2)]
        rl = [p.sbuf([128, 512], F32, f"rl{i}") for i in range(2)]
        ob = [p.sbuf([128, 512], F32, f"ob{i}") for i in range(2)]
        LOOK = 2
        onesb = p.sbuf([128, 128], BF16, "onesb"); p.op("pool", [], [onesb], lambda e: e.memset(onesb[:, :], 1.0))
        tiles = [(qb, j) for qb in range(NQB) for j in range(NKT)]
        NTL = len(tiles)
        nn = 0
        def front(t):
            nonlocal nn
            qb, j = tiles[t]
            qsl = slice(qb * 512, (qb + 1) * 512)
            ps = c["ps"][t % 4]; P = pt[t % 4]
            p.op("pe", [kb16, qb16], [ps], lambda e: e.matmul(ps[:, :], lhsT=kb16[:, j * 128:(j + 1) * 128], rhs=qb16[:, qsl], start=True, stop=True))
            m = j - 4 * qb
            if NEAR_LO <= m <= NEAR_HI:
                tt_ = nt[nn % 2]; nn += 1
                off = 1024 - 128 * m
                p.op("dve", [ps, rb], [tt_], lambda e: e.scalar_tensor_tensor(out=tt_[:, :], in0=ps[:, :], scalar=0.125, in1=rb[:, off:off + 512], op0=ALU.mult, op1=ALU.add))
                p.op("act", [tt_], [P], lambda e: e.activation(out=P[:, :], in_=tt_[:, :], func=AF.Exp))
            else:
                side = 0 if m < 0 else 1
                p.op("act", [ps, bf_], [P], lambda e: e.activation(out=P[:, :], in_=ps[:, :], func=AF.Exp, bias=bf_[:, side:side + 1], scale=0.125))
        def back(t):
            qb, j = tiles[t]
            qsl = slice(qb * 512, (qb + 1) * 512)
            P = pt[t % 4]; po = c["ps"][4 + qb % 2]; la = lacc[qb % 2]
            p.op("pe", [vb16, P], [po], lambda e: e.matmul(po[:, :], lhsT=vb16[:, j, :], rhs=P[:, :], start=(j == 0), stop=(j == NKT - 1)), same_ok=True)
            pl = c["ps"][6 + qb % 2]; r_ = rl[qb % 2]
            if j % 2 == 0:
                p.op("pe", [onesb, P], [pl], lambda e: e.matmul(pl[:, :], lhsT=onesb[:, :], rhs=P[:, :], start=(j == 0), stop=False), same_ok=True)
            elif j == 1:
                p.op("dve", [P], [la], lambda e: e.tensor_copy(out=la[:, :], in_=P[:, :]))
            else:
                p.op("dve", [P, la], [la], lambda e: e.tensor_tensor(out=la[:, :], in0=la[:, :], in1=P[:, :], op=ALU.add))
            if j == NKT - 1:
                p.op("pe", [c["ones"], la], [pl], lambda e: e.matmul(pl[:, :], lhsT=c["ones"][:, :], rhs=la[:, :], start=False, stop=True), same_ok=True)
                p.op("dve", [pl], [r_], lambda e: e.reciprocal(out=r_[:, :], in_=pl[:, :]))
                o = ob[qb % 2]
                p.op("dve", [po, r_], [o], lambda e: e.tensor_tensor(out=o[:, :], in0=po[:, :], in1=r_[:, :], op=ALU.mult))
                p.dma("pool", oT[:, qsl], o[:, :], [o], [oD])
        for t in range(NTL + LOOK):
            if t < NTL: front(t)
            if t >= LOOK: back(t - LOOK)
        print("k2c ninst", p.ninst)
    return None

def host_inputs_k2c(zT, inp, layer):
    CQ = 256 + 256 + 512 + 512 + 32 + 768 * 3
    CK = CQ + 512; CV = CK + 512
    t5 = inp["t5_bias"]
    kk = np.arange(128)[:, None]; jj = np.arange(RW)[None, :]
    bidx = t5_bucket_np(kk - jj + 1024)
    assert (t5_bucket_np(np.arange(-20000, -5 * 128 + 128 - 1 - 127 + 1)) == 15).all()
    maps = []
    for h in range(4):
        vT = zT[CV + h * 128:CV + (h + 1) * 128, :]
        vv = np.ascontiguousarray(vT.T.reshape(S // 128, 128, 128).transpose(1, 0, 2))
        Rb = np.ascontiguousarray(t5[:, 12 + h][bidx]).astype(np.float32)
        bfar = np.ascontiguousarray(np.broadcast_to(np.array([t5[15, 12 + h], t5[31, 12 + h]], np.float32)[None, :], (128, 2)))
        for cc in range(2):
            r0 = h * 128 + cc * 64
            maps.append({"qT": np.ascontiguousarray(zT[CQ + r0:CQ + r0 + 64]), "kT": np.ascontiguousarray(zT[CK + r0:CK + r0 + 64]), "v": vv,
                         "gq": np.ascontiguousarray(inp["diff_qnorm_g"][layer].reshape(64, 1)), "gk": np.ascontiguousarray(inp["diff_knorm_g"][layer].reshape(64, 1)),
                         "Rb": Rb, "bfar": bfar})
    return maps


S = 16384
NT = S // 128

def emit_gla_gen(nc, p, c, pre, NTL=NT, NB=16, banks=None):
    I = lambda n, s: nc.dram_tensor(pre + n, s, F32, kind="ExternalInput")
    qT = I("qT", [64, S]); kT = I("kT", [64, S]); ktm = I("ktm", [128, NT, 64]); vtm = I("vtm", [128, NT, 128])
    lrT = I("lrT", [16, S]); gw = I("gw", [16, 64]); gbb = I("gbb", [128, 64])
    tri = I("tri", [128, 128]); m1 = I("m1", [128, 128]); cind = I("cind", [128, 2]); maskA = I("maskA", [128, 128])
    oT = nc.dram_tensor(pre + "oT", [128, S], F32, kind="ExternalOutput")
    if True:
        iD = Buf(None); oD = Buf(None)
        def ld(name, shape, src, q="pool"):
            b = p.sbuf(shape, F32, name); p.dma(q, b[tuple(slice(None) for _ in shape)], src, [iD], [b]); return b
        gws = ld("gws", [16, 64], gw[:, :]); gbs = ld("gbs", [128, 64], gbb[:, :])
        tris = ld("tris", [128, 128], tri[:, :]); m1s = ld("m1s", [128, 128], m1[:, :]); cis = ld("cis", [128, 2], cind[:, :])
        mAs = ld("mAs", [128, 128], maskA[:, :])
        one1 = p.sbuf([128, 1], F32, "one1"); p.op("pool", [], [one1], lambda e: e.memset(one1[:, :], 1.0))
        St = p.sbuf([64, 128], F32, "St"); Sb = p.sbuf([64, 128], BF16, "Sb")
        p.op("pool", [], [St], lambda e: e.memset(St[:, :], 0.0)); p.op("pool", [], [Sb], lambda e: e.memset(Sb[:, :], 0.0))
        bk = banks or dict(pa=(0, 1), pb=(2, 3), pc=(4, 5), po=6, pf=7)
        lrst = [p.sbuf([16, NB * 128], F32, f"lrst{i}") for i in range(2)]
        def stg(name, shape): return [p.sbuf(shape, F32, f"{name}{i}") for i in range(2)]
        qst = stg("qst", [64, NB * 128]); kst = stg("kst", [64, NB * 128]); ktst = stg("ktst", [128, NB, 64]); vst = stg("vst", [128, NB, 128])
        vbf = [p.sbuf([128, NB, 128], BF16, f"vbf{i}") for i in range(2)]
        ost = [p.sbuf([128, NB * 128], F32, f"ost{i}") for i in range(2)]
        R = lambda n, shp, dt=F32, k=2: [p.sbuf(shp, dt, f"{n}{i}") for i in range(k)]
        xs = R("xs", [128, 64]); es_ = R("es", [128, 64]); ls = R("ls", [128, 64])
        eb = R("eb", [64, 128]); enb = R("enb", [64, 128]); ed = R("ed", [128, 64]); av = R("av", [64, 2])
        qg = R("qg", [64, 128], BF16); kg = R("kg", [64, 128], BF16); kd = R("kd", [128, 64], BF16); Am = R("Am", [128, 128], BF16)
        yield -1
        for i in range(NTL):
            blk, ib = divmod(i, NB); par = i % 2
            if ib == 0:
                bsl = slice(blk * NB * 128, (blk + 1) * NB * 128); tsl = slice(blk * NB, (blk + 1) * NB)
                p.dma("sp", lrst[blk % 2][:, :], lrT[:, bsl], [iD], [lrst[blk % 2]])
                p.dma("sp", qst[blk % 2][:, :], qT[:, bsl], [iD], [qst[blk % 2]])
                p.dma("sp", kst[blk % 2][:, :], kT[:, bsl], [iD], [kst[blk % 2]])
                p.dma("sp", ktst[blk % 2][:, :, :], ktm[:, tsl, :], [iD], [ktst[blk % 2]])
                p.dma("sp", vst[blk % 2][:, :, :], vtm[:, tsl, :], [iD], [vst[blk % 2]])
                vb = vbf[blk % 2]
                p.op("pool", [vst[blk % 2]], [vb], lambda e: e.tensor_copy(out=vb[:, :, :], in_=vst[blk % 2][:, :, :]))
            lrs = lrst[blk % 2]; q_s = qst[blk % 2]; k_s = kst[blk % 2]; kt_s = ktst[blk % 2]; vb = vbf[blk % 2]; o_s = ost[blk % 2]
            tsl = slice(i * 128, (i + 1) * 128); lsl = slice(ib * 128, (ib + 1) * 128)
            pa = c["ps"][bk["pa"][par % len(bk["pa"])]]; pb = c["ps"][bk["pb"][par % len(bk["pb"])]]; pc = c["ps"][bk["pc"][par % len(bk["pc"])]]; po = c["ps"][bk["po"]]; pf = c["ps"][bk["pf"]]
            x = xs[par]; e_ = es_[par]; l = ls[par]
            p.op("pe", [lrs, gws], [pa], lambda e: e.matmul(pa[:, 0:64], lhsT=lrs[:, lsl], rhs=gws[:, :], start=True, stop=True))
            p.op("dve", [pa, gbs], [x], lambda e: e.tensor_tensor(out=x[:, :], in0=pa[:, 0:64], in1=gbs[:, :], op=ALU.add))
            p.op("act", [x], [e_], lambda e: e.activation(out=e_[:, :], in_=x[:, :], func=AF.Exp, scale=-1.0))
            p.op("act", [e_, one1], [l], lambda e: e.activation(out=l[:, :], in_=e_[:, :], func=AF.Ln, bias=one1[:, 0:1], scale=1.0))
            p.op("pe", [l, tris], [pb], lambda e: e.matmul(pb[:64, 0:128], lhsT=l[:, :], rhs=tris[:, :], start=True, stop=True))
            p.op("pe", [l, cis], [pb], lambda e: e.matmul(pb[:64, 128:130], lhsT=l[:, :], rhs=cis[:, :], start=True, stop=True), same_ok=True)
            p.op("pe", [m1s, l], [pc], lambda e: e.matmul(pc[:, 0:64], lhsT=m1s[:, :], rhs=l[:, :], start=True, stop=True))
            p.op("act", [pb], [eb[par]], lambda e: e.activation(out=eb[par][:, :], in_=pb[:64, 0:128], func=AF.Exp, scale=-1.0 / 16))
            p.op("act", [pb], [enb[par]], lambda e: e.activation(out=enb[par][:, :], in_=pb[:64, 0:128], func=AF.Exp, scale=1.0 / 16))
            p.op("act", [pb], [av[par]], lambda e: e.activation(out=av[par][:, :], in_=pb[:64, 128:130], func=AF.Exp, scale=-1.0 / 16))
            p.op("act", [pc], [ed[par]], lambda e: e.activation(out=ed[par][:, :], in_=pc[:, 0:64], func=AF.Exp, scale=-1.0 / 16))
            p.op("dve", [q_s, eb[par]], [qg[par]], lambda e: e.scalar_tensor_tensor(out=qg[par][:, :], in0=q_s[:, lsl], scalar=0.125, in1=eb[par][:, :], op0=ALU.mult, op1=ALU.mult))
            p.op("dve", [k_s, enb[par]], [kg[par]], lambda e: e.tensor_tensor(out=kg[par][:, :], in0=k_s[:, lsl], in1=enb[par][:, :], op=ALU.mult))
            p.op("dve", [kt_s, ed[par]], [kd[par]], lambda e: e.tensor_tensor(out=kd[par][:, :], in0=kt_s[:, ib, :], in1=ed[par][:, :], op=ALU.mult))
            p.op("pe", [kg[par], qg[par]], [pc], lambda e: e.matmul(pc[:, 128:256], lhsT=kg[par][:, :], rhs=qg[par][:, :], start=True, stop=True))
            p.op("dve", [pc, mAs], [Am[par]], lambda e: e.tensor_tensor(out=Am[par][:, :], in0=pc[:, 128:256], in1=mAs[:, :], op=ALU.mult))
            p.op("pe", [vb, Am[par]], [po], lambda e: e.matmul(po[:, 0:128], lhsT=vb[:, ib, :], rhs=Am[par][:, :], start=True, stop=False))
            for ci in range(2):
                cs = slice(ci * 64, (ci + 1) * 64)
                p.op("pe", [Sb, qg[par]], [po], lambda e: e.matmul(po[:, cs], lhsT=Sb[:, :], rhs=qg[par][:, cs], start=False, stop=(ci == 1)), same_ok=True)
                p.op("pe", [kd[par], vb], [pf], lambda e: e.matmul(pf[:64, 0:128], lhsT=kd[par][cs, :], rhs=vb[cs, ib, :], start=True, stop=True))
                p.op("dve", [St, av[par], pf], [St], lambda e: e.scalar_tensor_tensor(out=St[:, :], in0=St[:, :], scalar=av[par][:, ci:ci + 1], in1=pf[:64, 0:128], op0=ALU.mult, op1=ALU.add))
                p.op("act", [St], [Sb], lambda e: e.copy(out=Sb[:, :], in_=St[:, :]))
            p.op("act", [po], [o_s], lambda e: e.copy(out=o_s[:, lsl], in_=po[:, 0:128]))
            if ib == NB - 1 or i == NTL - 1:
                p.dma("act", oT[:, blk * NB * 128:(blk * NB + ib + 1) * 128], o_s[:, :(ib + 1) * 128], [o_s], [oD])
            yield i
        print("k2a ninst", p.ninst)
    return None

def build_k2a(NTL=NT):
    raise NotImplementedError

def host_inputs_k2a(zT, inp, layer):
    AQ, AK, AV, AR, ALR = 0, 256, 512, 1024, 1536
    ss = np.arange(128)[:, None]; tt = np.arange(128)[None, :]
    same = (ss // 64) == (tt // 64)
    tri = (same & (ss <= tt)).astype(np.float32); m1 = (same & (ss > tt)).astype(np.float32)
    cind = (np.arange(128)[:, None] // 64 == np.arange(2)[None, :]).astype(np.float32)
    maps = []
    for h in range(4):
        for d in range(2):
            f = (lambda a: a[:, ::-1]) if d == 1 else (lambda a: a)
            q = np.ascontiguousarray(f(zT[AQ + h * 64:AQ + (h + 1) * 64])); k = np.ascontiguousarray(f(zT[AK + h * 64:AK + (h + 1) * 64]))
            v = f(zT[AV + h * 128:AV + (h + 1) * 128]); lr = np.ascontiguousarray(f(zT[ALR + d * 16:ALR + (d + 1) * 16]))
            ktm = np.ascontiguousarray(k.T.reshape(NT, 128, 64).transpose(1, 0, 2)); vtm = np.ascontiguousarray(v.T.reshape(NT, 128, 128).transpose(1, 0, 2))
            mA = (same & ((ss <= tt) if d == 0 else (ss < tt))).astype(np.float32)
            maps.append({"qT": q, "kT": k, "ktm": ktm, "vtm": vtm, "lrT": lr,
                         "gw": np.ascontiguousarray(inp["gla_gate_w"][layer, d][:, h * 64:(h + 1) * 64]),
                         "gbb": np.ascontiguousarray(np.broadcast_to(inp["gla_gate_b"][layer, d][None, h * 64:(h + 1) * 64], (128, 64))),
                         "tri": tri, "m1": m1, "cind": cind, "maskA": mA})
    return maps


S = 16384
DILS = (1, 4, 16)
NEG = -100.0

STAGE = 3
def emit_k2b(nc, p, c, pre):
    I = lambda n, s: nc.dram_tensor(pre + n, s, F32, kind="ExternalInput")
    qw = I("qw", [3, 256, 2048]); kw = I("kw", [3, 256, 4096]); vw = I("vw", [3, 128, 16 * 2 * 4 * 64])
    gq = I("gq", [128, 3]); gk = I("gk", [128, 3]); kval = I("kval", [128, 3 * 32]); bm = I("bm", [3, 2, 128, 512]); bones = I("bones", [128, 128])
    O = nc.dram_tensor(pre + "O", [3, 2, 64, 4, 2048], F32, kind="ExternalOutput")
    if True:
        iD = Buf(None); oD = Buf(None)
        def ld(name, shape, src, q="pool"):
            b = p.sbuf(shape, F32, name); p.dma(q, b[tuple(slice(None) for _ in shape)], src, [iD], [b]); return b
        gqs = ld("gqs", [128, 3], gq[:, :]); gks = ld("gks", [128, 3], gk[:, :]); kvs = ld("kvs", [128, 96], kval[:, :]); bos = ld("bos", [128, 128], bones[:, :])
        bms = [[ld(f"bm{g}{t}", [128, 512], bm[g, t, :, :]) for t in range(2)] for g in range(3)]
        qst = [p.sbuf([128, 2048], F32, f"qst{i}") for i in range(2)]; kst = [p.sbuf([128, 4096], F32, f"kst{i}") for i in range(2)]
        vst = p.sbuf([128, 16 * 2 * 4 * 64], F32, "vst"); vb = p.sbuf([128, 16, 2, 4, 64], BF16, "vb")
        qn = [p.sbuf([64, 2048], BF16, f"qn{i}") for i in range(4)]; kn = [p.sbuf([64, 4096], BF16, f"kn{i}") for i in range(4)]
        tmp = [p.sbuf([128, 512], F32, f"tmp{i}") for i in range(2)]; rtmp = [p.sbuf([128, 512], F32, f"rtmp{i}") for i in range(2)]
        tt = [p.sbuf([128, 512], F32, f"tt{i}") for i in range(2)]; P = [p.sbuf([128, 512], BF16, f"P{i}") for i in range(2)]
        ost = [p.sbuf([64, 2, 512], F32, f"ost{i}") for i in range(2)]
        onb = p.sbuf([128, 64], BF16, "onb"); p.op("pool", [], [onb], lambda e: e.memset(onb[:, :], 1.0))
        it = 0
        def norm(st, gcol, dst, n):
            for tb in range(n // 512):
                sl = slice(tb * 512, (tb + 1) * 512)
                s = tmp[tb % 2]; r = rtmp[tb % 2]; ps = c["ps"][6 + tb % 2]
                p.op("act", [st], [s], lambda e: e.activation(out=s[:64, :], in_=st[:64, sl], func=AF.Square))
                p.op("pe", [c["ones"], s], [ps], lambda e: e.matmul(ps[:64, :], lhsT=c["ones"][:64, :64], rhs=s[:64, :], start=True, stop=True))
                p.op("act", [ps, c["eps"]], [r], lambda e: e.activation(out=r[:64, :], in_=ps[:64, :], func=AF.Sqrt, bias=c["eps"][:64, 0:1], scale=1.0 / 64))
                p.op("dve", [r], [r], lambda e: e.reciprocal(out=r[:64, :], in_=r[:64, :]))
                p.op("dve", [st, r, gcol[0]], [dst], lambda e: e.scalar_tensor_tensor(out=dst[:, sl], in0=st[:64, sl], scalar=gcol[0][:64, gcol[1]:gcol[1] + 1], in1=r[:64, :], op0=ALU.mult, op1=ALU.mult))
        for g in range(3):
            p.dma("sp", vst[:, :], vw[g, :, :], [iD], [vst])
            p.op("pool", [vst], [vb], lambda e: e.tensor_copy(out=vb[:, :, :, :, :].rearrange("p a b c d -> p (a b c d)"), in_=vst[:, :]))
            for h in range(4):
                p.dma("sp", qst[h % 2][:64, :], qw[g, h * 64:(h + 1) * 64, :], [iD], [qst[h % 2]])
                p.dma("sp", kst[h % 2][:64, :], kw[g, h * 64:(h + 1) * 64, :], [iD], [kst[h % 2]])
                norm(qst[h % 2], (gqs, g), qn[h], 2048); norm(kst[h % 2], (gks, g), kn[h], 4096)
            for b in range(16):
                po = c["ps"][4 + b % 2]; pl = c["ps"][6 + b % 2]
                for tau in range(2 if STAGE >= 2 else 0):
                    ps = c["ps"][it % 4]; t = tt[it % 2]; Pt = P[it % 2]; it += 1
                    for h in range(4):
                        p.op("pe", [kn[h], qn[h]], [ps], lambda e: e.matmul(ps[:, h * 128:(h + 1) * 128], lhsT=kn[h][:, b * 256 + tau * 128:b * 256 + tau * 128 + 128], rhs=qn[h][:, b * 128:(b + 1) * 128], start=True, stop=True), same_ok=True)
                    p.op("dve", [ps, bms[g][tau]], [t], lambda e: e.scalar_tensor_tensor(out=t[:, :], in0=ps[:, :], scalar=0.125, in1=bms[g][tau][:, :], op0=ALU.mult, op1=ALU.add))
                    col = g * 32 + b * 2 + tau
                    p.op("act", [t, kvs], [Pt], lambda e: e.activation(out=Pt[:, :], in_=t[:, :], func=AF.Exp, bias=kvs[:, col:col + 1], scale=1.0))
                    if STAGE < 3: continue
                    for h in range(4):
                        p.op("pe", [vb, Pt], [po], lambda e: e.matmul(po[:64, h * 128:(h + 1) * 128], lhsT=vb[:, b, tau, h, :], rhs=Pt[:, h * 128:(h + 1) * 128], start=(tau == 0 and h == 0), stop=(tau == 1 and h == 3)), same_ok=True)
                    p.op("pe", [onb, Pt], [pl], lambda e: e.matmul(pl[:64, :], lhsT=onb[:, :], rhs=Pt[:, :], start=(tau == 0), stop=(tau == 1)), same_ok=True)
                o = ost[b % 2]
                if STAGE < 3:
                    p.op("dve", [], [o], lambda e: e.memset(o[:, :, :], 1.0))
                    for w_ in range(2):
                        p.dma("act", O[g, w_, :, :, b * 128:(b + 1) * 128], o[:, w_, :].rearrange("r (h q) -> r h q", q=128), [o], [oD])
                    continue
                p.op("act", [po], [o], lambda e: e.copy(out=o[:, 0, :], in_=po[:64, :]))
                p.op("dve", [pl], [o], lambda e: e.tensor_copy(out=o[:, 1, :], in_=pl[:64, :]))
                for w_ in range(2):
                    p.dma("act", O[g, w_, :, :, b * 128:(b + 1) * 128], o[:, w_, :].rearrange("r (h q) -> r h q", q=128), [o], [oD])
        print("k2b ninst", p.ninst)
    return None

def perm_index(d):
    M = S // d
    pos = np.arange(S)
    return (pos % M) * d + pos // M

def host_inputs_k2b(zT, inp, layer):
    BQ = 1568; BK = BQ + 768; BV = BK + 768
    t5 = inp["t5_bias"]
    kk = np.arange(128)[:, None]; qq = np.arange(128)[None, :]
    bm = np.zeros((3, 2, 128, 512), np.float32)
    for g, d in enumerate(DILS):
        for tau in range(2):
            delta = kk + (-64 if tau == 0 else 64) - qq
            ok = np.abs(delta) <= 64
            bidx = t5_bucket_np(delta * d)
            for h in range(4):
                bm[g, tau, :, h * 128:(h + 1) * 128] = np.where(ok, t5[:, g * 4 + h][bidx], np.float32(NEG))
    bones = (np.arange(128)[:, None] // 64 == np.arange(128)[None, :] // 64).astype(np.float32)
    gq = np.ascontiguousarray(np.tile(inp["dil_qnorm_g"][layer].T, (2, 1))); gk = np.ascontiguousarray(np.tile(inp["dil_knorm_g"][layer].T, (2, 1)))
    qP, kP, vP, valid = [], [], [], []
    for g, d in enumerate(DILS):
        idx = perm_index(d); M = S // d
        qP.append(zT[BQ + g * 256:BQ + (g + 1) * 256][:, idx])
        kP.append(zT[BK + g * 256:BK + (g + 1) * 256][:, idx]); vP.append(zT[BV + g * 256:BV + (g + 1) * 256][:, idx])
    maps = []
    for r in range(8):
        qw = np.zeros((3, 256, 2048), np.float32); kw = np.zeros((3, 256, 4096), np.float32)
        vw = np.zeros((3, 128, 16, 2, 4, 64), np.float32); kval = np.zeros((128, 96), np.float32)
        for g, d in enumerate(DILS):
            M = S // d
            qw[g] = qP[g][:, r * 2048:(r + 1) * 2048]
            for b in range(16):
                P0 = r * 2048 + b * 128; sub0 = (P0 // M) * M
                kpos = np.arange(P0 - 64, P0 + 192)
                ok = (kpos >= sub0) & (kpos < sub0 + M)
                kc = np.clip(kpos, 0, S - 1)
                kwin = np.where(ok[None, :], kP[g][:, kc], np.float32(0))
                kw[g][:, b * 256:(b + 1) * 256] = kwin
                vwin = np.where(ok[None, :], vP[g][:, kc], np.float32(0))
                vv = vwin.reshape(4, 64, 2, 128).transpose(3, 2, 0, 1)
                vw[g][:, b] = vv
                kval[:, g * 32 + b * 2:g * 32 + b * 2 + 2] = np.where(ok.reshape(2, 128).T, np.float32(0), np.float32(NEG))
        maps.append({"qw": qw, "kw": kw, "vw": np.ascontiguousarray(vw.reshape(3, 128, -1)), "gq": gq, "gk": gk, "kval": kval, "bm": bm, "bones": bones})
    return maps

def host_post_k2b(results):
    N = np.zeros((3, 256, S), np.float32); lb = np.zeros((3, 256, S), np.float32)
    for g, d in enumerate(DILS):
        Og = np.concatenate([r["O"][g] for r in results], axis=3)
        idx = perm_index(d)
        N[g][:, idx] = Og[0].transpose(1, 0, 2).reshape(256, S)
        lb[g][:, idx] = Og[1].transpose(1, 0, 2).reshape(256, S)
    return N, lb


import math
S = 16384
I32 = mybir.dt.int32
TWO_PI = float(2 * np.pi)

def emit_k2d(nc, p, c, pre, NROW=128, co=None):
    I = lambda n, s: nc.dram_tensor(pre + n, s, F32, kind="ExternalInput")
    ux1 = I("ux1", [128, S]); uv = I("uv", [128, S]); ux0 = I("ux0", [64, S])
    cw1 = I("cw1", [128, 4]); cwv = I("cwv", [128, 4]); cw0 = I("cw0", [64, 4])
    embT = I("embT", [33, S]); w1 = I("w1", [33, 64]); w2 = I("w2", [64, 64]); w3 = I("w3", [64, 128])
    bf1 = I("bf1", [64, 2]); bf2 = I("bf2", [64, 2])
    dlrow = I("dlrow", [1, 128]); tvec = I("tvec", [1, S]); pair = I("pair", [128, 128])
    Y = nc.dram_tensor(pre + "Y", [128, 128, 128], F32, kind="ExternalOutput")
    zz = nc.dram_tensor(pre + "zz", [64, S], F32, kind="ExternalOutput")
    x0c = nc.dram_tensor(pre + "x0c", [64, S], F32, kind="ExternalOutput")
    kx = nc.dram_tensor(pre + "kx", [128, S + 128], BF16, kind="Internal")
    if True:
        iD = Buf(None); oD = Buf(None); kxD = Buf(None)
        def ld(name, shape, src, q="pool"):
            b = p.sbuf(shape, F32, name); p.dma(q, b[tuple(slice(None) for _ in shape)], src, [iD], [b]); return b
        cw1s = ld("cw1s", [128, 4], cw1[:, :]); cwvs = ld("cwvs", [128, 4], cwv[:, :]); cw0s = ld("cw0s", [64, 4], cw0[:, :])
        w1s = ld("w1s", [33, 64], w1[:, :]); w2s = ld("w2s", [64, 64], w2[:, :]); w3s = ld("w3s", [64, 128], w3[:, :])
        bf1s = ld("bf1s", [64, 2], bf1[:, :]); bf2s = ld("bf2s", [64, 2], bf2[:, :])
        dls = ld("dls", [1, 128], dlrow[:, :]); prs = ld("prs", [128, 128], pair[:, :])
        for b in (bf1s, bf2s):
            p.op("dve", [b], [b], lambda e: e.tensor_scalar(out=b[:, 1:2], in0=b[:, 1:2], scalar1=1.0 / TWO_PI, scalar2=None, op0=ALU.mult))
        idf = p.sbuf([128, 128], F32, "idf"); idb = p.sbuf([128, 128], BF16, "idb")
        p.op("pool", [], [idf], lambda e: e.memset(idf[:, :], 1.0))
        p.op("pool", [idf], [idf], lambda e: e.affine_select(out=idf[:, :], in_=idf[:, :], pattern=[[-1, 128]], compare_op=ALU.is_equal, fill=0.0, base=0, channel_multiplier=1))
        p.op("dve", [idf], [idb], lambda e: e.tensor_copy(out=idb[:, :], in_=idf[:, :]))
        zpad = p.sbuf([128, 128], BF16, "zpad"); p.op("pool", [], [zpad], lambda e: e.memset(zpad[:, :], 0.0))
        p.dma("pool", kx[:, S:S + 128], zpad[:, :], [zpad], [kxD])
        Zall = p.sbuf([128, 128, 128], BF16, "Zall")
        rn = p.sbuf([128, 128], F32, "rn")
        scA = p.scope(); scA.__enter__()
        zzb = p.sbuf([128, S], BF16, "zzb")
        CB = 2048
        ub = [p.sbuf([128, CB + 2], F32, f"ub{i}") for i in range(2)]
        acc = [p.sbuf([128, CB], F32, f"acc{i}") for i in range(3)]
        k = 0
        def conv(src, rows, cws, blk, dst):
            nonlocal k
            u = ub[k % 2]; k += 1
            lo = blk * CB - 1; hi = (blk + 1) * CB + 1
            if blk == 0:
                p.op("pool", [], [u], lambda e: e.memset(u[:rows, 0:1], 0.0));
                p.dma("sp", u[:rows, 1:CB + 2], src[:, 0:hi], [iD], [u])
            elif blk == S // CB - 1:
                p.op("pool", [], [u], lambda e: e.memset(u[:rows, CB + 1:CB + 2], 0.0))
                p.dma("sp", u[:rows, 0:CB + 1], src[:, lo:S], [iD], [u])
            else:
                p.dma("sp", u[:rows, :], src[:, lo:hi], [iD], [u])
            p.op("dve", [u, cws], [dst], lambda e: e.tensor_scalar(out=dst[:rows, :], in0=u[:rows, 1:CB + 1], scalar1=cws[:rows, 1:2], scalar2=cws[:rows, 3:4], op0=ALU.mult, op1=ALU.add))
            p.op("dve", [u, cws, dst], [dst], lambda e: e.scalar_tensor_tensor(out=dst[:rows, :], in0=u[:rows, 0:CB], scalar=cws[:rows, 0:1], in1=dst[:rows, :], op0=ALU.mult, op1=ALU.add))
            p.op("dve", [u, cws, dst], [dst], lambda e: e.scalar_tensor_tensor(out=dst[:rows, :], in0=u[:rows, 2:CB + 2], scalar=cws[:rows, 2:3], in1=dst[:rows, :], op0=ALU.mult, op1=ALU.add))
        for blk in range(S // CB):
            bsl = slice(blk * CB, (blk + 1) * CB)
            conv(ux1, 128, cw1s, blk, acc[0]); conv(uv, 128, cwvs, blk, acc[1])
            p.op("pool", [acc[0], acc[1]], [acc[0]], lambda e: e.tensor_tensor(out=acc[0][:, :], in0=acc[0][:, :], in1=acc[1][:, :], op=ALU.mult))
            p.op("act", [acc[0]], [zzb], lambda e: e.copy(out=zzb[:, bsl], in_=acc[0][:, :]))
            p.dma("act", zz[:, bsl], acc[0][:64, :], [acc[0]], [oD])
            conv(ux0, 64, cw0s, blk, acc[2])
            p.dma("act", x0c[:, bsl], acc[2][:64, :], [acc[2]], [oD])
        for g in range(32):
            ps = c["ps"][g % 2]
            for q in range(4):
                sb = g * 4 + q
                p.op("pe", [zzb, idb], [ps], lambda e: e.matmul(ps[:, q * 128:(q + 1) * 128], lhsT=zzb[:, sb * 128:(sb + 1) * 128], rhs=idb[:, :], start=True, stop=True), same_ok=True)
            eng = "act" if g % 2 == 0 else "dve"
            if eng == "act":
                p.op("act", [ps], [Zall], lambda e: e.copy(out=Zall[:, :, g * 4:(g + 1) * 4].rearrange("p r b -> p b r"), in_=ps[:, :].rearrange("p (b r) -> p b r", r=128)))
            else:
                p.op("dve", [ps], [Zall], lambda e: e.tensor_copy(out=Zall[:, :, g * 4:(g + 1) * 4].rearrange("p r b -> p b r"), in_=ps[:, :].rearrange("p (b r) -> p b r", r=128)))
        scA.__exit__(None, None, None)
        scC = p.scope(); scC.__enter__()
        tvb = [p.sbuf([1, 512], F32, f"tvb{i}") for i in range(2)]
        emb = [p.sbuf([33, 512], F32, f"emb{i}") for i in range(2)]
        R = lambda n, shp, dt=F32, kk=2: [p.sbuf(shp, dt, f"{n}{i}") for i in range(kk)]
        ua = R("ua", [64, 512]); ti = R("ti", [64, 512], I32); tf = R("tf", [64, 512]); h1 = R("h1", [64, 512]); h2 = R("h2", [64, 512])
        dec = R("dec", [128, 512]); kf = R("kf", [128, 512]); kb = R("kb", [128, 512], BF16)
        npart = p.sbuf([128, 32], F32, "npart")
        def sin_layer(ps, bfs, out, par):
            u = ua[par]
            p.op("dve", [ps, bfs], [u], lambda e: e.tensor_scalar(out=u[:, :], in0=ps[:64, :], scalar1=bfs[:, 0:1], scalar2=bfs[:, 1:2], op0=ALU.add, op1=ALU.mult))
            p.op("dve", [u], [ti[par]], lambda e: e.tensor_copy(out=ti[par][:, :], in_=u[:, :]))
            p.op("dve", [ti[par]], [tf[par]], lambda e: e.tensor_copy(out=tf[par][:, :], in_=ti[par][:, :]))
            p.op("dve", [u, tf[par]], [u], lambda e: e.tensor_tensor(out=u[:, :], in0=u[:, :], in1=tf[par][:, :], op=ALU.subtract))
            p.op("act", [u], [out], lambda e: e.activation(out=out[:, :], in_=u[:, :], func=AF.Sin, scale=TWO_PI))
        for blk in range(32):
            par = blk % 2; bsl = slice(blk * 512, (blk + 1) * 512)
            em = emb[par]
            p.dma("sp", em[:, :], embT[:, bsl], [iD], [em])
            p.dma("sp", tvb[par][:, :], tvec[:, bsl], [iD], [tvb[par]])
            p1 = c["ps"][2 + par]; p2 = c["ps"][4 + par]; p3 = c["ps"][6]; pd = c["ps"][7]
            p.op("pe", [w1s, em], [p1], lambda e: e.matmul(p1[:64, :], lhsT=w1s[:, :], rhs=em[:, :], start=True, stop=True))
            sin_layer(p1, bf1s, h1[par], par)
            p.op("pe", [w2s, h1[par]], [p2], lambda e: e.matmul(p2[:64, :], lhsT=w2s[:, :], rhs=h1[par][:, :], start=True, stop=True))
            sin_layer(p2, bf2s, h2[par], par)
            p.op("pe", [w3s, h2[par]], [p3], lambda e: e.matmul(p3[:, :], lhsT=w3s[:, :], rhs=h2[par][:, :], start=True, stop=True))
            p.op("pe", [dls, tvb[par]], [pd], lambda e: e.matmul(pd[:, :], lhsT=dls[:, :], rhs=tvb[par][:, :], start=True, stop=True))
            p.op("act", [pd], [dec[par]], lambda e: e.activation(out=dec[par][:, :], in_=pd[:, :], func=AF.Exp))
            p.op("dve", [p3, dec[par]], [kf[par]], lambda e: e.tensor_tensor(out=kf[par][:, :], in0=p3[:, :], in1=dec[par][:, :], op=ALU.mult))
            if blk == 31:
                p.op("dve", [kf[par]], [kf[par]], lambda e: e.memset(kf[par][64:128, 511:512], 0.0))
            p.op("dve", [kf[par]], [npart], lambda e: e.tensor_reduce(out=npart[:, blk:blk + 1], in_=kf[par][:, :], axis=AX.X, op=ALU.add, apply_absolute_value=True))
            p.op("pool", [kf[par]], [kb[par]], lambda e: e.tensor_copy(out=kb[par][:, :], in_=kf[par][:, :]))
            p.dma("pool", kx[:, bsl], kb[par][:, :], [kb[par]], [kxD])
        nsum = p.sbuf([128, 1], F32, "nsum"); prn = p.sbuf([128, 128], F32, "prn")
        p.op("dve", [npart], [nsum], lambda e: e.tensor_reduce(out=nsum[:, 0:1], in_=npart[:, :], axis=AX.X, op=ALU.add))
        p.op("dve", [prs, nsum], [prn], lambda e: e.tensor_scalar(out=prn[:, :], in0=prs[:, :], scalar1=nsum[:, 0:1], scalar2=None, op0=ALU.mult))
        pn = c["ps"][0]
        p.op("pe", [c["ones"], prn], [pn], lambda e: e.matmul(pn[:, 0:128], lhsT=c["ones"][:, :], rhs=prn[:, :], start=True, stop=True))
        p.op("dve", [pn], [rn], lambda e: e.reciprocal(out=rn[:, :], in_=pn[:, 0:128]))
        scC.__exit__(None, None, None)
        if co is not None:
            next(co)
        strip = [p.sbuf([128, S], BF16, f"strip{i}") for i in range(2)]
        yst = [p.sbuf([128, 8, 128], F32, f"yst{i}") for i in range(2)]
        for row in range(NROW):
            st = strip[row % 2]
            src = bass.AP(tensor=kx, offset=row * (S + 128), ap=[[1, 128], [1, S]])
            p.dma("sp" if row % 2 == 0 else "pool", st[:, :], src, [kxD], [st])
            py = c["ps"][(6 if co is not None else 1) + row % 2]
            for d in range(128):
                base = 16256 - 128 * d
                p.op("pe", [st, Zall], [py], lambda e: e.matmul(py[:, d:128], lhsT=st[:, base:base + 128], rhs=Zall[:, row, 0:128 - d], start=(d == 0), stop=(d == 127)), same_ok=True)
            ys = yst[(row // 8) % 2]
            p.op("act", [py, rn], [ys], lambda e: e.activation(out=ys[:, row % 8, :], in_=py[:, 0:128], func=AF.Copy, scale=rn[:, row:row + 1]))
            if row % 8 == 7:
                p.dma("act", Y[:, row - 7:row + 1, :], ys[:, :, :], [ys], [oD])
            if co is not None:
                next(co, None)
        if co is not None:
            for _ in co:
                pass
        print("k2d ninst", p.ninst)
    return None

def hy_consts():
    L = S
    t = np.linspace(0.0, 1.0, L, dtype=np.float32)[:, None]
    bands = 16
    freqs = np.linspace(1e-4, bands - 1, bands, dtype=np.float32)[None]
    w = (np.float32(2.0 * math.pi) * np.arange(L, dtype=np.float32)[:, None] / np.float32(L)).astype(np.float32)
    z = np.concatenate([t, np.cos(freqs * w), -np.sin(freqs * w)], axis=-1).astype(np.float32)
    min_decay = math.log(1e-2) / 1.5; max_decay = math.log(1e-2) / 0.3
    deltas = np.abs(np.linspace(min_decay, max_decay, 512, dtype=np.float32))
    return z, t[:, 0], deltas

def host_inputs_k2d(zT, inp, layer):
    DU = 11040 - 4096 - 1536
    z, t, deltas = hy_consts()
    embT = np.ascontiguousarray(z[::-1].T); tvec = np.ascontiguousarray(t[::-1][None, :])
    pair = (np.arange(128)[:, None] % 64 == np.arange(128)[None, :] % 64).astype(np.float32)
    cw = inp["hy_conv_w"][layer]; cb = inp["hy_conv_b"][layer]
    def cwpack(cols, rev):
        a = cw[:, cols].T
        if rev: a = a[:, ::-1]
        return np.ascontiguousarray(np.concatenate([a, cb[cols][:, None]], 1).astype(np.float32))
    maps = []
    for r in range(8):
        ch = np.arange(64 * r, 64 * r + 64)
        x0 = zT[DU + ch]; x1 = zT[DU + 512 + ch]; v = zT[DU + 1024 + ch]
        m = {"ux1": np.ascontiguousarray(np.concatenate([x1, x1[:, ::-1]], 0)), "uv": np.ascontiguousarray(np.concatenate([v, v[:, ::-1]], 0)), "ux0": np.ascontiguousarray(x0),
             "cw1": np.concatenate([cwpack(512 + ch, False), cwpack(512 + ch, True)], 0), "cwv": np.concatenate([cwpack(1024 + ch, False), cwpack(1024 + ch, True)], 0), "cw0": cwpack(ch, False),
             "embT": embT, "w1": np.ascontiguousarray(inp["hy_w1"][layer]), "w2": np.ascontiguousarray(inp["hy_w2"][layer]),
             "w3": np.ascontiguousarray(np.concatenate([inp["hy_w3"][layer][:, ch], inp["hy_w3"][layer][:, 512 + ch]], 1)),
             "bf1": np.ascontiguousarray(np.stack([inp["hy_b1"][layer], inp["hy_freq1"][layer]], 1)), "bf2": np.ascontiguousarray(np.stack([inp["hy_b2"][layer], inp["hy_freq2"][layer]], 1)),
             "dlrow": np.ascontiguousarray(-np.concatenate([deltas[ch], deltas[ch]])[None, :].astype(np.float32)), "tvec": tvec, "pair": pair}
        maps.append(m)
    return maps

def host_post_k2d(results):
    ys, zs, xs = [], [], []
    for r in results:
        Yr = r["Y"]
        y = Yr[::-1].transpose(1, 2, 0).reshape(128, S)
        ys.append(y[:64] + y[64:, ::-1]); zs.append(r["zz"]); xs.append(r["x0c"])
    return np.concatenate(ys, 0), np.concatenate(zs, 0), np.concatenate(xs, 0)


S = 16384; T = 2048; D = 1024; NIN = 11040; DFF = 4096

def linear2(p, c, w_ap, K, N, rhs_fn, ntb, evac, wst, wbf, gw, psi=2, npsi=4):
    kc = K // 128
    wD = Buf(None)
    ng = (N + gw - 1) // gw
    k2 = 0
    for gi in range(ng):
        c0 = gi * gw; wg = min(gw, N - c0)
        st = wst[gi % len(wst)]; wb = wbf[gi % len(wbf)]
        p.dma("sp", st[:, :kc, :wg], w_ap[:, c0:c0 + wg].rearrange("(c p) n -> p c n", p=128), [wD], [st])
        h = kc // 2
        p.op("act", [st], [wb], lambda e: e.copy(out=wb[:, :h, :wg], in_=st[:, :h, :wg]))
        p.op("dve", [st], [wb], lambda e: e.tensor_copy(out=wb[:, h:kc, :wg], in_=st[:, h:kc, :wg]))
        for mi in range((wg + 127) // 128):
            mw = min(128, wg - mi * 128)
            for tb in range(ntb):
                ps = c["ps"][psi + k2 % npsi]; k2 += 1
                for ch in range(kc):
                    rb, rap = rhs_fn(ch, tb)
                    p.op("pe", [wb, rb], [ps], lambda e: e.matmul(ps[:mw, :], lhsT=wb[:, ch, mi * 128:mi * 128 + mw], rhs=rap, start=(ch == 0), stop=(ch == kc - 1)), same_ok=True)
                evac(c0 + mi * 128, mw, tb, ps)

def build_k3(lam_init, last):
    nc = bass.Bass("TRN2", target_bir_lowering=False)
    I = lambda n, s: nc.dram_tensor(n, s, F32, kind="ExternalInput")
    xT = I("xT", [D, T])
    of = I("of", [512, T]); ob = I("ob", [512, T]); rT = I("rT", [512, T]); gla_g = I("gla_g", [128, 1])
    dN = I("dN", [3, 256, T]); dL = I("dL", [3, 256, T])
    oc0 = I("oc0", [512, T]); oc1 = I("oc1", [512, T]); lamb = I("lamb", [128, 256]); sub_g = I("sub_g", [128, 1])
    yconv = I("yconv", [512, T]); zzT = I("zzT", [512, T]); x0c = I("x0c", [512, T]); skip = I("skip", [128, 4])
    gates = I("gates", [4096, T])
    projs = I("projs", [1792, D]); w_out = I("w_out", [D, D]); g2 = I("g2", [128, 8]); w1 = I("w1", [D, DFF]); w2 = I("w2", [DFF, D])
    xo = nc.dram_tensor("xo", [D, T], F32, kind="ExternalOutput")
    if not last:
        g1n = I("g1n", [128, 8]); w_in = I("w_in", [D, NIN])
        zT = nc.dram_tensor("zT", [NIN, T], F32, kind="ExternalOutput")
    with ExitStack() as es:
        p = Prog(nc, es)
        c = consts(p)
        iD = Buf(None); oD = Buf(None)
        def ld(name, shape, src, q="pool"):
            b = p.sbuf(shape, F32, name); p.dma(q, b[tuple(slice(None) for _ in shape)], src, [iD], [b]); return b
        glag = ld("glag", [128, 1], gla_g[:, :]); subg = ld("subg", [128, 1], sub_g[:, :]); lams = ld("lams", [128, 256], lamb[:, :]); skp = ld("skp", [128, 4], skip[:, :])
        g2s = ld("g2s", [128, 8], g2[:, :])
        if not last: g1s = ld("g1s", [128, 8], g1n[:, :])
        tmp = [p.sbuf([128, 512], F32, f"tmp{i}") for i in range(2)]
        rstds = [p.sbuf([128, 512], F32, f"rstd{i}") for i in range(4)]
        mb = [p.sbuf([128, T], BF16, f"mb{i}") for i in range(8)]
        lcol = p.sbuf([128, 4], F32, "lcol"); lpr = p.sbuf([128, 128], F32, "lpr")
        p.op("dve", [lams], [lpr], lambda e: e.tensor_tensor(out=lpr[:, 0:64], in0=lams[:, 0:64], in1=lams[:, 64:128], op=ALU.mult))
        p.op("dve", [lams], [lpr], lambda e: e.tensor_tensor(out=lpr[:, 64:128], in0=lams[:, 128:192], in1=lams[:, 192:256], op=ALU.mult))
        p.op("dve", [lpr], [lcol], lambda e: e.tensor_reduce(out=lcol[:, 0:1], in_=lpr[:, 0:64], axis=AX.X, op=ALU.add))
        p.op("dve", [lpr], [lcol], lambda e: e.tensor_reduce(out=lcol[:, 1:2], in_=lpr[:, 64:128], axis=AX.X, op=ALU.add))
        p.op("act", [lcol], [lcol], lambda e: e.activation(out=lcol[:, 0:2], in_=lcol[:, 0:2], func=AF.Exp))
        p.op("dve", [lcol], [lcol], lambda e: e.tensor_tensor(out=lcol[:, 2:3], in0=lcol[:, 0:1], in1=lcol[:, 1:2], op=ALU.subtract))
        p.op("dve", [lcol], [lcol], lambda e: e.tensor_scalar(out=lcol[:, 3:4], in0=lcol[:, 2:3], scalar1=float(lam_init), scalar2=-1.0, op0=ALU.add, op1=ALU.mult))
        sc1 = p.scope(); sc1.__enter__()
        ybf = [p.sbuf([128, T], BF16, f"ybf{i}") for i in range(14)]
        A = [p.sbuf([128, T], F32, f"A{i}") for i in range(2)]; B = [p.sbuf([128, T], F32, f"B{i}") for i in range(2)]; Cc = [p.sbuf([128, T], F32, f"C{i}") for i in range(2)]
        def rms128(src, gcol, dst, extra_scale=None):
            for tb in range(T // 512):
                sl = slice(tb * 512, (tb + 1) * 512); s = tmp[tb % 2]; r = rstds[tb % 4]; ps = c["ps"][tb % 2]
                p.op("act", [src], [s], lambda e: e.activation(out=s[:, :], in_=src[:, sl], func=AF.Square))
                p.op("pe", [c["ones"], s], [ps], lambda e: e.matmul(ps[:, :], lhsT=c["ones"][:, :], rhs=s[:, :], start=True, stop=True))
                p.op("act", [ps, c["eps"]], [r], lambda e: e.activation(out=r[:, :], in_=ps[:, :], func=AF.Sqrt, bias=c["eps"][:, 0:1], scale=1.0 / 128))
                p.op("dve", [r], [r], lambda e: e.reciprocal(out=r[:, :], in_=r[:, :]))
                p.op("dve", [src, r, gcol], [dst], lambda e: e.scalar_tensor_tensor(out=dst[:, sl], in0=src[:, sl], scalar=gcol[:, 0:1], in1=r[:, :], op0=ALU.mult, op1=ALU.mult))
        for h in range(4):
            hs = slice(h * 128, (h + 1) * 128); a = A[h % 2]; b = B[h % 2]; cc = Cc[h % 2]
            p.dma("sp", a[:, :], of[hs, :], [iD], [a]); p.dma("sp", b[:, :], ob[hs, :], [iD], [b]); p.dma("sp", cc[:, :], rT[hs, :], [iD], [cc])
            p.op("pool", [a, b], [a], lambda e: e.tensor_tensor(out=a[:, :], in0=a[:, :], in1=b[:, :], op=ALU.add))
            rms128(a, glag, b)
            p.op("act", [cc], [cc], lambda e: e.activation(out=cc[:, :], in_=cc[:, :], func=AF.Silu))
            p.op("dve", [b, cc], [ybf[h]], lambda e: e.tensor_tensor(out=ybf[h][:, :], in0=b[:, :], in1=cc[:, :], op=ALU.mult))
        for h in range(4):
            hs = slice(h * 128, (h + 1) * 128); a = A[h % 2]; b = B[h % 2]; cc = Cc[h % 2]
            p.dma("sp", a[:, :], oc0[hs, :], [iD], [a]); p.dma("sp", b[:, :], oc1[hs, :], [iD], [b])
            p.op("dve", [a, b, lcol], [a], lambda e: e.scalar_tensor_tensor(out=a[:, :], in0=b[:, :], scalar=lcol[:, 3:4], in1=a[:, :], op0=ALU.mult, op1=ALU.add))
            rms128(a, subg, cc)
            p.op("pool", [cc], [ybf[6 + h]], lambda e: e.tensor_scalar(out=ybf[6 + h][:, :], in0=cc[:, :], scalar1=float(1.0 - lam_init), scalar2=None, op0=ALU.mult))
        for h in range(4):
            hs = slice(h * 128, (h + 1) * 128); a = A[h % 2]; b = B[h % 2]; cc = Cc[h % 2]
            p.dma("sp", a[:, :], yconv[hs, :], [iD], [a]); p.dma("sp", b[:, :], zzT[hs, :], [iD], [b]); p.dma("sp", cc[:, :], x0c[hs, :], [iD], [cc])
            p.op("dve", [a, b, skp], [a], lambda e: e.scalar_tensor_tensor(out=a[:, :], in0=b[:, :], scalar=skp[:, h:h + 1], in1=a[:, :], op0=ALU.mult, op1=ALU.add))
            p.op("pool", [a, cc], [ybf[10 + h]], lambda e: e.tensor_tensor(out=ybf[10 + h][:, :], in0=a[:, :], in1=cc[:, :], op=ALU.mult))
        for hh in range(2):
            hs = slice(hh * 128, (hh + 1) * 128); a = A[hh % 2]; b = B[hh % 2]; cc = Cc[hh % 2]
            p.dma("sp", a[:, :], dN[0, hs, :], [iD], [a]); p.dma("sp", b[:, :], dL[0, hs, :], [iD], [b])
            for g in (1, 2):
                p.dma("sp", cc[:, :], dN[g, hs, :], [iD], [cc])
                p.op("dve", [a, cc], [a], lambda e: e.tensor_tensor(out=a[:, :], in0=a[:, :], in1=cc[:, :], op=ALU.add))
                p.dma("sp", cc[:, :], dL[g, hs, :], [iD], [cc])
                p.op("dve", [b, cc], [b], lambda e: e.tensor_tensor(out=b[:, :], in0=b[:, :], in1=cc[:, :], op=ALU.add))
            p.op("dve", [b], [b], lambda e: e.reciprocal(out=b[:, :], in_=b[:, :]))
            p.op("dve", [a, b], [ybf[4 + hh]], lambda e: e.tensor_tensor(out=ybf[4 + hh][:, :], in0=a[:, :], in1=b[:, :], op=ALU.mult))
        pst = [p.sbuf([128, D], F32, f"pst{i}") for i in range(2)]; pbf = p.sbuf([128, 14, D], BF16, "pbf")
        for kc_ in range(14):
            p.dma("sp", pst[kc_ % 2][:, :], projs[kc_ * 128:(kc_ + 1) * 128, :], [iD], [pst[kc_ % 2]])
            p.op("pool", [pst[kc_ % 2]], [pbf], lambda e: e.tensor_copy(out=pbf[:, kc_, :], in_=pst[kc_ % 2][:, :]))
        branches = [(0, 4), (4, 6), (6, 10), (10, 14)]
        gst = A + B
        macc = Cc
        k = 0; kk = 0
        for mi in range(8):
            ms = slice(mi * 128, (mi + 1) * 128); ma = macc[mi % 2]
            for bi, (k0, k1) in enumerate(branches):
                gt = gst[k % 4]; k += 1
                p.dma("sp", gt[:, :], gates[bi * 1024 + mi * 128:bi * 1024 + (mi + 1) * 128, :], [iD], [gt])
                p.op("act", [gt], [gt], lambda e: e.activation(out=gt[:, :], in_=gt[:, :], func=AF.Sigmoid))
                for tb in range(4):
                    sl = slice(tb * 512, (tb + 1) * 512); ps = c["ps"][2 + kk % 4]; kk += 1
                    for kc_ in range(k0, k1):
                        p.op("pe", [pbf, ybf[kc_]], [ps], lambda e: e.matmul(ps[:, :], lhsT=pbf[:, kc_, ms], rhs=ybf[kc_][:, sl], start=(kc_ == k0), stop=(kc_ == k1 - 1)), same_ok=True)
                    if bi == 0:
                        p.op("dve", [ps, gt], [ma], lambda e: e.tensor_tensor(out=ma[:, sl], in0=ps[:, :], in1=gt[:, sl], op=ALU.mult))
                    else:
                        t = tmp[kk % 2]
                        p.op("dve", [ps, gt], [t], lambda e: e.tensor_tensor(out=t[:, :], in0=ps[:, :], in1=gt[:, sl], op=ALU.mult))
                        p.op("pool", [t, ma], [ma], lambda e: e.tensor_tensor(out=ma[:, sl], in0=ma[:, sl], in1=t[:, :], op=ALU.add))
            p.op("act", [ma], [mb[mi]], lambda e: e.copy(out=mb[mi][:, :], in_=ma[:, :]))
        sc1.__exit__(None, None, None)
        xs = [p.sbuf([128, T], F32, f"x{i}") for i in range(8)]
        for i in range(8):
            p.dma("pool", xs[i][:, :], xT[i * 128:(i + 1) * 128, :], [iD], [xs[i]])
        GW = 256
        wst = [p.sbuf([128, 8, GW], F32, f"wst{i}") for i in range(2)]; wbf = [p.sbuf([128, 8, GW], BF16, f"wbf{i}") for i in range(2)]
        def evac_addx(m0, mw, tb, ps):
            ci = m0 // 128; sl = slice(tb * 512, (tb + 1) * 512)
            p.op("dve", [ps, xs[ci]], [xs[ci]], lambda e: e.tensor_tensor(out=xs[ci][:, sl], in0=xs[ci][:, sl], in1=ps[:, :], op=ALU.add))
        linear2(p, c, w_out, D, D, lambda ch, tb: (mb[ch], mb[ch][:, tb * 512:(tb + 1) * 512]), 4, evac_addx, wst, wbf, GW)
        hs_ = mb
        rms_fm(p, c, xs, g2s, hs_, 8, T, tmp, rstds)
        sc2 = p.scope(); sc2.__enter__()
        ub = [p.sbuf([128, 512], BF16, f"ub{i}") for i in range(32)]
        w2st = [p.sbuf([128, 32, 128], F32, "w2st0")]; w2bf = [p.sbuf([128, 32, 128], BF16, f"w2bf{i}") for i in range(2)]
        for tb in range(4):
            sl = slice(tb * 512, (tb + 1) * 512)
            def evac_u(m0, mw, tb_, ps):
                mi = m0 // 128; t = tmp[mi % 2]
                p.op("act", [ps], [t], lambda e: e.activation(out=t[:, :], in_=ps[:, :], func=AF.Relu))
                p.op("dve", [t], [ub[mi]], lambda e: e.tensor_tensor(out=ub[mi][:, :], in0=t[:, :], in1=t[:, :], op=ALU.mult))
            linear2(p, c, w1, D, DFF, lambda ch, tb_: (hs_[ch], hs_[ch][:, sl]), 1, evac_u, wst, wbf, GW)
            def evac_x2(m0, mw, tb_, ps):
                ci = m0 // 128
                p.op("dve", [ps, xs[ci]], [xs[ci]], lambda e: e.tensor_tensor(out=xs[ci][:mw, sl], in0=xs[ci][:mw, sl], in1=ps[:mw, :], op=ALU.add))
            linear2(p, c, w2, DFF, D, lambda ch, tb_: (ub[ch], ub[ch][:, :]), 1, evac_x2, w2st, w2bf, 128)
        sc2.__exit__(None, None, None)
        for i in range(8):
            p.dma("act", xo[i * 128:(i + 1) * 128, :], xs[i][:, :], [xs[i]], [oD])
        if not last:
            rms_fm(p, c, xs, g1s, hs_, 8, T, tmp, rstds)
            ost = [p.sbuf([128, T], F32, f"ost{i}") for i in range(2)]
            state = {"k": 0}
            def evac_z(m0, mw, tb, ps):
                o = ost[state["k"] % 2]
                p.op("act", [ps], [o], lambda e: e.copy(out=o[:mw, tb * 512:(tb + 1) * 512], in_=ps[:mw, :]))
                if tb == 3:
                    p.dma("act", zT[m0:m0 + mw, :], o[:mw, :], [o], [oD]); state["k"] += 1
            linear2(p, c, w_in, D, NIN, lambda ch, tb: (hs_[ch], hs_[ch][:, tb * 512:(tb + 1) * 512]), 4, evac_z, wst, wbf, GW)
        p.finish([oD])
        print("k3 ninst", p.ninst)
    return nc


def build_B():
    nc = bass.Bass("TRN2", target_bir_lowering=False)
    with ExitStack() as es:
        p = Prog(nc, es)
        c = consts(p)
        p.prefix = "d_"
        with p.scope():
            co = emit_gla_gen(nc, p, c, "a_", NB=8, banks=dict(pa=(0,), pb=(1,), pc=(2,), po=3, pf=4))
            emit_k2d(nc, p, c, "d_", co=co)
        for pre, emit in (("b_", emit_k2b), ("c_", emit_k2c)):
            p.prefix = pre
            with p.scope():
                emit(nc, p, c, pre)
        p.prefix = ""
        p.finish([])
        print("B ninst", p.ninst)
    return nc


import math as _math
_NC_CACHE = {}
def _get_nc(key, builder):
    if key not in _NC_CACHE:
        _NC_CACHE[key] = builder()
    return _NC_CACHE[key]

def _run(nc, maps):
    return run_bass_kernel_spmd(nc, maps, core_ids=list(range(8))).results

def kernel(**inputs):
    inp = {k: np.asarray(v) for k, v in inputs.items()}
    x = inp["x"][0].astype(np.float32)
    NCORE = 8; Tn = S // NCORE
    xT = np.ascontiguousarray(x.T)
    g = np.ascontiguousarray(inp["norm1_g"][0].reshape(8, 128).T)
    maps = [{"xT": np.ascontiguousarray(xT[:, r * Tn:(r + 1) * Tn]), "g": g, "w": np.ascontiguousarray(inp["w_in"][0])} for r in range(NCORE)]
    res = _run(_get_nc("k1", build_k1), maps)
    zT = np.concatenate([r["zT"] for r in res], axis=1)
    for layer in range(4):
        last = layer == 3
        lam_init = 0.8 - 0.6 * _math.exp(-0.3 * layer)
        ma = host_inputs_k2a(zT, inp, layer); mb_ = host_inputs_k2b(zT, inp, layer); mc = host_inputs_k2c(zT, inp, layer); md = host_inputs_k2d(zT, inp, layer)
        mapsB = []
        for r in range(NCORE):
            m = {}
            for pre, mm in (("a_", ma), ("b_", mb_), ("c_", mc), ("d_", md)):
                m.update({pre + k: v for k, v in mm[r].items()})
            mapsB.append(m)
        rB = _run(_get_nc("B", build_B), mapsB)
        ra = [{"oT": r["a_oT"]} for r in rB]; rb = [{"O": r["b_O"]} for r in rB]; rc = [{"oT": r["c_oT"]} for r in rB]
        rd = [{"Y": r["d_Y"], "zz": r["d_zz"], "x0c": r["d_x0c"]} for r in rB]
        of = np.concatenate([ra[2 * h]["oT"] for h in range(4)], 0); ob = np.concatenate([ra[2 * h + 1]["oT"][:, ::-1] for h in range(4)], 0)
        dN, dL = host_post_k2b(rb)
        oc0 = np.concatenate([rc[2 * h]["oT"] for h in range(4)], 0); oc1 = np.concatenate([rc[2 * h + 1]["oT"] for h in range(4)], 0)
        yconv, zzf, x0c = host_post_k2d(rd)
        projs = np.ascontiguousarray(np.concatenate([inp["proj_a"][layer], inp["proj_b"][layer], inp["proj_c"][layer], inp["proj_d"][layer]], 0))
        common = {"gla_g": np.ascontiguousarray(inp["gla_norm_g"][layer].reshape(128, 1)),
                  "lamb": np.ascontiguousarray(np.broadcast_to(inp["diff_lambda"][layer].reshape(1, 256), (128, 256))),
                  "sub_g": np.ascontiguousarray(inp["diff_subln_g"][layer].reshape(128, 1)),
                  "skip": np.ascontiguousarray(inp["hy_skip"][layer].reshape(4, 128).T),
                  "projs": projs, "w_out": np.ascontiguousarray(inp["w_out"][layer]),
                  "g2": np.ascontiguousarray(inp["norm2_g"][layer].reshape(8, 128).T),
                  "w1": np.ascontiguousarray(inp["mlp_w1"][layer]), "w2": np.ascontiguousarray(inp["mlp_w2"][layer])}
        if not last:
            common["g1n"] = np.ascontiguousarray(inp["norm1_g"][layer + 1].reshape(8, 128).T)
            common["w_in"] = np.ascontiguousarray(inp["w_in"][layer + 1])
        maps = []
        for r in range(NCORE):
            cs = slice(r * Tn, (r + 1) * Tn)
            cc = lambda a: np.ascontiguousarray(a[..., cs])
            m = dict(common)
            m.update({"xT": cc(xT), "of": cc(of), "ob": cc(ob), "rT": cc(zT[1024:1536]), "dN": cc(dN), "dL": cc(dL),
                      "oc0": cc(oc0), "oc1": cc(oc1), "yconv": cc(yconv), "zzT": cc(zzf), "x0c": cc(x0c), "gates": cc(zT[6944:11040])})
            maps.append(m)
        res = _run(_get_nc(("k3", layer), lambda: build_k3(lam_init, last)), maps)
        xT = np.concatenate([r["xo"] for r in res], axis=1)
        if not last:
            zT = np.concatenate([r["zT"] for r in res], axis=1)
    return np.ascontiguousarray(xT.T)[None].astype(np.float32)
```
